# Optimizing a Trainium2 kernel written in Bass

```python
import math
import jax, jax.numpy as jnp
from jax import lax
import numpy as np

D_MODEL = 2048
BATCH = 4
SEQ = 8192
DEPTH = 2

N_MIXERS = 2
N_CONV_LAYERS = (DEPTH + 1) // 2
N_SB_LAYERS = DEPTH // 2
N_HEADS = 16
HEAD_DIM = D_MODEL // N_HEADS
CONV_WIDTH = 31
FFN_CONV_WIDTH = 3
D_FF = 5632
BLOCK_Q = 128
RMS_EPS = 1e-6
LN_EPS = 1e-5

kernel_name = "hybrid_conformer_stickbreaking_convffn_adaln"


def rms_norm(x, g):
    xf = x.astype(jnp.float32)
    y = xf * lax.rsqrt(jnp.mean(xf * xf, axis=-1, keepdims=True) + RMS_EPS)
    return (y * g.astype(jnp.float32)).astype(x.dtype)


def layer_norm(x, g, b):
    xf = x.astype(jnp.float32)
    mu = jnp.mean(xf, axis=-1, keepdims=True)
    var = jnp.mean(jnp.square(xf - mu), axis=-1, keepdims=True)
    y = (xf - mu) * lax.rsqrt(var + LN_EPS)
    return (y * g.astype(jnp.float32) + b.astype(jnp.float32)).astype(x.dtype)


def ada_ln(c, w, b):
    mod = jax.nn.silu(c) @ w + b
    shift, scale, gate = jnp.split(mod, 3, axis=-1)
    return shift[:, None, :], scale[:, None, :], gate[:, None, :]


def causal_depthwise_conv(x, w, b):
    width, ch = w.shape
    y = lax.conv_general_dilated(
        x, w[:, None, :].astype(x.dtype), window_strides=(1,),
        padding=[(width - 1, 0)], dimension_numbers=('NWC', 'WIO', 'NWC'),
        feature_group_count=ch)
    return y + b


def conformer_conv_module(h, pw1_w, pw1_b, dw_w, dw_b, ln_g, ln_b, pw2_w, pw2_b):
    u = h @ pw1_w + pw1_b
    val, gt = jnp.split(u, 2, axis=-1)
    u = val * jax.nn.sigmoid(gt)
    u = causal_depthwise_conv(u, dw_w, dw_b)
    u = jax.nn.silu(layer_norm(u, ln_g, ln_b))
    return u @ pw2_w + pw2_b


def stick_breaking_attention(q, k, v):
    bsz, nh, t_len, dh = q.shape
    n_blk = t_len // BLOCK_Q
    scale = 1.0 / math.sqrt(dh)
    qb = q.reshape(bsz, nh, n_blk, BLOCK_Q, dh).transpose(2, 0, 1, 3, 4)
    kpos = jnp.arange(t_len)

    def one_block(args):
        qi, blk = args
        z = jnp.einsum('bhqd,bhkd->bhqk', qi, k).astype(jnp.float32) * scale
        qpos = blk * BLOCK_Q + jnp.arange(BLOCK_Q)
        mask = kpos[None, :] < qpos[:, None]
        log_beta = jax.nn.log_sigmoid(z)
        log_keep = jnp.where(mask, jax.nn.log_sigmoid(-z), 0.0)
        after = lax.cumsum(log_keep, axis=3, reverse=True) - log_keep
        a = jnp.where(mask, jnp.exp(log_beta + after), 0.0)
        return jnp.einsum('bhqk,bhkd->bhqd', a.astype(v.dtype), v)

    out = lax.map(one_block, (qb, jnp.arange(n_blk)))
    return out.transpose(1, 2, 0, 3, 4).reshape(bsz, nh, t_len, dh)


def stick_breaking_mixer(h, qkv_w, o_w):
    bsz, t_len, d = h.shape
    qkv = (h @ qkv_w).reshape(bsz, t_len, 3, N_HEADS, HEAD_DIM)
    qkv = qkv.transpose(2, 0, 3, 1, 4)
    o = stick_breaking_attention(qkv[0], qkv[1], qkv[2])
    o = o.transpose(0, 2, 1, 3).reshape(bsz, t_len, d)
    return o @ o_w


def conv_ffn(h, up_w, dw_w, dw_b, down_w):
    u = causal_depthwise_conv(h @ up_w, dw_w, dw_b)
    gt, val = jnp.split(u, 2, axis=-1)
    return (jax.nn.silu(gt) * val) @ down_w


def setup_inputs(seed: int = 0) -> dict:
    key = jax.random.key(seed)
    ks = jax.random.split(key, 24)
    D, F = D_MODEL, D_FF
    nrm = jax.random.normal
    f32 = jnp.float32
    return {
        'x': nrm(ks[0], (BATCH, SEQ, D), f32),
        'c': nrm(ks[1], (BATCH, D), f32),
        'mix_norm_g': 1.0 + 0.01 * nrm(ks[2], (DEPTH, D), f32),
        'mix_mod_w': nrm(ks[3], (DEPTH, D, 3 * D), f32) * (0.5 * D ** -0.5),
        'mix_mod_b': 0.01 * nrm(ks[4], (DEPTH, 3 * D), f32),
        'cv_pw1_w': nrm(ks[5], (N_CONV_LAYERS, D, 2 * D), f32) * D ** -0.5,
        'cv_pw1_b': 0.01 * nrm(ks[6], (N_CONV_LAYERS, 2 * D), f32),
        'cv_dw_w': nrm(ks[7], (N_CONV_LAYERS, CONV_WIDTH, D), f32) * CONV_WIDTH ** -0.5,
        'cv_dw_b': 0.01 * nrm(ks[8], (N_CONV_LAYERS, D), f32),
        'cv_ln_g': 1.0 + 0.01 * nrm(ks[9], (N_CONV_LAYERS, D), f32),
        'cv_ln_b': 0.01 * nrm(ks[10], (N_CONV_LAYERS, D), f32),
        'cv_pw2_w': nrm(ks[11], (N_CONV_LAYERS, D, D), f32) * D ** -0.5,
        'cv_pw2_b': 0.01 * nrm(ks[12], (N_CONV_LAYERS, D), f32),
        'sb_qkv_w': nrm(ks[13], (N_SB_LAYERS, D, 3 * D), f32) * D ** -0.5,
        'sb_o_w': nrm(ks[14], (N_SB_LAYERS, D, D), f32) * D ** -0.5,
        'ffn_norm_g': 1.0 + 0.01 * nrm(ks[15], (DEPTH, D), f32),
        'ffn_mod_w': nrm(ks[16], (DEPTH, D, 3 * D), f32) * (0.5 * D ** -0.5),
        'ffn_mod_b': 0.01 * nrm(ks[17], (DEPTH, 3 * D), f32),
        'ffn_up_w': nrm(ks[18], (DEPTH, D, 2 * F), f32) * D ** -0.5,
        'ffn_dw_w': nrm(ks[19], (DEPTH, FFN_CONV_WIDTH, 2 * F), f32) * FFN_CONV_WIDTH ** -0.5,
        'ffn_dw_b': 0.01 * nrm(ks[20], (DEPTH, 2 * F), f32),
        'ffn_down_w': nrm(ks[21], (DEPTH, F, D), f32) * F ** -0.5,
        'final_norm_g': 1.0 + 0.01 * nrm(ks[22], (D,), f32),
    }


def reference(x, c, mix_norm_g, mix_mod_w, mix_mod_b, cv_pw1_w, cv_pw1_b, cv_dw_w,
              cv_dw_b, cv_ln_g, cv_ln_b, cv_pw2_w, cv_pw2_b, sb_qkv_w, sb_o_w,
              ffn_norm_g, ffn_mod_w, ffn_mod_b, ffn_up_w, ffn_dw_w, ffn_dw_b,
              ffn_down_w, final_norm_g):
    for i in range(DEPTH):
        j = i // N_MIXERS
        shift, scale, gate = ada_ln(c, mix_mod_w[i], mix_mod_b[i])
        h = rms_norm(x, mix_norm_g[i]) * (1.0 + scale) + shift
        if i % N_MIXERS == 0:
            y = conformer_conv_module(h, cv_pw1_w[j], cv_pw1_b[j], cv_dw_w[j], cv_dw_b[j],
                                      cv_ln_g[j], cv_ln_b[j], cv_pw2_w[j], cv_pw2_b[j])
        else:
            y = stick_breaking_mixer(h, sb_qkv_w[j], sb_o_w[j])
        x = x + gate * y
        shift, scale, gate = ada_ln(c, ffn_mod_w[i], ffn_mod_b[i])
        h = rms_norm(x, ffn_norm_g[i]) * (1.0 + scale) + shift
        x = x + gate * conv_ffn(h, ffn_up_w[i], ffn_dw_w[i], ffn_dw_b[i], ffn_down_w[i])
    return rms_norm(x, final_norm_g)
```

```python
import numpy as np
import ml_dtypes
import concourse.bass as bass
import concourse.mybir as mybir
from concourse.bass_utils import run_bass_kernel_spmd
from contextlib import ExitStack

F32 = mybir.dt.float32
BF16 = mybir.dt.bfloat16
AF = mybir.ActivationFunctionType
ALU = mybir.AluOpType

D = 2048
KC = 16
FF = 5632
FC = 44
CW = 31
NH = 16
SEQ = 8192
HALF = 4096
PRE = 64
NLOC = HALF + PRE
RMS_EPS = 1e-6
LN_EPS = 1e-5
PAIRS = [[0, 1], [2, 3], [4, 5], [6, 7]]
QUADS = [[0, 1, 2, 3], [4, 5, 6, 7]]
NP2 = NLOC


class Res:
    __slots__ = ("name", "w", "r")

    def __init__(self, name=""):
        self.name = name
        self.w = None
        self.r = {}


class Eng:
    def __init__(self, P, name, eng, is_pe=False):
        self.name = name
        self.eng = eng
        self.is_pe = is_pe
        self.sem = P.new_sem("e_" + name)
        self.count = 0
        self.waited = {}


class Prog:
    def __init__(self, nc):
        self.nc = nc
        self.sems = {}
        self.nsem = 0
        self.pe = Eng(self, "pe", nc.tensor, is_pe=True)
        self.act = Eng(self, "act", nc.scalar)
        self.dve = Eng(self, "dve", nc.vector)
        self.pool = Eng(self, "pool", nc.gpsimd)
        self.sp = Eng(self, "sp", nc.sync)
        self.dma_vals = {}

    def new_sem(self, name):
        h = self.nc.semaphore(name).__enter__()
        key = "s%d_%s" % (self.nsem, name)
        self.nsem += 1
        self.sems[key] = h
        return key

    def _wait(self, E, needs):
        best = {}
        for (k, v) in needs:
            if best.get(k, 0) < v:
                best[k] = v
        for k, v in best.items():
            if k == E.sem:
                if E.is_pe:
                    continue
                if E.count - v >= 2:
                    continue
            if E.waited.get(k, 0) >= v:
                continue
            E.eng.wait_ge(self.sems[k], v)
            E.waited[k] = v

    @staticmethod
    def _deps(reads, writes):
        needs = []
        for r in reads:
            if r.w is not None:
                needs.append(r.w)
        for w in writes:
            if w.w is not None:
                needs.append(w.w)
            needs.extend(w.r.items())
        return needs

    @staticmethod
    def _mark(tok, reads, writes):
        k, v = tok
        for w in writes:
            w.w = tok
            w.r = {}
        for r in reads:
            if r.w is tok:
                continue
            if r.r.get(k, 0) < v:
                r.r[k] = v

    def op(self, E, fn, reads=(), writes=()):
        self._wait(E, self._deps(reads, writes))
        ins = fn()
        ins.then_inc(self.sems[E.sem], 1)
        E.count += 1
        self._mark((E.sem, E.count), reads, writes)
        return ins

    def group(self, E, fns, reads=(), writes=()):
        self._wait(E, self._deps(reads, writes))
        ins = None
        for fn in fns:
            ins = fn()
        ins.then_inc(self.sems[E.sem], 1)
        E.count += 1
        self._mark((E.sem, E.count), reads, writes)
        return ins

    def dma(self, Q, out, in_, sem, reads=(), writes=()):
        self._wait(Q, self._deps(reads, writes))
        ins = Q.eng.dma_start(out=out, in_=in_)
        ins.then_inc(self.sems[sem], 16)
        v = self.dma_vals.get(sem, 0) + 16
        self.dma_vals[sem] = v
        self._mark((sem, v), reads, writes)
        return ins

    def collective(self, kind, ins, outs, groups, sem, reads=(), writes=(), op=None):
        Q = self.pool
        self._wait(Q, self._deps(reads, writes))
        ins_ = self.nc.gpsimd.collective_compute(kind, op if op is not None else ALU.bypass, replica_groups=groups,
                                                 ins=ins, outs=outs)
        ins_.then_inc(self.sems[sem], 1)
        v = self.dma_vals.get(sem, 0) + 1
        self.dma_vals[sem] = v
        self._mark((sem, v), reads, writes)

    def wait_all(self, E, ress):
        needs = []
        for r in ress:
            if r.w is not None:
                needs.append(r.w)
            needs.extend(r.r.items())
        self._wait(E, needs)


class Buf:
    def __init__(self, t, n, name):
        self.t = t
        self.r = [Res("%s%d" % (name, i)) for i in range(n)]


class Builder:
    def __init__(self, nc):
        self.nc = nc
        self.P = Prog(nc)
        P = self.P
        self.stack = None
        self.uid = 0
        self.bg_sems = set()
        self.banks = []
        for i in range(8):
            t = nc.alloc_psum_tensor("bank%d" % i, [128, 512], F32)
            self.banks.append((t, Res("bank%d" % i)))
        self.bank_i = 0
        self.dsems = {}
        self.ident = self.sb("ident", [128, 128], F32)
        self.ones_bf = self.sb("ones_bf", [128, 128], BF16)
        self.r_const = Res("const")
        ones_f = self.sb("ones_f", [128, 128], F32)
        P.op(P.pool, lambda: nc.gpsimd.memset(ones_f[:], 1.0), writes=[self.r_const])
        P.op(P.pool, lambda: nc.gpsimd.affine_select(self.ident[:], ones_f[:], [[1, 128]], ALU.is_equal, 0.0,
                                                     base=0, channel_multiplier=-1),
             reads=[self.r_const], writes=[self.r_const])
        P.op(P.pool, lambda: nc.gpsimd.memset(self.ones_bf[:], 1.0), writes=[self.r_const])
        self.ones_f = ones_f
        self.wstage = [(self.sb("wstage%d" % i, [128, 2048], BF16), Res("wstage%d" % i), self.dsem("wstage%d" % i)) for i in range(2)]
        self.wstage_i = 0
        pid = nc.sync.partition_id()
        self.par = pid % 2
        self.bsel = (pid % 4) // 2
        self.bg = []
        self.bg_i = 0

    def bg_add_weight(self, wi, w):
        P = self.P
        sem = self.dsem("agw%d" % wi)
        self.bg_sems.add(sem)
        Ks, wb, nb = w["Ks"], w["wb"], w["nb"]
        N = wb * nb
        sres = [Res("wsh%d_%d" % (wi, k)) for k in range(2)]

        def unit(r0, c0, wd):
            t, r, sm = self.wstage[self.wstage_i % 2]
            k = self.wstage_i % 2
            self.wstage_i += 1
            P.dma(P.pool, t[:, :wd], w["shard"][r0:r0 + 128, c0:c0 + wd], sm, writes=[r])
            P.dma(P.sp, w["wsh"][c0 // wb:(c0 + wd) // wb, r0:r0 + 128, :].rearrange("b p n -> p b n"),
                  t[:, :wd].rearrange("p (b n) -> p b n", n=wb), sm, reads=[r], writes=[sres[k]])
        for r0 in range(0, Ks, 128):
            for c0 in range(0, N, 2048):
                wd = min(2048, N - c0)
                self.bg.append(lambda r0=r0, c0=c0, wd=wd: unit(r0, c0, wd))

        def gather(bi):
            P.collective("AllGather", [w["wsh"][bi]], [w["full"][bi]], QUADS, sem, reads=sres, writes=[w["res"]])
        for bi in range(nb):
            self.bg.append(lambda bi=bi: gather(bi))

    def bg_pump(self, n=None):
        end = len(self.bg) if n is None else min(len(self.bg), self.bg_i + n)
        while self.bg_i < end:
            self.bg[self.bg_i]()
            self.bg_i += 1

    def sb(self, name, shape, dt):
        if self.stack is None:
            return self.nc.alloc_sbuf_tensor(name, shape, dt)
        self.uid += 1
        return self.stack.enter_context(self.nc.sbuf_tensor("%s_%d" % (name, self.uid), shape, dt))

    def begin_phase(self):
        self.stack = ExitStack()

    def barrier(self, full=False):
        P = self.P
        engs = [P.pe, P.act, P.dve, P.pool, P.sp]
        needs = [(e.sem, e.count) for e in engs if e.count > 0] + \
            [kv for kv in P.dma_vals.items() if full or kv[0] not in self.bg_sems]
        for e in engs:
            P._wait(e, [x for x in needs if x[0] != e.sem])

    def end_phase(self):
        self.barrier()
        self.stack.close()
        self.stack = None

    def dsem(self, name):
        if name not in self.dsems:
            self.dsems[name] = self.P.new_sem("d_" + name)
        return self.dsems[name]

    def bank(self):
        b = self.banks[self.bank_i]
        self.bank_i = (self.bank_i + 1) % 8
        return b

    def mm(self, bank, pairs, n, reads, m=128):
        nc = self.nc
        t, r = bank
        fns = []
        last = len(pairs) - 1
        for i, (l, rh) in enumerate(pairs):
            fns.append(lambda l=l, rh=rh, i=i: nc.tensor.matmul(t[:m, :n], l, rh, start=(i == 0), stop=(i == last)))
        self.P.group(self.P.pe, fns, reads=reads, writes=[r])

    def A(self, out, in_, func, reads, writes, bias=0.0, scale=1.0):
        nc = self.nc
        return self.P.op(self.P.act, lambda: nc.scalar.activation(out, in_, func, bias=bias, scale=scale),
                         reads=reads, writes=writes)

    def TT(self, out, a, b, op, reads, writes, eng=None):
        E = eng or self.P.dve
        return self.P.op(E, lambda: E.eng.tensor_tensor(out, a, b, op), reads=reads, writes=writes)

    def TS(self, out, a, s1, s2, op0, op1, reads, writes, eng=None):
        E = eng or self.P.dve
        if op1 is None:
            return self.P.op(E, lambda: E.eng.tensor_scalar(out, a, s1, None, op0), reads=reads, writes=writes)
        return self.P.op(E, lambda: E.eng.tensor_scalar(out, a, s1, s2, op0, op1), reads=reads, writes=writes)

    def STT(self, out, a, s, b, op0, op1, reads, writes):
        nc = self.nc
        return self.P.op(self.P.dve, lambda: nc.vector.scalar_tensor_tensor(out, a, s, b, op0, op1),
                         reads=reads, writes=writes)

    def CP(self, out, in_, reads, writes, eng=None):
        E = eng or self.P.dve
        return self.P.op(E, lambda: E.eng.tensor_copy(out, in_), reads=reads, writes=writes)

    def init_wstream(self, nslots=3):
        self.wslots = []
        for i in range(nslots):
            t = self.sb("wslot%d" % i, [128, 8192], BF16)
            self.wslots.append((t, Res("wslot%d" % i), self.dsem("wslot%d" % i)))
        self.wplan = []
        self.wnext_issue = 0
        self.wnext_use = 0

    def wplan_add(self, w, k_rows, pieces):
        kc = k_rows // 128
        tot = sum(p[-1] for p in pieces)
        assert kc * tot <= 8192
        self.wplan.append((w, kc, tot, pieces))

    def _wissue(self, upto):
        P = self.P
        while self.wnext_issue < min(upto, len(self.wplan)):
            i = self.wnext_issue
            w, kc, tot, pieces = self.wplan[i]
            t, r, s = self.wslots[i % len(self.wslots)]
            dst = t[:, 0:kc * tot].rearrange("p (kc n) -> p kc n", kc=kc)
            wb = w["wb"]
            o = 0
            for pc in pieces:
                if len(pc) == 2:
                    c0, ncols = pc
                    bi, off = c0 // wb, c0 % wb
                    assert off + ncols <= wb
                    src = w["full"][bi].rearrange("(kc p) n -> p kc n", p=128)[:, :, off:off + ncols]
                else:
                    bi, off, ncols = pc
                    src = w["full"][bass.ds(bi, 1)].rearrange("o (kc p) n -> p (o kc) n", p=128)[:, :, off:off + ncols]
                P.dma(P.pool, dst[:, :, o:o + ncols], src, s, reads=[w["res"]], writes=[r])
                o += ncols
            self.wnext_issue += 1

    def wget(self):
        i = self.wnext_use
        self._wissue(i + len(self.wslots))
        w, kc, tot, pieces = self.wplan[i]
        t, r, s = self.wslots[i % len(self.wslots)]
        self.wnext_use += 1
        return t[:, 0:kc * tot].rearrange("p (kc n) -> p kc n", kc=kc), r

    def rms_mod(self, xT, hT, n, a_vec, sh_vec):
        nc, P = self.nc, self.P
        sq = self.tmpbf
        bk = self.bank()
        for c in range(KC):
            self.A(sq.t[:, c % 2, :n], xT.t[:, c, :n], AF.Square, reads=[xT.r[c]], writes=[sq.r[c % 2]])
            nc_ = nc
            P.group(P.pe, [lambda c=c: nc_.tensor.matmul(bk[0][:, :n], self.ones_bf[:], sq.t[:, c % 2, :n],
                                                          start=(c == 0), stop=(c == KC - 1))],
                    reads=[sq.r[c % 2], self.r_const], writes=[bk[1]])
        rstd = self.rstd
        self.A(rstd.t[:, 0, :n], bk[0][:, :n], AF.Ln, reads=[bk[1]], writes=[rstd.r[0]], bias=self.eps_rms[:, 0:1], scale=1.0)
        self.A(rstd.t[:, 0, :n], rstd.t[:, 0, :n], AF.Exp, reads=[rstd.r[0]], writes=[rstd.r[0]], scale=-0.5)
        for c in range(KC):
            tm = self.tmpf
            i = c % 2
            self.TT(tm.t[:, i, :n], xT.t[:, c, :n], rstd.t[:, 0, :n], ALU.mult, reads=[xT.r[c], rstd.r[0]], writes=[tm.r[i]])
            self.A(hT.t[:, c, :n], tm.t[:, i, :n], AF.Identity, reads=[tm.r[i], self.r_vec], writes=[hT.r[c]],
                   bias=(sh_vec[:, c:c + 1] if sh_vec is not None else 0.0), scale=a_vec[:, c:c + 1])

    def linear(self, act, kc_n, n, slots, epilogue):
        for ids in slots:
            wv, wr = self.wget()
            for jj, cid in enumerate(ids):
                bk = self.bank()
                pairs = [(wv[:, k, jj * 128:(jj + 1) * 128], act.t[:, k, :n]) for k in range(kc_n)]
                self.mm(bk, pairs, n, reads=[wr] + [act.r[k] for k in range(kc_n)])
                epilogue(bk, cid)

    def load_rows_T(self, name, ap2d, rows, ncols):
        nc, P = self.nc, self.P
        nch = ncols // 128
        out = self.sb("m_" + name, [128, nch, rows], F32)
        r = self.r_vec
        stg, rs = self.vec_stage, self.vec_stage_r
        for c0 in range(0, ncols, 2048):
            w = min(2048, ncols - c0)
            if isinstance(ap2d, list):
                ro = 0
                for (apx, nr) in ap2d:
                    P.dma(P.sp, stg[ro:ro + nr, :w], apx[0:nr, c0:c0 + w], self.xrow[0][2], writes=[rs])
                    ro += nr
            else:
                P.dma(P.sp, stg[:rows, :w], ap2d[0:rows, c0:c0 + w], self.xrow[0][2], writes=[rs])
            for cc in range(w // 128):
                bk = self.bank()
                P.group(P.pe, [lambda cc=cc, bk=bk: nc.tensor.transpose(bk[0][:, :rows], stg[:rows, cc * 128:(cc + 1) * 128],
                                                                        self.ident[:rows, :rows])],
                        reads=[rs, self.r_const], writes=[bk[1]])
                self.CP(out[:, c0 // 128 + cc, :], bk[0][:, :rows], reads=[bk[1]], writes=[r])
        return out

    def common_init(self):
        nc, P = self.nc, self.P
        self.xrow = [(self.sb("xrow0", [128, D], F32), Res("xrow0"), self.dsem("xrow0"))] * 2
        self.xrow_i = 0
        self.vec_stage = self.xrow[0][0]
        self.vec_stage_r = self.xrow[0][1]
        self.r_vec = Res("vecs")
        self.sqb = Buf(self.sb("sqb", [128, 2, 512], BF16), 2, "sqb")
        self.tmpf = Buf(self.sb("tmpf", [128, 2, 512], F32), 2, "tmpf")
        self.tmpbf = Buf(self.sb("tmpbf", [128, 2, 512], BF16), 2, "tmpbf")
        self.rstd = Buf(self.sb("rstd", [128, 1, 512], F32), 1, "rstd")
        self.eps_rms = self.sb("eps_rms", [128, 2], F32)
        P.op(P.pool, lambda: nc.gpsimd.memset(self.eps_rms[:, 0:1], float(D * RMS_EPS)), writes=[self.r_const])
        P.op(P.pool, lambda: nc.gpsimd.memset(self.eps_rms[:, 1:2], float(LN_EPS)), writes=[self.r_const])

    def mod_vecs(self, mv, m, gvec, name):
        a = self.sb("a_" + name, [128, KC], F32)
        r = self.r_vec
        self.TS(a[:, :], mv[:, :, 3 * m + 1], 1.0, float(np.sqrt(D)), ALU.add, ALU.mult, reads=[r], writes=[r])
        self.TT(a[:, :], a[:, :], gvec, ALU.mult, reads=[r], writes=[r])
        return a, mv[:, :, 3 * m], mv[:, :, 3 * m + 2]

    def load_mods(self, io):
        nc = self.nc
        bsel = self.bsel
        src = [(io["msum"][m, bass.ds(bsel, 1), :].rearrange("o (t k) -> (o t) k", t=3), 3) for m in range(4)]
        mv = self.load_rows_T("mods", src, 12, D)
        bv = self.load_rows_T("modb", io["mb"].rearrange("m (t k) -> (m t) k", t=3), 12, D)
        self.TT(mv[:, :, :], mv[:, :, :], bv[:, :, :], ALU.add, reads=[self.r_vec], writes=[self.r_vec])
        return mv

    def load_mods_old(self, modall):
        nc, P = self.nc, self.P
        out = self.sb("m_mods", [128, KC, 12], F32)
        stg, rs = self.vec_stage, self.vec_stage_r
        P.dma(P.sp, stg[:12, :].rearrange("q (r k) -> q r k", r=2), modall.rearrange("r m t k -> (m t) r k"),
              self.xrow[0][2], writes=[rs])
        for cc in range(KC):
            bk = self.bank()
            P.group(P.pe, [lambda cc=cc, bk=bk: nc.tensor.transpose(bk[0][:, :12], stg[:12, cc * 128:(cc + 1) * 128],
                                                                    self.ident[:12, :12])],
                    reads=[rs, self.r_const], writes=[bk[1]])
            self.CP(out[:, cc, :], bk[0][:, :12], reads=[bk[1]], writes=[self.r_vec])
        return out

    def load_x_tile(self, xin, start, n, xT):
        nc, P = self.nc, self.P
        nb = (n + 127) // 128
        for b in range(nb):
            nt = min(128, n - b * 128)
            xr, rr, sm = self.xrow[self.xrow_i % 2]
            self.xrow_i += 1
            P.dma(P.sp, xr[:nt, :], xin[start + b * 128:start + b * 128 + nt, :], sm, writes=[rr])
            for g in range(4):
                bk = self.bank()
                fns = []
                for q in range(4):
                    c = 4 * g + q
                    fns.append(lambda q=q, c=c, bk=bk: nc.tensor.transpose(bk[0][:, q * 128:q * 128 + nt],
                                                                           xr[:nt, c * 128:(c + 1) * 128], self.ident[:nt, :nt]))
                P.group(P.pe, fns, reads=[rr, self.r_const], writes=[bk[1]])
                src = bk[0][:, :].rearrange("p (q t) -> p q t", q=4)[:, :, :nt]
                dst = xT.t[:, 4 * g:4 * g + 4, b * 128:b * 128 + nt]
                eng = self.P.act if (g % 2 == 0) else self.P.dve
                if eng is self.P.act:
                    P.op(P.act, lambda dst=dst, src=src: nc.scalar.copy(dst, src), reads=[bk[1]],
                         writes=[xT.r[4 * g + q] for q in range(4)])
                else:
                    self.CP(dst, src, reads=[bk[1]], writes=[xT.r[4 * g + q] for q in range(4)])

    def phase_a(self, io, tiles):
        nc, P = self.nc, self.P
        self.begin_phase()
        self.common_init()
        self.init_wstream(3)
        mv = self.load_mods(io)
        sv = self.load_rows_T("svA", io["svecs"], 7, D)
        pw1b = self.load_rows_T("pw1b", io["pw1b"], 2, D)
        dww = self.load_rows_T("dww", io["dww"], CW, D)
        fdw = self.load_rows_T("fdw", io["ffn_dw"], 4, 2 * FF)
        pm = self.sb("pm_sb", [128, 1], F32)
        P.dma(P.sp, pm[:, :], io["pm"][:, :], self.dsem("pm"), writes=[self.r_vec])
        rv = self.r_vec
        a0, sh0, g0 = self.mod_vecs(mv, 0, sv[:, :, 0], "mix0")
        a1, sh1, g1 = self.mod_vecs(mv, 1, sv[:, :, 1], "ffn0")
        a2, sh2, g2 = self.mod_vecs(mv, 2, sv[:, :, 2], "mix1")
        gb2 = self.sb("gb2", [128, KC], F32)
        self.TT(gb2[:, :], g0, sv[:, :, 6], ALU.mult, reads=[rv], writes=[rv])

        xT = Buf(self.sb("xT", [128, KC, 512], F32), KC, "xT")
        hT = Buf(self.sb("hT", [128, KC, 512], BF16), KC, "hT")
        uT = Buf(self.sb("uT", [128, KC, CW - 1 + 512], BF16), KC, "uT")
        gTt = self.sb("gTt", [128, FC, 512], BF16)
        gT = Buf(gTt, FC, "gT")
        cvt = gTt[:, 0:32, :].bitcast(F32).rearrange("p a b -> p (a b)").rearrange("p (c n) -> p c n", c=KC)
        Tg = Buf(self.sb("Tg", [128, 2, 512], F32), 2, "Tg")
        Tv = Buf(self.sb("Tv", [128, 2, 512], F32), 2, "Tv")
        Hff = self.sb("Hff", [128, 2 * FC, 2], F32)
        rH = [Res("Hff%d" % i) for i in range(2 * FC)]
        stat = Buf(self.sb("stat", [128, 3, 512], F32), 3, "stat")
        P.op(P.pool, lambda: nc.gpsimd.memset(Hff[:, :, :], 0.0), writes=rH)
        P.op(P.pool, lambda: nc.gpsimd.memset(uT.t[:, :, 0:CW - 1], 0.0), writes=uT.r)

        for _ in tiles:
            for i in range(8):
                self.wplan_add(io["pw1_w"], D, [(256 * i, 256), (D + 256 * i, 256)])
            for i in range(4):
                self.wplan_add(io["pw2_w"], D, [(512 * i, 512)])
            for i in range(22):
                self.wplan_add(io["up_w"], D, [(256 * i, 256), (FF + 256 * i, 256)])
            for i in range(16):
                self.wplan_add(io["down_w"], FF, [(128 * i, 128)])

        x1T_d, h1T_d = io["x1T"], io["h1T"]
        osem_x = [self.dsem("ox0"), self.dsem("ox1")]
        osem_h = [self.dsem("oh0"), self.dsem("oh1")]

        per_tile = (getattr(self, "n_bg_rest", 0) + len(tiles) - 2) // max(1, len(tiles) - 1)
        for ti, (start, n) in enumerate(tiles):
            first = (ti == 0)
            self.bg_pump(per_tile)
            self.load_x_tile(io["xin"], start, n, xT)
            self.rms_mod(xT, hT, n, a0, sh0)
            if first:
                pass
            sg = self.tmpf

            def ep_pw1(bk, cid, n=n, first=first):
                if cid < KC:
                    self._valbank[cid] = bk
                else:
                    j = cid - KC
                    vb = self._valbank.pop(j)
                    i = j % 2
                    self.A(sg.t[:, i, :n], bk[0][:, :n], AF.Sigmoid, reads=[bk[1], rv], writes=[sg.r[i]],
                           bias=pw1b[:, j, 1:2], scale=1.0)
                    self.STT(uT.t[:, j, CW - 1:CW - 1 + n], vb[0][:, :n], pw1b[:, j, 0:1], sg.t[:, i, :n], ALU.add, ALU.mult,
                             reads=[vb[1], sg.r[i], rv], writes=[uT.r[j]])
            self._valbank = {}
            slots = [[2 * i, 2 * i + 1, KC + 2 * i, KC + 2 * i + 1] for i in range(8)]
            self.linear(hT, KC, n, slots, ep_pw1)
            if first:
                self.TS(uT.t[:, :, CW - 1:CW - 1 + PRE], uT.t[:, :, CW - 1:CW - 1 + PRE], pm[:, 0:1], None, ALU.mult, None,
                        reads=uT.r + [rv], writes=uT.r)
            bk_m = self.bank()
            bk_s = self.bank()
            cv_r = [[gT.r[2 * c], gT.r[2 * c + 1]] for c in range(KC)]
            for c in range(KC):
                self.A(cvt[:, c, :n], uT.t[:, c, 0:n], AF.Identity, reads=[uT.r[c], rv], writes=cv_r[c],
                       bias=sv[:, c, 3:4], scale=dww[:, c, 0:1])
            for k in range(1, CW):
                for c in range(KC):
                    self.STT(cvt[:, c, :n], uT.t[:, c, k:k + n], dww[:, c, k:k + 1], cvt[:, c, :n], ALU.mult, ALU.add,
                             reads=[uT.r[c], rv] + cv_r[c], writes=cv_r[c])
            for c in range(KC):
                i = c % 2
                self.A(self.tmpbf.t[:, i, :n], cvt[:, c, :n], AF.Identity, reads=cv_r[c], writes=[self.tmpbf.r[i]])
                P.group(P.pe, [lambda c=c, i=i: nc.tensor.matmul(bk_m[0][:, :n], self.ones_bf[:], self.tmpbf.t[:, i, :n],
                                                                 start=(c == 0), stop=(c == KC - 1))],
                        reads=[self.tmpbf.r[i], self.r_const], writes=[bk_m[1]])
                self.A(self.sqb.t[:, i, :n], cvt[:, c, :n], AF.Square, reads=cv_r[c], writes=[self.sqb.r[i]])
                P.group(P.pe, [lambda c=c, i=i: nc.tensor.matmul(bk_s[0][:, :n], self.ones_bf[:], self.sqb.t[:, i, :n],
                                                                 start=(c == 0), stop=(c == KC - 1))],
                        reads=[self.sqb.r[i], self.r_const], writes=[bk_s[1]])
            mean, var, rs = stat.t[:, 0, :n], stat.t[:, 1, :n], stat.t[:, 2, :n]
            self.A(mean, bk_m[0][:, :n], AF.Identity, reads=[bk_m[1]], writes=[stat.r[0]], scale=1.0 / D)
            self.TT(var, mean, mean, ALU.mult, reads=[stat.r[0]], writes=[stat.r[1]])
            self.STT(var, bk_s[0][:, :n], 1.0 / D, var, ALU.mult, ALU.subtract, reads=[bk_s[1], stat.r[1]], writes=[stat.r[1]])
            self.A(rs, var, AF.Ln, reads=[stat.r[1], self.r_const], writes=[stat.r[2]], bias=self.eps_rms[:, 1:2], scale=1.0)
            self.A(rs, rs, AF.Exp, reads=[stat.r[2]], writes=[stat.r[2]], scale=-0.5)
            for c in range(KC):
                i = c % 2
                self.TT(sg.t[:, i, :n], cvt[:, c, :n], mean, ALU.subtract, reads=cv_r[c] + [stat.r[0]], writes=[sg.r[i]])
                self.TT(sg.t[:, i, :n], sg.t[:, i, :n], rs, ALU.mult, reads=[sg.r[i], stat.r[2]], writes=[sg.r[i]])
                self.A(hT.t[:, c, :n], sg.t[:, i, :n], AF.Silu, reads=[sg.r[i], rv], writes=[hT.r[c]],
                       bias=sv[:, c, 5:6], scale=sv[:, c, 4:5])
            self.CP(uT.t[:, :, 0:CW - 1], uT.t[:, :, n:n + CW - 1], reads=uT.r, writes=uT.r)

            def ep_pw2(bk, cid, n=n):
                i = cid % 2
                self.A(sg.t[:, i, :n], bk[0][:, :n], AF.Identity, reads=[bk[1], rv], writes=[sg.r[i]],
                       bias=gb2[:, cid:cid + 1], scale=g0[:, cid:cid + 1])
                self.TT(xT.t[:, cid, :n], xT.t[:, cid, :n], sg.t[:, i, :n], ALU.add, reads=[xT.r[cid], sg.r[i]], writes=[xT.r[cid]])
            self.linear(hT, KC, n, [[4 * i + q for q in range(4)] for i in range(4)], ep_pw2)

            self.rms_mod(xT, hT, n, a1, sh1)
            if first:
                self.TS(hT.t[:, :, 0:PRE], hT.t[:, :, 0:PRE], pm[:, 0:1], None, ALU.mult, None, reads=hT.r + [rv], writes=hT.r)
            self.ffn(hT, xT, gT, Tg, Tv, Hff, rH, fdw, g1, n)

            k = ti % 2
            P.dma(P.sp, x1T_d[:, :, start:start + n].rearrange("c p n -> p c n"), xT.t[:, :, :n], osem_x[k], reads=xT.r)
            self.rms_mod(xT, hT, n, a2, sh2)
            P.dma(P.sp, h1T_d[:, :, start:start + n].rearrange("c p n -> p c n"), hT.t[:, :, :n], osem_h[k], reads=hT.r)
        self.end_phase()

    def ffn(self, hT, xT, gT, Tg, Tv, Hff, rH, fdw, gate, n):
        nc, P = self.nc, self.P
        rv = self.r_vec

        def conv3(bk, ch, T, i):
            p = bk[0]
            w0, w1, w2, b = fdw[:, ch, 0:1], fdw[:, ch, 1:2], fdw[:, ch, 2:3], fdw[:, ch, 3:4]
            t = T.t[:, i, :]
            self.A(t[:, :n], p[:, :n], AF.Identity, reads=[bk[1], rv], writes=[T.r[i]], bias=b, scale=w2)
            self.STT(t[:, 1:n], p[:, 0:n - 1], w1, t[:, 1:n], ALU.mult, ALU.add, reads=[bk[1], rv, T.r[i]], writes=[T.r[i]])
            self.STT(t[:, 2:n], p[:, 0:n - 2], w0, t[:, 2:n], ALU.mult, ALU.add, reads=[bk[1], rv, T.r[i]], writes=[T.r[i]])
            h = Hff[:, ch, :]
            self.STT(t[:, 0:1], h[:, 1:2], w1, t[:, 0:1], ALU.mult, ALU.add, reads=[rH[ch], rv, T.r[i]], writes=[T.r[i]])
            self.STT(t[:, 0:2], h[:, 0:2], w0, t[:, 0:2], ALU.mult, ALU.add, reads=[rH[ch], rv, T.r[i]], writes=[T.r[i]])
            P.op(P.act, lambda: nc.scalar.copy(h[:, 0:2], p[:, n - 2:n]), reads=[bk[1], T.r[i]], writes=[rH[ch]])

        def ep_up(bk, cid, n=n):
            if cid < FC:
                conv3(bk, cid, Tg, cid % 2)
            else:
                j = cid - FC
                i = j % 2
                conv3(bk, cid, Tv, i)
                self.A(Tg.t[:, i, :n], Tg.t[:, i, :n], AF.Silu, reads=[Tg.r[i]], writes=[Tg.r[i]])
                self.TT(gT.t[:, j, :n], Tg.t[:, i, :n], Tv.t[:, i, :n], ALU.mult, reads=[Tg.r[i], Tv.r[i]], writes=[gT.r[j]])
        slots = [[2 * i, 2 * i + 1, FC + 2 * i, FC + 2 * i + 1] for i in range(22)]
        self.linear(hT, KC, n, slots, ep_up)

        def ep_down(bk, cid, n=n):
            self.STT(xT.t[:, cid, :n], bk[0][:, :n], gate[:, cid:cid + 1], xT.t[:, cid, :n], ALU.mult, ALU.add,
                     reads=[bk[1], rv, xT.r[cid]], writes=[xT.r[cid]])
        self.linear(gT, FC, n, [[i] for i in range(16)], ep_down)


def phase_mods(self, io):
    nc, P = self.nc, self.P
    self.begin_phase()
    self.common_init()
    self.init_wstream(3)
    rv = self.r_vec
    cT = self.load_rows_T("cq", io["cq"], 2, 512)
    sc = self.sb("sc", [128, 4, 2], BF16)
    self.A(sc[:, :, :], cT[:, :, :], AF.Silu, reads=[rv], writes=[rv])
    NM = 3 * D
    orow = self.sb("orow", [2, NM], F32)
    r_o = Res("orow")
    sem_o = self.dsem("orow")
    wi = 0
    for m in range(4):
        for g3 in range(3):
            t, r, sm = self.wslots[wi % 3]
            wi += 1
            wv = t[:, :].rearrange("p (kc n) -> p kc n", kc=4)
            P.dma(P.pool, wv, io["mw"][m].rearrange("(kc p) n -> p kc n", p=128)[:, :, g3 * 2048:(g3 + 1) * 2048], sm, writes=[r])
            for g in range(4):
                bk = self.bank()
                pairs = [(sc[:, k, :], wv[:, k, g * 512:(g + 1) * 512]) for k in range(4)]
                self.mm(bk, pairs, 512, reads=[r, rv], m=2)
                o = g3 * 2048 + g * 512
                self.CP(orow[0:2, o:o + 512], bk[0][0:2, :512], reads=[bk[1]], writes=[r_o])
        P.dma(P.sp, io["part"][m], orow[0:2, :], sem_o, reads=[r_o])
    P.collective("AllReduce", [io["part"].rearrange("m b k -> (m b) k")], [io["msum"].rearrange("m b k -> (m b) k")], QUADS,
                 self.dsem("ar_mods"), reads=[r_o], op=ALU.add)
    self.end_phase()
    self.barrier(full=True)


def phase_w(self, wl):
    nc, P = self.nc, self.P
    self.begin_phase()
    self.init_wstream(3)
    i = 0
    for wi, w in enumerate(wl):
        sem = self.dsem("agw%d" % wi)
        self.bg_sems.add(sem)
        Ks, wb, nb = w["Ks"], w["wb"], w["nb"]
        N = wb * nb
        sres = [Res("wsh%d_%d" % (wi, k)) for k in range(3)]
        for r0 in range(0, Ks, 128):
            for c0 in range(0, N, 8192):
                wd = min(8192, N - c0)
                t, r, sm = self.wslots[i % 3]
                P.dma(P.pool, t[:, :wd], w["shard"][r0:r0 + 128, c0:c0 + wd], sm, writes=[r])
                P.dma(P.sp, w["wsh"][c0 // wb:(c0 + wd) // wb, r0:r0 + 128, :].rearrange("b p n -> p b n"),
                      t[:, :wd].rearrange("p (b n) -> p b n", n=wb), sm, reads=[r], writes=[sres[i % 3]])
                i += 1
        for bi in range(nb):
            P.collective("AllGather", [w["wsh"][bi]], [w["full"][bi]], QUADS, sem, reads=sres, writes=[w["res"]])
    self.end_phase()


def phase_qkv(self, io, ntile, nh):
    nc, P = self.nc, self.P
    self.begin_phase()
    self.init_wstream(3)
    ns = nh // 4
    hTs = [Buf(self.sb("hq%d" % i, [128, KC, 512], BF16), KC, "hq%d" % i) for i in range(2)]
    hsem = [self.dsem("hq0"), self.dsem("hq1")]
    qst = Buf(self.sb("qst", [128, nh, 512], BF16), nh, "qst")
    kst = Buf(self.sb("kst", [128, nh, 512], BF16), nh, "kst")
    vst = Buf(self.sb("vst", [128, 4, nh * 128], BF16), 4, "vst")
    sq, sk, sv_ = self.dsem("oq"), self.dsem("ok"), self.dsem("ov")
    par = nc.gpsimd.partition_id() % 2
    for _ in range(2 * ntile):
        for which in range(3):
            for i in range(ns):
                self.wplan_add(io["qkv"], D, [(par + 2 * which, 512 * i, 512)])
    it = 0
    for s in range(2):
        for t in range(ntile):
            p0 = (s * ntile + t) * 512
            lc = PRE + 512 * t
            hT = hTs[it % 2]
            P.dma(P.sp, hT.t[:, :, :], io["h1all"][:, s, :, lc:lc + 512].rearrange("c p n -> p c n"), hsem[it % 2], writes=hT.r)
            it += 1
            for (st, dst, sem) in ((qst, io["QT"], sq), (kst, io["KT"], sk)):
                def ep(bk, cid, st=st):
                    if cid % 2 == 0:
                        P.op(P.act, lambda: nc.scalar.copy(st.t[:, cid, :], bk[0][:, :512]), reads=[bk[1]], writes=[st.r[cid]])
                    else:
                        self.CP(st.t[:, cid, :], bk[0][:, :512], reads=[bk[1]], writes=[st.r[cid]])
                self.linear(hT, KC, 512, [[4 * i + q for q in range(4)] for i in range(ns)], ep)
                P.dma(P.sp, dst[:, :, p0:p0 + 512].rearrange("h p n -> p h n"), st.t[:, :, :], sem, reads=st.r)
            for cg in range(ns):
                wv, wr = self.wget()
                for tb in range(4):
                    bk = self.bank()
                    pairs = [(hT.t[:, k, tb * 128:(tb + 1) * 128], wv[:, k, :]) for k in range(KC)]
                    self.mm(bk, pairs, 512, reads=[wr] + hT.r)
                    if tb % 2 == 0:
                        P.op(P.act, lambda tb=tb, bk=bk: nc.scalar.copy(vst.t[:, tb, cg * 512:(cg + 1) * 512], bk[0][:, :512]),
                             reads=[bk[1]], writes=[vst.r[tb]])
                    else:
                        self.CP(vst.t[:, tb, cg * 512:(cg + 1) * 512], bk[0][:, :512], reads=[bk[1]], writes=[vst.r[tb]])
            P.dma(P.sp, io["V"][p0:p0 + 512, :].rearrange("(tb p) f -> p tb f", p=128), vst.t[:, :, :], sv_, reads=vst.r)
    self.end_phase()


def phase_attn(self, io, nq, nh):
    nc, P = self.nc, self.P
    self.begin_phase()
    T = nq * 512
    NKB = nq * 4
    scale = 1.0 / float(np.sqrt(128.0))
    rc = self.r_const
    masks = self.sb("masks", [128, 4, 512], F32)
    tri = self.sb("tri", [128, 128], BF16)
    comp = self.sb("comp", [128, 128], BF16)
    onesw = self.sb("onesw", [128, 512], F32)
    P.op(P.pool, lambda: nc.gpsimd.memset(onesw[:, :], 1.0), writes=[rc])
    for i in range(4):
        P.op(P.pool, lambda i=i: nc.gpsimd.affine_select(masks[:, i, :], onesw[:, :], [[1, 512]], ALU.is_gt, 0.0,
                                                         base=-128 * i, channel_multiplier=-1), reads=[rc], writes=[rc])
    P.op(P.pool, lambda: nc.gpsimd.affine_select(tri[:, :], onesw[:, 0:128], [[-1, 128]], ALU.is_gt, 0.0,
                                                 base=1, channel_multiplier=1), reads=[rc], writes=[rc])
    P.op(P.pool, lambda: nc.gpsimd.affine_select(comp[:, :], onesw[:, 0:128], [[1, 128]], ALU.is_gt, 0.0,
                                                 base=0, channel_multiplier=-1), reads=[rc], writes=[rc])
    Ksb = [(self.sb("Ksb%d" % i, [128, T], BF16), Res("Ksb%d" % i), self.dsem("Ksb%d" % i)) for i in range(2)]
    Vsb = [(self.sb("Vsb%d" % i, [128, NKB, 128], BF16), Res("Vsb%d" % i), self.dsem("Vsb%d" % i)) for i in range(2)]
    NL = 2
    Eb = [Buf(self.sb("Eb%d" % l, [128, 3, 512], F32), 3, "Eb%d" % l) for l in range(NL)]
    Lb = [Buf(self.sb("Lb%d" % l, [128, 3, 512], BF16), 3, "Lb%d" % l) for l in range(NL)]
    Gb = [Buf(self.sb("Gb%d" % l, [128, 2, 512], F32), 2, "Gb%d" % l) for l in range(NL)]
    Ab = [Buf(self.sb("Ab%d" % l, [128, 2, 512], BF16), 2, "Ab%d" % l) for l in range(NL)]
    Ob = [Buf(self.sb("Ob%d" % l, [128, 2, 512], BF16), 2, "Ob%d" % l) for l in range(NL)]
    Qsb = [[(self.sb("Qsb%d_%d" % (l, i), [128, 512], BF16), Res("Qsb%d_%d" % (l, i)), self.dsem("Qsb%d_%d" % (l, i)))
            for i in range(2)] for l in range(NL)]
    osem = [[self.dsem("oo%d_%d" % (l, i)) for i in range(2)] for l in range(NL)]
    zt = self.sb("zpad", [128, nh, PRE], BF16)
    r_z = Res("zpad")
    P.op(P.pool, lambda: nc.gpsimd.memset(zt[:, :, :], 0.0), writes=[r_z])
    P.dma(P.sp, io["oT"][:, 0, :, 0:PRE].rearrange("h p n -> p h n"), zt[:, :, :], self.dsem("zpad"), reads=[r_z])
    Sbk = [self.banks[0:2], self.banks[2:4]]
    Rbk = self.banks[4:6]
    Obk = self.banks[6:8]
    loaded = {}

    def load_head(h):
        if h in loaded or h >= nh:
            return
        Kt, Kr, Ks = Ksb[h % 2]
        Vt, Vr, Vs = Vsb[h % 2]
        P.dma(P.sp, Kt[:, :], io["KT"][h, :, :], Ks, writes=[Kr])
        P.dma(P.sp, Vt[:, :, :], io["V"][:, h * 128:(h + 1) * 128].rearrange("(kb p) d -> p kb d", p=128), Vs, writes=[Vr])
        loaded[h] = True

    cnt = [0] * NL

    class Chain:
        pass

    def start_chain(l, h, j):
        load_head(h)
        if j == nq // 2:
            load_head(h + 1)
        c = Chain()
        c.l, c.h, c.j = l, h, j
        c.Kt, c.Kr, _ = Ksb[h % 2]
        c.Vt, c.Vr, _ = Vsb[h % 2]
        c.ci = cnt[l]
        cnt[l] += 1
        c.Qt, c.Qr, Qs = Qsb[l][c.ci % 2]
        P.dma(P.sp, c.Qt[:, :], io["QT"][h, :, j * 512:(j + 1) * 512], Qs, writes=[c.Qr])
        c.steps = list(range(4 * j + 3, -1, -1))
        c.N = len(c.steps)
        c.k = 0
        mm1(c, 0)
        if c.N > 1:
            mm1(c, 1)
        e_(c, 0)
        l_(c, 0)
        mm2(c, 0)
        return c

    def mm1(c, k):
        kb = c.steps[k]
        sbk = Sbk[c.l][k % 2]
        self.mm(sbk, [(c.Kt[:, kb * 128:(kb + 1) * 128], c.Qt[:, :])], 512, reads=[c.Kr, c.Qr])

    def e_(c, k):
        kb = c.steps[k]
        sbk = Sbk[c.l][k % 2]
        E = Eb[c.l]
        e = E.t[:, k % 3, :]
        self.A(e, sbk[0][:, :], AF.Exp, reads=[sbk[1]], writes=[E.r[k % 3]], scale=scale)
        i = kb - 4 * c.j
        if i >= 0:
            self.TT(e, e, masks[:, i, :], ALU.mult, reads=[E.r[k % 3], rc], writes=[E.r[k % 3]], eng=P.pool)

    def l_(c, k):
        E, L = Eb[c.l], Lb[c.l]
        self.A(L.t[:, k % 3, :], E.t[:, k % 3, :], AF.Ln, reads=[E.r[k % 3]], writes=[L.r[k % 3]], bias=1.0, scale=1.0)

    def mm2(c, k):
        L, Rb = Lb[c.l], Rbk[c.l]
        P.group(P.pe, [lambda: nc.tensor.matmul(Rb[0][:, :], tri[:, :], L.t[:, k % 3, :], start=(k == 0), stop=False)],
                reads=[L.r[k % 3], rc], writes=[Rb[1]])

    def g_(c, k):
        G, Rb = Gb[c.l], Rbk[c.l]
        self.A(G.t[:, k % 2, :], Rb[0][:, :], AF.Exp, reads=[Rb[1]], writes=[G.r[k % 2]], scale=-1.0)

    def a_(c, k):
        E, G, A_ = Eb[c.l], Gb[c.l], Ab[c.l]
        self.TT(A_.t[:, k % 2, :], E.t[:, k % 3, :], G.t[:, k % 2, :], ALU.mult,
                reads=[E.r[k % 3], G.r[k % 2]], writes=[A_.r[k % 2]])

    def mm3(c, k):
        L, Rb = Lb[c.l], Rbk[c.l]
        P.group(P.pe, [lambda: nc.tensor.matmul(Rb[0][:, :], comp[:, :], L.t[:, k % 3, :], start=False, stop=(k == c.N - 1))],
                reads=[L.r[k % 3], rc], writes=[Rb[1]])

    def mm4(c, k):
        A_, OB = Ab[c.l], Obk[c.l]
        kb = c.steps[k]
        P.group(P.pe, [lambda: nc.tensor.matmul(OB[0][:, :], c.Vt[:, kb, :], A_.t[:, k % 2, :], start=(k == 0), stop=(k == c.N - 1))],
                reads=[c.Vr, A_.r[k % 2]], writes=[OB[1]])

    def finish(c):
        l, h, j, ci = c.l, c.h, c.j, c.ci
        ob, OB = Ob[l], Obk[l]
        self.CP(ob.t[:, ci % 2, :], OB[0][:, :], reads=[OB[1]], writes=[ob.r[ci % 2]])
        sem = osem[l][ci % 2]
        pc0 = PRE + j * 512
        if pc0 + 512 <= NP2:
            P.dma(P.sp, io["oT"][h, 0, :, pc0:pc0 + 512], ob.t[:, ci % 2, :], sem, reads=[ob.r[ci % 2]])
            if pc0 + 512 > HALF:
                P.dma(P.sp, io["oT"][h, 1, :, 0:pc0 + 512 - HALF], ob.t[:, ci % 2, HALF - pc0:512], sem, reads=[ob.r[ci % 2]])
        else:
            P.dma(P.sp, io["oT"][h, 1, :, pc0 - HALF:pc0 - HALF + 512], ob.t[:, ci % 2, :], sem, reads=[ob.r[ci % 2]])

    work = [(h, j) for h in range(nh) for j in range(nq)]
    lanes = [None] * NL
    wi = 0
    while True:
        for l in range(NL):
            if lanes[l] is None and wi < len(work):
                lanes[l] = start_chain(l, *work[wi])
                wi += 1
        act = [c for c in lanes if c is not None]
        if not act:
            break
        for c in act:
            if c.k + 2 < c.N:
                mm1(c, c.k + 2)
        for c in act:
            if c.k + 1 < c.N:
                e_(c, c.k + 1)
        for c in act:
            if c.k + 1 < c.N:
                l_(c, c.k + 1)
        for c in act:
            g_(c, c.k)
        for c in act:
            a_(c, c.k)
        for c in act:
            mm3(c, c.k)
            if c.k + 1 < c.N:
                mm2(c, c.k + 1)
            if c.k >= 1:
                mm4(c, c.k - 1)
            if c.k == c.N - 1:
                mm4(c, c.k)
                finish(c)
                lanes[c.l] = None
            c.k += 1
    self.end_phase()


def phase_b2(self, io, tiles, dyn_o):
    nc, P = self.nc, self.P
    self.begin_phase()
    self.common_init()
    self.init_wstream(3)
    rv = self.r_vec
    mv = self.load_mods(io)
    sv = self.load_rows_T("svB", io["svecs2"], 2, D)
    fdw = self.load_rows_T("fdwB", io["ffn_dw"], 4, 2 * FF)
    pm = self.sb("pm_sb", [128, 1], F32)
    P.dma(P.sp, pm[:, :], io["pm"][:, :], self.dsem("pm"), writes=[rv])
    a3, sh3, g3 = self.mod_vecs(mv, 3, sv[:, :, 0], "ffn1")
    g_mix1 = mv[:, :, 8]
    afin = self.sb("afin", [128, KC], F32)
    self.TS(afin[:, :], sv[:, :, 1], float(np.sqrt(D)), None, ALU.mult, None, reads=[rv], writes=[rv])
    xT = Buf(self.sb("xT", [128, KC, 512], F32), KC, "xT")
    hT = Buf(self.sb("hT", [128, KC, 512], BF16), KC, "hT")
    gTt = self.sb("gTt", [128, FC, 512], BF16)
    gT = Buf(gTt, FC, "gT")
    yv = gTt[:, 0:32, :].bitcast(F32).rearrange("p a b -> p (a b)").rearrange("p (c n) -> p c n", c=KC)
    Tg = Buf(self.sb("Tg", [128, 2, 512], F32), 2, "Tg")
    Tv = Buf(self.sb("Tv", [128, 2, 512], F32), 2, "Tv")
    Hff = self.sb("Hff", [128, 2 * FC, 2], F32)
    rH = [Res("Hff%d" % i) for i in range(2 * FC)]
    P.op(P.pool, lambda: nc.gpsimd.memset(Hff[:, :, :], 0.0), writes=rH)
    for _ in tiles:
        for i in range(4):
            self.wplan_add(io["o_w"], D, [(512 * i, 512)])
        for i in range(22):
            self.wplan_add(io["up_w"], D, [(256 * i, 256), (FF + 256 * i, 256)])
        for i in range(16):
            self.wplan_add(io["down_w"], FF, [(128 * i, 128)])
    sx, so = self.dsem("ldx"), self.dsem("ldo")
    if dyn_o:
        par = self.par
    for ti, (start, n) in enumerate(tiles):
        P.dma(P.sp, xT.t[:, :, :n], io["x1T"][:, :, start:start + n].rearrange("c p n -> p c n"), sx, writes=xT.r)
        for rk in range(2):
            if n == 512:
                src = io["oT"][:, bass.ds(par, 1), rk, :, start:start + n].rearrange("h o p n -> p (h o) n")
                P.dma(P.sp, hT.t[:, 8 * rk:8 * rk + 8, :n], src, so, writes=hT.r[8 * rk:8 * rk + 8])
            else:
                parg = nc.gpsimd.partition_id() % 2
                src = io["oT"][:, bass.ds(parg, 1), rk, :, start:start + n].rearrange("h o p n -> p (h o) n")
                P.dma(P.pool, hT.t[:, 8 * rk:8 * rk + 8, :n], src, so, writes=hT.r[8 * rk:8 * rk + 8])

        def ep_o(bk, cid, n=n):
            self.STT(xT.t[:, cid, :n], bk[0][:, :n], g_mix1[:, cid:cid + 1], xT.t[:, cid, :n], ALU.mult, ALU.add,
                     reads=[bk[1], rv, xT.r[cid]], writes=[xT.r[cid]])
        self.linear(hT, KC, n, [[4 * i + q for q in range(4)] for i in range(4)], ep_o)
        self.rms_mod(xT, hT, n, a3, sh3)
        if ti == 0:
            self.TS(hT.t[:, :, 0:PRE], hT.t[:, :, 0:PRE], pm[:, 0:1], None, ALU.mult, None, reads=hT.r + [rv], writes=hT.r)
        self.ffn(hT, xT, gT, Tg, Tv, Hff, rH, fdw, g3, n)
        self.rms_mod_multi(xT, yv, [[gT.r[2 * c], gT.r[2 * c + 1]] for c in range(KC)], n, afin)
        nb = (n + 127) // 128
        for tb in range(nb):
            nt = min(128, n - tb * 128)
            orow, orr, osm = self.xrow[self.xrow_i % 2]
            self.xrow_i += 1
            for gq in range(4):
                bk = self.bank()
                fns = []
                for q in range(4):
                    c = 4 * gq + q
                    fns.append(lambda q=q, c=c, bk=bk: nc.tensor.transpose(bk[0][:nt, q * 128:(q + 1) * 128],
                                                                           yv[:, c, tb * 128:tb * 128 + nt], self.ident[:, :]))
                rr = []
                for q in range(4):
                    rr += [gT.r[2 * (4 * gq + q)], gT.r[2 * (4 * gq + q) + 1]]
                P.group(P.pe, fns, reads=rr + [self.r_const], writes=[bk[1]])
                if gq % 2 == 0:
                    P.op(P.act, lambda gq=gq, bk=bk: nc.scalar.copy(orow[:nt, gq * 512:(gq + 1) * 512], bk[0][:nt, :]),
                         reads=[bk[1]], writes=[orr])
                else:
                    self.CP(orow[:nt, gq * 512:(gq + 1) * 512], bk[0][:nt, :], reads=[bk[1]], writes=[orr])
            P.dma(P.sp, io["y"][start + tb * 128:start + tb * 128 + nt, :], orow[:nt, :], osm, reads=[orr])
    self.end_phase()


def rms_mod_multi(self, xT, yv, yres, n, a_vec):
    nc, P = self.nc, self.P
    sq = self.tmpbf
    bk = self.bank()
    for c in range(KC):
        self.A(sq.t[:, c % 2, :n], xT.t[:, c, :n], AF.Square, reads=[xT.r[c]], writes=[sq.r[c % 2]])
        P.group(P.pe, [lambda c=c: nc.tensor.matmul(bk[0][:, :n], self.ones_bf[:], sq.t[:, c % 2, :n],
                                                    start=(c == 0), stop=(c == KC - 1))],
                reads=[sq.r[c % 2], self.r_const], writes=[bk[1]])
    rstd = self.rstd
    self.A(rstd.t[:, 0, :n], bk[0][:, :n], AF.Ln, reads=[bk[1]], writes=[rstd.r[0]], bias=self.eps_rms[:, 0:1], scale=1.0)
    self.A(rstd.t[:, 0, :n], rstd.t[:, 0, :n], AF.Exp, reads=[rstd.r[0]], writes=[rstd.r[0]], scale=-0.5)
    for c in range(KC):
        tm = self.tmpf
        i = c % 2
        self.TT(tm.t[:, i, :n], xT.t[:, c, :n], rstd.t[:, 0, :n], ALU.mult, reads=[xT.r[c], rstd.r[0]], writes=[tm.r[i]])
        self.A(yv[:, c, :n], tm.t[:, i, :n], AF.Identity, reads=[tm.r[i], self.r_vec], writes=yres[c],
               bias=0.0, scale=a_vec[:, c:c + 1])


Builder.phase_mods = phase_mods
Builder.phase_w = phase_w
Builder.phase_qkv = phase_qkv
Builder.phase_attn = phase_attn
Builder.phase_b2 = phase_b2
Builder.rms_mod_multi = rms_mod_multi


def tiles_for(ntok):
    t = []
    st = 0
    while st < ntok:
        n = min(512, ntok - st)
        t.append((st, n))
        st += n
    return t


WSPEC = [
    ("mod0", D, 3 * D), ("mod1", D, 3 * D), ("mod2", D, 3 * D), ("mod3", D, 3 * D),
    ("pw1", D, 2 * D), ("pw2", D, D), ("up0", D, 2 * FF), ("down0", FF, D),
    ("qkv", D, 3 * D), ("ow", D, D), ("up1", D, 2 * FF), ("down1", FF, D),
]


def build_fused():
    nc = bass.Bass("TRN2", target_bir_lowering=False)
    ext = lambda name, shape, d=F32, kind="ExternalInput": nc.dram_tensor(name, shape, d, kind=kind).ap()
    itn = lambda name, shape, d: nc.dram_tensor(name, shape, d).ap()
    B = Builder(nc)
    W = {}
    n_first = 0
    for wi, (name, K_, N_) in enumerate(WSPEC):
        Ks = K_ // 4
        if name.startswith("mod"):
            W[name] = ext("w_" + name, [Ks, N_])
            continue
        wb = 1024 if K_ == D else 256
        nb = N_ // wb
        w = {"shard": ext("w_" + name, [Ks, N_]), "Ks": Ks, "wb": wb, "nb": nb, "res": Res("wf_" + name),
             "wsh": itn("wsh_" + name, [nb, Ks, wb], BF16), "full": itn("wf_" + name, [nb, K_, wb], BF16)}
        W[name] = w
        B.bg_add_weight(wi, w)
        if name == "down0":
            n_first = len(B.bg)
    B.bg_pump(n_first)
    B.n_bg_rest = len(B.bg) - n_first
    part = itn("modpart", [4, 2, 3 * D], F32)
    msum = itn("modsum", [4, 2, 3 * D], F32)
    mb = ext("mb", [4, 3 * D])
    B.phase_mods({"cq": ext("cq", [2, 512]), "mw": [W["mod%d" % m] for m in range(4)], "part": part, "msum": msum})
    x1T = itn("x1T", [KC, 128, NLOC], F32)
    h1loc = itn("h1loc", [KC, 128, NLOC], BF16)
    pm = ext("pm", [128, 1])
    B.phase_a({"xin": ext("xin", [NLOC, D]), "msum": msum, "mb": mb, "svecs": ext("svecs", [7, D]), "pw1b": ext("pw1b", [2, D]),
               "dww": ext("dww", [CW, D]), "ffn_dw": ext("ffn_dw0", [4, 2 * FF]), "pm": pm,
               "pw1_w": W["pw1"], "pw2_w": W["pw2"], "up_w": W["up0"], "down_w": W["down0"],
               "x1T": x1T, "h1T": h1loc}, tiles_for(NLOC))
    B.bg_pump()
    h1all = itn("h1all", [KC, 2, 128, NLOC], BF16)
    sem = B.dsem("ag_h1")
    for cc in range(KC):
        B.P.collective("AllGather", [h1loc[cc]], [h1all[cc].rearrange("r p n -> (r p) n")], PAIRS, sem)
    B.barrier(full=True)
    nh = NH // 2
    QT = itn("QT", [nh, 128, SEQ], BF16)
    KT = itn("KT", [nh, 128, SEQ], BF16)
    V = itn("V", [SEQ, nh * 128], BF16)
    oTloc = itn("oTloc", [nh, 2, 128, NP2], BF16)
    io = {"h1all": h1all, "qkv": W["qkv"], "QT": QT, "KT": KT, "V": V, "oT": oTloc}
    B.phase_qkv(io, HALF // 512, nh)
    B.phase_attn(io, SEQ // 512, nh)
    oall = itn("oall", [nh, 2, 2, 128, NP2], BF16)
    sem = B.dsem("ag_o")
    for h in range(nh):
        for pt in range(2):
            B.P.collective("AllGather", [oTloc[h, pt]], [oall[h, pt].rearrange("r p n -> (r p) n")], PAIRS, sem)
    B.barrier(full=True)
    B.phase_b2({"x1T": x1T, "oT": oall, "msum": msum, "mb": mb, "svecs2": ext("svecs2", [2, D]), "ffn_dw": ext("ffn_dw1", [4, 2 * FF]),
                "pm": pm, "o_w": W["ow"], "up_w": W["up1"], "down_w": W["down1"],
                "y": ext("y", [NLOC, D], F32, "ExternalOutput")}, tiles_for(NLOC), dyn_o=True)
    B.barrier(full=True)
    return nc


def _f32(a):
    return np.ascontiguousarray(np.asarray(a, dtype=np.float32))


def kernel(x, c, mix_norm_g, mix_mod_w, mix_mod_b, cv_pw1_w, cv_pw1_b, cv_dw_w, cv_dw_b, cv_ln_g, cv_ln_b,
           cv_pw2_w, cv_pw2_b, sb_qkv_w, sb_o_w, ffn_norm_g, ffn_mod_w, ffn_mod_b, ffn_up_w, ffn_dw_w,
           ffn_dw_b, ffn_down_w, final_norm_g):
    x = np.asarray(x)
    c = np.asarray(c)
    cores = list(range(8))
    full = {"mod0": mix_mod_w[0], "mod1": ffn_mod_w[0], "mod2": mix_mod_w[1], "mod3": ffn_mod_w[1],
            "pw1": cv_pw1_w[0], "pw2": cv_pw2_w[0], "up0": ffn_up_w[0], "down0": ffn_down_w[0],
            "qkv": sb_qkv_w[0], "ow": sb_o_w[0], "up1": ffn_up_w[1], "down1": ffn_down_w[1]}
    shared = {
        "mb": _f32(np.stack([mix_mod_b[0], ffn_mod_b[0], mix_mod_b[1], ffn_mod_b[1]])),
        "svecs": _f32(np.stack([mix_norm_g[0], ffn_norm_g[0], mix_norm_g[1], cv_dw_b[0], cv_ln_g[0], cv_ln_b[0], cv_pw2_b[0]])),
        "pw1b": _f32(np.asarray(cv_pw1_b[0]).reshape(2, D)),
        "dww": _f32(cv_dw_w[0]),
        "ffn_dw0": _f32(np.concatenate([np.asarray(ffn_dw_w[0]), np.asarray(ffn_dw_b[0])[None]], 0)),
        "ffn_dw1": _f32(np.concatenate([np.asarray(ffn_dw_w[1]), np.asarray(ffn_dw_b[1])[None]], 0)),
        "svecs2": _f32(np.stack([ffn_norm_g[1], final_norm_g])),
    }
    ims = []
    for i in cores:
        b, r = i // 2, i % 2
        im = dict(shared)
        for name, K_, N_ in WSPEC:
            ks = K_ // 4
            q = i % 4
            im["w_" + name] = _f32(np.asarray(full[name])[q * ks:(q + 1) * ks])
        qd = i // 4
        im["cq"] = _f32(c[2 * qd:2 * qd + 2, 512 * q:512 * q + 512])
        im["pm"] = np.full((128, 1), float(r), np.float32)
        if r == 0:
            im["xin"] = _f32(np.concatenate([np.zeros((PRE, D), np.float32), x[b, 0:HALF]], 0))
        else:
            im["xin"] = _f32(x[b, HALF - PRE:SEQ])
        ims.append(im)
    res = run_bass_kernel_spmd(build_fused(), ims, core_ids=cores)
    out = np.empty((4, SEQ, D), np.float32)
    for i in cores:
        b, r = i // 2, i % 2
        out[b, HALF * r:HALF * (r + 1)] = np.asarray(res.results[i]["y"])[PRE:]
    return out


def build_attn_test(nq, nh):
    T = nq * 512
    nc = bass.Bass("TRN2", target_bir_lowering=False)
    dt = lambda name, shape, d=F32, kind="ExternalInput": nc.dram_tensor(name, shape, d, kind=kind).ap()
    io = {"QT": dt("QT", [nh, 128, T], BF16), "KT": dt("KT", [nh, 128, T], BF16), "V": dt("V", [T, nh * 128], BF16),
          "oT": dt("oT", [nh, 2, 128, NP2], BF16, "ExternalOutput")}
    B = Builder(nc)
    B.phase_attn(io, nq, nh)
    B.barrier(full=True)
    return nc
```

```python
import numpy as np
import ml_dtypes
import concourse.bass as bass
import concourse.mybir as mybir
from concourse.bass_utils import run_bass_kernel_spmd
from contextlib import ExitStack

F32 = mybir.dt.float32
BF16 = mybir.dt.bfloat16
AF = mybir.ActivationFunctionType
ALU = mybir.AluOpType

D = 2048
KC = 16
FF = 5632
FC = 44
CW = 31
NH = 16
SEQ = 8192
HALF = 4096
PRE = 64
NLOC = HALF + PRE
RMS_EPS = 1e-6
LN_EPS = 1e-5
PAIRS = [[0, 1], [2, 3], [4, 5], [6, 7]]
QUADS = [[0, 1, 2, 3], [4, 5, 6, 7]]
NP2 = NLOC


class Res:
    __slots__ = ("name", "w", "r")

    def __init__(self, name=""):
        self.name = name
        self.w = None
        self.r = {}


class Eng:
    def __init__(self, P, name, eng, is_pe=False):
        self.name = name
        self.eng = eng
        self.is_pe = is_pe
        self.sem = P.new_sem("e_" + name)
        self.count = 0
        self.waited = {}


class Prog:
    def __init__(self, nc):
        self.nc = nc
        self.sems = {}
        self.nsem = 0
        self.pe = Eng(self, "pe", nc.tensor, is_pe=True)
        self.act = Eng(self, "act", nc.scalar)
        self.dve = Eng(self, "dve", nc.vector)
        self.pool = Eng(self, "pool", nc.gpsimd)
        self.sp = Eng(self, "sp", nc.sync)
        self.dma_vals = {}

    def new_sem(self, name):
        h = self.nc.semaphore(name).__enter__()
        key = "s%d_%s" % (self.nsem, name)
        self.nsem += 1
        self.sems[key] = h
        return key

    def _wait(self, E, needs):
        best = {}
        for (k, v) in needs:
            if best.get(k, 0) < v:
                best[k] = v
        for k, v in best.items():
            if k == E.sem:
                if E.is_pe:
                    continue
                if E.count - v >= 2:
                    continue
            if E.waited.get(k, 0) >= v:
                continue
            E.eng.wait_ge(self.sems[k], v)
            E.waited[k] = v

    @staticmethod
    def _deps(reads, writes):
        needs = []
        for r in reads:
            if r.w is not None:
                needs.append(r.w)
        for w in writes:
            if w.w is not None:
                needs.append(w.w)
            needs.extend(w.r.items())
        return needs

    @staticmethod
    def _mark(tok, reads, writes):
        k, v = tok
        for w in writes:
            w.w = tok
            w.r = {}
        for r in reads:
            if r.w is tok:
                continue
            if r.r.get(k, 0) < v:
                r.r[k] = v

    def op(self, E, fn, reads=(), writes=()):
        self._wait(E, self._deps(reads, writes))
        ins = fn()
        ins.then_inc(self.sems[E.sem], 1)
        E.count += 1
        self._mark((E.sem, E.count), reads, writes)
        return ins

    def group(self, E, fns, reads=(), writes=()):
        self._wait(E, self._deps(reads, writes))
        ins = None
        for fn in fns:
            ins = fn()
        ins.then_inc(self.sems[E.sem], 1)
        E.count += 1
        self._mark((E.sem, E.count), reads, writes)
        return ins

    def dma(self, Q, out, in_, sem, reads=(), writes=()):
        self._wait(Q, self._deps(reads, writes))
        ins = Q.eng.dma_start(out=out, in_=in_)
        ins.then_inc(self.sems[sem], 16)
        v = self.dma_vals.get(sem, 0) + 16
        self.dma_vals[sem] = v
        self._mark((sem, v), reads, writes)
        return ins

    def collective(self, kind, ins, outs, groups, sem, reads=(), writes=(), op=None):
        Q = self.pool
        self._wait(Q, self._deps(reads, writes))
        ins_ = self.nc.gpsimd.collective_compute(kind, op if op is not None else ALU.bypass, replica_groups=groups,
                                                 ins=ins, outs=outs)
        ins_.then_inc(self.sems[sem], 1)
        v = self.dma_vals.get(sem, 0) + 1
        self.dma_vals[sem] = v
        self._mark((sem, v), reads, writes)

    def wait_all(self, E, ress):
        needs = []
        for r in ress:
            if r.w is not None:
                needs.append(r.w)
            needs.extend(r.r.items())
        self._wait(E, needs)


class Buf:
    def __init__(self, t, n, name):
        self.t = t
        self.r = [Res("%s%d" % (name, i)) for i in range(n)]


class Builder:
    def __init__(self, nc):
        self.nc = nc
        self.P = Prog(nc)
        P = self.P
        self.stack = None
        self.uid = 0
        self.bg_sems = set()
        self.banks = []
        self.bankpairs = []
        for i in range(4):
            t2 = nc.alloc_psum_tensor("bankp%d" % i, [128, 2, 512], F32)
            self.bankpairs.append(t2)
            for j in range(2):
                self.banks.append((t2[:, j, :], Res("bank%d" % (2 * i + j))))
        self.bank_i = 0
        self.dsems = {}
        self.ident = self.sb("ident", [128, 128], F32)
        self.ones_bf = self.sb("ones_bf", [128, 128], BF16)
        self.r_const = Res("const")
        ones_f = self.sb("ones_f", [128, 128], F32)
        P.op(P.pool, lambda: nc.gpsimd.memset(ones_f[:], 1.0), writes=[self.r_const])
        P.op(P.pool, lambda: nc.gpsimd.affine_select(self.ident[:], ones_f[:], [[1, 128]], ALU.is_equal, 0.0,
                                                     base=0, channel_multiplier=-1),
             reads=[self.r_const], writes=[self.r_const])
        P.op(P.pool, lambda: nc.gpsimd.memset(self.ones_bf[:], 1.0), writes=[self.r_const])
        self.ones_f = ones_f
        self.wstage = [(self.sb("wstage%d" % i, [128, 2048], BF16), Res("wstage%d" % i), self.dsem("wstage%d" % i)) for i in range(2)]
        self.wstage_i = 0
        pid = nc.sync.partition_id()
        self.par = pid % 2
        self.bsel = (pid % 4) // 2
        self.bg = []
        self.bg_i = 0

    def bg_add_weight(self, wi, w):
        P = self.P
        sem = self.dsem("agw%d" % wi)
        self.bg_sems.add(sem)
        Ks, wb, nb = w["Ks"], w["wb"], w["nb"]
        N = wb * nb
        sres = [Res("wsh%d_%d" % (wi, k)) for k in range(2)]

        def unit(r0, c0, wd):
            t, r, sm = self.wstage[self.wstage_i % 2]
            k = self.wstage_i % 2
            self.wstage_i += 1
            P.dma(P.pool, t[:, :wd], w["shard"][r0:r0 + 128, c0:c0 + wd], sm, writes=[r])
            P.dma(P.sp, w["wsh"][c0 // wb:(c0 + wd) // wb, r0:r0 + 128, :].rearrange("b p n -> p b n"),
                  t[:, :wd].rearrange("p (b n) -> p b n", n=wb), sm, reads=[r], writes=[sres[k]])
        for r0 in range(0, Ks, 128):
            for c0 in range(0, N, 2048):
                wd = min(2048, N - c0)
                self.bg.append(lambda r0=r0, c0=c0, wd=wd: unit(r0, c0, wd))

        def gather(bi):
            P.collective("AllGather", [w["wsh"][bi]], [w["full"][bi]], QUADS, sem, reads=sres, writes=[w["res"]])
        for bi in range(nb):
            self.bg.append(lambda bi=bi: gather(bi))

    def bg_pump(self, n=None):
        end = len(self.bg) if n is None else min(len(self.bg), self.bg_i + n)
        while self.bg_i < end:
            self.bg[self.bg_i]()
            self.bg_i += 1

    def sb(self, name, shape, dt):
        if self.stack is None:
            return self.nc.alloc_sbuf_tensor(name, shape, dt)
        self.uid += 1
        return self.stack.enter_context(self.nc.sbuf_tensor("%s_%d" % (name, self.uid), shape, dt))

    def begin_phase(self):
        self.stack = ExitStack()

    def barrier(self, full=False):
        P = self.P
        engs = [P.pe, P.act, P.dve, P.pool, P.sp]
        needs = [(e.sem, e.count) for e in engs if e.count > 0] + \
            [kv for kv in P.dma_vals.items() if full or kv[0] not in self.bg_sems]
        for e in engs:
            P._wait(e, [x for x in needs if x[0] != e.sem])

    def end_phase(self):
        self.barrier()
        self.stack.close()
        self.stack = None

    def dsem(self, name):
        if name not in self.dsems:
            self.dsems[name] = self.P.new_sem("d_" + name)
        return self.dsems[name]

    def bank(self):
        b = self.banks[self.bank_i]
        self.bank_i = (self.bank_i + 1) % 8
        return b

    def mm(self, bank, pairs, n, reads, m=128):
        nc = self.nc
        t, r = bank
        fns = []
        last = len(pairs) - 1
        for i, (l, rh) in enumerate(pairs):
            fns.append(lambda l=l, rh=rh, i=i: nc.tensor.matmul(t[:m, :n], l, rh, start=(i == 0), stop=(i == last)))
        self.P.group(self.P.pe, fns, reads=reads, writes=[r])

    def A(self, out, in_, func, reads, writes, bias=0.0, scale=1.0):
        nc = self.nc
        return self.P.op(self.P.act, lambda: nc.scalar.activation(out, in_, func, bias=bias, scale=scale),
                         reads=reads, writes=writes)

    def TT(self, out, a, b, op, reads, writes, eng=None):
        E = eng or self.P.dve
        return self.P.op(E, lambda: E.eng.tensor_tensor(out, a, b, op), reads=reads, writes=writes)

    def TS(self, out, a, s1, s2, op0, op1, reads, writes, eng=None):
        E = eng or self.P.dve
        if op1 is None:
            return self.P.op(E, lambda: E.eng.tensor_scalar(out, a, s1, None, op0), reads=reads, writes=writes)
        return self.P.op(E, lambda: E.eng.tensor_scalar(out, a, s1, s2, op0, op1), reads=reads, writes=writes)

    def STT(self, out, a, s, b, op0, op1, reads, writes):
        nc = self.nc
        return self.P.op(self.P.dve, lambda: nc.vector.scalar_tensor_tensor(out, a, s, b, op0, op1),
                         reads=reads, writes=writes)

    def CP(self, out, in_, reads, writes, eng=None):
        E = eng or self.P.dve
        return self.P.op(E, lambda: E.eng.tensor_copy(out, in_), reads=reads, writes=writes)

    def init_wstream(self, nslots=3):
        self.wslots = []
        for i in range(nslots):
            t = self.sb("wslot%d" % i, [128, 8192], BF16)
            self.wslots.append((t, Res("wslot%d" % i), self.dsem("wslot%d" % i)))
        self.wplan = []
        self.wnext_issue = 0
        self.wnext_use = 0

    def wplan_add(self, w, k_rows, pieces):
        kc = k_rows // 128
        tot = sum(p[-1] for p in pieces)
        assert kc * tot <= 8192
        self.wplan.append((w, kc, tot, pieces))

    def _wissue(self, upto):
        P = self.P
        while self.wnext_issue < min(upto, len(self.wplan)):
            i = self.wnext_issue
            w, kc, tot, pieces = self.wplan[i]
            t, r, s = self.wslots[i % len(self.wslots)]
            dst = t[:, 0:kc * tot].rearrange("p (kc n) -> p kc n", kc=kc)
            wb = w["wb"]
            o = 0
            for pc in pieces:
                if len(pc) == 2:
                    c0, ncols = pc
                    bi, off = c0 // wb, c0 % wb
                    assert off + ncols <= wb
                    src = w["full"][bi].rearrange("(kc p) n -> p kc n", p=128)[:, :, off:off + ncols]
                else:
                    bi, off, ncols = pc
                    src = w["full"][bass.ds(bi, 1)].rearrange("o (kc p) n -> p (o kc) n", p=128)[:, :, off:off + ncols]
                P.dma(P.pool, dst[:, :, o:o + ncols], src, s, reads=[w["res"]], writes=[r])
                o += ncols
            self.wnext_issue += 1

    def wget(self):
        i = self.wnext_use
        self._wissue(i + len(self.wslots))
        w, kc, tot, pieces = self.wplan[i]
        t, r, s = self.wslots[i % len(self.wslots)]
        self.wnext_use += 1
        return t[:, 0:kc * tot].rearrange("p (kc n) -> p kc n", kc=kc), r

    def rms_mod(self, xT, hT, n, a_vec, sh_vec):
        nc, P = self.nc, self.P
        sq = self.tmpbf
        bk = self.bank()
        for c in range(KC):
            self.A(sq.t[:, c % 2, :n], xT.t[:, c, :n], AF.Square, reads=[xT.r[c]], writes=[sq.r[c % 2]])
            nc_ = nc
            P.group(P.pe, [lambda c=c: nc_.tensor.matmul(bk[0][:, :n], self.ones_bf[:], sq.t[:, c % 2, :n],
                                                          start=(c == 0), stop=(c == KC - 1))],
                    reads=[sq.r[c % 2], self.r_const], writes=[bk[1]])
        rstd = self.rstd
        self.A(rstd.t[:, 0, :n], bk[0][:, :n], AF.Ln, reads=[bk[1]], writes=[rstd.r[0]], bias=self.eps_rms[:, 0:1], scale=1.0)
        self.A(rstd.t[:, 0, :n], rstd.t[:, 0, :n], AF.Exp, reads=[rstd.r[0]], writes=[rstd.r[0]], scale=-0.5)
        for c in range(KC):
            tm = self.tmpf
            i = c % 2
            self.TT(tm.t[:, i, :n], xT.t[:, c, :n], rstd.t[:, 0, :n], ALU.mult, reads=[xT.r[c], rstd.r[0]], writes=[tm.r[i]])
            self.A(hT.t[:, c, :n], tm.t[:, i, :n], AF.Identity, reads=[tm.r[i], self.r_vec], writes=[hT.r[c]],
                   bias=(sh_vec[:, c:c + 1] if sh_vec is not None else 0.0), scale=a_vec[:, c:c + 1])

    def linear(self, act, kc_n, n, slots, epilogue):
        for ids in slots:
            wv, wr = self.wget()
            for jj, cid in enumerate(ids):
                bk = self.bank()
                pairs = [(wv[:, k, jj * 128:(jj + 1) * 128], act.t[:, k, :n]) for k in range(kc_n)]
                self.mm(bk, pairs, n, reads=[wr] + [act.r[k] for k in range(kc_n)])
                epilogue(bk, cid)

    def load_rows_T(self, name, ap2d, rows, ncols):
        nc, P = self.nc, self.P
        nch = ncols // 128
        out = self.sb("m_" + name, [128, nch, rows], F32)
        r = self.r_vec
        stg, rs = self.vec_stage, self.vec_stage_r
        for c0 in range(0, ncols, 2048):
            w = min(2048, ncols - c0)
            if isinstance(ap2d, list):
                ro = 0
                for (apx, nr) in ap2d:
                    P.dma(P.sp, stg[ro:ro + nr, :w], apx[0:nr, c0:c0 + w], self.xrow[0][2], writes=[rs])
                    ro += nr
            else:
                P.dma(P.sp, stg[:rows, :w], ap2d[0:rows, c0:c0 + w], self.xrow[0][2], writes=[rs])
            for cc in range(w // 128):
                bk = self.bank()
                P.group(P.pe, [lambda cc=cc, bk=bk: nc.tensor.transpose(bk[0][:, :rows], stg[:rows, cc * 128:(cc + 1) * 128],
                                                                        self.ident[:rows, :rows])],
                        reads=[rs, self.r_const], writes=[bk[1]])
                self.CP(out[:, c0 // 128 + cc, :], bk[0][:, :rows], reads=[bk[1]], writes=[r])
        return out

    def common_init(self):
        nc, P = self.nc, self.P
        self.xrow = [(self.sb("xrow0", [128, D], F32), Res("xrow0"), self.dsem("xrow0"))] * 2
        self.xrow_i = 0
        self.vec_stage = self.xrow[0][0]
        self.vec_stage_r = self.xrow[0][1]
        self.r_vec = Res("vecs")
        self.sqb = Buf(self.sb("sqb", [128, 2, 512], BF16), 2, "sqb")
        self.tmpf = Buf(self.sb("tmpf", [128, 2, 512], F32), 2, "tmpf")
        self.tmpbf = Buf(self.sb("tmpbf", [128, 2, 512], BF16), 2, "tmpbf")
        self.rstd = Buf(self.sb("rstd", [128, 1, 512], F32), 1, "rstd")
        self.eps_rms = self.sb("eps_rms", [128, 2], F32)
        P.op(P.pool, lambda: nc.gpsimd.memset(self.eps_rms[:, 0:1], float(D * RMS_EPS)), writes=[self.r_const])
        P.op(P.pool, lambda: nc.gpsimd.memset(self.eps_rms[:, 1:2], float(LN_EPS)), writes=[self.r_const])

    def mod_vecs(self, mv, m, gvec, name):
        a = self.sb("a_" + name, [128, KC], F32)
        r = self.r_vec
        self.TS(a[:, :], mv[:, :, 3 * m + 1], 1.0, float(np.sqrt(D)), ALU.add, ALU.mult, reads=[r], writes=[r])
        self.TT(a[:, :], a[:, :], gvec, ALU.mult, reads=[r], writes=[r])
        return a, mv[:, :, 3 * m], mv[:, :, 3 * m + 2]

    def load_mods(self, io):
        nc = self.nc
        bsel = self.bsel
        src = [(io["msum"][m, bass.ds(bsel, 1), :].rearrange("o (t k) -> (o t) k", t=3), 3) for m in range(4)]
        mv = self.load_rows_T("mods", src, 12, D)
        bv = self.load_rows_T("modb", io["mb"].rearrange("m (t k) -> (m t) k", t=3), 12, D)
        self.TT(mv[:, :, :], mv[:, :, :], bv[:, :, :], ALU.add, reads=[self.r_vec], writes=[self.r_vec])
        return mv

    def load_mods_old(self, modall):
        nc, P = self.nc, self.P
        out = self.sb("m_mods", [128, KC, 12], F32)
        stg, rs = self.vec_stage, self.vec_stage_r
        P.dma(P.sp, stg[:12, :].rearrange("q (r k) -> q r k", r=2), modall.rearrange("r m t k -> (m t) r k"),
              self.xrow[0][2], writes=[rs])
        for cc in range(KC):
            bk = self.bank()
            P.group(P.pe, [lambda cc=cc, bk=bk: nc.tensor.transpose(bk[0][:, :12], stg[:12, cc * 128:(cc + 1) * 128],
                                                                    self.ident[:12, :12])],
                    reads=[rs, self.r_const], writes=[bk[1]])
            self.CP(out[:, cc, :], bk[0][:, :12], reads=[bk[1]], writes=[self.r_vec])
        return out

    def load_x_tile(self, xin, start, n, xT):
        nc, P = self.nc, self.P
        nb = (n + 127) // 128
        for b in range(nb):
            nt = min(128, n - b * 128)
            xr, rr, sm = self.xrow[self.xrow_i % 2]
            self.xrow_i += 1
            P.dma(P.sp, xr[:nt, :], xin[start + b * 128:start + b * 128 + nt, :], sm, writes=[rr])
            for g in range(4):
                bk = self.bank()
                fns = []
                for q in range(4):
                    c = 4 * g + q
                    fns.append(lambda q=q, c=c, bk=bk: nc.tensor.transpose(bk[0][:, q * 128:q * 128 + nt],
                                                                           xr[:nt, c * 128:(c + 1) * 128], self.ident[:nt, :nt]))
                P.group(P.pe, fns, reads=[rr, self.r_const], writes=[bk[1]])
                src = bk[0][:, :].rearrange("p (q t) -> p q t", q=4)[:, :, :nt]
                dst = xT.t[:, 4 * g:4 * g + 4, b * 128:b * 128 + nt]
                eng = self.P.act if (g % 2 == 0) else self.P.dve
                if eng is self.P.act:
                    P.op(P.act, lambda dst=dst, src=src: nc.scalar.copy(dst, src), reads=[bk[1]],
                         writes=[xT.r[4 * g + q] for q in range(4)])
                else:
                    self.CP(dst, src, reads=[bk[1]], writes=[xT.r[4 * g + q] for q in range(4)])

    def phase_a(self, io, tiles):
        nc, P = self.nc, self.P
        self.begin_phase()
        self.common_init()
        self.init_wstream(3)
        mv = self.load_mods(io)
        sv = self.load_rows_T("svA", io["svecs"], 7, D)
        pw1b = self.load_rows_T("pw1b", io["pw1b"], 2, D)
        dww = self.load_rows_T("dww", io["dww"], CW, D)
        fdw = self.load_rows_T("fdw", io["ffn_dw"], 4, 2 * FF)
        pm = self.sb("pm_sb", [128, 1], F32)
        P.dma(P.sp, pm[:, :], io["pm"][:, :], self.dsem("pm"), writes=[self.r_vec])
        rv = self.r_vec
        a0, sh0, g0 = self.mod_vecs(mv, 0, sv[:, :, 0], "mix0")
        a1, sh1, g1 = self.mod_vecs(mv, 1, sv[:, :, 1], "ffn0")
        a2, sh2, g2 = self.mod_vecs(mv, 2, sv[:, :, 2], "mix1")
        gb2 = self.sb("gb2", [128, KC], F32)
        self.TT(gb2[:, :], g0, sv[:, :, 6], ALU.mult, reads=[rv], writes=[rv])

        xT = Buf(self.sb("xT", [128, KC, 512], F32), KC, "xT")
        hT = Buf(self.sb("hT", [128, KC, 512], BF16), KC, "hT")
        uT = Buf(self.sb("uT", [128, KC, CW - 1 + 512], BF16), KC, "uT")
        gTt = self.sb("gTt", [128, FC, 512], BF16)
        gT = Buf(gTt, FC, "gT")
        cvt = gTt[:, 0:32, :].bitcast(F32).rearrange("p a b -> p (a b)").rearrange("p (c n) -> p c n", c=KC)
        Tg = Buf(self.sb("Tg", [128, 2, 512], F32), 2, "Tg")
        Tv = Buf(self.sb("Tv", [128, 2, 512], F32), 2, "Tv")
        Hff = self.sb("Hff", [128, 2 * FC, 2], F32)
        rH = [Res("Hff%d" % i) for i in range(2 * FC)]
        stat = Buf(self.sb("stat", [128, 3, 512], F32), 3, "stat")
        P.op(P.pool, lambda: nc.gpsimd.memset(Hff[:, :, :], 0.0), writes=rH)
        P.op(P.pool, lambda: nc.gpsimd.memset(uT.t[:, :, 0:CW - 1], 0.0), writes=uT.r)

        for _ in tiles:
            for i in range(8):
                self.wplan_add(io["pw1_w"], D, [(256 * i, 256), (D + 256 * i, 256)])
            for i in range(4):
                self.wplan_add(io["pw2_w"], D, [(512 * i, 512)])
            for i in range(22):
                self.wplan_add(io["up_w"], D, [(256 * i, 256), (FF + 256 * i, 256)])
            for i in range(16):
                self.wplan_add(io["down_w"], FF, [(128 * i, 128)])

        x1T_d, h1T_d = io["x1T"], io["h1T"]
        osem_x = [self.dsem("ox0"), self.dsem("ox1")]
        osem_h = [self.dsem("oh0"), self.dsem("oh1")]

        per_tile = (getattr(self, "n_bg_rest", 0) + len(tiles) - 2) // max(1, len(tiles) - 1)
        for ti, (start, n) in enumerate(tiles):
            first = (ti == 0)
            self.bg_pump(per_tile)
            self.load_x_tile(io["xin"], start, n, xT)
            self.rms_mod(xT, hT, n, a0, sh0)
            if first:
                pass
            sg = self.tmpf

            def ep_pw1(bk, cid, n=n, first=first):
                if cid < KC:
                    self._valbank[cid] = bk
                else:
                    j = cid - KC
                    vb = self._valbank.pop(j)
                    i = j % 2
                    self.A(sg.t[:, i, :n], bk[0][:, :n], AF.Sigmoid, reads=[bk[1], rv], writes=[sg.r[i]],
                           bias=pw1b[:, j, 1:2], scale=1.0)
                    self.STT(uT.t[:, j, CW - 1:CW - 1 + n], vb[0][:, :n], pw1b[:, j, 0:1], sg.t[:, i, :n], ALU.add, ALU.mult,
                             reads=[vb[1], sg.r[i], rv], writes=[uT.r[j]])
            self._valbank = {}
            slots = [[2 * i, 2 * i + 1, KC + 2 * i, KC + 2 * i + 1] for i in range(8)]
            self.linear(hT, KC, n, slots, ep_pw1)
            if first:
                self.TS(uT.t[:, :, CW - 1:CW - 1 + PRE], uT.t[:, :, CW - 1:CW - 1 + PRE], pm[:, 0:1], None, ALU.mult, None,
                        reads=uT.r + [rv], writes=uT.r)
            bk_m = self.bank()
            bk_s = self.bank()
            cv_r = [[gT.r[2 * c], gT.r[2 * c + 1]] for c in range(KC)]
            for c in range(KC):
                self.A(cvt[:, c, :n], uT.t[:, c, 0:n], AF.Identity, reads=[uT.r[c], rv], writes=cv_r[c],
                       bias=sv[:, c, 3:4], scale=dww[:, c, 0:1])
            for k in range(1, CW):
                for c in range(KC):
                    self.STT(cvt[:, c, :n], uT.t[:, c, k:k + n], dww[:, c, k:k + 1], cvt[:, c, :n], ALU.mult, ALU.add,
                             reads=[uT.r[c], rv] + cv_r[c], writes=cv_r[c])
            for c in range(KC):
                i = c % 2
                self.A(self.tmpbf.t[:, i, :n], cvt[:, c, :n], AF.Identity, reads=cv_r[c], writes=[self.tmpbf.r[i]])
                P.group(P.pe, [lambda c=c, i=i: nc.tensor.matmul(bk_m[0][:, :n], self.ones_bf[:], self.tmpbf.t[:, i, :n],
                                                                 start=(c == 0), stop=(c == KC - 1))],
                        reads=[self.tmpbf.r[i], self.r_const], writes=[bk_m[1]])
                self.A(self.sqb.t[:, i, :n], cvt[:, c, :n], AF.Square, reads=cv_r[c], writes=[self.sqb.r[i]])
                P.group(P.pe, [lambda c=c, i=i: nc.tensor.matmul(bk_s[0][:, :n], self.ones_bf[:], self.sqb.t[:, i, :n],
                                                                 start=(c == 0), stop=(c == KC - 1))],
                        reads=[self.sqb.r[i], self.r_const], writes=[bk_s[1]])
            mean, var, rs = stat.t[:, 0, :n], stat.t[:, 1, :n], stat.t[:, 2, :n]
            self.A(mean, bk_m[0][:, :n], AF.Identity, reads=[bk_m[1]], writes=[stat.r[0]], scale=1.0 / D)
            self.TT(var, mean, mean, ALU.mult, reads=[stat.r[0]], writes=[stat.r[1]])
            self.STT(var, bk_s[0][:, :n], 1.0 / D, var, ALU.mult, ALU.subtract, reads=[bk_s[1], stat.r[1]], writes=[stat.r[1]])
            self.A(rs, var, AF.Ln, reads=[stat.r[1], self.r_const], writes=[stat.r[2]], bias=self.eps_rms[:, 1:2], scale=1.0)
            self.A(rs, rs, AF.Exp, reads=[stat.r[2]], writes=[stat.r[2]], scale=-0.5)
            for c in range(KC):
                i = c % 2
                self.TT(sg.t[:, i, :n], cvt[:, c, :n], mean, ALU.subtract, reads=cv_r[c] + [stat.r[0]], writes=[sg.r[i]])
                self.TT(sg.t[:, i, :n], sg.t[:, i, :n], rs, ALU.mult, reads=[sg.r[i], stat.r[2]], writes=[sg.r[i]])
                self.A(hT.t[:, c, :n], sg.t[:, i, :n], AF.Silu, reads=[sg.r[i], rv], writes=[hT.r[c]],
                       bias=sv[:, c, 5:6], scale=sv[:, c, 4:5])
            self.CP(uT.t[:, :, 0:CW - 1], uT.t[:, :, n:n + CW - 1], reads=uT.r, writes=uT.r)

            def ep_pw2(bk, cid, n=n):
                i = cid % 2
                self.A(sg.t[:, i, :n], bk[0][:, :n], AF.Identity, reads=[bk[1], rv], writes=[sg.r[i]],
                       bias=gb2[:, cid:cid + 1], scale=g0[:, cid:cid + 1])
                self.TT(xT.t[:, cid, :n], xT.t[:, cid, :n], sg.t[:, i, :n], ALU.add, reads=[xT.r[cid], sg.r[i]], writes=[xT.r[cid]])
            self.linear(hT, KC, n, [[4 * i + q for q in range(4)] for i in range(4)], ep_pw2)

            self.rms_mod(xT, hT, n, a1, sh1)
            if first:
                self.TS(hT.t[:, :, 0:PRE], hT.t[:, :, 0:PRE], pm[:, 0:1], None, ALU.mult, None, reads=hT.r + [rv], writes=hT.r)
            self.ffn(hT, xT, gT, Tg, Tv, Hff, rH, fdw, g1, n)

            k = ti % 2
            P.dma(P.sp, x1T_d[:, :, start:start + n].rearrange("c p n -> p c n"), xT.t[:, :, :n], osem_x[k], reads=xT.r)
            self.rms_mod(xT, hT, n, a2, sh2)
            P.dma(P.sp, h1T_d[:, :, start:start + n].rearrange("c p n -> p c n"), hT.t[:, :, :n], osem_h[k], reads=hT.r)
        self.end_phase()

    def ffn(self, hT, xT, gT, Tg, Tv, Hff, rH, fdw, gate, n):
        nc, P = self.nc, self.P
        rv = self.r_vec

        def conv3(bk, ch, T, i):
            p = bk[0]
            w0, w1, w2, b = fdw[:, ch, 0:1], fdw[:, ch, 1:2], fdw[:, ch, 2:3], fdw[:, ch, 3:4]
            t = T.t[:, i, :]
            self.A(t[:, :n], p[:, :n], AF.Identity, reads=[bk[1], rv], writes=[T.r[i]], bias=b, scale=w2)
            self.STT(t[:, 1:n], p[:, 0:n - 1], w1, t[:, 1:n], ALU.mult, ALU.add, reads=[bk[1], rv, T.r[i]], writes=[T.r[i]])
            self.STT(t[:, 2:n], p[:, 0:n - 2], w0, t[:, 2:n], ALU.mult, ALU.add, reads=[bk[1], rv, T.r[i]], writes=[T.r[i]])
            h = Hff[:, ch, :]
            self.STT(t[:, 0:1], h[:, 1:2], w1, t[:, 0:1], ALU.mult, ALU.add, reads=[rH[ch], rv, T.r[i]], writes=[T.r[i]])
            self.STT(t[:, 0:2], h[:, 0:2], w0, t[:, 0:2], ALU.mult, ALU.add, reads=[rH[ch], rv, T.r[i]], writes=[T.r[i]])
            P.op(P.act, lambda: nc.scalar.copy(h[:, 0:2], p[:, n - 2:n]), reads=[bk[1], T.r[i]], writes=[rH[ch]])

        def ep_up(bk, cid, n=n):
            if cid < FC:
                conv3(bk, cid, Tg, cid % 2)
            else:
                j = cid - FC
                i = j % 2
                conv3(bk, cid, Tv, i)
                self.A(Tg.t[:, i, :n], Tg.t[:, i, :n], AF.Silu, reads=[Tg.r[i]], writes=[Tg.r[i]])
                self.TT(gT.t[:, j, :n], Tg.t[:, i, :n], Tv.t[:, i, :n], ALU.mult, reads=[Tg.r[i], Tv.r[i]], writes=[gT.r[j]])
        slots = [[2 * i, 2 * i + 1, FC + 2 * i, FC + 2 * i + 1] for i in range(22)]
        self.linear(hT, KC, n, slots, ep_up)

        def ep_down(bk, cid, n=n):
            self.STT(xT.t[:, cid, :n], bk[0][:, :n], gate[:, cid:cid + 1], xT.t[:, cid, :n], ALU.mult, ALU.add,
                     reads=[bk[1], rv, xT.r[cid]], writes=[xT.r[cid]])
        self.linear(gT, FC, n, [[i] for i in range(16)], ep_down)


def phase_mods(self, io):
    nc, P = self.nc, self.P
    self.begin_phase()
    self.common_init()
    self.init_wstream(3)
    rv = self.r_vec
    cT = self.load_rows_T("cq", io["cq"], 2, 512)
    sc = self.sb("sc", [128, 4, 2], BF16)
    self.A(sc[:, :, :], cT[:, :, :], AF.Silu, reads=[rv], writes=[rv])
    NM = 3 * D
    orow = self.sb("orow", [2, NM], F32)
    r_o = Res("orow")
    sem_o = self.dsem("orow")
    wi = 0
    for m in range(4):
        for g3 in range(3):
            t, r, sm = self.wslots[wi % 3]
            wi += 1
            wv = t[:, :].rearrange("p (kc n) -> p kc n", kc=4)
            P.dma(P.pool, wv, io["mw"][m].rearrange("(kc p) n -> p kc n", p=128)[:, :, g3 * 2048:(g3 + 1) * 2048], sm, writes=[r])
            for g in range(4):
                bk = self.bank()
                pairs = [(sc[:, k, :], wv[:, k, g * 512:(g + 1) * 512]) for k in range(4)]
                self.mm(bk, pairs, 512, reads=[r, rv], m=2)
                o = g3 * 2048 + g * 512
                self.CP(orow[0:2, o:o + 512], bk[0][0:2, :512], reads=[bk[1]], writes=[r_o])
        P.dma(P.sp, io["part"][m], orow[0:2, :], sem_o, reads=[r_o])
    P.collective("AllReduce", [io["part"].rearrange("m b k -> (m b) k")], [io["msum"].rearrange("m b k -> (m b) k")], QUADS,
                 self.dsem("ar_mods"), reads=[r_o], op=ALU.add)
    self.end_phase()
    self.barrier(full=True)


def phase_w(self, wl):
    nc, P = self.nc, self.P
    self.begin_phase()
    self.init_wstream(3)
    i = 0
    for wi, w in enumerate(wl):
        sem = self.dsem("agw%d" % wi)
        self.bg_sems.add(sem)
        Ks, wb, nb = w["Ks"], w["wb"], w["nb"]
        N = wb * nb
        sres = [Res("wsh%d_%d" % (wi, k)) for k in range(3)]
        for r0 in range(0, Ks, 128):
            for c0 in range(0, N, 8192):
                wd = min(8192, N - c0)
                t, r, sm = self.wslots[i % 3]
                P.dma(P.pool, t[:, :wd], w["shard"][r0:r0 + 128, c0:c0 + wd], sm, writes=[r])
                P.dma(P.sp, w["wsh"][c0 // wb:(c0 + wd) // wb, r0:r0 + 128, :].rearrange("b p n -> p b n"),
                      t[:, :wd].rearrange("p (b n) -> p b n", n=wb), sm, reads=[r], writes=[sres[i % 3]])
                i += 1
        for bi in range(nb):
            P.collective("AllGather", [w["wsh"][bi]], [w["full"][bi]], QUADS, sem, reads=sres, writes=[w["res"]])
    self.end_phase()


def phase_qkv(self, io, ntile, nh):
    nc, P = self.nc, self.P
    self.begin_phase()
    self.init_wstream(3)
    ns = nh // 4
    hTs = [Buf(self.sb("hq%d" % i, [128, KC, 512], BF16), KC, "hq%d" % i) for i in range(2)]
    hsem = [self.dsem("hq0"), self.dsem("hq1")]
    qst = Buf(self.sb("qst", [128, nh, 512], BF16), nh, "qst")
    kst = Buf(self.sb("kst", [128, nh, 512], BF16), nh, "kst")
    vst = Buf(self.sb("vst", [128, 4, nh * 128], BF16), 4, "vst")
    sq, sk, sv_ = self.dsem("oq"), self.dsem("ok"), self.dsem("ov")
    par = nc.gpsimd.partition_id() % 2
    for _ in range(2 * ntile):
        for which in range(3):
            for i in range(ns):
                self.wplan_add(io["qkv"], D, [(par + 2 * which, 512 * i, 512)])
    it = 0
    for s in range(2):
        for t in range(ntile):
            p0 = (s * ntile + t) * 512
            lc = PRE + 512 * t
            hT = hTs[it % 2]
            P.dma(P.sp, hT.t[:, :, :], io["h1all"][:, s, :, lc:lc + 512].rearrange("c p n -> p c n"), hsem[it % 2], writes=hT.r)
            it += 1
            for (st, dst, sem) in ((qst, io["QT"], sq), (kst, io["KT"], sk)):
                def ep(bk, cid, st=st):
                    if cid % 2 == 0:
                        P.op(P.act, lambda: nc.scalar.copy(st.t[:, cid, :], bk[0][:, :512]), reads=[bk[1]], writes=[st.r[cid]])
                    else:
                        self.CP(st.t[:, cid, :], bk[0][:, :512], reads=[bk[1]], writes=[st.r[cid]])
                self.linear(hT, KC, 512, [[4 * i + q for q in range(4)] for i in range(ns)], ep)
                P.dma(P.sp, dst[:, :, p0:p0 + 512].rearrange("h p n -> p h n"), st.t[:, :, :], sem, reads=st.r)
            for cg in range(ns):
                wv, wr = self.wget()
                for tb in range(4):
                    bk = self.bank()
                    pairs = [(hT.t[:, k, tb * 128:(tb + 1) * 128], wv[:, k, :]) for k in range(KC)]
                    self.mm(bk, pairs, 512, reads=[wr] + hT.r)
                    if tb % 2 == 0:
                        P.op(P.act, lambda tb=tb, bk=bk: nc.scalar.copy(vst.t[:, tb, cg * 512:(cg + 1) * 512], bk[0][:, :512]),
                             reads=[bk[1]], writes=[vst.r[tb]])
                    else:
                        self.CP(vst.t[:, tb, cg * 512:(cg + 1) * 512], bk[0][:, :512], reads=[bk[1]], writes=[vst.r[tb]])
            P.dma(P.sp, io["V"][p0:p0 + 512, :].rearrange("(tb p) f -> p tb f", p=128), vst.t[:, :, :], sv_, reads=vst.r)
    self.end_phase()


def phase_attn(self, io, nq, nh):
    nc, P = self.nc, self.P
    self.begin_phase()
    T = nq * 512
    NKB = nq * 4
    scale = 1.0 / float(np.sqrt(128.0))
    rc = self.r_const
    masks = self.sb("masks", [128, 4, 512], F32)
    tri = self.sb("tri", [128, 128], BF16)
    comp = self.sb("comp", [128, 128], BF16)
    onesw = self.sb("onesw", [128, 512], F32)
    P.op(P.pool, lambda: nc.gpsimd.memset(onesw[:, :], 1.0), writes=[rc])
    for i in range(4):
        P.op(P.pool, lambda i=i: nc.gpsimd.affine_select(masks[:, i, :], onesw[:, :], [[1, 512]], ALU.is_gt, 0.0,
                                                         base=-128 * i, channel_multiplier=-1), reads=[rc], writes=[rc])
    P.op(P.pool, lambda: nc.gpsimd.affine_select(tri[:, :], onesw[:, 0:128], [[-1, 128]], ALU.is_gt, 0.0,
                                                 base=1, channel_multiplier=1), reads=[rc], writes=[rc])
    P.op(P.pool, lambda: nc.gpsimd.affine_select(comp[:, :], onesw[:, 0:128], [[1, 128]], ALU.is_gt, 0.0,
                                                 base=0, channel_multiplier=-1), reads=[rc], writes=[rc])
    Ksb = [(self.sb("Ksb%d" % i, [128, T], BF16), Res("Ksb%d" % i), self.dsem("Ksb%d" % i)) for i in range(2)]
    Vsb = [(self.sb("Vsb%d" % i, [128, NKB, 128], BF16), Res("Vsb%d" % i), self.dsem("Vsb%d" % i)) for i in range(2)]
    NL = 2
    Eb = [Buf(self.sb("Eb%d" % l, [128, 3, 512], F32), 3, "Eb%d" % l) for l in range(NL)]
    Lb = [Buf(self.sb("Lb%d" % l, [128, 3, 512], BF16), 3, "Lb%d" % l) for l in range(NL)]
    Gb = [Buf(self.sb("Gb%d" % l, [128, 2, 512], F32), 2, "Gb%d" % l) for l in range(NL)]
    Ab = [Buf(self.sb("Ab%d" % l, [128, 2, 512], BF16), 2, "Ab%d" % l) for l in range(NL)]
    Ob = [Buf(self.sb("Ob%d" % l, [128, 2, 512], BF16), 2, "Ob%d" % l) for l in range(NL)]
    Qsb = [[(self.sb("Qsb%d_%d" % (l, i), [128, 512], BF16), Res("Qsb%d_%d" % (l, i)), self.dsem("Qsb%d_%d" % (l, i)))
            for i in range(2)] for l in range(NL)]
    osem = [[self.dsem("oo%d_%d" % (l, i)) for i in range(2)] for l in range(NL)]
    zt = self.sb("zpad", [128, nh, PRE], BF16)
    r_z = Res("zpad")
    P.op(P.pool, lambda: nc.gpsimd.memset(zt[:, :, :], 0.0), writes=[r_z])
    P.dma(P.sp, io["oT"][:, 0, :, 0:PRE].rearrange("h p n -> p h n"), zt[:, :, :], self.dsem("zpad"), reads=[r_z])
    Sbk = [self.banks[0:2], self.banks[2:4]]
    Rbk = self.banks[4:6]
    Obk = self.banks[6:8]
    loaded = {}

    def load_head(h):
        if h in loaded or h >= nh:
            return
        Kt, Kr, Ks = Ksb[h % 2]
        Vt, Vr, Vs = Vsb[h % 2]
        P.dma(P.sp, Kt[:, :], io["KT"][h, :, :], Ks, writes=[Kr])
        P.dma(P.sp, Vt[:, :, :], io["V"][:, h * 128:(h + 1) * 128].rearrange("(kb p) d -> p kb d", p=128), Vs, writes=[Vr])
        loaded[h] = True

    cnt = [0] * NL

    class Chain:
        pass

    def start_chain(l, h, j):
        load_head(h)
        if j == nq // 2:
            load_head(h + 1)
        c = Chain()
        c.l, c.h, c.j = l, h, j
        c.Kt, c.Kr, _ = Ksb[h % 2]
        c.Vt, c.Vr, _ = Vsb[h % 2]
        c.ci = cnt[l]
        cnt[l] += 1
        c.Qt, c.Qr, Qs = Qsb[l][c.ci % 2]
        P.dma(P.sp, c.Qt[:, :], io["QT"][h, :, j * 512:(j + 1) * 512], Qs, writes=[c.Qr])
        c.steps = list(range(4 * j + 3, -1, -1))
        c.N = len(c.steps)
        c.k = 0
        mm1(c, 0)
        if c.N > 1:
            mm1(c, 1)
        e_(c, 0)
        l_(c, 0)
        mm2(c, 0)
        return c

    def mm1(c, k):
        kb = c.steps[k]
        sbk = Sbk[c.l][k % 2]
        self.mm(sbk, [(c.Kt[:, kb * 128:(kb + 1) * 128], c.Qt[:, :])], 512, reads=[c.Kr, c.Qr])

    def e_(c, k):
        kb = c.steps[k]
        sbk = Sbk[c.l][k % 2]
        E = Eb[c.l]
        e = E.t[:, k % 3, :]
        self.A(e, sbk[0][:, :], AF.Exp, reads=[sbk[1]], writes=[E.r[k % 3]], scale=scale)
        i = kb - 4 * c.j
        if i >= 0:
            self.TT(e, e, masks[:, i, :], ALU.mult, reads=[E.r[k % 3], rc], writes=[E.r[k % 3]], eng=P.pool)

    def l_(c, k):
        E, L = Eb[c.l], Lb[c.l]
        self.A(L.t[:, k % 3, :], E.t[:, k % 3, :], AF.Ln, reads=[E.r[k % 3]], writes=[L.r[k % 3]], bias=1.0, scale=1.0)

    def mm2(c, k):
        L, Rb = Lb[c.l], Rbk[c.l]
        P.group(P.pe, [lambda: nc.tensor.matmul(Rb[0][:, :], tri[:, :], L.t[:, k % 3, :], start=(k == 0), stop=False)],
                reads=[L.r[k % 3], rc], writes=[Rb[1]])

    def g_(c, k):
        G, Rb = Gb[c.l], Rbk[c.l]
        self.A(G.t[:, k % 2, :], Rb[0][:, :], AF.Exp, reads=[Rb[1]], writes=[G.r[k % 2]], scale=-1.0)

    def a_(c, k):
        E, G, A_ = Eb[c.l], Gb[c.l], Ab[c.l]
        self.TT(A_.t[:, k % 2, :], E.t[:, k % 3, :], G.t[:, k % 2, :], ALU.mult,
                reads=[E.r[k % 3], G.r[k % 2]], writes=[A_.r[k % 2]])

    def mm3(c, k):
        L, Rb = Lb[c.l], Rbk[c.l]
        P.group(P.pe, [lambda: nc.tensor.matmul(Rb[0][:, :], comp[:, :], L.t[:, k % 3, :], start=False, stop=(k == c.N - 1))],
                reads=[L.r[k % 3], rc], writes=[Rb[1]])

    def mm4(c, k):
        A_, OB = Ab[c.l], Obk[c.l]
        kb = c.steps[k]
        P.group(P.pe, [lambda: nc.tensor.matmul(OB[0][:, :], c.Vt[:, kb, :], A_.t[:, k % 2, :], start=(k == 0), stop=(k == c.N - 1))],
                reads=[c.Vr, A_.r[k % 2]], writes=[OB[1]])

    def finish(c):
        l, h, j, ci = c.l, c.h, c.j, c.ci
        ob, OB = Ob[l], Obk[l]
        self.CP(ob.t[:, ci % 2, :], OB[0][:, :], reads=[OB[1]], writes=[ob.r[ci % 2]])
        sem = osem[l][ci % 2]
        pc0 = PRE + j * 512
        if pc0 + 512 <= NP2:
            P.dma(P.sp, io["oT"][h, 0, :, pc0:pc0 + 512], ob.t[:, ci % 2, :], sem, reads=[ob.r[ci % 2]])
            if pc0 + 512 > HALF:
                P.dma(P.sp, io["oT"][h, 1, :, 0:pc0 + 512 - HALF], ob.t[:, ci % 2, HALF - pc0:512], sem, reads=[ob.r[ci % 2]])
        else:
            P.dma(P.sp, io["oT"][h, 1, :, pc0 - HALF:pc0 - HALF + 512], ob.t[:, ci % 2, :], sem, reads=[ob.r[ci % 2]])

    work = [(h, j) for h in range(nh) for j in range(nq)]
    lanes = [None] * NL
    wi = 0
    while True:
        for l in range(NL):
            if lanes[l] is None and wi < len(work):
                lanes[l] = start_chain(l, *work[wi])
                wi += 1
        act = [c for c in lanes if c is not None]
        if not act:
            break
        for c in act:
            if c.k + 2 < c.N:
                mm1(c, c.k + 2)
        for c in act:
            if c.k + 1 < c.N:
                e_(c, c.k + 1)
        for c in act:
            if c.k + 1 < c.N:
                l_(c, c.k + 1)
        for c in act:
            g_(c, c.k)
        for c in act:
            a_(c, c.k)
        for c in act:
            mm3(c, c.k)
            if c.k + 1 < c.N:
                mm2(c, c.k + 1)
            if c.k >= 1:
                mm4(c, c.k - 1)
            if c.k == c.N - 1:
                mm4(c, c.k)
                finish(c)
                lanes[c.l] = None
            c.k += 1
    self.end_phase()


def phase_attn2(self, io, nq, nh):
    nc, P = self.nc, self.P
    self.begin_phase()
    T = nq * 512
    NKB = nq * 4
    scale = 1.0 / float(np.sqrt(128.0))
    rc = self.r_const
    masks = self.sb("masks", [128, 4, 2, 512], F32)
    tri = self.sb("tri", [128, 128], BF16)
    comp = self.sb("comp", [128, 128], BF16)
    onesw = self.sb("onesw", [128, 512], F32)
    P.op(P.pool, lambda: nc.gpsimd.memset(onesw[:, :], 1.0), writes=[rc])
    for i in range(4):
        for l in range(2):
            P.op(P.pool, lambda i=i, l=l: nc.gpsimd.affine_select(masks[:, i, l, :], onesw[:, :], [[1, 512]], ALU.is_gt, 0.0,
                                                                  base=-128 * i, channel_multiplier=-1), reads=[rc], writes=[rc])
    P.op(P.pool, lambda: nc.gpsimd.affine_select(tri[:, :], onesw[:, 0:128], [[-1, 128]], ALU.is_gt, 0.0,
                                                 base=1, channel_multiplier=1), reads=[rc], writes=[rc])
    P.op(P.pool, lambda: nc.gpsimd.affine_select(comp[:, :], onesw[:, 0:128], [[1, 128]], ALU.is_gt, 0.0,
                                                 base=0, channel_multiplier=-1), reads=[rc], writes=[rc])
    NB = 4
    Ksb = [(self.sb("Ksb%d" % i, [128, T], BF16), Res("Ksb%d" % i), self.dsem("Ksb%d" % i)) for i in range(NB)]
    Vsb = [(self.sb("Vsb%d" % i, [128, NKB, 128], BF16), Res("Vsb%d" % i), self.dsem("Vsb%d" % i)) for i in range(NB)]
    Eb = Buf(self.sb("Eb", [128, 3, 2, 512], F32), 3, "Eb")
    Lb = Buf(self.sb("Lb", [128, 3, 2, 512], BF16), 3, "Lb")
    Gb = Buf(self.sb("Gb", [128, 2, 2, 512], F32), 2, "Gb")
    Ab = Buf(self.sb("Ab", [128, 2, 2, 512], BF16), 2, "Ab")
    Ob = Buf(self.sb("Ob", [128, 2, 2, 512], BF16), 2, "Ob")
    NQB = 3
    Qsb = [(self.sb("Qsb%d" % i, [128, 2, 512], BF16), Res("Qsb%d" % i), self.dsem("Qsb%d" % i)) for i in range(NQB)]
    osem = [self.dsem("oo%d" % i) for i in range(2)]
    zt = self.sb("zpad", [128, nh, PRE], BF16)
    r_z = Res("zpad")
    P.op(P.pool, lambda: nc.gpsimd.memset(zt[:, :, :], 0.0), writes=[r_z])
    P.dma(P.sp, io["oT"][:, 0, :, 0:PRE].rearrange("h p n -> p h n"), zt[:, :, :], self.dsem("zpad"), reads=[r_z])
    Sp = [self.bankpairs[0], self.bankpairs[1]]
    Sr = [[self.banks[0][1], self.banks[1][1]], [self.banks[2][1], self.banks[3][1]]]
    Rp = self.bankpairs[2]
    Rr = [self.banks[4][1], self.banks[5][1]]
    Op = self.bankpairs[3]
    Or = [self.banks[6][1], self.banks[7][1]]
    loaded = {}

    def load_head(h):
        if h in loaded or h >= nh:
            return
        Kt, Kr, Ks = Ksb[h % NB]
        Vt, Vr, Vs = Vsb[h % NB]
        P.dma(P.sp, Kt[:, :], io["KT"][h, :, :], Ks, writes=[Kr])
        P.dma(P.sp, Vt[:, :, :], io["V"][:, h * 128:(h + 1) * 128].rearrange("(kb p) d -> p kb d", p=128), Vs, writes=[Vr])
        loaded[h] = True

    steps = []
    for hp in range(nh // 2):
        for j in range(nq):
            N = 4 * j + 4
            for k in range(N):
                steps.append((hp, j, k, N))
    NS = len(steps)
    qbuf = {}
    qcount = [0]

    def get_q(hp, j):
        key = (hp, j)
        if key not in qbuf:
            load_head(2 * hp)
            load_head(2 * hp + 1)
            if j == nq // 2:
                load_head(2 * hp + 2)
                load_head(2 * hp + 3)
            Qt, Qr, Qs = Qsb[qcount[0] % NQB]
            qcount[0] += 1
            for l in range(2):
                P.dma(P.sp, Qt[:, l, :], io["QT"][2 * hp + l, :, j * 512:(j + 1) * 512], Qs, writes=[Qr])
            qbuf[key] = (Qt, Qr)
        return qbuf[key]

    def mm1(s):
        hp, j, k, N = steps[s]
        kb = 4 * j + 3 - k
        Qt, Qr = get_q(hp, j)
        for l in range(2):
            h = 2 * hp + l
            Kt, Kr, _ = Ksb[h % NB]
            t = Sp[s % 2]
            P.group(P.pe, [lambda: nc.tensor.matmul(t[:, l, :], Kt[:, kb * 128:(kb + 1) * 128], Qt[:, l, :], start=True, stop=True)],
                    reads=[Kr, Qr], writes=[Sr[s % 2][l]])

    def el(s):
        hp, j, k, N = steps[s]
        kb = 4 * j + 3 - k
        self.A(Eb.t[:, s % 3, :, :], Sp[s % 2][:, :, :], AF.Exp, reads=Sr[s % 2], writes=[Eb.r[s % 3]], scale=scale)
        i = kb - 4 * j
        if i >= 0:
            self.TT(Eb.t[:, s % 3, :, :], Eb.t[:, s % 3, :, :], masks[:, i, :, :], ALU.mult, reads=[Eb.r[s % 3], rc],
                    writes=[Eb.r[s % 3]], eng=P.pool)
        self.A(Lb.t[:, s % 3, :, :], Eb.t[:, s % 3, :, :], AF.Ln, reads=[Eb.r[s % 3]], writes=[Lb.r[s % 3]], bias=1.0, scale=1.0)

    def mm2(s):
        hp, j, k, N = steps[s]
        for l in range(2):
            P.group(P.pe, [lambda: nc.tensor.matmul(Rp[:, l, :], tri[:, :], Lb.t[:, s % 3, l, :], start=(k == 0), stop=False)],
                    reads=[Lb.r[s % 3], rc], writes=[Rr[l]])

    def mm3(s):
        hp, j, k, N = steps[s]
        for l in range(2):
            P.group(P.pe, [lambda: nc.tensor.matmul(Rp[:, l, :], comp[:, :], Lb.t[:, s % 3, l, :], start=False, stop=(k == N - 1))],
                    reads=[Lb.r[s % 3], rc], writes=[Rr[l]])

    def mm4(s):
        hp, j, k, N = steps[s]
        kb = 4 * j + 3 - k
        for l in range(2):
            h = 2 * hp + l
            Vt, Vr, _ = Vsb[h % NB]
            P.group(P.pe, [lambda: nc.tensor.matmul(Op[:, l, :], Vt[:, kb, :], Ab.t[:, s % 2, l, :], start=(k == 0), stop=(k == N - 1))],
                    reads=[Vr, Ab.r[s % 2]], writes=[Or[l]])
        if k == N - 1:
            ci = hp * nq + j
            self.CP(Ob.t[:, ci % 2, :, :], Op[:, :, :], reads=Or, writes=[Ob.r[ci % 2]])
            sem = osem[ci % 2]
            pc0 = PRE + j * 512
            for l in range(2):
                h = 2 * hp + l
                ob = Ob.t[:, ci % 2, l, :]
                if pc0 + 512 <= NP2:
                    P.dma(P.sp, io["oT"][h, 0, :, pc0:pc0 + 512], ob, sem, reads=[Ob.r[ci % 2]])
                    if pc0 + 512 > HALF:
                        P.dma(P.sp, io["oT"][h, 1, :, 0:pc0 + 512 - HALF], Ob.t[:, ci % 2, l, HALF - pc0:512], sem, reads=[Ob.r[ci % 2]])
                else:
                    P.dma(P.sp, io["oT"][h, 1, :, pc0 - HALF:pc0 - HALF + 512], ob, sem, reads=[Ob.r[ci % 2]])

    mm1(0)
    if NS > 1:
        mm1(1)
    el(0)
    mm2(0)
    for s in range(NS):
        if s + 2 < NS:
            mm1(s + 2)
        if s + 1 < NS:
            el(s + 1)
        self.A(Gb.t[:, s % 2, :, :], Rp[:, :, :], AF.Exp, reads=Rr, writes=[Gb.r[s % 2]], scale=-1.0)
        self.TT(Ab.t[:, s % 2, :, :], Eb.t[:, s % 3, :, :], Gb.t[:, s % 2, :, :], ALU.mult,
                reads=[Eb.r[s % 3], Gb.r[s % 2]], writes=[Ab.r[s % 2]])
        mm3(s)
        if s + 1 < NS:
            mm2(s + 1)
        if s >= 1:
            mm4(s - 1)
    mm4(NS - 1)
    self.end_phase()


Builder.phase_attn2 = phase_attn2


def phase_b2(self, io, tiles, dyn_o):
    nc, P = self.nc, self.P
    self.begin_phase()
    self.common_init()
    self.init_wstream(3)
    rv = self.r_vec
    mv = self.load_mods(io)
    sv = self.load_rows_T("svB", io["svecs2"], 2, D)
    fdw = self.load_rows_T("fdwB", io["ffn_dw"], 4, 2 * FF)
    pm = self.sb("pm_sb", [128, 1], F32)
    P.dma(P.sp, pm[:, :], io["pm"][:, :], self.dsem("pm"), writes=[rv])
    a3, sh3, g3 = self.mod_vecs(mv, 3, sv[:, :, 0], "ffn1")
    g_mix1 = mv[:, :, 8]
    afin = self.sb("afin", [128, KC], F32)
    self.TS(afin[:, :], sv[:, :, 1], float(np.sqrt(D)), None, ALU.mult, None, reads=[rv], writes=[rv])
    xT = Buf(self.sb("xT", [128, KC, 512], F32), KC, "xT")
    hT = Buf(self.sb("hT", [128, KC, 512], BF16), KC, "hT")
    gTt = self.sb("gTt", [128, FC, 512], BF16)
    gT = Buf(gTt, FC, "gT")
    yv = gTt[:, 0:32, :].bitcast(F32).rearrange("p a b -> p (a b)").rearrange("p (c n) -> p c n", c=KC)
    Tg = Buf(self.sb("Tg", [128, 2, 512], F32), 2, "Tg")
    Tv = Buf(self.sb("Tv", [128, 2, 512], F32), 2, "Tv")
    Hff = self.sb("Hff", [128, 2 * FC, 2], F32)
    rH = [Res("Hff%d" % i) for i in range(2 * FC)]
    P.op(P.pool, lambda: nc.gpsimd.memset(Hff[:, :, :], 0.0), writes=rH)
    for _ in tiles:
        for i in range(4):
            self.wplan_add(io["o_w"], D, [(512 * i, 512)])
        for i in range(22):
            self.wplan_add(io["up_w"], D, [(256 * i, 256), (FF + 256 * i, 256)])
        for i in range(16):
            self.wplan_add(io["down_w"], FF, [(128 * i, 128)])
    sx, so = self.dsem("ldx"), self.dsem("ldo")
    if dyn_o:
        par = self.par
    for ti, (start, n) in enumerate(tiles):
        P.dma(P.sp, xT.t[:, :, :n], io["x1T"][:, :, start:start + n].rearrange("c p n -> p c n"), sx, writes=xT.r)
        for rk in range(2):
            if n == 512:
                src = io["oT"][:, bass.ds(par, 1), rk, :, start:start + n].rearrange("h o p n -> p (h o) n")
                P.dma(P.sp, hT.t[:, 8 * rk:8 * rk + 8, :n], src, so, writes=hT.r[8 * rk:8 * rk + 8])
            else:
                parg = nc.gpsimd.partition_id() % 2
                src = io["oT"][:, bass.ds(parg, 1), rk, :, start:start + n].rearrange("h o p n -> p (h o) n")
                P.dma(P.pool, hT.t[:, 8 * rk:8 * rk + 8, :n], src, so, writes=hT.r[8 * rk:8 * rk + 8])

        def ep_o(bk, cid, n=n):
            self.STT(xT.t[:, cid, :n], bk[0][:, :n], g_mix1[:, cid:cid + 1], xT.t[:, cid, :n], ALU.mult, ALU.add,
                     reads=[bk[1], rv, xT.r[cid]], writes=[xT.r[cid]])
        self.linear(hT, KC, n, [[4 * i + q for q in range(4)] for i in range(4)], ep_o)
        self.rms_mod(xT, hT, n, a3, sh3)
        if ti == 0:
            self.TS(hT.t[:, :, 0:PRE], hT.t[:, :, 0:PRE], pm[:, 0:1], None, ALU.mult, None, reads=hT.r + [rv], writes=hT.r)
        self.ffn(hT, xT, gT, Tg, Tv, Hff, rH, fdw, g3, n)
        self.rms_mod_multi(xT, yv, [[gT.r[2 * c], gT.r[2 * c + 1]] for c in range(KC)], n, afin)
        nb = (n + 127) // 128
        for tb in range(nb):
            nt = min(128, n - tb * 128)
            orow, orr, osm = self.xrow[self.xrow_i % 2]
            self.xrow_i += 1
            for gq in range(4):
                bk = self.bank()
                fns = []
                for q in range(4):
                    c = 4 * gq + q
                    fns.append(lambda q=q, c=c, bk=bk: nc.tensor.transpose(bk[0][:nt, q * 128:(q + 1) * 128],
                                                                           yv[:, c, tb * 128:tb * 128 + nt], self.ident[:, :]))
                rr = []
                for q in range(4):
                    rr += [gT.r[2 * (4 * gq + q)], gT.r[2 * (4 * gq + q) + 1]]
                P.group(P.pe, fns, reads=rr + [self.r_const], writes=[bk[1]])
                if gq % 2 == 0:
                    P.op(P.act, lambda gq=gq, bk=bk: nc.scalar.copy(orow[:nt, gq * 512:(gq + 1) * 512], bk[0][:nt, :]),
                         reads=[bk[1]], writes=[orr])
                else:
                    self.CP(orow[:nt, gq * 512:(gq + 1) * 512], bk[0][:nt, :], reads=[bk[1]], writes=[orr])
            P.dma(P.sp, io["y"][start + tb * 128:start + tb * 128 + nt, :], orow[:nt, :], osm, reads=[orr])
    self.end_phase()


def rms_mod_multi(self, xT, yv, yres, n, a_vec):
    nc, P = self.nc, self.P
    sq = self.tmpbf
    bk = self.bank()
    for c in range(KC):
        self.A(sq.t[:, c % 2, :n], xT.t[:, c, :n], AF.Square, reads=[xT.r[c]], writes=[sq.r[c % 2]])
        P.group(P.pe, [lambda c=c: nc.tensor.matmul(bk[0][:, :n], self.ones_bf[:], sq.t[:, c % 2, :n],
                                                    start=(c == 0), stop=(c == KC - 1))],
                reads=[sq.r[c % 2], self.r_const], writes=[bk[1]])
    rstd = self.rstd
    self.A(rstd.t[:, 0, :n], bk[0][:, :n], AF.Ln, reads=[bk[1]], writes=[rstd.r[0]], bias=self.eps_rms[:, 0:1], scale=1.0)
    self.A(rstd.t[:, 0, :n], rstd.t[:, 0, :n], AF.Exp, reads=[rstd.r[0]], writes=[rstd.r[0]], scale=-0.5)
    for c in range(KC):
        tm = self.tmpf
        i = c % 2
        self.TT(tm.t[:, i, :n], xT.t[:, c, :n], rstd.t[:, 0, :n], ALU.mult, reads=[xT.r[c], rstd.r[0]], writes=[tm.r[i]])
        self.A(yv[:, c, :n], tm.t[:, i, :n], AF.Identity, reads=[tm.r[i], self.r_vec], writes=yres[c],
               bias=0.0, scale=a_vec[:, c:c + 1])


Builder.phase_mods = phase_mods
Builder.phase_w = phase_w
Builder.phase_qkv = phase_qkv
Builder.phase_attn = phase_attn
Builder.phase_b2 = phase_b2
Builder.rms_mod_multi = rms_mod_multi


def tiles_for(ntok):
    t = []
    st = 0
    while st < ntok:
        n = min(512, ntok - st)
        t.append((st, n))
        st += n
    return t


WSPEC = [
    ("mod0", D, 3 * D), ("mod1", D, 3 * D), ("mod2", D, 3 * D), ("mod3", D, 3 * D),
    ("pw1", D, 2 * D), ("pw2", D, D), ("up0", D, 2 * FF), ("down0", FF, D),
    ("qkv", D, 3 * D), ("ow", D, D), ("up1", D, 2 * FF), ("down1", FF, D),
]


def build_fused():
    nc = bass.Bass("TRN2", target_bir_lowering=False)
    ext = lambda name, shape, d=F32, kind="ExternalInput": nc.dram_tensor(name, shape, d, kind=kind).ap()
    itn = lambda name, shape, d: nc.dram_tensor(name, shape, d).ap()
    B = Builder(nc)
    W = {}
    n_first = 0
    for wi, (name, K_, N_) in enumerate(WSPEC):
        Ks = K_ // 4
        if name.startswith("mod"):
            W[name] = ext("w_" + name, [Ks, N_])
            continue
        wb = 1024 if K_ == D else 256
        nb = N_ // wb
        w = {"shard": ext("w_" + name, [Ks, N_]), "Ks": Ks, "wb": wb, "nb": nb, "res": Res("wf_" + name),
             "wsh": itn("wsh_" + name, [nb, Ks, wb], BF16), "full": itn("wf_" + name, [nb, K_, wb], BF16)}
        W[name] = w
        B.bg_add_weight(wi, w)
        if name == "down0":
            n_first = len(B.bg)
    B.bg_pump(n_first)
    B.n_bg_rest = len(B.bg) - n_first
    part = itn("modpart", [4, 2, 3 * D], F32)
    msum = itn("modsum", [4, 2, 3 * D], F32)
    mb = ext("mb", [4, 3 * D])
    B.phase_mods({"cq": ext("cq", [2, 512]), "mw": [W["mod%d" % m] for m in range(4)], "part": part, "msum": msum})
    x1T = itn("x1T", [KC, 128, NLOC], F32)
    h1loc = itn("h1loc", [KC, 128, NLOC], BF16)
    pm = ext("pm", [128, 1])
    B.phase_a({"xin": ext("xin", [NLOC, D]), "msum": msum, "mb": mb, "svecs": ext("svecs", [7, D]), "pw1b": ext("pw1b", [2, D]),
               "dww": ext("dww", [CW, D]), "ffn_dw": ext("ffn_dw0", [4, 2 * FF]), "pm": pm,
               "pw1_w": W["pw1"], "pw2_w": W["pw2"], "up_w": W["up0"], "down_w": W["down0"],
               "x1T": x1T, "h1T": h1loc}, tiles_for(NLOC))
    B.bg_pump()
    h1all = itn("h1all", [KC, 2, 128, NLOC], BF16)
    sem = B.dsem("ag_h1")
    for cc in range(KC):
        B.P.collective("AllGather", [h1loc[cc]], [h1all[cc].rearrange("r p n -> (r p) n")], PAIRS, sem)
    B.barrier(full=True)
    nh = NH // 2
    QT = itn("QT", [nh, 128, SEQ], BF16)
    KT = itn("KT", [nh, 128, SEQ], BF16)
    V = itn("V", [SEQ, nh * 128], BF16)
    oTloc = itn("oTloc", [nh, 2, 128, NP2], BF16)
    io = {"h1all": h1all, "qkv": W["qkv"], "QT": QT, "KT": KT, "V": V, "oT": oTloc}
    B.phase_qkv(io, HALF // 512, nh)
    B.phase_attn2(io, SEQ // 512, nh)
    oall = itn("oall", [nh, 2, 2, 128, NP2], BF16)
    sem = B.dsem("ag_o")
    for h in range(nh):
        for pt in range(2):
            B.P.collective("AllGather", [oTloc[h, pt]], [oall[h, pt].rearrange("r p n -> (r p) n")], PAIRS, sem)
    B.barrier(full=True)
    B.phase_b2({"x1T": x1T, "oT": oall, "msum": msum, "mb": mb, "svecs2": ext("svecs2", [2, D]), "ffn_dw": ext("ffn_dw1", [4, 2 * FF]),
                "pm": pm, "o_w": W["ow"], "up_w": W["up1"], "down_w": W["down1"],
                "y": ext("y", [NLOC, D], F32, "ExternalOutput")}, tiles_for(NLOC), dyn_o=True)
    B.barrier(full=True)
    return nc


def _f32(a):
    return np.ascontiguousarray(np.asarray(a, dtype=np.float32))


def kernel(x, c, mix_norm_g, mix_mod_w, mix_mod_b, cv_pw1_w, cv_pw1_b, cv_dw_w, cv_dw_b, cv_ln_g, cv_ln_b,
           cv_pw2_w, cv_pw2_b, sb_qkv_w, sb_o_w, ffn_norm_g, ffn_mod_w, ffn_mod_b, ffn_up_w, ffn_dw_w,
           ffn_dw_b, ffn_down_w, final_norm_g):
    x = np.asarray(x)
    c = np.asarray(c)
    cores = list(range(8))
    full = {"mod0": mix_mod_w[0], "mod1": ffn_mod_w[0], "mod2": mix_mod_w[1], "mod3": ffn_mod_w[1],
            "pw1": cv_pw1_w[0], "pw2": cv_pw2_w[0], "up0": ffn_up_w[0], "down0": ffn_down_w[0],
            "qkv": sb_qkv_w[0], "ow": sb_o_w[0], "up1": ffn_up_w[1], "down1": ffn_down_w[1]}
    shared = {
        "mb": _f32(np.stack([mix_mod_b[0], ffn_mod_b[0], mix_mod_b[1], ffn_mod_b[1]])),
        "svecs": _f32(np.stack([mix_norm_g[0], ffn_norm_g[0], mix_norm_g[1], cv_dw_b[0], cv_ln_g[0], cv_ln_b[0], cv_pw2_b[0]])),
        "pw1b": _f32(np.asarray(cv_pw1_b[0]).reshape(2, D)),
        "dww": _f32(cv_dw_w[0]),
        "ffn_dw0": _f32(np.concatenate([np.asarray(ffn_dw_w[0]), np.asarray(ffn_dw_b[0])[None]], 0)),
        "ffn_dw1": _f32(np.concatenate([np.asarray(ffn_dw_w[1]), np.asarray(ffn_dw_b[1])[None]], 0)),
        "svecs2": _f32(np.stack([ffn_norm_g[1], final_norm_g])),
    }
    ims = []
    for i in cores:
        b, r = i // 2, i % 2
        im = dict(shared)
        for name, K_, N_ in WSPEC:
            ks = K_ // 4
            q = i % 4
            im["w_" + name] = _f32(np.asarray(full[name])[q * ks:(q + 1) * ks])
        qd = i // 4
        im["cq"] = _f32(c[2 * qd:2 * qd + 2, 512 * q:512 * q + 512])
        im["pm"] = np.full((128, 1), float(r), np.float32)
        if r == 0:
            im["xin"] = _f32(np.concatenate([np.zeros((PRE, D), np.float32), x[b, 0:HALF]], 0))
        else:
            im["xin"] = _f32(x[b, HALF - PRE:SEQ])
        ims.append(im)
    res = run_bass_kernel_spmd(build_fused(), ims, core_ids=cores)
    out = np.empty((4, SEQ, D), np.float32)
    for i in cores:
        b, r = i // 2, i % 2
        out[b, HALF * r:HALF * (r + 1)] = np.asarray(res.results[i]["y"])[PRE:]
    return out


def build_attn_test(nq, nh):
    T = nq * 512
    nc = bass.Bass("TRN2", target_bir_lowering=False)
    dt = lambda name, shape, d=F32, kind="ExternalInput": nc.dram_tensor(name, shape, d, kind=kind).ap()
    io = {"QT": dt("QT", [nh, 128, T], BF16), "KT": dt("KT", [nh, 128, T], BF16), "V": dt("V", [T, nh * 128], BF16),
          "oT": dt("oT", [nh, 2, 128, NP2], BF16, "ExternalOutput")}
    B = Builder(nc)
    B.phase_attn2(io, nq, nh)
    B.barrier(full=True)
    return nc
```

```python
import numpy as np
import ml_dtypes
import concourse.bass as bass
import concourse.mybir as mybir
from concourse.bass_utils import run_bass_kernel_spmd
from contextlib import ExitStack

F32 = mybir.dt.float32
BF16 = mybir.dt.bfloat16
AF = mybir.ActivationFunctionType
ALU = mybir.AluOpType

D = 2048
KC = 16
FF = 5632
FC = 44
CW = 31
NH = 16
SEQ = 8192
HALF = 4096
PRE = 64
NLOC = HALF + PRE
RMS_EPS = 1e-6
LN_EPS = 1e-5
PAIRS = [[0, 1], [2, 3], [4, 5], [6, 7]]
QUADS = [[0, 1, 2, 3], [4, 5, 6, 7]]
NP2 = NLOC


class Res:
    __slots__ = ("name", "w", "r")

    def __init__(self, name=""):
        self.name = name
        self.w = None
        self.r = {}


class Eng:
    def __init__(self, P, name, eng, is_pe=False):
        self.name = name
        self.eng = eng
        self.is_pe = is_pe
        self.sem = P.new_sem("e_" + name)
        self.count = 0
        self.waited = {}


class Prog:
    def __init__(self, nc):
        self.nc = nc
        self.sems = {}
        self.nsem = 0
        self.pe = Eng(self, "pe", nc.tensor, is_pe=True)
        self.act = Eng(self, "act", nc.scalar)
        self.dve = Eng(self, "dve", nc.vector)
        self.pool = Eng(self, "pool", nc.gpsimd)
        self.sp = Eng(self, "sp", nc.sync)
        self.dma_vals = {}

    def new_sem(self, name):
        h = self.nc.semaphore(name).__enter__()
        key = "s%d_%s" % (self.nsem, name)
        self.nsem += 1
        self.sems[key] = h
        return key

    def _wait(self, E, needs):
        best = {}
        for (k, v) in needs:
            if best.get(k, 0) < v:
                best[k] = v
        for k, v in best.items():
            if k == E.sem:
                if E.is_pe:
                    continue
                if E.count - v >= 2:
                    continue
            if E.waited.get(k, 0) >= v:
                continue
            E.eng.wait_ge(self.sems[k], v)
            E.waited[k] = v

    @staticmethod
    def _deps(reads, writes):
        needs = []
        for r in reads:
            if r.w is not None:
                needs.append(r.w)
        for w in writes:
            if w.w is not None:
                needs.append(w.w)
            needs.extend(w.r.items())
        return needs

    @staticmethod
    def _mark(tok, reads, writes):
        k, v = tok
        for w in writes:
            w.w = tok
            w.r = {}
        for r in reads:
            if r.w is tok:
                continue
            if r.r.get(k, 0) < v:
                r.r[k] = v

    def op(self, E, fn, reads=(), writes=()):
        self._wait(E, self._deps(reads, writes))
        ins = fn()
        ins.then_inc(self.sems[E.sem], 1)
        E.count += 1
        self._mark((E.sem, E.count), reads, writes)
        return ins

    def group(self, E, fns, reads=(), writes=()):
        self._wait(E, self._deps(reads, writes))
        ins = None
        for fn in fns:
            ins = fn()
        ins.then_inc(self.sems[E.sem], 1)
        E.count += 1
        self._mark((E.sem, E.count), reads, writes)
        return ins

    def dma(self, Q, out, in_, sem, reads=(), writes=()):
        self._wait(Q, self._deps(reads, writes))
        ins = Q.eng.dma_start(out=out, in_=in_)
        ins.then_inc(self.sems[sem], 16)
        v = self.dma_vals.get(sem, 0) + 16
        self.dma_vals[sem] = v
        self._mark((sem, v), reads, writes)
        return ins

    def collective(self, kind, ins, outs, groups, sem, reads=(), writes=(), op=None):
        Q = self.pool
        self._wait(Q, self._deps(reads, writes))
        ins_ = self.nc.gpsimd.collective_compute(kind, op if op is not None else ALU.bypass, replica_groups=groups,
                                                 ins=ins, outs=outs)
        ins_.then_inc(self.sems[sem], 1)
        v = self.dma_vals.get(sem, 0) + 1
        self.dma_vals[sem] = v
        self._mark((sem, v), reads, writes)

    def wait_all(self, E, ress):
        needs = []
        for r in ress:
            if r.w is not None:
                needs.append(r.w)
            needs.extend(r.r.items())
        self._wait(E, needs)


class Buf:
    def __init__(self, t, n, name):
        self.t = t
        self.r = [Res("%s%d" % (name, i)) for i in range(n)]


class Builder:
    def __init__(self, nc):
        self.nc = nc
        self.P = Prog(nc)
        P = self.P
        self.stack = None
        self.uid = 0
        self.bg_sems = set()
        self.banks = []
        self.bankpairs = []
        for i in range(4):
            t2 = nc.alloc_psum_tensor("bankp%d" % i, [128, 2, 512], F32)
            self.bankpairs.append(t2)
            for j in range(2):
                self.banks.append((t2[:, j, :], Res("bank%d" % (2 * i + j))))
        self.bank_i = 0
        self.dsems = {}
        self.ident = self.sb("ident", [128, 128], F32)
        self.ones_bf = self.sb("ones_bf", [128, 128], BF16)
        self.r_const = Res("const")
        ones_f = self.sb("ones_f", [128, 128], F32)
        P.op(P.pool, lambda: nc.gpsimd.memset(ones_f[:], 1.0), writes=[self.r_const])
        P.op(P.pool, lambda: nc.gpsimd.affine_select(self.ident[:], ones_f[:], [[1, 128]], ALU.is_equal, 0.0,
                                                     base=0, channel_multiplier=-1),
             reads=[self.r_const], writes=[self.r_const])
        P.op(P.pool, lambda: nc.gpsimd.memset(self.ones_bf[:], 1.0), writes=[self.r_const])
        self.ones_f = ones_f
        self.wstage = [(self.sb("wstage%d" % i, [128, 2048], BF16), Res("wstage%d" % i), self.dsem("wstage%d" % i)) for i in range(2)]
        self.wstage_i = 0
        pid = nc.sync.partition_id()
        self.par = pid % 2
        self.bsel = (pid % 4) // 2
        self.bg = []
        self.bg_i = 0

    def bg_add_weight(self, wi, w):
        P = self.P
        sem = self.dsem("agw%d" % wi)
        self.bg_sems.add(sem)
        Ks, wb, nb = w["Ks"], w["wb"], w["nb"]
        N = wb * nb
        sres = [Res("wsh%d_%d" % (wi, k)) for k in range(2)]

        def unit(r0, c0, wd):
            t, r, sm = self.wstage[self.wstage_i % 2]
            k = self.wstage_i % 2
            self.wstage_i += 1
            P.dma(P.pool, t[:, :wd], w["shard"][r0:r0 + 128, c0:c0 + wd], sm, writes=[r])
            P.dma(P.sp, w["wsh"][c0 // wb:(c0 + wd) // wb, r0:r0 + 128, :].rearrange("b p n -> p b n"),
                  t[:, :wd].rearrange("p (b n) -> p b n", n=wb), sm, reads=[r], writes=[sres[k]])
        for r0 in range(0, Ks, 128):
            for c0 in range(0, N, 2048):
                wd = min(2048, N - c0)
                self.bg.append(lambda r0=r0, c0=c0, wd=wd: unit(r0, c0, wd))

        def gather(bi):
            P.collective("AllGather", [w["wsh"][bi]], [w["full"][bi]], QUADS, sem, reads=sres, writes=[w["res"]])
        for bi in range(nb):
            self.bg.append(lambda bi=bi: gather(bi))

    def bg_pump(self, n=None):
        end = len(self.bg) if n is None else min(len(self.bg), self.bg_i + n)
        while self.bg_i < end:
            self.bg[self.bg_i]()
            self.bg_i += 1

    def sb(self, name, shape, dt):
        if self.stack is None:
            return self.nc.alloc_sbuf_tensor(name, shape, dt)
        self.uid += 1
        return self.stack.enter_context(self.nc.sbuf_tensor("%s_%d" % (name, self.uid), shape, dt))

    def begin_phase(self):
        self.stack = ExitStack()

    def barrier(self, full=False):
        P = self.P
        engs = [P.pe, P.act, P.dve, P.pool, P.sp]
        needs = [(e.sem, e.count) for e in engs if e.count > 0] + \
            [kv for kv in P.dma_vals.items() if full or kv[0] not in self.bg_sems]
        for e in engs:
            P._wait(e, [x for x in needs if x[0] != e.sem])

    def end_phase(self):
        self.barrier()
        self.stack.close()
        self.stack = None

    def dsem(self, name):
        if name not in self.dsems:
            self.dsems[name] = self.P.new_sem("d_" + name)
        return self.dsems[name]

    def bank(self):
        b = self.banks[self.bank_i]
        self.bank_i = (self.bank_i + 1) % 8
        return b

    def mm(self, bank, pairs, n, reads, m=128):
        nc = self.nc
        t, r = bank
        fns = []
        last = len(pairs) - 1
        for i, (l, rh) in enumerate(pairs):
            fns.append(lambda l=l, rh=rh, i=i: nc.tensor.matmul(t[:m, :n], l, rh, start=(i == 0), stop=(i == last)))
        self.P.group(self.P.pe, fns, reads=reads, writes=[r])

    def A(self, out, in_, func, reads, writes, bias=0.0, scale=1.0):
        nc = self.nc
        return self.P.op(self.P.act, lambda: nc.scalar.activation(out, in_, func, bias=bias, scale=scale),
                         reads=reads, writes=writes)

    def TT(self, out, a, b, op, reads, writes, eng=None):
        E = eng or self.P.dve
        return self.P.op(E, lambda: E.eng.tensor_tensor(out, a, b, op), reads=reads, writes=writes)

    def TS(self, out, a, s1, s2, op0, op1, reads, writes, eng=None):
        E = eng or self.P.dve
        if op1 is None:
            return self.P.op(E, lambda: E.eng.tensor_scalar(out, a, s1, None, op0), reads=reads, writes=writes)
        return self.P.op(E, lambda: E.eng.tensor_scalar(out, a, s1, s2, op0, op1), reads=reads, writes=writes)

    def STT(self, out, a, s, b, op0, op1, reads, writes):
        nc = self.nc
        return self.P.op(self.P.dve, lambda: nc.vector.scalar_tensor_tensor(out, a, s, b, op0, op1),
                         reads=reads, writes=writes)

    def CP(self, out, in_, reads, writes, eng=None):
        E = eng or self.P.dve
        return self.P.op(E, lambda: E.eng.tensor_copy(out, in_), reads=reads, writes=writes)

    def init_wstream(self, nslots=3):
        self.wslots = []
        for i in range(nslots):
            t = self.sb("wslot%d" % i, [128, 8192], BF16)
            self.wslots.append((t, Res("wslot%d" % i), self.dsem("wslot%d" % i)))
        self.wplan = []
        self.wnext_issue = 0
        self.wnext_use = 0

    def wplan_add(self, w, k_rows, pieces):
        kc = k_rows // 128
        tot = sum(p[-1] for p in pieces)
        assert kc * tot <= 8192
        self.wplan.append((w, kc, tot, pieces))

    def _wissue(self, upto):
        P = self.P
        while self.wnext_issue < min(upto, len(self.wplan)):
            i = self.wnext_issue
            w, kc, tot, pieces = self.wplan[i]
            t, r, s = self.wslots[i % len(self.wslots)]
            if pieces is None:
                P.dma(P.pool, t[:, 0:kc * tot], w[0], s, reads=[w[1]], writes=[r])
                self.wnext_issue += 1
                continue
            dst = t[:, 0:kc * tot].rearrange("p (kc n) -> p kc n", kc=kc)
            wb = w["wb"]
            o = 0
            for pc in pieces:
                if len(pc) == 2:
                    c0, ncols = pc
                    bi, off = c0 // wb, c0 % wb
                    assert off + ncols <= wb
                    src = w["full"][bi].rearrange("(kc p) n -> p kc n", p=128)[:, :, off:off + ncols]
                else:
                    bi, off, ncols = pc
                    src = w["full"][bass.ds(bi, 1)].rearrange("o (kc p) n -> p (o kc) n", p=128)[:, :, off:off + ncols]
                P.dma(P.pool, dst[:, :, o:o + ncols], src, s, reads=[w["res"]], writes=[r])
                o += ncols
            self.wnext_issue += 1

    def wget(self):
        i = self.wnext_use
        self._wissue(i + len(self.wslots))
        w, kc, tot, pieces = self.wplan[i]
        t, r, s = self.wslots[i % len(self.wslots)]
        self.wnext_use += 1
        return t[:, 0:kc * tot].rearrange("p (kc n) -> p kc n", kc=kc), r

    def rms_mod(self, xT, hT, n, a_vec, sh_vec):
        nc, P = self.nc, self.P
        sq = self.tmpbf
        bk = self.bank()
        for c in range(KC):
            self.A(sq.t[:, c % 2, :n], xT.t[:, c, :n], AF.Square, reads=[xT.r[c]], writes=[sq.r[c % 2]])
            nc_ = nc
            P.group(P.pe, [lambda c=c: nc_.tensor.matmul(bk[0][:, :n], self.ones_bf[:], sq.t[:, c % 2, :n],
                                                          start=(c == 0), stop=(c == KC - 1))],
                    reads=[sq.r[c % 2], self.r_const], writes=[bk[1]])
        rstd = self.rstd
        self.A(rstd.t[:, 0, :n], bk[0][:, :n], AF.Ln, reads=[bk[1]], writes=[rstd.r[0]], bias=self.eps_rms[:, 0:1], scale=1.0)
        self.A(rstd.t[:, 0, :n], rstd.t[:, 0, :n], AF.Exp, reads=[rstd.r[0]], writes=[rstd.r[0]], scale=-0.5)
        for c in range(KC):
            tm = self.tmpf
            i = c % 2
            self.TT(tm.t[:, i, :n], xT.t[:, c, :n], rstd.t[:, 0, :n], ALU.mult, reads=[xT.r[c], rstd.r[0]], writes=[tm.r[i]])
            self.A(hT.t[:, c, :n], tm.t[:, i, :n], AF.Identity, reads=[tm.r[i], self.r_vec], writes=[hT.r[c]],
                   bias=(sh_vec[:, c:c + 1] if sh_vec is not None else 0.0), scale=a_vec[:, c:c + 1])

    def linear(self, act, kc_n, n, slots, epilogue):
        for ids in slots:
            wv, wr = self.wget()
            for jj, cid in enumerate(ids):
                bk = self.bank()
                pairs = [(wv[:, k, jj * 128:(jj + 1) * 128], act.t[:, k, :n]) for k in range(kc_n)]
                self.mm(bk, pairs, n, reads=[wr] + [act.r[k] for k in range(kc_n)])
                epilogue(bk, cid)

    def load_rows_T(self, name, ap2d, rows, ncols):
        nc, P = self.nc, self.P
        nch = ncols // 128
        out = self.sb("m_" + name, [128, nch, rows], F32)
        r = self.r_vec
        stg, rs = self.vec_stage, self.vec_stage_r
        for c0 in range(0, ncols, 2048):
            w = min(2048, ncols - c0)
            if isinstance(ap2d, list):
                ro = 0
                for (apx, nr) in ap2d:
                    P.dma(P.sp, stg[ro:ro + nr, :w], apx[0:nr, c0:c0 + w], self.xrow[0][2], writes=[rs])
                    ro += nr
            else:
                P.dma(P.sp, stg[:rows, :w], ap2d[0:rows, c0:c0 + w], self.xrow[0][2], writes=[rs])
            for cc in range(w // 128):
                bk = self.bank()
                P.group(P.pe, [lambda cc=cc, bk=bk: nc.tensor.transpose(bk[0][:, :rows], stg[:rows, cc * 128:(cc + 1) * 128],
                                                                        self.ident[:rows, :rows])],
                        reads=[rs, self.r_const], writes=[bk[1]])
                self.CP(out[:, c0 // 128 + cc, :], bk[0][:, :rows], reads=[bk[1]], writes=[r])
        return out

    def common_init(self):
        nc, P = self.nc, self.P
        self.xrow = [(self.sb("xrow0", [128, D], F32), Res("xrow0"), self.dsem("xrow0"))] * 2
        self.xrow_i = 0
        self.vec_stage = self.xrow[0][0]
        self.vec_stage_r = self.xrow[0][1]
        self.r_vec = Res("vecs")
        self.sqb = Buf(self.sb("sqb", [128, 2, 512], BF16), 2, "sqb")
        self.tmpf = Buf(self.sb("tmpf", [128, 2, 512], F32), 2, "tmpf")
        self.tmpbf = Buf(self.sb("tmpbf", [128, 2, 512], BF16), 2, "tmpbf")
        self.rstd = Buf(self.sb("rstd", [128, 1, 512], F32), 1, "rstd")
        self.eps_rms = self.sb("eps_rms", [128, 2], F32)
        P.op(P.pool, lambda: nc.gpsimd.memset(self.eps_rms[:, 0:1], float(D * RMS_EPS)), writes=[self.r_const])
        P.op(P.pool, lambda: nc.gpsimd.memset(self.eps_rms[:, 1:2], float(LN_EPS)), writes=[self.r_const])

    def mod_vecs(self, mv, m, gvec, name):
        a = self.sb("a_" + name, [128, KC], F32)
        r = self.r_vec
        self.TS(a[:, :], mv[:, :, 3 * m + 1], 1.0, float(np.sqrt(D)), ALU.add, ALU.mult, reads=[r], writes=[r])
        self.TT(a[:, :], a[:, :], gvec, ALU.mult, reads=[r], writes=[r])
        return a, mv[:, :, 3 * m], mv[:, :, 3 * m + 2]

    def load_mods(self, io):
        nc = self.nc
        bsel = self.bsel
        src = [(io["msum"][m, bass.ds(bsel, 1), :].rearrange("o (t k) -> (o t) k", t=3), 3) for m in range(4)]
        mv = self.load_rows_T("mods", src, 12, D)
        bv = self.load_rows_T("modb", io["mb"].rearrange("m (t k) -> (m t) k", t=3), 12, D)
        self.TT(mv[:, :, :], mv[:, :, :], bv[:, :, :], ALU.add, reads=[self.r_vec], writes=[self.r_vec])
        return mv

    def load_mods_old(self, modall):
        nc, P = self.nc, self.P
        out = self.sb("m_mods", [128, KC, 12], F32)
        stg, rs = self.vec_stage, self.vec_stage_r
        P.dma(P.sp, stg[:12, :].rearrange("q (r k) -> q r k", r=2), modall.rearrange("r m t k -> (m t) r k"),
              self.xrow[0][2], writes=[rs])
        for cc in range(KC):
            bk = self.bank()
            P.group(P.pe, [lambda cc=cc, bk=bk: nc.tensor.transpose(bk[0][:, :12], stg[:12, cc * 128:(cc + 1) * 128],
                                                                    self.ident[:12, :12])],
                    reads=[rs, self.r_const], writes=[bk[1]])
            self.CP(out[:, cc, :], bk[0][:, :12], reads=[bk[1]], writes=[self.r_vec])
        return out

    def load_x_tile(self, xin, start, n, xT):
        nc, P = self.nc, self.P
        nb = (n + 127) // 128
        for b in range(nb):
            nt = min(128, n - b * 128)
            xr, rr, sm = self.xrow[self.xrow_i % 2]
            self.xrow_i += 1
            P.dma(P.sp, xr[:nt, :], xin[start + b * 128:start + b * 128 + nt, :], sm, writes=[rr])
            for g in range(4):
                bk = self.bank()
                fns = []
                for q in range(4):
                    c = 4 * g + q
                    fns.append(lambda q=q, c=c, bk=bk: nc.tensor.transpose(bk[0][:, q * 128:q * 128 + nt],
                                                                           xr[:nt, c * 128:(c + 1) * 128], self.ident[:nt, :nt]))
                P.group(P.pe, fns, reads=[rr, self.r_const], writes=[bk[1]])
                src = bk[0][:, :].rearrange("p (q t) -> p q t", q=4)[:, :, :nt]
                dst = xT.t[:, 4 * g:4 * g + 4, b * 128:b * 128 + nt]
                eng = self.P.act if (g % 2 == 0) else self.P.dve
                if eng is self.P.act:
                    P.op(P.act, lambda dst=dst, src=src: nc.scalar.copy(dst, src), reads=[bk[1]],
                         writes=[xT.r[4 * g + q] for q in range(4)])
                else:
                    self.CP(dst, src, reads=[bk[1]], writes=[xT.r[4 * g + q] for q in range(4)])

    def phase_a(self, io, tiles):
        nc, P = self.nc, self.P
        self.begin_phase()
        self.common_init()
        self.init_wstream(3)
        mv = self.load_mods(io)
        sv = self.load_rows_T("svA", io["svecs"], 7, D)
        pw1b = self.load_rows_T("pw1b", io["pw1b"], 2, D)
        dww = self.load_rows_T("dww", io["dww"], CW, D)
        fdw = self.load_rows_T("fdw", io["ffn_dw"], 4, 2 * FF)
        pm = self.sb("pm_sb", [128, 1], F32)
        P.dma(P.sp, pm[:, :], io["pm"][:, :], self.dsem("pm"), writes=[self.r_vec])
        rv = self.r_vec
        a0, sh0, g0 = self.mod_vecs(mv, 0, sv[:, :, 0], "mix0")
        a1, sh1, g1 = self.mod_vecs(mv, 1, sv[:, :, 1], "ffn0")
        a2, sh2, g2 = self.mod_vecs(mv, 2, sv[:, :, 2], "mix1")
        gb2 = self.sb("gb2", [128, KC], F32)
        self.TT(gb2[:, :], g0, sv[:, :, 6], ALU.mult, reads=[rv], writes=[rv])

        xT = Buf(self.sb("xT", [128, KC, 512], F32), KC, "xT")
        hT = Buf(self.sb("hT", [128, KC, 512], BF16), KC, "hT")
        uT = Buf(self.sb("uT", [128, KC, CW - 1 + 512], BF16), KC, "uT")
        gTt = self.sb("gTt", [128, FC, 512], BF16)
        gT = Buf(gTt, FC, "gT")
        cvt = gTt[:, 0:32, :].bitcast(F32).rearrange("p a b -> p (a b)").rearrange("p (c n) -> p c n", c=KC)
        Tg = Buf(self.sb("Tg", [128, 2, 512], F32), 2, "Tg")
        Tv = Buf(self.sb("Tv", [128, 2, 512], F32), 2, "Tv")
        Hff = self.sb("Hff", [128, 2 * FC, 2], F32)
        rH = [Res("Hff%d" % i) for i in range(2 * FC)]
        stat = Buf(self.sb("stat", [128, 3, 512], F32), 3, "stat")
        P.op(P.pool, lambda: nc.gpsimd.memset(Hff[:, :, :], 0.0), writes=rH)
        P.op(P.pool, lambda: nc.gpsimd.memset(uT.t[:, :, 0:CW - 1], 0.0), writes=uT.r)

        r_diag = Res("diagw")
        for _ in tiles:
            for i in range(8):
                self.wplan_add(io["pw1_w"], D, [(256 * i, 256), (D + 256 * i, 256)])
            for c in range(KC):
                self.wplan.append(((io["diagw"][c], r_diag), CW, 128, None))
            for i in range(4):
                self.wplan_add(io["pw2_w"], D, [(512 * i, 512)])
            for i in range(22):
                self.wplan_add(io["up_w"], D, [(256 * i, 256), (FF + 256 * i, 256)])
            for i in range(16):
                self.wplan_add(io["down_w"], FF, [(128 * i, 128)])

        ident_bf = self.sb("ident_bf", [128, 128], BF16)
        self.CP(ident_bf[:, :], self.ident[:, :], reads=[self.r_const], writes=[self.r_const])
        for c in range(KC):
            t, r, sm = self.wslots[c % 3]
            dv = t[:, 0:CW * 128].rearrange("p (k m) -> p k m", k=CW)
            for k in range(CW):
                self.TS(dv[:, k, :], ident_bf[:, :], dww[:, c, k:k + 1], None, ALU.mult, None, reads=[self.r_const, rv],
                        writes=[r])
            P.dma(P.sp, io["diagw"][c], t[:, 0:CW * 128], sm, reads=[r], writes=[r_diag])

        x1T_d, h1T_d = io["x1T"], io["h1T"]
        osem_x = [self.dsem("ox0"), self.dsem("ox1")]
        osem_h = [self.dsem("oh0"), self.dsem("oh1")]

        per_tile = (getattr(self, "n_bg_rest", 0) + len(tiles) - 2) // max(1, len(tiles) - 1)
        for ti, (start, n) in enumerate(tiles):
            first = (ti == 0)
            self.bg_pump(per_tile)
            self.load_x_tile(io["xin"], start, n, xT)
            self.rms_mod(xT, hT, n, a0, sh0)
            if first:
                pass
            sg = self.tmpf

            def ep_pw1(bk, cid, n=n, first=first):
                if cid < KC:
                    self._valbank[cid] = bk
                else:
                    j = cid - KC
                    vb = self._valbank.pop(j)
                    i = j % 2
                    self.A(sg.t[:, i, :n], bk[0][:, :n], AF.Sigmoid, reads=[bk[1], rv], writes=[sg.r[i]],
                           bias=pw1b[:, j, 1:2], scale=1.0)
                    self.STT(uT.t[:, j, CW - 1:CW - 1 + n], vb[0][:, :n], pw1b[:, j, 0:1], sg.t[:, i, :n], ALU.add, ALU.mult,
                             reads=[vb[1], sg.r[i], rv], writes=[uT.r[j]])
            self._valbank = {}
            slots = [[2 * i, 2 * i + 1, KC + 2 * i, KC + 2 * i + 1] for i in range(8)]
            self.linear(hT, KC, n, slots, ep_pw1)
            if first:
                self.TS(uT.t[:, :, CW - 1:CW - 1 + PRE], uT.t[:, :, CW - 1:CW - 1 + PRE], pm[:, 0:1], None, ALU.mult, None,
                        reads=uT.r + [rv], writes=uT.r)
            cv_r = [[gT.r[2 * c], gT.r[2 * c + 1]] for c in range(KC)]
            for c in range(KC):
                wv, wr = self.wget()
                dvw = wv[:, :, :]
                bkc = self.bank()
                pairs = [(dvw[:, k, :], uT.t[:, c, k:k + n]) for k in range(CW)]
                self.mm(bkc, pairs, n, reads=[wr, uT.r[c]])
                self.A(cvt[:, c, :n], bkc[0][:, :n], AF.Identity, reads=[bkc[1], rv], writes=cv_r[c],
                       bias=sv[:, c, 3:4], scale=1.0)
            bk_m = self.bank()
            bk_s = self.bank()
            for c in range(KC):
                i = c % 2
                self.A(self.tmpbf.t[:, i, :n], cvt[:, c, :n], AF.Identity, reads=cv_r[c], writes=[self.tmpbf.r[i]])
                P.group(P.pe, [lambda c=c, i=i: nc.tensor.matmul(bk_m[0][:, :n], self.ones_bf[:], self.tmpbf.t[:, i, :n],
                                                                 start=(c == 0), stop=(c == KC - 1))],
                        reads=[self.tmpbf.r[i], self.r_const], writes=[bk_m[1]])
                self.A(self.sqb.t[:, i, :n], cvt[:, c, :n], AF.Square, reads=cv_r[c], writes=[self.sqb.r[i]])
                P.group(P.pe, [lambda c=c, i=i: nc.tensor.matmul(bk_s[0][:, :n], self.ones_bf[:], self.sqb.t[:, i, :n],
                                                                 start=(c == 0), stop=(c == KC - 1))],
                        reads=[self.sqb.r[i], self.r_const], writes=[bk_s[1]])
            mean, var, rs = stat.t[:, 0, :n], stat.t[:, 1, :n], stat.t[:, 2, :n]
            self.A(mean, bk_m[0][:, :n], AF.Identity, reads=[bk_m[1]], writes=[stat.r[0]], scale=1.0 / D)
            self.TT(var, mean, mean, ALU.mult, reads=[stat.r[0]], writes=[stat.r[1]])
            self.STT(var, bk_s[0][:, :n], 1.0 / D, var, ALU.mult, ALU.subtract, reads=[bk_s[1], stat.r[1]], writes=[stat.r[1]])
            self.A(rs, var, AF.Ln, reads=[stat.r[1], self.r_const], writes=[stat.r[2]], bias=self.eps_rms[:, 1:2], scale=1.0)
            self.A(rs, rs, AF.Exp, reads=[stat.r[2]], writes=[stat.r[2]], scale=-0.5)
            for c in range(KC):
                i = c % 2
                self.TT(sg.t[:, i, :n], cvt[:, c, :n], mean, ALU.subtract, reads=cv_r[c] + [stat.r[0]], writes=[sg.r[i]])
                self.TT(sg.t[:, i, :n], sg.t[:, i, :n], rs, ALU.mult, reads=[sg.r[i], stat.r[2]], writes=[sg.r[i]])
                self.A(hT.t[:, c, :n], sg.t[:, i, :n], AF.Silu, reads=[sg.r[i], rv], writes=[hT.r[c]],
                       bias=sv[:, c, 5:6], scale=sv[:, c, 4:5])
            self.CP(uT.t[:, :, 0:CW - 1], uT.t[:, :, n:n + CW - 1], reads=uT.r, writes=uT.r)

            def ep_pw2(bk, cid, n=n):
                i = cid % 2
                self.A(sg.t[:, i, :n], bk[0][:, :n], AF.Identity, reads=[bk[1], rv], writes=[sg.r[i]],
                       bias=gb2[:, cid:cid + 1], scale=g0[:, cid:cid + 1])
                self.TT(xT.t[:, cid, :n], xT.t[:, cid, :n], sg.t[:, i, :n], ALU.add, reads=[xT.r[cid], sg.r[i]], writes=[xT.r[cid]])
            self.linear(hT, KC, n, [[4 * i + q for q in range(4)] for i in range(4)], ep_pw2)

            self.rms_mod(xT, hT, n, a1, sh1)
            if first:
                self.TS(hT.t[:, :, 0:PRE], hT.t[:, :, 0:PRE], pm[:, 0:1], None, ALU.mult, None, reads=hT.r + [rv], writes=hT.r)
            self.ffn(hT, xT, gT, Tg, Tv, Hff, rH, fdw, g1, n)

            k = ti % 2
            P.dma(P.sp, x1T_d[:, :, start:start + n].rearrange("c p n -> p c n"), xT.t[:, :, :n], osem_x[k], reads=xT.r)
            self.rms_mod(xT, hT, n, a2, sh2)
            P.dma(P.sp, h1T_d[:, :, start:start + n].rearrange("c p n -> p c n"), hT.t[:, :, :n], osem_h[k], reads=hT.r)
        self.end_phase()

    def ffn(self, hT, xT, gT, Tg, Tv, Hff, rH, fdw, gate, n):
        nc, P = self.nc, self.P
        rv = self.r_vec

        def conv3(bk, ch, T, i):
            p = bk[0]
            w0, w1, w2, b = fdw[:, ch, 0:1], fdw[:, ch, 1:2], fdw[:, ch, 2:3], fdw[:, ch, 3:4]
            t = T.t[:, i, :]
            self.A(t[:, :n], p[:, :n], AF.Identity, reads=[bk[1], rv], writes=[T.r[i]], bias=b, scale=w2)
            self.STT(t[:, 1:n], p[:, 0:n - 1], w1, t[:, 1:n], ALU.mult, ALU.add, reads=[bk[1], rv, T.r[i]], writes=[T.r[i]])
            self.STT(t[:, 2:n], p[:, 0:n - 2], w0, t[:, 2:n], ALU.mult, ALU.add, reads=[bk[1], rv, T.r[i]], writes=[T.r[i]])
            h = Hff[:, ch, :]
            self.STT(t[:, 0:1], h[:, 1:2], w1, t[:, 0:1], ALU.mult, ALU.add, reads=[rH[ch], rv, T.r[i]], writes=[T.r[i]])
            self.STT(t[:, 0:2], h[:, 0:2], w0, t[:, 0:2], ALU.mult, ALU.add, reads=[rH[ch], rv, T.r[i]], writes=[T.r[i]])
            P.op(P.act, lambda: nc.scalar.copy(h[:, 0:2], p[:, n - 2:n]), reads=[bk[1], T.r[i]], writes=[rH[ch]])

        def ep_up(bk, cid, n=n):
            if cid < FC:
                conv3(bk, cid, Tg, cid % 2)
            else:
                j = cid - FC
                i = j % 2
                conv3(bk, cid, Tv, i)
                self.A(Tg.t[:, i, :n], Tg.t[:, i, :n], AF.Silu, reads=[Tg.r[i]], writes=[Tg.r[i]])
                self.TT(gT.t[:, j, :n], Tg.t[:, i, :n], Tv.t[:, i, :n], ALU.mult, reads=[Tg.r[i], Tv.r[i]], writes=[gT.r[j]])
        slots = [[2 * i, 2 * i + 1, FC + 2 * i, FC + 2 * i + 1] for i in range(22)]
        self.linear(hT, KC, n, slots, ep_up)

        def ep_down(bk, cid, n=n):
            self.STT(xT.t[:, cid, :n], bk[0][:, :n], gate[:, cid:cid + 1], xT.t[:, cid, :n], ALU.mult, ALU.add,
                     reads=[bk[1], rv, xT.r[cid]], writes=[xT.r[cid]])
        self.linear(gT, FC, n, [[i] for i in range(16)], ep_down)


def phase_mods(self, io):
    nc, P = self.nc, self.P
    self.begin_phase()
    self.common_init()
    self.init_wstream(3)
    rv = self.r_vec
    cT = self.load_rows_T("cq", io["cq"], 2, 512)
    sc = self.sb("sc", [128, 4, 2], BF16)
    self.A(sc[:, :, :], cT[:, :, :], AF.Silu, reads=[rv], writes=[rv])
    NM = 3 * D
    orow = self.sb("orow", [2, NM], F32)
    r_o = Res("orow")
    sem_o = self.dsem("orow")
    wi = 0
    for m in range(4):
        for g3 in range(3):
            t, r, sm = self.wslots[wi % 3]
            wi += 1
            wv = t[:, :].rearrange("p (kc n) -> p kc n", kc=4)
            P.dma(P.pool, wv, io["mw"][m].rearrange("(kc p) n -> p kc n", p=128)[:, :, g3 * 2048:(g3 + 1) * 2048], sm, writes=[r])
            for g in range(4):
                bk = self.bank()
                pairs = [(sc[:, k, :], wv[:, k, g * 512:(g + 1) * 512]) for k in range(4)]
                self.mm(bk, pairs, 512, reads=[r, rv], m=2)
                o = g3 * 2048 + g * 512
                self.CP(orow[0:2, o:o + 512], bk[0][0:2, :512], reads=[bk[1]], writes=[r_o])
        P.dma(P.sp, io["part"][m], orow[0:2, :], sem_o, reads=[r_o])
    P.collective("AllReduce", [io["part"].rearrange("m b k -> (m b) k")], [io["msum"].rearrange("m b k -> (m b) k")], QUADS,
                 self.dsem("ar_mods"), reads=[r_o], op=ALU.add)
    self.end_phase()
    self.barrier(full=True)


def phase_w(self, wl):
    nc, P = self.nc, self.P
    self.begin_phase()
    self.init_wstream(3)
    i = 0
    for wi, w in enumerate(wl):
        sem = self.dsem("agw%d" % wi)
        self.bg_sems.add(sem)
        Ks, wb, nb = w["Ks"], w["wb"], w["nb"]
        N = wb * nb
        sres = [Res("wsh%d_%d" % (wi, k)) for k in range(3)]
        for r0 in range(0, Ks, 128):
            for c0 in range(0, N, 8192):
                wd = min(8192, N - c0)
                t, r, sm = self.wslots[i % 3]
                P.dma(P.pool, t[:, :wd], w["shard"][r0:r0 + 128, c0:c0 + wd], sm, writes=[r])
                P.dma(P.sp, w["wsh"][c0 // wb:(c0 + wd) // wb, r0:r0 + 128, :].rearrange("b p n -> p b n"),
                      t[:, :wd].rearrange("p (b n) -> p b n", n=wb), sm, reads=[r], writes=[sres[i % 3]])
                i += 1
        for bi in range(nb):
            P.collective("AllGather", [w["wsh"][bi]], [w["full"][bi]], QUADS, sem, reads=sres, writes=[w["res"]])
    self.end_phase()


def phase_qkv(self, io, ntile, nh):
    nc, P = self.nc, self.P
    self.begin_phase()
    self.init_wstream(3)
    ns = nh // 4
    hTs = [Buf(self.sb("hq%d" % i, [128, KC, 512], BF16), KC, "hq%d" % i) for i in range(2)]
    hsem = [self.dsem("hq0"), self.dsem("hq1")]
    qst = Buf(self.sb("qst", [128, nh, 512], BF16), nh, "qst")
    kst = Buf(self.sb("kst", [128, nh, 512], BF16), nh, "kst")
    vst = Buf(self.sb("vst", [128, 4, nh * 128], BF16), 4, "vst")
    sq, sk, sv_ = self.dsem("oq"), self.dsem("ok"), self.dsem("ov")
    par = nc.gpsimd.partition_id() % 2
    for _ in range(2 * ntile):
        for which in range(3):
            for i in range(ns):
                self.wplan_add(io["qkv"], D, [(par + 2 * which, 512 * i, 512)])
    it = 0
    for s in range(2):
        for t in range(ntile):
            p0 = (s * ntile + t) * 512
            lc = PRE + 512 * t
            hT = hTs[it % 2]
            P.dma(P.sp, hT.t[:, :, :], io["h1all"][:, s, :, lc:lc + 512].rearrange("c p n -> p c n"), hsem[it % 2], writes=hT.r)
            it += 1
            for (st, dst, sem) in ((qst, io["QT"], sq), (kst, io["KT"], sk)):
                def ep(bk, cid, st=st):
                    if cid % 2 == 0:
                        P.op(P.act, lambda: nc.scalar.copy(st.t[:, cid, :], bk[0][:, :512]), reads=[bk[1]], writes=[st.r[cid]])
                    else:
                        self.CP(st.t[:, cid, :], bk[0][:, :512], reads=[bk[1]], writes=[st.r[cid]])
                self.linear(hT, KC, 512, [[4 * i + q for q in range(4)] for i in range(ns)], ep)
                P.dma(P.sp, dst[:, :, p0:p0 + 512].rearrange("h p n -> p h n"), st.t[:, :, :], sem, reads=st.r)
            for cg in range(ns):
                wv, wr = self.wget()
                for tb in range(4):
                    bk = self.bank()
                    pairs = [(hT.t[:, k, tb * 128:(tb + 1) * 128], wv[:, k, :]) for k in range(KC)]
                    self.mm(bk, pairs, 512, reads=[wr] + hT.r)
                    if tb % 2 == 0:
                        P.op(P.act, lambda tb=tb, bk=bk: nc.scalar.copy(vst.t[:, tb, cg * 512:(cg + 1) * 512], bk[0][:, :512]),
                             reads=[bk[1]], writes=[vst.r[tb]])
                    else:
                        self.CP(vst.t[:, tb, cg * 512:(cg + 1) * 512], bk[0][:, :512], reads=[bk[1]], writes=[vst.r[tb]])
            P.dma(P.sp, io["V"][p0:p0 + 512, :].rearrange("(tb p) f -> p tb f", p=128), vst.t[:, :, :], sv_, reads=vst.r)
    self.end_phase()


def phase_attn(self, io, nq, nh):
    nc, P = self.nc, self.P
    self.begin_phase()
    T = nq * 512
    NKB = nq * 4
    scale = 1.0 / float(np.sqrt(128.0))
    rc = self.r_const
    masks = self.sb("masks", [128, 4, 512], F32)
    tri = self.sb("tri", [128, 128], BF16)
    comp = self.sb("comp", [128, 128], BF16)
    onesw = self.sb("onesw", [128, 512], F32)
    P.op(P.pool, lambda: nc.gpsimd.memset(onesw[:, :], 1.0), writes=[rc])
    for i in range(4):
        P.op(P.pool, lambda i=i: nc.gpsimd.affine_select(masks[:, i, :], onesw[:, :], [[1, 512]], ALU.is_gt, 0.0,
                                                         base=-128 * i, channel_multiplier=-1), reads=[rc], writes=[rc])
    P.op(P.pool, lambda: nc.gpsimd.affine_select(tri[:, :], onesw[:, 0:128], [[-1, 128]], ALU.is_gt, 0.0,
                                                 base=1, channel_multiplier=1), reads=[rc], writes=[rc])
    P.op(P.pool, lambda: nc.gpsimd.affine_select(comp[:, :], onesw[:, 0:128], [[1, 128]], ALU.is_gt, 0.0,
                                                 base=0, channel_multiplier=-1), reads=[rc], writes=[rc])
    Ksb = [(self.sb("Ksb%d" % i, [128, T], BF16), Res("Ksb%d" % i), self.dsem("Ksb%d" % i)) for i in range(2)]
    Vsb = [(self.sb("Vsb%d" % i, [128, NKB, 128], BF16), Res("Vsb%d" % i), self.dsem("Vsb%d" % i)) for i in range(2)]
    NL = 2
    Eb = [Buf(self.sb("Eb%d" % l, [128, 3, 512], F32), 3, "Eb%d" % l) for l in range(NL)]
    Lb = [Buf(self.sb("Lb%d" % l, [128, 3, 512], BF16), 3, "Lb%d" % l) for l in range(NL)]
    Gb = [Buf(self.sb("Gb%d" % l, [128, 2, 512], F32), 2, "Gb%d" % l) for l in range(NL)]
    Ab = [Buf(self.sb("Ab%d" % l, [128, 2, 512], BF16), 2, "Ab%d" % l) for l in range(NL)]
    Ob = [Buf(self.sb("Ob%d" % l, [128, 2, 512], BF16), 2, "Ob%d" % l) for l in range(NL)]
    Qsb = [[(self.sb("Qsb%d_%d" % (l, i), [128, 512], BF16), Res("Qsb%d_%d" % (l, i)), self.dsem("Qsb%d_%d" % (l, i)))
            for i in range(2)] for l in range(NL)]
    osem = [[self.dsem("oo%d_%d" % (l, i)) for i in range(2)] for l in range(NL)]
    zt = self.sb("zpad", [128, nh, PRE], BF16)
    r_z = Res("zpad")
    P.op(P.pool, lambda: nc.gpsimd.memset(zt[:, :, :], 0.0), writes=[r_z])
    P.dma(P.sp, io["oT"][:, 0, :, 0:PRE].rearrange("h p n -> p h n"), zt[:, :, :], self.dsem("zpad"), reads=[r_z])
    Sbk = [self.banks[0:2], self.banks[2:4]]
    Rbk = self.banks[4:6]
    Obk = self.banks[6:8]
    loaded = {}

    def load_head(h):
        if h in loaded or h >= nh:
            return
        Kt, Kr, Ks = Ksb[h % 2]
        Vt, Vr, Vs = Vsb[h % 2]
        P.dma(P.sp, Kt[:, :], io["KT"][h, :, :], Ks, writes=[Kr])
        P.dma(P.sp, Vt[:, :, :], io["V"][:, h * 128:(h + 1) * 128].rearrange("(kb p) d -> p kb d", p=128), Vs, writes=[Vr])
        loaded[h] = True

    cnt = [0] * NL

    class Chain:
        pass

    def start_chain(l, h, j):
        load_head(h)
        if j == nq // 2:
            load_head(h + 1)
        c = Chain()
        c.l, c.h, c.j = l, h, j
        c.Kt, c.Kr, _ = Ksb[h % 2]
        c.Vt, c.Vr, _ = Vsb[h % 2]
        c.ci = cnt[l]
        cnt[l] += 1
        c.Qt, c.Qr, Qs = Qsb[l][c.ci % 2]
        P.dma(P.sp, c.Qt[:, :], io["QT"][h, :, j * 512:(j + 1) * 512], Qs, writes=[c.Qr])
        c.steps = list(range(4 * j + 3, -1, -1))
        c.N = len(c.steps)
        c.k = 0
        mm1(c, 0)
        if c.N > 1:
            mm1(c, 1)
        e_(c, 0)
        l_(c, 0)
        mm2(c, 0)
        return c

    def mm1(c, k):
        kb = c.steps[k]
        sbk = Sbk[c.l][k % 2]
        self.mm(sbk, [(c.Kt[:, kb * 128:(kb + 1) * 128], c.Qt[:, :])], 512, reads=[c.Kr, c.Qr])

    def e_(c, k):
        kb = c.steps[k]
        sbk = Sbk[c.l][k % 2]
        E = Eb[c.l]
        e = E.t[:, k % 3, :]
        self.A(e, sbk[0][:, :], AF.Exp, reads=[sbk[1]], writes=[E.r[k % 3]], scale=scale)
        i = kb - 4 * c.j
        if i >= 0:
            self.TT(e, e, masks[:, i, :], ALU.mult, reads=[E.r[k % 3], rc], writes=[E.r[k % 3]], eng=P.pool)

    def l_(c, k):
        E, L = Eb[c.l], Lb[c.l]
        self.A(L.t[:, k % 3, :], E.t[:, k % 3, :], AF.Ln, reads=[E.r[k % 3]], writes=[L.r[k % 3]], bias=1.0, scale=1.0)

    def mm2(c, k):
        L, Rb = Lb[c.l], Rbk[c.l]
        P.group(P.pe, [lambda: nc.tensor.matmul(Rb[0][:, :], tri[:, :], L.t[:, k % 3, :], start=(k == 0), stop=False)],
                reads=[L.r[k % 3], rc], writes=[Rb[1]])

    def g_(c, k):
        G, Rb = Gb[c.l], Rbk[c.l]
        self.A(G.t[:, k % 2, :], Rb[0][:, :], AF.Exp, reads=[Rb[1]], writes=[G.r[k % 2]], scale=-1.0)

    def a_(c, k):
        E, G, A_ = Eb[c.l], Gb[c.l], Ab[c.l]
        self.TT(A_.t[:, k % 2, :], E.t[:, k % 3, :], G.t[:, k % 2, :], ALU.mult,
                reads=[E.r[k % 3], G.r[k % 2]], writes=[A_.r[k % 2]])

    def mm3(c, k):
        L, Rb = Lb[c.l], Rbk[c.l]
        P.group(P.pe, [lambda: nc.tensor.matmul(Rb[0][:, :], comp[:, :], L.t[:, k % 3, :], start=False, stop=(k == c.N - 1))],
                reads=[L.r[k % 3], rc], writes=[Rb[1]])

    def mm4(c, k):
        A_, OB = Ab[c.l], Obk[c.l]
        kb = c.steps[k]
        P.group(P.pe, [lambda: nc.tensor.matmul(OB[0][:, :], c.Vt[:, kb, :], A_.t[:, k % 2, :], start=(k == 0), stop=(k == c.N - 1))],
                reads=[c.Vr, A_.r[k % 2]], writes=[OB[1]])

    def finish(c):
        l, h, j, ci = c.l, c.h, c.j, c.ci
        ob, OB = Ob[l], Obk[l]
        self.CP(ob.t[:, ci % 2, :], OB[0][:, :], reads=[OB[1]], writes=[ob.r[ci % 2]])
        sem = osem[l][ci % 2]
        pc0 = PRE + j * 512
        if pc0 + 512 <= NP2:
            P.dma(P.sp, io["oT"][h, 0, :, pc0:pc0 + 512], ob.t[:, ci % 2, :], sem, reads=[ob.r[ci % 2]])
            if pc0 + 512 > HALF:
                P.dma(P.sp, io["oT"][h, 1, :, 0:pc0 + 512 - HALF], ob.t[:, ci % 2, HALF - pc0:512], sem, reads=[ob.r[ci % 2]])
        else:
            P.dma(P.sp, io["oT"][h, 1, :, pc0 - HALF:pc0 - HALF + 512], ob.t[:, ci % 2, :], sem, reads=[ob.r[ci % 2]])

    work = [(h, j) for h in range(nh) for j in range(nq)]
    lanes = [None] * NL
    wi = 0
    while True:
        for l in range(NL):
            if lanes[l] is None and wi < len(work):
                lanes[l] = start_chain(l, *work[wi])
                wi += 1
        act = [c for c in lanes if c is not None]
        if not act:
            break
        for c in act:
            if c.k + 2 < c.N:
                mm1(c, c.k + 2)
        for c in act:
            if c.k + 1 < c.N:
                e_(c, c.k + 1)
        for c in act:
            if c.k + 1 < c.N:
                l_(c, c.k + 1)
        for c in act:
            g_(c, c.k)
        for c in act:
            a_(c, c.k)
        for c in act:
            mm3(c, c.k)
            if c.k + 1 < c.N:
                mm2(c, c.k + 1)
            if c.k >= 1:
                mm4(c, c.k - 1)
            if c.k == c.N - 1:
                mm4(c, c.k)
                finish(c)
                lanes[c.l] = None
            c.k += 1
    self.end_phase()


def phase_attn2(self, io, nq, nh):
    nc, P = self.nc, self.P
    self.begin_phase()
    T = nq * 512
    NKB = nq * 4
    scale = 1.0 / float(np.sqrt(128.0))
    rc = self.r_const
    masks = self.sb("masks", [128, 4, 2, 512], F32)
    tri = self.sb("tri", [128, 128], BF16)
    comp = self.sb("comp", [128, 128], BF16)
    onesw = self.sb("onesw", [128, 512], F32)
    P.op(P.pool, lambda: nc.gpsimd.memset(onesw[:, :], 1.0), writes=[rc])
    for i in range(4):
        for l in range(2):
            P.op(P.pool, lambda i=i, l=l: nc.gpsimd.affine_select(masks[:, i, l, :], onesw[:, :], [[1, 512]], ALU.is_gt, 0.0,
                                                                  base=-128 * i, channel_multiplier=-1), reads=[rc], writes=[rc])
    P.op(P.pool, lambda: nc.gpsimd.affine_select(tri[:, :], onesw[:, 0:128], [[-1, 128]], ALU.is_gt, 0.0,
                                                 base=1, channel_multiplier=1), reads=[rc], writes=[rc])
    P.op(P.pool, lambda: nc.gpsimd.affine_select(comp[:, :], onesw[:, 0:128], [[1, 128]], ALU.is_gt, 0.0,
                                                 base=0, channel_multiplier=-1), reads=[rc], writes=[rc])
    NB = 4
    Ksb = [(self.sb("Ksb%d" % i, [128, T], BF16), Res("Ksb%d" % i), self.dsem("Ksb%d" % i)) for i in range(NB)]
    Vsb = [(self.sb("Vsb%d" % i, [128, NKB, 128], BF16), Res("Vsb%d" % i), self.dsem("Vsb%d" % i)) for i in range(NB)]
    Eb = Buf(self.sb("Eb", [128, 3, 2, 512], F32), 3, "Eb")
    Lb = Buf(self.sb("Lb", [128, 3, 2, 512], BF16), 3, "Lb")
    Gb = Buf(self.sb("Gb", [128, 2, 2, 512], F32), 2, "Gb")
    Ab = Buf(self.sb("Ab", [128, 2, 2, 512], BF16), 2, "Ab")
    Ob = Buf(self.sb("Ob", [128, 2, 2, 512], BF16), 2, "Ob")
    NQB = 3
    Qsb = [(self.sb("Qsb%d" % i, [128, 2, 512], BF16), Res("Qsb%d" % i), self.dsem("Qsb%d" % i)) for i in range(NQB)]
    osem = [self.dsem("oo%d" % i) for i in range(2)]
    zt = self.sb("zpad", [128, nh, PRE], BF16)
    r_z = Res("zpad")
    P.op(P.pool, lambda: nc.gpsimd.memset(zt[:, :, :], 0.0), writes=[r_z])
    P.dma(P.sp, io["oT"][:, 0, :, 0:PRE].rearrange("h p n -> p h n"), zt[:, :, :], self.dsem("zpad"), reads=[r_z])
    Sp = [self.bankpairs[0], self.bankpairs[1]]
    Sr = [[self.banks[0][1], self.banks[1][1]], [self.banks[2][1], self.banks[3][1]]]
    Rp = self.bankpairs[2]
    Rr = [self.banks[4][1], self.banks[5][1]]
    Op = self.bankpairs[3]
    Or = [self.banks[6][1], self.banks[7][1]]
    loaded = {}

    def load_head(h):
        if h in loaded or h >= nh:
            return
        Kt, Kr, Ks = Ksb[h % NB]
        Vt, Vr, Vs = Vsb[h % NB]
        P.dma(P.sp, Kt[:, :], io["KT"][h, :, :], Ks, writes=[Kr])
        P.dma(P.sp, Vt[:, :, :], io["V"][:, h * 128:(h + 1) * 128].rearrange("(kb p) d -> p kb d", p=128), Vs, writes=[Vr])
        loaded[h] = True

    steps = []
    for hp in range(nh // 2):
        for j in range(nq):
            N = 4 * j + 4
            for k in range(N):
                steps.append((hp, j, k, N))
    NS = len(steps)
    qbuf = {}
    qcount = [0]

    def get_q(hp, j):
        key = (hp, j)
        if key not in qbuf:
            load_head(2 * hp)
            load_head(2 * hp + 1)
            if j == nq // 2:
                load_head(2 * hp + 2)
                load_head(2 * hp + 3)
            Qt, Qr, Qs = Qsb[qcount[0] % NQB]
            qcount[0] += 1
            for l in range(2):
                P.dma(P.sp, Qt[:, l, :], io["QT"][2 * hp + l, :, j * 512:(j + 1) * 512], Qs, writes=[Qr])
            qbuf[key] = (Qt, Qr)
        return qbuf[key]

    def mm1(s):
        hp, j, k, N = steps[s]
        kb = 4 * j + 3 - k
        Qt, Qr = get_q(hp, j)
        for l in range(2):
            h = 2 * hp + l
            Kt, Kr, _ = Ksb[h % NB]
            t = Sp[s % 2]
            P.group(P.pe, [lambda: nc.tensor.matmul(t[:, l, :], Kt[:, kb * 128:(kb + 1) * 128], Qt[:, l, :], start=True, stop=True)],
                    reads=[Kr, Qr], writes=[Sr[s % 2][l]])

    def el(s):
        hp, j, k, N = steps[s]
        kb = 4 * j + 3 - k
        self.A(Eb.t[:, s % 3, :, :], Sp[s % 2][:, :, :], AF.Exp, reads=Sr[s % 2], writes=[Eb.r[s % 3]], scale=scale)
        i = kb - 4 * j
        if i >= 0:
            self.TT(Eb.t[:, s % 3, :, :], Eb.t[:, s % 3, :, :], masks[:, i, :, :], ALU.mult, reads=[Eb.r[s % 3], rc],
                    writes=[Eb.r[s % 3]], eng=P.pool)
        self.A(Lb.t[:, s % 3, :, :], Eb.t[:, s % 3, :, :], AF.Ln, reads=[Eb.r[s % 3]], writes=[Lb.r[s % 3]], bias=1.0, scale=1.0)

    def mm2(s):
        hp, j, k, N = steps[s]
        for l in range(2):
            P.group(P.pe, [lambda: nc.tensor.matmul(Rp[:, l, :], tri[:, :], Lb.t[:, s % 3, l, :], start=(k == 0), stop=False)],
                    reads=[Lb.r[s % 3], rc], writes=[Rr[l]])

    def mm3(s):
        hp, j, k, N = steps[s]
        for l in range(2):
            P.group(P.pe, [lambda: nc.tensor.matmul(Rp[:, l, :], comp[:, :], Lb.t[:, s % 3, l, :], start=False, stop=(k == N - 1))],
                    reads=[Lb.r[s % 3], rc], writes=[Rr[l]])

    def mm4(s):
        hp, j, k, N = steps[s]
        kb = 4 * j + 3 - k
        for l in range(2):
            h = 2 * hp + l
            Vt, Vr, _ = Vsb[h % NB]
            P.group(P.pe, [lambda: nc.tensor.matmul(Op[:, l, :], Vt[:, kb, :], Ab.t[:, s % 2, l, :], start=(k == 0), stop=(k == N - 1))],
                    reads=[Vr, Ab.r[s % 2]], writes=[Or[l]])
        if k == N - 1:
            ci = hp * nq + j
            self.CP(Ob.t[:, ci % 2, :, :], Op[:, :, :], reads=Or, writes=[Ob.r[ci % 2]])
            sem = osem[ci % 2]
            pc0 = PRE + j * 512
            for l in range(2):
                h = 2 * hp + l
                ob = Ob.t[:, ci % 2, l, :]
                if pc0 + 512 <= NP2:
                    P.dma(P.sp, io["oT"][h, 0, :, pc0:pc0 + 512], ob, sem, reads=[Ob.r[ci % 2]])
                    if pc0 + 512 > HALF:
                        P.dma(P.sp, io["oT"][h, 1, :, 0:pc0 + 512 - HALF], Ob.t[:, ci % 2, l, HALF - pc0:512], sem, reads=[Ob.r[ci % 2]])
                else:
                    P.dma(P.sp, io["oT"][h, 1, :, pc0 - HALF:pc0 - HALF + 512], ob, sem, reads=[Ob.r[ci % 2]])

    mm1(0)
    if NS > 1:
        mm1(1)
    el(0)
    mm2(0)
    for s in range(NS):
        if s + 2 < NS:
            mm1(s + 2)
        if s + 1 < NS:
            el(s + 1)
        self.A(Gb.t[:, s % 2, :, :], Rp[:, :, :], AF.Exp, reads=Rr, writes=[Gb.r[s % 2]], scale=-1.0)
        self.TT(Ab.t[:, s % 2, :, :], Eb.t[:, s % 3, :, :], Gb.t[:, s % 2, :, :], ALU.mult,
                reads=[Eb.r[s % 3], Gb.r[s % 2]], writes=[Ab.r[s % 2]])
        mm3(s)
        if s + 1 < NS:
            mm2(s + 1)
        if s >= 1:
            mm4(s - 1)
    mm4(NS - 1)
    self.end_phase()


Builder.phase_attn2 = phase_attn2


def phase_b2(self, io, tiles, dyn_o):
    nc, P = self.nc, self.P
    self.begin_phase()
    self.common_init()
    self.init_wstream(3)
    rv = self.r_vec
    mv = self.load_mods(io)
    sv = self.load_rows_T("svB", io["svecs2"], 2, D)
    fdw = self.load_rows_T("fdwB", io["ffn_dw"], 4, 2 * FF)
    pm = self.sb("pm_sb", [128, 1], F32)
    P.dma(P.sp, pm[:, :], io["pm"][:, :], self.dsem("pm"), writes=[rv])
    a3, sh3, g3 = self.mod_vecs(mv, 3, sv[:, :, 0], "ffn1")
    g_mix1 = mv[:, :, 8]
    afin = self.sb("afin", [128, KC], F32)
    self.TS(afin[:, :], sv[:, :, 1], float(np.sqrt(D)), None, ALU.mult, None, reads=[rv], writes=[rv])
    xT = Buf(self.sb("xT", [128, KC, 512], F32), KC, "xT")
    hT = Buf(self.sb("hT", [128, KC, 512], BF16), KC, "hT")
    gTt = self.sb("gTt", [128, FC, 512], BF16)
    gT = Buf(gTt, FC, "gT")
    yv = gTt[:, 0:32, :].bitcast(F32).rearrange("p a b -> p (a b)").rearrange("p (c n) -> p c n", c=KC)
    Tg = Buf(self.sb("Tg", [128, 2, 512], F32), 2, "Tg")
    Tv = Buf(self.sb("Tv", [128, 2, 512], F32), 2, "Tv")
    Hff = self.sb("Hff", [128, 2 * FC, 2], F32)
    rH = [Res("Hff%d" % i) for i in range(2 * FC)]
    P.op(P.pool, lambda: nc.gpsimd.memset(Hff[:, :, :], 0.0), writes=rH)
    for _ in tiles:
        for i in range(4):
            self.wplan_add(io["o_w"], D, [(512 * i, 512)])
        for i in range(22):
            self.wplan_add(io["up_w"], D, [(256 * i, 256), (FF + 256 * i, 256)])
        for i in range(16):
            self.wplan_add(io["down_w"], FF, [(128 * i, 128)])
    sx, so = self.dsem("ldx"), self.dsem("ldo")
    if dyn_o:
        par = self.par
    for ti, (start, n) in enumerate(tiles):
        P.dma(P.sp, xT.t[:, :, :n], io["x1T"][:, :, start:start + n].rearrange("c p n -> p c n"), sx, writes=xT.r)
        for rk in range(2):
            if n == 512:
                src = io["oT"][:, bass.ds(par, 1), rk, :, start:start + n].rearrange("h o p n -> p (h o) n")
                P.dma(P.sp, hT.t[:, 8 * rk:8 * rk + 8, :n], src, so, writes=hT.r[8 * rk:8 * rk + 8])
            else:
                parg = nc.gpsimd.partition_id() % 2
                src = io["oT"][:, bass.ds(parg, 1), rk, :, start:start + n].rearrange("h o p n -> p (h o) n")
                P.dma(P.pool, hT.t[:, 8 * rk:8 * rk + 8, :n], src, so, writes=hT.r[8 * rk:8 * rk + 8])

        def ep_o(bk, cid, n=n):
            self.STT(xT.t[:, cid, :n], bk[0][:, :n], g_mix1[:, cid:cid + 1], xT.t[:, cid, :n], ALU.mult, ALU.add,
                     reads=[bk[1], rv, xT.r[cid]], writes=[xT.r[cid]])
        self.linear(hT, KC, n, [[4 * i + q for q in range(4)] for i in range(4)], ep_o)
        self.rms_mod(xT, hT, n, a3, sh3)
        if ti == 0:
            self.TS(hT.t[:, :, 0:PRE], hT.t[:, :, 0:PRE], pm[:, 0:1], None, ALU.mult, None, reads=hT.r + [rv], writes=hT.r)
        self.ffn(hT, xT, gT, Tg, Tv, Hff, rH, fdw, g3, n)
        self.rms_mod_multi(xT, yv, [[gT.r[2 * c], gT.r[2 * c + 1]] for c in range(KC)], n, afin)
        nb = (n + 127) // 128
        for tb in range(nb):
            nt = min(128, n - tb * 128)
            orow, orr, osm = self.xrow[self.xrow_i % 2]
            self.xrow_i += 1
            for gq in range(4):
                bk = self.bank()
                fns = []
                for q in range(4):
                    c = 4 * gq + q
                    fns.append(lambda q=q, c=c, bk=bk: nc.tensor.transpose(bk[0][:nt, q * 128:(q + 1) * 128],
                                                                           yv[:, c, tb * 128:tb * 128 + nt], self.ident[:, :]))
                rr = []
                for q in range(4):
                    rr += [gT.r[2 * (4 * gq + q)], gT.r[2 * (4 * gq + q) + 1]]
                P.group(P.pe, fns, reads=rr + [self.r_const], writes=[bk[1]])
                if gq % 2 == 0:
                    P.op(P.act, lambda gq=gq, bk=bk: nc.scalar.copy(orow[:nt, gq * 512:(gq + 1) * 512], bk[0][:nt, :]),
                         reads=[bk[1]], writes=[orr])
                else:
                    self.CP(orow[:nt, gq * 512:(gq + 1) * 512], bk[0][:nt, :], reads=[bk[1]], writes=[orr])
            P.dma(P.sp, io["y"][start + tb * 128:start + tb * 128 + nt, :], orow[:nt, :], osm, reads=[orr])
    self.end_phase()


def rms_mod_multi(self, xT, yv, yres, n, a_vec):
    nc, P = self.nc, self.P
    sq = self.tmpbf
    bk = self.bank()
    for c in range(KC):
        self.A(sq.t[:, c % 2, :n], xT.t[:, c, :n], AF.Square, reads=[xT.r[c]], writes=[sq.r[c % 2]])
        P.group(P.pe, [lambda c=c: nc.tensor.matmul(bk[0][:, :n], self.ones_bf[:], sq.t[:, c % 2, :n],
                                                    start=(c == 0), stop=(c == KC - 1))],
                reads=[sq.r[c % 2], self.r_const], writes=[bk[1]])
    rstd = self.rstd
    self.A(rstd.t[:, 0, :n], bk[0][:, :n], AF.Ln, reads=[bk[1]], writes=[rstd.r[0]], bias=self.eps_rms[:, 0:1], scale=1.0)
    self.A(rstd.t[:, 0, :n], rstd.t[:, 0, :n], AF.Exp, reads=[rstd.r[0]], writes=[rstd.r[0]], scale=-0.5)
    for c in range(KC):
        tm = self.tmpf
        i = c % 2
        self.TT(tm.t[:, i, :n], xT.t[:, c, :n], rstd.t[:, 0, :n], ALU.mult, reads=[xT.r[c], rstd.r[0]], writes=[tm.r[i]])
        self.A(yv[:, c, :n], tm.t[:, i, :n], AF.Identity, reads=[tm.r[i], self.r_vec], writes=yres[c],
               bias=0.0, scale=a_vec[:, c:c + 1])


Builder.phase_mods = phase_mods
Builder.phase_w = phase_w
Builder.phase_qkv = phase_qkv
Builder.phase_attn = phase_attn
Builder.phase_b2 = phase_b2
Builder.rms_mod_multi = rms_mod_multi


def tiles_for(ntok):
    t = []
    st = 0
    while st < ntok:
        n = min(512, ntok - st)
        t.append((st, n))
        st += n
    return t


WSPEC = [
    ("mod0", D, 3 * D), ("mod1", D, 3 * D), ("mod2", D, 3 * D), ("mod3", D, 3 * D),
    ("pw1", D, 2 * D), ("pw2", D, D), ("up0", D, 2 * FF), ("down0", FF, D),
    ("qkv", D, 3 * D), ("ow", D, D), ("up1", D, 2 * FF), ("down1", FF, D),
]


def build_fused():
    nc = bass.Bass("TRN2", target_bir_lowering=False)
    ext = lambda name, shape, d=F32, kind="ExternalInput": nc.dram_tensor(name, shape, d, kind=kind).ap()
    itn = lambda name, shape, d: nc.dram_tensor(name, shape, d).ap()
    B = Builder(nc)
    W = {}
    n_first = 0
    for wi, (name, K_, N_) in enumerate(WSPEC):
        Ks = K_ // 4
        if name.startswith("mod"):
            W[name] = ext("w_" + name, [Ks, N_])
            continue
        wb = 1024 if K_ == D else 256
        nb = N_ // wb
        w = {"shard": ext("w_" + name, [Ks, N_]), "Ks": Ks, "wb": wb, "nb": nb, "res": Res("wf_" + name),
             "wsh": itn("wsh_" + name, [nb, Ks, wb], BF16), "full": itn("wf_" + name, [nb, K_, wb], BF16)}
        W[name] = w
        B.bg_add_weight(wi, w)
        if name == "down0":
            n_first = len(B.bg)
    B.bg_pump(n_first)
    B.n_bg_rest = len(B.bg) - n_first
    part = itn("modpart", [4, 2, 3 * D], F32)
    msum = itn("modsum", [4, 2, 3 * D], F32)
    mb = ext("mb", [4, 3 * D])
    B.phase_mods({"cq": ext("cq", [2, 512]), "mw": [W["mod%d" % m] for m in range(4)], "part": part, "msum": msum})
    x1T = itn("x1T", [KC, 128, NLOC], F32)
    h1loc = itn("h1loc", [KC, 128, NLOC], BF16)
    pm = ext("pm", [128, 1])
    B.phase_a({"xin": ext("xin", [NLOC, D]), "msum": msum, "mb": mb, "svecs": ext("svecs", [7, D]), "pw1b": ext("pw1b", [2, D]),
               "dww": ext("dww", [CW, D]), "ffn_dw": ext("ffn_dw0", [4, 2 * FF]), "pm": pm,
               "diagw": itn("diagw", [KC, 128, CW * 128], BF16),
               "pw1_w": W["pw1"], "pw2_w": W["pw2"], "up_w": W["up0"], "down_w": W["down0"],
               "x1T": x1T, "h1T": h1loc}, tiles_for(NLOC))
    B.bg_pump()
    h1all = itn("h1all", [KC, 2, 128, NLOC], BF16)
    sem = B.dsem("ag_h1")
    for cc in range(KC):
        B.P.collective("AllGather", [h1loc[cc]], [h1all[cc].rearrange("r p n -> (r p) n")], PAIRS, sem)
    B.barrier(full=True)
    nh = NH // 2
    QT = itn("QT", [nh, 128, SEQ], BF16)
    KT = itn("KT", [nh, 128, SEQ], BF16)
    V = itn("V", [SEQ, nh * 128], BF16)
    oTloc = itn("oTloc", [nh, 2, 128, NP2], BF16)
    io = {"h1all": h1all, "qkv": W["qkv"], "QT": QT, "KT": KT, "V": V, "oT": oTloc}
    B.phase_qkv(io, HALF // 512, nh)
    B.phase_attn2(io, SEQ // 512, nh)
    oall = itn("oall", [nh, 2, 2, 128, NP2], BF16)
    sem = B.dsem("ag_o")
    for h in range(nh):
        for pt in range(2):
            B.P.collective("AllGather", [oTloc[h, pt]], [oall[h, pt].rearrange("r p n -> (r p) n")], PAIRS, sem)
    B.barrier(full=True)
    B.phase_b2({"x1T": x1T, "oT": oall, "msum": msum, "mb": mb, "svecs2": ext("svecs2", [2, D]), "ffn_dw": ext("ffn_dw1", [4, 2 * FF]),
                "pm": pm, "o_w": W["ow"], "up_w": W["up1"], "down_w": W["down1"],
                "y": ext("y", [NLOC, D], F32, "ExternalOutput")}, tiles_for(NLOC), dyn_o=True)
    B.barrier(full=True)
    return nc


def _f32(a):
    return np.ascontiguousarray(np.asarray(a, dtype=np.float32))


def kernel(x, c, mix_norm_g, mix_mod_w, mix_mod_b, cv_pw1_w, cv_pw1_b, cv_dw_w, cv_dw_b, cv_ln_g, cv_ln_b,
           cv_pw2_w, cv_pw2_b, sb_qkv_w, sb_o_w, ffn_norm_g, ffn_mod_w, ffn_mod_b, ffn_up_w, ffn_dw_w,
           ffn_dw_b, ffn_down_w, final_norm_g):
    x = np.asarray(x)
    c = np.asarray(c)
    cores = list(range(8))
    full = {"mod0": mix_mod_w[0], "mod1": ffn_mod_w[0], "mod2": mix_mod_w[1], "mod3": ffn_mod_w[1],
            "pw1": cv_pw1_w[0], "pw2": cv_pw2_w[0], "up0": ffn_up_w[0], "down0": ffn_down_w[0],
            "qkv": sb_qkv_w[0], "ow": sb_o_w[0], "up1": ffn_up_w[1], "down1": ffn_down_w[1]}
    shared = {
        "mb": _f32(np.stack([mix_mod_b[0], ffn_mod_b[0], mix_mod_b[1], ffn_mod_b[1]])),
        "svecs": _f32(np.stack([mix_norm_g[0], ffn_norm_g[0], mix_norm_g[1], cv_dw_b[0], cv_ln_g[0], cv_ln_b[0], cv_pw2_b[0]])),
        "pw1b": _f32(np.asarray(cv_pw1_b[0]).reshape(2, D)),
        "dww": _f32(cv_dw_w[0]),
        "ffn_dw0": _f32(np.concatenate([np.asarray(ffn_dw_w[0]), np.asarray(ffn_dw_b[0])[None]], 0)),
        "ffn_dw1": _f32(np.concatenate([np.asarray(ffn_dw_w[1]), np.asarray(ffn_dw_b[1])[None]], 0)),
        "svecs2": _f32(np.stack([ffn_norm_g[1], final_norm_g])),
    }
    ims = []
    for i in cores:
        b, r = i // 2, i % 2
        im = dict(shared)
        for name, K_, N_ in WSPEC:
            ks = K_ // 4
            q = i % 4
            im["w_" + name] = _f32(np.asarray(full[name])[q * ks:(q + 1) * ks])
        qd = i // 4
        im["cq"] = _f32(c[2 * qd:2 * qd + 2, 512 * q:512 * q + 512])
        im["pm"] = np.full((128, 1), float(r), np.float32)
        if r == 0:
            im["xin"] = _f32(np.concatenate([np.zeros((PRE, D), np.float32), x[b, 0:HALF]], 0))
        else:
            im["xin"] = _f32(x[b, HALF - PRE:SEQ])
        ims.append(im)
    res = run_bass_kernel_spmd(build_fused(), ims, core_ids=cores)
    out = np.empty((4, SEQ, D), np.float32)
    for i in cores:
        b, r = i // 2, i % 2
        out[b, HALF * r:HALF * (r + 1)] = np.asarray(res.results[i]["y"])[PRE:]
    return out


def build_attn_test(nq, nh):
    T = nq * 512
    nc = bass.Bass("TRN2", target_bir_lowering=False)
    dt = lambda name, shape, d=F32, kind="ExternalInput": nc.dram_tensor(name, shape, d, kind=kind).ap()
    io = {"QT": dt("QT", [nh, 128, T], BF16), "KT": dt("KT", [nh, 128, T], BF16), "V": dt("V", [T, nh * 128], BF16),
          "oT": dt("oT", [nh, 2, 128, NP2], BF16, "ExternalOutput")}
    B = Builder(nc)
    B.phase_attn2(io, nq, nh)
    B.barrier(full=True)
    return nc
```

```python
import numpy as np
import ml_dtypes
import concourse.bass as bass
import concourse.mybir as mybir
from concourse.bass_utils import run_bass_kernel_spmd
from contextlib import ExitStack

F32 = mybir.dt.float32
BF16 = mybir.dt.bfloat16
AF = mybir.ActivationFunctionType
ALU = mybir.AluOpType

D = 2048
KC = 16
FF = 5632
FC = 44
CW = 31
NH = 16
SEQ = 8192
HALF = 4096
PRE = 64
NLOC = HALF + PRE
RMS_EPS = 1e-6
LN_EPS = 1e-5
PAIRS = [[0, 1], [2, 3], [4, 5], [6, 7]]
QUADS = [[0, 1, 2, 3], [4, 5, 6, 7]]
NP2 = NLOC


class Res:
    __slots__ = ("name", "w", "r")

    def __init__(self, name=""):
        self.name = name
        self.w = None
        self.r = {}


class Eng:
    def __init__(self, P, name, eng, is_pe=False):
        self.name = name
        self.eng = eng
        self.is_pe = is_pe
        self.sem = P.new_sem("e_" + name)
        self.count = 0
        self.waited = {}


class Prog:
    def __init__(self, nc):
        self.nc = nc
        self.sems = {}
        self.nsem = 0
        self.pe = Eng(self, "pe", nc.tensor, is_pe=True)
        self.act = Eng(self, "act", nc.scalar)
        self.dve = Eng(self, "dve", nc.vector)
        self.pool = Eng(self, "pool", nc.gpsimd)
        self.sp = Eng(self, "sp", nc.sync)
        self.dma_vals = {}

    def new_sem(self, name):
        h = self.nc.semaphore(name).__enter__()
        key = "s%d_%s" % (self.nsem, name)
        self.nsem += 1
        self.sems[key] = h
        return key

    def _wait(self, E, needs):
        best = {}
        for (k, v) in needs:
            if best.get(k, 0) < v:
                best[k] = v
        for k, v in best.items():
            if k == E.sem:
                if E.is_pe:
                    continue
                if E.count - v >= 2:
                    continue
            if E.waited.get(k, 0) >= v:
                continue
            E.eng.wait_ge(self.sems[k], v)
            E.waited[k] = v

    @staticmethod
    def _deps(reads, writes):
        needs = []
        for r in reads:
            if r.w is not None:
                needs.append(r.w)
        for w in writes:
            if w.w is not None:
                needs.append(w.w)
            needs.extend(w.r.items())
        return needs

    @staticmethod
    def _mark(tok, reads, writes):
        k, v = tok
        for w in writes:
            w.w = tok
            w.r = {}
        for r in reads:
            if r.w is tok:
                continue
            if r.r.get(k, 0) < v:
                r.r[k] = v

    def op(self, E, fn, reads=(), writes=()):
        self._wait(E, self._deps(reads, writes))
        ins = fn()
        ins.then_inc(self.sems[E.sem], 1)
        E.count += 1
        self._mark((E.sem, E.count), reads, writes)
        return ins

    def group(self, E, fns, reads=(), writes=()):
        self._wait(E, self._deps(reads, writes))
        ins = None
        for fn in fns:
            ins = fn()
        ins.then_inc(self.sems[E.sem], 1)
        E.count += 1
        self._mark((E.sem, E.count), reads, writes)
        return ins

    def dma(self, Q, out, in_, sem, reads=(), writes=()):
        self._wait(Q, self._deps(reads, writes))
        ins = Q.eng.dma_start(out=out, in_=in_)
        ins.then_inc(self.sems[sem], 16)
        v = self.dma_vals.get(sem, 0) + 16
        self.dma_vals[sem] = v
        self._mark((sem, v), reads, writes)
        return ins

    def collective(self, kind, ins, outs, groups, sem, reads=(), writes=(), op=None):
        Q = self.pool
        self._wait(Q, self._deps(reads, writes))
        ins_ = self.nc.gpsimd.collective_compute(kind, op if op is not None else ALU.bypass, replica_groups=groups,
                                                 ins=ins, outs=outs)
        ins_.then_inc(self.sems[sem], 1)
        v = self.dma_vals.get(sem, 0) + 1
        self.dma_vals[sem] = v
        self._mark((sem, v), reads, writes)

    def wait_all(self, E, ress):
        needs = []
        for r in ress:
            if r.w is not None:
                needs.append(r.w)
            needs.extend(r.r.items())
        self._wait(E, needs)


class Buf:
    def __init__(self, t, n, name):
        self.t = t
        self.r = [Res("%s%d" % (name, i)) for i in range(n)]


class Builder:
    def __init__(self, nc):
        self.nc = nc
        self.P = Prog(nc)
        P = self.P
        self.stack = None
        self.uid = 0
        self.bg_sems = set()
        self.banks = []
        self.bankpairs = []
        for i in range(4):
            t2 = nc.alloc_psum_tensor("bankp%d" % i, [128, 2, 512], F32)
            self.bankpairs.append(t2)
            for j in range(2):
                self.banks.append((t2[:, j, :], Res("bank%d" % (2 * i + j))))
        self.bank_i = 0
        self.dsems = {}
        self.ident = self.sb("ident", [128, 128], F32)
        self.ones_bf = self.sb("ones_bf", [128, 128], BF16)
        self.r_const = Res("const")
        ones_f = self.sb("ones_f", [128, 128], F32)
        P.op(P.pool, lambda: nc.gpsimd.memset(ones_f[:], 1.0), writes=[self.r_const])
        P.op(P.pool, lambda: nc.gpsimd.affine_select(self.ident[:], ones_f[:], [[1, 128]], ALU.is_equal, 0.0,
                                                     base=0, channel_multiplier=-1),
             reads=[self.r_const], writes=[self.r_const])
        P.op(P.pool, lambda: nc.gpsimd.memset(self.ones_bf[:], 1.0), writes=[self.r_const])
        self.ones_f = ones_f
        self.wstage = [(self.sb("wstage%d" % i, [128, 2048], BF16), Res("wstage%d" % i), self.dsem("wstage%d" % i)) for i in range(2)]
        self.wstage_i = 0
        pid = nc.sync.partition_id()
        self.par = pid % 2
        self.bsel = (pid % 4) // 2
        self.bg = []
        self.bg_i = 0

    def bg_add_weight(self, wi, w):
        P = self.P
        sem = self.dsem("agw%d" % wi)
        self.bg_sems.add(sem)
        Ks, wb, nb = w["Ks"], w["wb"], w["nb"]
        N = wb * nb
        sres = [Res("wsh%d_%d" % (wi, k)) for k in range(2)]

        def unit(r0, c0, wd):
            t, r, sm = self.wstage[self.wstage_i % 2]
            k = self.wstage_i % 2
            self.wstage_i += 1
            P.dma(P.pool, t[:, :wd], w["shard"][r0:r0 + 128, c0:c0 + wd], sm, writes=[r])
            P.dma(P.sp, w["wsh"][c0 // wb:(c0 + wd) // wb, r0:r0 + 128, :].rearrange("b p n -> p b n"),
                  t[:, :wd].rearrange("p (b n) -> p b n", n=wb), sm, reads=[r], writes=[sres[k]])
        for r0 in range(0, Ks, 128):
            for c0 in range(0, N, 2048):
                wd = min(2048, N - c0)
                self.bg.append(lambda r0=r0, c0=c0, wd=wd: unit(r0, c0, wd))

        def gather(bi):
            P.collective("AllGather", [w["wsh"][bi]], [w["full"][bi]], QUADS, sem, reads=sres, writes=[w["res"]])
        for bi in range(nb):
            self.bg.append(lambda bi=bi: gather(bi))
        w["bg_end"] = len(self.bg)

    def bg_pump(self, n=None):
        end = len(self.bg) if n is None else min(len(self.bg), self.bg_i + n)
        while self.bg_i < end:
            self.bg[self.bg_i]()
            self.bg_i += 1

    def sb(self, name, shape, dt):
        if self.stack is None:
            return self.nc.alloc_sbuf_tensor(name, shape, dt)
        self.uid += 1
        return self.stack.enter_context(self.nc.sbuf_tensor("%s_%d" % (name, self.uid), shape, dt))

    def begin_phase(self):
        self.stack = ExitStack()

    def barrier(self, full=False):
        P = self.P
        engs = [P.pe, P.act, P.dve, P.pool, P.sp]
        needs = [(e.sem, e.count) for e in engs if e.count > 0] + \
            [kv for kv in P.dma_vals.items() if full or kv[0] not in self.bg_sems]
        for e in engs:
            P._wait(e, [x for x in needs if x[0] != e.sem])

    def end_phase(self):
        self.barrier()
        self.stack.close()
        self.stack = None

    def dsem(self, name):
        if name not in self.dsems:
            self.dsems[name] = self.P.new_sem("d_" + name)
        return self.dsems[name]

    def bank(self):
        b = self.banks[self.bank_i]
        self.bank_i = (self.bank_i + 1) % 8
        return b

    def mm(self, bank, pairs, n, reads, m=128):
        nc = self.nc
        t, r = bank
        fns = []
        last = len(pairs) - 1
        for i, (l, rh) in enumerate(pairs):
            fns.append(lambda l=l, rh=rh, i=i: nc.tensor.matmul(t[:m, :n], l, rh, start=(i == 0), stop=(i == last)))
        self.P.group(self.P.pe, fns, reads=reads, writes=[r])

    def A(self, out, in_, func, reads, writes, bias=0.0, scale=1.0):
        nc = self.nc
        return self.P.op(self.P.act, lambda: nc.scalar.activation(out, in_, func, bias=bias, scale=scale),
                         reads=reads, writes=writes)

    def TT(self, out, a, b, op, reads, writes, eng=None):
        E = eng or self.P.dve
        return self.P.op(E, lambda: E.eng.tensor_tensor(out, a, b, op), reads=reads, writes=writes)

    def TS(self, out, a, s1, s2, op0, op1, reads, writes, eng=None):
        E = eng or self.P.dve
        if op1 is None:
            return self.P.op(E, lambda: E.eng.tensor_scalar(out, a, s1, None, op0), reads=reads, writes=writes)
        return self.P.op(E, lambda: E.eng.tensor_scalar(out, a, s1, s2, op0, op1), reads=reads, writes=writes)

    def STT(self, out, a, s, b, op0, op1, reads, writes):
        nc = self.nc
        return self.P.op(self.P.dve, lambda: nc.vector.scalar_tensor_tensor(out, a, s, b, op0, op1),
                         reads=reads, writes=writes)

    def CP(self, out, in_, reads, writes, eng=None):
        E = eng or self.P.dve
        return self.P.op(E, lambda: E.eng.tensor_copy(out, in_), reads=reads, writes=writes)

    def init_wstream(self, nslots=3):
        self.wslots = []
        for i in range(nslots):
            t = self.sb("wslot%d" % i, [128, 8192], BF16)
            self.wslots.append((t, Res("wslot%d" % i), self.dsem("wslot%d" % i)))
        self.wplan = []
        self.wnext_issue = 0
        self.wnext_use = 0

    def wplan_add(self, w, k_rows, pieces):
        kc = k_rows // 128
        tot = sum(p[-1] for p in pieces)
        assert kc * tot <= 8192
        self.wplan.append((w, kc, tot, pieces))

    def _wissue(self, upto):
        P = self.P
        while self.wnext_issue < min(upto, len(self.wplan)):
            i = self.wnext_issue
            w, kc, tot, pieces = self.wplan[i]
            t, r, s = self.wslots[i % len(self.wslots)]
            if pieces is None:
                P.dma(P.pool, t[:, 0:kc * tot], w[0], s, reads=[w[1]], writes=[r])
                self.wnext_issue += 1
                continue
            if self.bg_i < w.get("bg_end", 0):
                self.bg_pump(w["bg_end"] - self.bg_i)
            dst = t[:, 0:kc * tot].rearrange("p (kc n) -> p kc n", kc=kc)
            wb = w["wb"]
            o = 0
            for pc in pieces:
                if len(pc) == 2:
                    c0, ncols = pc
                    bi, off = c0 // wb, c0 % wb
                    assert off + ncols <= wb
                    src = w["full"][bi].rearrange("(kc p) n -> p kc n", p=128)[:, :, off:off + ncols]
                else:
                    bi, off, ncols = pc
                    src = w["full"][bass.ds(bi, 1)].rearrange("o (kc p) n -> p (o kc) n", p=128)[:, :, off:off + ncols]
                P.dma(P.pool, dst[:, :, o:o + ncols], src, s, reads=[w["res"]], writes=[r])
                o += ncols
            self.wnext_issue += 1

    def wget(self):
        i = self.wnext_use
        self._wissue(i + len(self.wslots))
        w, kc, tot, pieces = self.wplan[i]
        t, r, s = self.wslots[i % len(self.wslots)]
        self.wnext_use += 1
        return t[:, 0:kc * tot].rearrange("p (kc n) -> p kc n", kc=kc), r

    def rms_mod(self, xT, hT, n, a_vec, sh_vec):
        nc, P = self.nc, self.P
        sq = self.tmpbf
        bk = self.bank()
        for c in range(KC):
            self.A(sq.t[:, c % 2, :n], xT.t[:, c, :n], AF.Square, reads=[xT.r[c]], writes=[sq.r[c % 2]])
            nc_ = nc
            P.group(P.pe, [lambda c=c: nc_.tensor.matmul(bk[0][:, :n], self.ones_bf[:], sq.t[:, c % 2, :n],
                                                          start=(c == 0), stop=(c == KC - 1))],
                    reads=[sq.r[c % 2], self.r_const], writes=[bk[1]])
        rstd = self.rstd
        self.A(rstd.t[:, 0, :n], bk[0][:, :n], AF.Ln, reads=[bk[1]], writes=[rstd.r[0]], bias=self.eps_rms[:, 0:1], scale=1.0)
        self.A(rstd.t[:, 0, :n], rstd.t[:, 0, :n], AF.Exp, reads=[rstd.r[0]], writes=[rstd.r[0]], scale=-0.5)
        for c in range(KC):
            tm = self.tmpf
            i = c % 2
            self.TT(tm.t[:, i, :n], xT.t[:, c, :n], rstd.t[:, 0, :n], ALU.mult, reads=[xT.r[c], rstd.r[0]], writes=[tm.r[i]])
            self.A(hT.t[:, c, :n], tm.t[:, i, :n], AF.Identity, reads=[tm.r[i], self.r_vec], writes=[hT.r[c]],
                   bias=(sh_vec[:, c:c + 1] if sh_vec is not None else 0.0), scale=a_vec[:, c:c + 1])

    def linear(self, act, kc_n, n, slots, epilogue):
        for ids in slots:
            wv, wr = self.wget()
            for jj, cid in enumerate(ids):
                bk = self.bank()
                pairs = [(wv[:, k, jj * 128:(jj + 1) * 128], act.t[:, k, :n]) for k in range(kc_n)]
                self.mm(bk, pairs, n, reads=[wr] + [act.r[k] for k in range(kc_n)])
                epilogue(bk, cid)

    def load_rows_T(self, name, ap2d, rows, ncols):
        nc, P = self.nc, self.P
        nch = ncols // 128
        out = self.sb("m_" + name, [128, nch, rows], F32)
        r = self.r_vec
        stg, rs = self.vec_stage, self.vec_stage_r
        for c0 in range(0, ncols, 2048):
            w = min(2048, ncols - c0)
            if isinstance(ap2d, list):
                ro = 0
                for (apx, nr) in ap2d:
                    P.dma(P.sp, stg[ro:ro + nr, :w], apx[0:nr, c0:c0 + w], self.xrow[0][2], writes=[rs])
                    ro += nr
            else:
                P.dma(P.sp, stg[:rows, :w], ap2d[0:rows, c0:c0 + w], self.xrow[0][2], writes=[rs])
            for cc in range(w // 128):
                bk = self.bank()
                P.group(P.pe, [lambda cc=cc, bk=bk: nc.tensor.transpose(bk[0][:, :rows], stg[:rows, cc * 128:(cc + 1) * 128],
                                                                        self.ident[:rows, :rows])],
                        reads=[rs, self.r_const], writes=[bk[1]])
                self.CP(out[:, c0 // 128 + cc, :], bk[0][:, :rows], reads=[bk[1]], writes=[r])
        return out

    def common_init(self):
        nc, P = self.nc, self.P
        self.xrow = [(self.sb("xrow0", [128, D], F32), Res("xrow0"), self.dsem("xrow0"))] * 2
        self.xrow_i = 0
        self.vec_stage = self.xrow[0][0]
        self.vec_stage_r = self.xrow[0][1]
        self.r_vec = Res("vecs")
        self.sqb = Buf(self.sb("sqb", [128, 2, 512], BF16), 2, "sqb")
        self.tmpf = Buf(self.sb("tmpf", [128, 2, 512], F32), 2, "tmpf")
        self.tmpbf = Buf(self.sb("tmpbf", [128, 2, 512], BF16), 2, "tmpbf")
        self.rstd = Buf(self.sb("rstd", [128, 1, 512], F32), 1, "rstd")
        self.eps_rms = self.sb("eps_rms", [128, 2], F32)
        P.op(P.pool, lambda: nc.gpsimd.memset(self.eps_rms[:, 0:1], float(D * RMS_EPS)), writes=[self.r_const])
        P.op(P.pool, lambda: nc.gpsimd.memset(self.eps_rms[:, 1:2], float(LN_EPS)), writes=[self.r_const])

    def mod_vecs(self, mv, m, gvec, name):
        a = self.sb("a_" + name, [128, KC], F32)
        r = self.r_vec
        self.TS(a[:, :], mv[:, :, 3 * m + 1], 1.0, float(np.sqrt(D)), ALU.add, ALU.mult, reads=[r], writes=[r])
        self.TT(a[:, :], a[:, :], gvec, ALU.mult, reads=[r], writes=[r])
        return a, mv[:, :, 3 * m], mv[:, :, 3 * m + 2]

    def load_mods(self, io):
        nc = self.nc
        bsel = self.bsel
        self.P._wait(self.P.sp, [self.ar_token])
        src = [(io["msum"][m, bass.ds(bsel, 1), :].rearrange("o (t k) -> (o t) k", t=3), 3) for m in range(4)]
        mv = self.load_rows_T("mods", src, 12, D)
        bv = self.load_rows_T("modb", io["mb"].rearrange("m (t k) -> (m t) k", t=3), 12, D)
        self.TT(mv[:, :, :], mv[:, :, :], bv[:, :, :], ALU.add, reads=[self.r_vec], writes=[self.r_vec])
        return mv

    def load_mods_old(self, modall):
        nc, P = self.nc, self.P
        out = self.sb("m_mods", [128, KC, 12], F32)
        stg, rs = self.vec_stage, self.vec_stage_r
        P.dma(P.sp, stg[:12, :].rearrange("q (r k) -> q r k", r=2), modall.rearrange("r m t k -> (m t) r k"),
              self.xrow[0][2], writes=[rs])
        for cc in range(KC):
            bk = self.bank()
            P.group(P.pe, [lambda cc=cc, bk=bk: nc.tensor.transpose(bk[0][:, :12], stg[:12, cc * 128:(cc + 1) * 128],
                                                                    self.ident[:12, :12])],
                    reads=[rs, self.r_const], writes=[bk[1]])
            self.CP(out[:, cc, :], bk[0][:, :12], reads=[bk[1]], writes=[self.r_vec])
        return out

    def load_x_tile(self, xin, start, n, xT):
        nc, P = self.nc, self.P
        nb = (n + 127) // 128
        for b in range(nb):
            nt = min(128, n - b * 128)
            xr, rr, sm = self.xrow[self.xrow_i % 2]
            self.xrow_i += 1
            P.dma(P.sp, xr[:nt, :], xin[start + b * 128:start + b * 128 + nt, :], sm, writes=[rr])
            for g in range(4):
                bk = self.bank()
                fns = []
                for q in range(4):
                    c = 4 * g + q
                    fns.append(lambda q=q, c=c, bk=bk: nc.tensor.transpose(bk[0][:, q * 128:q * 128 + nt],
                                                                           xr[:nt, c * 128:(c + 1) * 128], self.ident[:nt, :nt]))
                P.group(P.pe, fns, reads=[rr, self.r_const], writes=[bk[1]])
                src = bk[0][:, :].rearrange("p (q t) -> p q t", q=4)[:, :, :nt]
                dst = xT.t[:, 4 * g:4 * g + 4, b * 128:b * 128 + nt]
                eng = self.P.act if (g % 2 == 0) else self.P.dve
                if eng is self.P.act:
                    P.op(P.act, lambda dst=dst, src=src: nc.scalar.copy(dst, src), reads=[bk[1]],
                         writes=[xT.r[4 * g + q] for q in range(4)])
                else:
                    self.CP(dst, src, reads=[bk[1]], writes=[xT.r[4 * g + q] for q in range(4)])

    def phase_a(self, io, tiles):
        nc, P = self.nc, self.P
        self.begin_phase()
        self.common_init()
        self.init_wstream(3)
        mv = self.load_mods(io)
        sv = self.load_rows_T("svA", io["svecs"], 7, D)
        pw1b = self.load_rows_T("pw1b", io["pw1b"], 2, D)
        dww = self.load_rows_T("dww", io["dww"], CW, D)
        fdw = self.load_rows_T("fdw", io["ffn_dw"], 4, 2 * FF)
        pm = self.sb("pm_sb", [128, 1], F32)
        P.dma(P.sp, pm[:, :], io["pm"][:, :], self.dsem("pm"), writes=[self.r_vec])
        rv = self.r_vec
        a0, sh0, g0 = self.mod_vecs(mv, 0, sv[:, :, 0], "mix0")
        a1, sh1, g1 = self.mod_vecs(mv, 1, sv[:, :, 1], "ffn0")
        a2, sh2, g2 = self.mod_vecs(mv, 2, sv[:, :, 2], "mix1")
        gb2 = self.sb("gb2", [128, KC], F32)
        self.TT(gb2[:, :], g0, sv[:, :, 6], ALU.mult, reads=[rv], writes=[rv])

        xT = Buf(self.sb("xT", [128, KC, 512], F32), KC, "xT")
        hT = Buf(self.sb("hT", [128, KC, 512], BF16), KC, "hT")
        uT = Buf(self.sb("uT", [128, KC, CW - 1 + 512], BF16), KC, "uT")
        gTt = self.sb("gTt", [128, FC, 512], BF16)
        gT = Buf(gTt, FC, "gT")
        cvt = gTt[:, 0:32, :].bitcast(F32).rearrange("p a b -> p (a b)").rearrange("p (c n) -> p c n", c=KC)
        Tg = Buf(self.sb("Tg", [128, 2, 512], F32), 2, "Tg")
        Tv = Buf(self.sb("Tv", [128, 2, 512], F32), 2, "Tv")
        Hff = self.sb("Hff", [128, 2 * FC, 2], F32)
        rH = [Res("Hff%d" % i) for i in range(2 * FC)]
        stat = Buf(self.sb("stat", [128, 3, 512], F32), 3, "stat")
        P.op(P.pool, lambda: nc.gpsimd.memset(Hff[:, :, :], 0.0), writes=rH)
        P.op(P.pool, lambda: nc.gpsimd.memset(uT.t[:, :, 0:CW - 1], 0.0), writes=uT.r)

        r_diag = Res("diagw")
        for _ in tiles:
            for i in range(8):
                self.wplan_add(io["pw1_w"], D, [(256 * i, 256), (D + 256 * i, 256)])
            for c in range(KC):
                self.wplan.append(((io["diagw"][c], r_diag), CW, 128, None))
            for i in range(4):
                self.wplan_add(io["pw2_w"], D, [(512 * i, 512)])
            for i in range(22):
                self.wplan_add(io["up_w"], D, [(256 * i, 256), (FF + 256 * i, 256)])
            for i in range(16):
                self.wplan_add(io["down_w"], FF, [(128 * i, 128)])

        ident_bf = self.sb("ident_bf", [128, 128], BF16)
        self.CP(ident_bf[:, :], self.ident[:, :], reads=[self.r_const], writes=[self.r_const])
        for c in range(KC):
            t, r, sm = self.wslots[c % 3]
            dv = t[:, 0:CW * 128].rearrange("p (k m) -> p k m", k=CW)
            for k in range(CW):
                self.TS(dv[:, k, :], ident_bf[:, :], dww[:, c, k:k + 1], None, ALU.mult, None, reads=[self.r_const, rv],
                        writes=[r])
            P.dma(P.sp, io["diagw"][c], t[:, 0:CW * 128], sm, reads=[r], writes=[r_diag])

        x1T_d, h1T_d = io["x1T"], io["h1T"]
        osem_x = [self.dsem("ox0"), self.dsem("ox1")]
        osem_h = [self.dsem("oh0"), self.dsem("oh1")]

        per_tile = (getattr(self, "n_bg_rest", 0) + len(tiles) - 2) // max(1, len(tiles) - 1)
        for ti, (start, n) in enumerate(tiles):
            first = (ti == 0)
            if not first:
                self.bg_pump(per_tile)
            self.load_x_tile(io["xin"], start, n, xT)
            self.rms_mod(xT, hT, n, a0, sh0)
            if first:
                pass
            sg = self.tmpf

            def ep_pw1(bk, cid, n=n, first=first):
                if cid < KC:
                    self._valbank[cid] = bk
                else:
                    j = cid - KC
                    vb = self._valbank.pop(j)
                    i = j % 2
                    self.A(sg.t[:, i, :n], bk[0][:, :n], AF.Sigmoid, reads=[bk[1], rv], writes=[sg.r[i]],
                           bias=pw1b[:, j, 1:2], scale=1.0)
                    self.STT(uT.t[:, j, CW - 1:CW - 1 + n], vb[0][:, :n], pw1b[:, j, 0:1], sg.t[:, i, :n], ALU.add, ALU.mult,
                             reads=[vb[1], sg.r[i], rv], writes=[uT.r[j]])
            self._valbank = {}
            slots = [[2 * i, 2 * i + 1, KC + 2 * i, KC + 2 * i + 1] for i in range(8)]
            self.linear(hT, KC, n, slots, ep_pw1)
            if first:
                self.TS(uT.t[:, :, CW - 1:CW - 1 + PRE], uT.t[:, :, CW - 1:CW - 1 + PRE], pm[:, 0:1], None, ALU.mult, None,
                        reads=uT.r + [rv], writes=uT.r)
            cv_r = [[gT.r[2 * c], gT.r[2 * c + 1]] for c in range(KC)]
            for c in range(KC):
                wv, wr = self.wget()
                dvw = wv[:, :, :]
                bkc = self.bank()
                pairs = [(dvw[:, k, :], uT.t[:, c, k:k + n]) for k in range(CW)]
                self.mm(bkc, pairs, n, reads=[wr, uT.r[c]])
                self.A(cvt[:, c, :n], bkc[0][:, :n], AF.Identity, reads=[bkc[1], rv], writes=cv_r[c],
                       bias=sv[:, c, 3:4], scale=1.0)
            bk_m = self.bank()
            bk_s = self.bank()
            for c in range(KC):
                i = c % 2
                self.A(self.tmpbf.t[:, i, :n], cvt[:, c, :n], AF.Identity, reads=cv_r[c], writes=[self.tmpbf.r[i]])
                P.group(P.pe, [lambda c=c, i=i: nc.tensor.matmul(bk_m[0][:, :n], self.ones_bf[:], self.tmpbf.t[:, i, :n],
                                                                 start=(c == 0), stop=(c == KC - 1))],
                        reads=[self.tmpbf.r[i], self.r_const], writes=[bk_m[1]])
                self.A(self.sqb.t[:, i, :n], cvt[:, c, :n], AF.Square, reads=cv_r[c], writes=[self.sqb.r[i]])
                P.group(P.pe, [lambda c=c, i=i: nc.tensor.matmul(bk_s[0][:, :n], self.ones_bf[:], self.sqb.t[:, i, :n],
                                                                 start=(c == 0), stop=(c == KC - 1))],
                        reads=[self.sqb.r[i], self.r_const], writes=[bk_s[1]])
            mean, var, rs = stat.t[:, 0, :n], stat.t[:, 1, :n], stat.t[:, 2, :n]
            self.A(mean, bk_m[0][:, :n], AF.Identity, reads=[bk_m[1]], writes=[stat.r[0]], scale=1.0 / D)
            self.TT(var, mean, mean, ALU.mult, reads=[stat.r[0]], writes=[stat.r[1]])
            self.STT(var, bk_s[0][:, :n], 1.0 / D, var, ALU.mult, ALU.subtract, reads=[bk_s[1], stat.r[1]], writes=[stat.r[1]])
            self.A(rs, var, AF.Ln, reads=[stat.r[1], self.r_const], writes=[stat.r[2]], bias=self.eps_rms[:, 1:2], scale=1.0)
            self.A(rs, rs, AF.Exp, reads=[stat.r[2]], writes=[stat.r[2]], scale=-0.5)
            for c in range(KC):
                i = c % 2
                self.TT(sg.t[:, i, :n], cvt[:, c, :n], mean, ALU.subtract, reads=cv_r[c] + [stat.r[0]], writes=[sg.r[i]])
                self.TT(sg.t[:, i, :n], sg.t[:, i, :n], rs, ALU.mult, reads=[sg.r[i], stat.r[2]], writes=[sg.r[i]])
                self.A(hT.t[:, c, :n], sg.t[:, i, :n], AF.Silu, reads=[sg.r[i], rv], writes=[hT.r[c]],
                       bias=sv[:, c, 5:6], scale=sv[:, c, 4:5])
            self.CP(uT.t[:, :, 0:CW - 1], uT.t[:, :, n:n + CW - 1], reads=uT.r, writes=uT.r)

            def ep_pw2(bk, cid, n=n):
                i = cid % 2
                self.A(sg.t[:, i, :n], bk[0][:, :n], AF.Identity, reads=[bk[1], rv], writes=[sg.r[i]],
                       bias=gb2[:, cid:cid + 1], scale=g0[:, cid:cid + 1])
                self.TT(xT.t[:, cid, :n], xT.t[:, cid, :n], sg.t[:, i, :n], ALU.add, reads=[xT.r[cid], sg.r[i]], writes=[xT.r[cid]])
            self.linear(hT, KC, n, [[4 * i + q for q in range(4)] for i in range(4)], ep_pw2)

            self.rms_mod(xT, hT, n, a1, sh1)
            if first:
                self.TS(hT.t[:, :, 0:PRE], hT.t[:, :, 0:PRE], pm[:, 0:1], None, ALU.mult, None, reads=hT.r + [rv], writes=hT.r)
            self.ffn(hT, xT, gT, Tg, Tv, Hff, rH, fdw, g1, n)

            k = ti % 2
            P.dma(P.sp, x1T_d[:, :, start:start + n].rearrange("c p n -> p c n"), xT.t[:, :, :n], osem_x[k], reads=xT.r)
            self.rms_mod(xT, hT, n, a2, sh2)
            P.dma(P.sp, h1T_d[:, :, start:start + n].rearrange("c p n -> p c n"), hT.t[:, :, :n], osem_h[k], reads=hT.r)
        self.end_phase()

    def ffn(self, hT, xT, gT, Tg, Tv, Hff, rH, fdw, gate, n):
        nc, P = self.nc, self.P
        rv = self.r_vec

        def conv3(bk, ch, T, i):
            p = bk[0]
            w0, w1, w2, b = fdw[:, ch, 0:1], fdw[:, ch, 1:2], fdw[:, ch, 2:3], fdw[:, ch, 3:4]
            t = T.t[:, i, :]
            self.A(t[:, :n], p[:, :n], AF.Identity, reads=[bk[1], rv], writes=[T.r[i]], bias=b, scale=w2)
            self.STT(t[:, 1:n], p[:, 0:n - 1], w1, t[:, 1:n], ALU.mult, ALU.add, reads=[bk[1], rv, T.r[i]], writes=[T.r[i]])
            self.STT(t[:, 2:n], p[:, 0:n - 2], w0, t[:, 2:n], ALU.mult, ALU.add, reads=[bk[1], rv, T.r[i]], writes=[T.r[i]])
            h = Hff[:, ch, :]
            self.STT(t[:, 0:1], h[:, 1:2], w1, t[:, 0:1], ALU.mult, ALU.add, reads=[rH[ch], rv, T.r[i]], writes=[T.r[i]])
            self.STT(t[:, 0:2], h[:, 0:2], w0, t[:, 0:2], ALU.mult, ALU.add, reads=[rH[ch], rv, T.r[i]], writes=[T.r[i]])
            P.op(P.act, lambda: nc.scalar.copy(h[:, 0:2], p[:, n - 2:n]), reads=[bk[1], T.r[i]], writes=[rH[ch]])

        def ep_up(bk, cid, n=n):
            if cid < FC:
                conv3(bk, cid, Tg, cid % 2)
            else:
                j = cid - FC
                i = j % 2
                conv3(bk, cid, Tv, i)
                self.A(Tg.t[:, i, :n], Tg.t[:, i, :n], AF.Silu, reads=[Tg.r[i]], writes=[Tg.r[i]])
                self.TT(gT.t[:, j, :n], Tg.t[:, i, :n], Tv.t[:, i, :n], ALU.mult, reads=[Tg.r[i], Tv.r[i]], writes=[gT.r[j]])
        slots = [[2 * i, 2 * i + 1, FC + 2 * i, FC + 2 * i + 1] for i in range(22)]
        self.linear(hT, KC, n, slots, ep_up)

        def ep_down(bk, cid, n=n):
            self.STT(xT.t[:, cid, :n], bk[0][:, :n], gate[:, cid:cid + 1], xT.t[:, cid, :n], ALU.mult, ALU.add,
                     reads=[bk[1], rv, xT.r[cid]], writes=[xT.r[cid]])
        self.linear(gT, FC, n, [[i] for i in range(16)], ep_down)


def phase_mods(self, io):
    nc, P = self.nc, self.P
    self.begin_phase()
    self.common_init()
    self.init_wstream(3)
    rv = self.r_vec
    cT = self.load_rows_T("cq", io["cq"], 2, 512)
    sc = self.sb("sc", [128, 4, 2], BF16)
    self.A(sc[:, :, :], cT[:, :, :], AF.Silu, reads=[rv], writes=[rv])
    NM = 3 * D
    orow = self.sb("orow", [2, NM], F32)
    r_o = Res("orow")
    sem_o = self.dsem("orow")
    wi = 0
    for m in range(4):
        for g3 in range(3):
            t, r, sm = self.wslots[wi % 3]
            wi += 1
            wv = t[:, :].rearrange("p (kc n) -> p kc n", kc=4)
            P.dma(P.pool, wv, io["mw"][m].rearrange("(kc p) n -> p kc n", p=128)[:, :, g3 * 2048:(g3 + 1) * 2048], sm, writes=[r])
            for g in range(4):
                bk = self.bank()
                pairs = [(sc[:, k, :], wv[:, k, g * 512:(g + 1) * 512]) for k in range(4)]
                self.mm(bk, pairs, 512, reads=[r, rv], m=2)
                o = g3 * 2048 + g * 512
                self.CP(orow[0:2, o:o + 512], bk[0][0:2, :512], reads=[bk[1]], writes=[r_o])
        P.dma(P.sp, io["part"][m], orow[0:2, :], sem_o, reads=[r_o])
    ar_sem = self.dsem("ar_mods")
    P.collective("AllReduce", [io["part"].rearrange("m b k -> (m b) k")], [io["msum"].rearrange("m b k -> (m b) k")], QUADS,
                 ar_sem, reads=[r_o], op=ALU.add)
    self.ar_token = (ar_sem, P.dma_vals[ar_sem])
    self.bg_sems.add(ar_sem)
    self.end_phase()


def phase_w(self, wl):
    nc, P = self.nc, self.P
    self.begin_phase()
    self.init_wstream(3)
    i = 0
    for wi, w in enumerate(wl):
        sem = self.dsem("agw%d" % wi)
        self.bg_sems.add(sem)
        Ks, wb, nb = w["Ks"], w["wb"], w["nb"]
        N = wb * nb
        sres = [Res("wsh%d_%d" % (wi, k)) for k in range(3)]
        for r0 in range(0, Ks, 128):
            for c0 in range(0, N, 8192):
                wd = min(8192, N - c0)
                t, r, sm = self.wslots[i % 3]
                P.dma(P.pool, t[:, :wd], w["shard"][r0:r0 + 128, c0:c0 + wd], sm, writes=[r])
                P.dma(P.sp, w["wsh"][c0 // wb:(c0 + wd) // wb, r0:r0 + 128, :].rearrange("b p n -> p b n"),
                      t[:, :wd].rearrange("p (b n) -> p b n", n=wb), sm, reads=[r], writes=[sres[i % 3]])
                i += 1
        for bi in range(nb):
            P.collective("AllGather", [w["wsh"][bi]], [w["full"][bi]], QUADS, sem, reads=sres, writes=[w["res"]])
    self.end_phase()


def phase_qkv(self, io, ntile, nh):
    nc, P = self.nc, self.P
    self.begin_phase()
    self.init_wstream(3)
    ns = nh // 4
    hTs = [Buf(self.sb("hq%d" % i, [128, KC, 512], BF16), KC, "hq%d" % i) for i in range(2)]
    hsem = [self.dsem("hq0"), self.dsem("hq1")]
    qst = Buf(self.sb("qst", [128, nh, 512], BF16), nh, "qst")
    kst = Buf(self.sb("kst", [128, nh, 512], BF16), nh, "kst")
    vst = Buf(self.sb("vst", [128, 4, nh * 128], BF16), 4, "vst")
    sq, sk, sv_ = self.dsem("oq"), self.dsem("ok"), self.dsem("ov")
    par = nc.gpsimd.partition_id() % 2
    for _ in range(2 * ntile):
        for which in range(3):
            for i in range(ns):
                self.wplan_add(io["qkv"], D, [(par + 2 * which, 512 * i, 512)])
    it = 0
    for s in range(2):
        for t in range(ntile):
            p0 = (s * ntile + t) * 512
            lc = PRE + 512 * t
            hT = hTs[it % 2]
            P.dma(P.sp, hT.t[:, :, :], io["h1all"][:, s, :, lc:lc + 512].rearrange("c p n -> p c n"), hsem[it % 2], writes=hT.r)
            it += 1
            for (st, dst, sem) in ((qst, io["QT"], sq), (kst, io["KT"], sk)):
                def ep(bk, cid, st=st):
                    if cid % 2 == 0:
                        P.op(P.act, lambda: nc.scalar.copy(st.t[:, cid, :], bk[0][:, :512]), reads=[bk[1]], writes=[st.r[cid]])
                    else:
                        self.CP(st.t[:, cid, :], bk[0][:, :512], reads=[bk[1]], writes=[st.r[cid]])
                self.linear(hT, KC, 512, [[4 * i + q for q in range(4)] for i in range(ns)], ep)
                P.dma(P.sp, dst[:, :, p0:p0 + 512].rearrange("h p n -> p h n"), st.t[:, :, :], sem, reads=st.r)
            for cg in range(ns):
                wv, wr = self.wget()
                for tb in range(4):
                    bk = self.bank()
                    pairs = [(hT.t[:, k, tb * 128:(tb + 1) * 128], wv[:, k, :]) for k in range(KC)]
                    self.mm(bk, pairs, 512, reads=[wr] + hT.r)
                    if tb % 2 == 0:
                        P.op(P.act, lambda tb=tb, bk=bk: nc.scalar.copy(vst.t[:, tb, cg * 512:(cg + 1) * 512], bk[0][:, :512]),
                             reads=[bk[1]], writes=[vst.r[tb]])
                    else:
                        self.CP(vst.t[:, tb, cg * 512:(cg + 1) * 512], bk[0][:, :512], reads=[bk[1]], writes=[vst.r[tb]])
            P.dma(P.sp, io["V"][p0:p0 + 512, :].rearrange("(tb p) f -> p tb f", p=128), vst.t[:, :, :], sv_, reads=vst.r)
    self.end_phase()


def phase_attn(self, io, nq, nh):
    nc, P = self.nc, self.P
    self.begin_phase()
    T = nq * 512
    NKB = nq * 4
    scale = 1.0 / float(np.sqrt(128.0))
    rc = self.r_const
    masks = self.sb("masks", [128, 4, 512], F32)
    tri = self.sb("tri", [128, 128], BF16)
    comp = self.sb("comp", [128, 128], BF16)
    onesw = self.sb("onesw", [128, 512], F32)
    P.op(P.pool, lambda: nc.gpsimd.memset(onesw[:, :], 1.0), writes=[rc])
    for i in range(4):
        P.op(P.pool, lambda i=i: nc.gpsimd.affine_select(masks[:, i, :], onesw[:, :], [[1, 512]], ALU.is_gt, 0.0,
                                                         base=-128 * i, channel_multiplier=-1), reads=[rc], writes=[rc])
    P.op(P.pool, lambda: nc.gpsimd.affine_select(tri[:, :], onesw[:, 0:128], [[-1, 128]], ALU.is_gt, 0.0,
                                                 base=1, channel_multiplier=1), reads=[rc], writes=[rc])
    P.op(P.pool, lambda: nc.gpsimd.affine_select(comp[:, :], onesw[:, 0:128], [[1, 128]], ALU.is_gt, 0.0,
                                                 base=0, channel_multiplier=-1), reads=[rc], writes=[rc])
    Ksb = [(self.sb("Ksb%d" % i, [128, T], BF16), Res("Ksb%d" % i), self.dsem("Ksb%d" % i)) for i in range(2)]
    Vsb = [(self.sb("Vsb%d" % i, [128, NKB, 128], BF16), Res("Vsb%d" % i), self.dsem("Vsb%d" % i)) for i in range(2)]
    NL = 2
    Eb = [Buf(self.sb("Eb%d" % l, [128, 3, 512], F32), 3, "Eb%d" % l) for l in range(NL)]
    Lb = [Buf(self.sb("Lb%d" % l, [128, 3, 512], BF16), 3, "Lb%d" % l) for l in range(NL)]
    Gb = [Buf(self.sb("Gb%d" % l, [128, 2, 512], F32), 2, "Gb%d" % l) for l in range(NL)]
    Ab = [Buf(self.sb("Ab%d" % l, [128, 2, 512], BF16), 2, "Ab%d" % l) for l in range(NL)]
    Ob = [Buf(self.sb("Ob%d" % l, [128, 2, 512], BF16), 2, "Ob%d" % l) for l in range(NL)]
    Qsb = [[(self.sb("Qsb%d_%d" % (l, i), [128, 512], BF16), Res("Qsb%d_%d" % (l, i)), self.dsem("Qsb%d_%d" % (l, i)))
            for i in range(2)] for l in range(NL)]
    osem = [[self.dsem("oo%d_%d" % (l, i)) for i in range(2)] for l in range(NL)]
    zt = self.sb("zpad", [128, nh, PRE], BF16)
    r_z = Res("zpad")
    P.op(P.pool, lambda: nc.gpsimd.memset(zt[:, :, :], 0.0), writes=[r_z])
    P.dma(P.sp, io["oT"][:, 0, :, 0:PRE].rearrange("h p n -> p h n"), zt[:, :, :], self.dsem("zpad"), reads=[r_z])
    Sbk = [self.banks[0:2], self.banks[2:4]]
    Rbk = self.banks[4:6]
    Obk = self.banks[6:8]
    loaded = {}

    def load_head(h):
        if h in loaded or h >= nh:
            return
        Kt, Kr, Ks = Ksb[h % 2]
        Vt, Vr, Vs = Vsb[h % 2]
        P.dma(P.sp, Kt[:, :], io["KT"][h, :, :], Ks, writes=[Kr])
        P.dma(P.sp, Vt[:, :, :], io["V"][:, h * 128:(h + 1) * 128].rearrange("(kb p) d -> p kb d", p=128), Vs, writes=[Vr])
        loaded[h] = True

    cnt = [0] * NL

    class Chain:
        pass

    def start_chain(l, h, j):
        load_head(h)
        if j == nq // 2:
            load_head(h + 1)
        c = Chain()
        c.l, c.h, c.j = l, h, j
        c.Kt, c.Kr, _ = Ksb[h % 2]
        c.Vt, c.Vr, _ = Vsb[h % 2]
        c.ci = cnt[l]
        cnt[l] += 1
        c.Qt, c.Qr, Qs = Qsb[l][c.ci % 2]
        P.dma(P.sp, c.Qt[:, :], io["QT"][h, :, j * 512:(j + 1) * 512], Qs, writes=[c.Qr])
        c.steps = list(range(4 * j + 3, -1, -1))
        c.N = len(c.steps)
        c.k = 0
        mm1(c, 0)
        if c.N > 1:
            mm1(c, 1)
        e_(c, 0)
        l_(c, 0)
        mm2(c, 0)
        return c

    def mm1(c, k):
        kb = c.steps[k]
        sbk = Sbk[c.l][k % 2]
        self.mm(sbk, [(c.Kt[:, kb * 128:(kb + 1) * 128], c.Qt[:, :])], 512, reads=[c.Kr, c.Qr])

    def e_(c, k):
        kb = c.steps[k]
        sbk = Sbk[c.l][k % 2]
        E = Eb[c.l]
        e = E.t[:, k % 3, :]
        self.A(e, sbk[0][:, :], AF.Exp, reads=[sbk[1]], writes=[E.r[k % 3]], scale=scale)
        i = kb - 4 * c.j
        if i >= 0:
            self.TT(e, e, masks[:, i, :], ALU.mult, reads=[E.r[k % 3], rc], writes=[E.r[k % 3]], eng=P.pool)

    def l_(c, k):
        E, L = Eb[c.l], Lb[c.l]
        self.A(L.t[:, k % 3, :], E.t[:, k % 3, :], AF.Ln, reads=[E.r[k % 3]], writes=[L.r[k % 3]], bias=1.0, scale=1.0)

    def mm2(c, k):
        L, Rb = Lb[c.l], Rbk[c.l]
        P.group(P.pe, [lambda: nc.tensor.matmul(Rb[0][:, :], tri[:, :], L.t[:, k % 3, :], start=(k == 0), stop=False)],
                reads=[L.r[k % 3], rc], writes=[Rb[1]])

    def g_(c, k):
        G, Rb = Gb[c.l], Rbk[c.l]
        self.A(G.t[:, k % 2, :], Rb[0][:, :], AF.Exp, reads=[Rb[1]], writes=[G.r[k % 2]], scale=-1.0)

    def a_(c, k):
        E, G, A_ = Eb[c.l], Gb[c.l], Ab[c.l]
        self.TT(A_.t[:, k % 2, :], E.t[:, k % 3, :], G.t[:, k % 2, :], ALU.mult,
                reads=[E.r[k % 3], G.r[k % 2]], writes=[A_.r[k % 2]])

    def mm3(c, k):
        L, Rb = Lb[c.l], Rbk[c.l]
        P.group(P.pe, [lambda: nc.tensor.matmul(Rb[0][:, :], comp[:, :], L.t[:, k % 3, :], start=False, stop=(k == c.N - 1))],
                reads=[L.r[k % 3], rc], writes=[Rb[1]])

    def mm4(c, k):
        A_, OB = Ab[c.l], Obk[c.l]
        kb = c.steps[k]
        P.group(P.pe, [lambda: nc.tensor.matmul(OB[0][:, :], c.Vt[:, kb, :], A_.t[:, k % 2, :], start=(k == 0), stop=(k == c.N - 1))],
                reads=[c.Vr, A_.r[k % 2]], writes=[OB[1]])

    def finish(c):
        l, h, j, ci = c.l, c.h, c.j, c.ci
        ob, OB = Ob[l], Obk[l]
        self.CP(ob.t[:, ci % 2, :], OB[0][:, :], reads=[OB[1]], writes=[ob.r[ci % 2]])
        sem = osem[l][ci % 2]
        pc0 = PRE + j * 512
        if pc0 + 512 <= NP2:
            P.dma(P.sp, io["oT"][h, 0, :, pc0:pc0 + 512], ob.t[:, ci % 2, :], sem, reads=[ob.r[ci % 2]])
            if pc0 + 512 > HALF:
                P.dma(P.sp, io["oT"][h, 1, :, 0:pc0 + 512 - HALF], ob.t[:, ci % 2, HALF - pc0:512], sem, reads=[ob.r[ci % 2]])
        else:
            P.dma(P.sp, io["oT"][h, 1, :, pc0 - HALF:pc0 - HALF + 512], ob.t[:, ci % 2, :], sem, reads=[ob.r[ci % 2]])

    work = [(h, j) for h in range(nh) for j in range(nq)]
    lanes = [None] * NL
    wi = 0
    while True:
        for l in range(NL):
            if lanes[l] is None and wi < len(work):
                lanes[l] = start_chain(l, *work[wi])
                wi += 1
        act = [c for c in lanes if c is not None]
        if not act:
            break
        for c in act:
            if c.k + 2 < c.N:
                mm1(c, c.k + 2)
        for c in act:
            if c.k + 1 < c.N:
                e_(c, c.k + 1)
        for c in act:
            if c.k + 1 < c.N:
                l_(c, c.k + 1)
        for c in act:
            g_(c, c.k)
        for c in act:
            a_(c, c.k)
        for c in act:
            mm3(c, c.k)
            if c.k + 1 < c.N:
                mm2(c, c.k + 1)
            if c.k >= 1:
                mm4(c, c.k - 1)
            if c.k == c.N - 1:
                mm4(c, c.k)
                finish(c)
                lanes[c.l] = None
            c.k += 1
    self.end_phase()


def phase_attn2(self, io, nq, nh):
    nc, P = self.nc, self.P
    self.begin_phase()
    T = nq * 512
    NKB = nq * 4
    scale = 1.0 / float(np.sqrt(128.0))
    rc = self.r_const
    masks = self.sb("masks", [128, 4, 2, 512], F32)
    tri = self.sb("tri", [128, 128], BF16)
    comp = self.sb("comp", [128, 128], BF16)
    onesw = self.sb("onesw", [128, 512], F32)
    P.op(P.pool, lambda: nc.gpsimd.memset(onesw[:, :], 1.0), writes=[rc])
    for i in range(4):
        for l in range(2):
            P.op(P.pool, lambda i=i, l=l: nc.gpsimd.affine_select(masks[:, i, l, :], onesw[:, :], [[1, 512]], ALU.is_gt, 0.0,
                                                                  base=-128 * i, channel_multiplier=-1), reads=[rc], writes=[rc])
    P.op(P.pool, lambda: nc.gpsimd.affine_select(tri[:, :], onesw[:, 0:128], [[-1, 128]], ALU.is_gt, 0.0,
                                                 base=1, channel_multiplier=1), reads=[rc], writes=[rc])
    P.op(P.pool, lambda: nc.gpsimd.affine_select(comp[:, :], onesw[:, 0:128], [[1, 128]], ALU.is_gt, 0.0,
                                                 base=0, channel_multiplier=-1), reads=[rc], writes=[rc])
    NB = 4
    Ksb = [(self.sb("Ksb%d" % i, [128, T], BF16), Res("Ksb%d" % i), self.dsem("Ksb%d" % i)) for i in range(NB)]
    Vsb = [(self.sb("Vsb%d" % i, [128, NKB, 128], BF16), Res("Vsb%d" % i), self.dsem("Vsb%d" % i)) for i in range(NB)]
    Eb = Buf(self.sb("Eb", [128, 3, 2, 512], F32), 3, "Eb")
    Lb = Buf(self.sb("Lb", [128, 3, 2, 512], BF16), 3, "Lb")
    Gb = Buf(self.sb("Gb", [128, 2, 2, 512], F32), 2, "Gb")
    Ab = Buf(self.sb("Ab", [128, 2, 2, 512], BF16), 2, "Ab")
    Ob = Buf(self.sb("Ob", [128, 2, 2, 512], BF16), 2, "Ob")
    NQB = 3
    Qsb = [(self.sb("Qsb%d" % i, [128, 2, 512], BF16), Res("Qsb%d" % i), self.dsem("Qsb%d" % i)) for i in range(NQB)]
    osem = [self.dsem("oo%d" % i) for i in range(2)]
    zt = self.sb("zpad", [128, nh, PRE], BF16)
    r_z = Res("zpad")
    P.op(P.pool, lambda: nc.gpsimd.memset(zt[:, :, :], 0.0), writes=[r_z])
    P.dma(P.sp, io["oT"][:, 0, :, 0:PRE].rearrange("h p n -> p h n"), zt[:, :, :], self.dsem("zpad"), reads=[r_z])
    Sp = [self.bankpairs[0], self.bankpairs[1]]
    Sr = [[self.banks[0][1], self.banks[1][1]], [self.banks[2][1], self.banks[3][1]]]
    Rp = self.bankpairs[2]
    Rr = [self.banks[4][1], self.banks[5][1]]
    Op = self.bankpairs[3]
    Or = [self.banks[6][1], self.banks[7][1]]
    loaded = {}

    def load_head(h):
        if h in loaded or h >= nh:
            return
        Kt, Kr, Ks = Ksb[h % NB]
        Vt, Vr, Vs = Vsb[h % NB]
        P.dma(P.sp, Kt[:, :], io["KT"][h, :, :], Ks, writes=[Kr])
        P.dma(P.sp, Vt[:, :, :], io["V"][:, h * 128:(h + 1) * 128].rearrange("(kb p) d -> p kb d", p=128), Vs, writes=[Vr])
        loaded[h] = True

    steps = []
    for hp in range(nh // 2):
        for j in range(nq):
            N = 4 * j + 4
            for k in range(N):
                steps.append((hp, j, k, N))
    NS = len(steps)
    qbuf = {}
    qcount = [0]

    def get_q(hp, j):
        key = (hp, j)
        if key not in qbuf:
            load_head(2 * hp)
            load_head(2 * hp + 1)
            if j == nq // 2:
                load_head(2 * hp + 2)
                load_head(2 * hp + 3)
            Qt, Qr, Qs = Qsb[qcount[0] % NQB]
            qcount[0] += 1
            for l in range(2):
                P.dma(P.sp, Qt[:, l, :], io["QT"][2 * hp + l, :, j * 512:(j + 1) * 512], Qs, writes=[Qr])
            qbuf[key] = (Qt, Qr)
        return qbuf[key]

    def mm1(s):
        hp, j, k, N = steps[s]
        kb = 4 * j + 3 - k
        Qt, Qr = get_q(hp, j)
        for l in range(2):
            h = 2 * hp + l
            Kt, Kr, _ = Ksb[h % NB]
            t = Sp[s % 2]
            P.group(P.pe, [lambda: nc.tensor.matmul(t[:, l, :], Kt[:, kb * 128:(kb + 1) * 128], Qt[:, l, :], start=True, stop=True)],
                    reads=[Kr, Qr], writes=[Sr[s % 2][l]])

    def el(s):
        hp, j, k, N = steps[s]
        kb = 4 * j + 3 - k
        self.A(Eb.t[:, s % 3, :, :], Sp[s % 2][:, :, :], AF.Exp, reads=Sr[s % 2], writes=[Eb.r[s % 3]], scale=scale)
        i = kb - 4 * j
        if i >= 0:
            self.TT(Eb.t[:, s % 3, :, :], Eb.t[:, s % 3, :, :], masks[:, i, :, :], ALU.mult, reads=[Eb.r[s % 3], rc],
                    writes=[Eb.r[s % 3]], eng=P.pool)
        self.A(Lb.t[:, s % 3, :, :], Eb.t[:, s % 3, :, :], AF.Ln, reads=[Eb.r[s % 3]], writes=[Lb.r[s % 3]], bias=1.0, scale=1.0)

    def mm2(s):
        hp, j, k, N = steps[s]
        for l in range(2):
            P.group(P.pe, [lambda: nc.tensor.matmul(Rp[:, l, :], tri[:, :], Lb.t[:, s % 3, l, :], start=(k == 0), stop=False)],
                    reads=[Lb.r[s % 3], rc], writes=[Rr[l]])

    def mm3(s):
        hp, j, k, N = steps[s]
        for l in range(2):
            P.group(P.pe, [lambda: nc.tensor.matmul(Rp[:, l, :], comp[:, :], Lb.t[:, s % 3, l, :], start=False, stop=(k == N - 1))],
                    reads=[Lb.r[s % 3], rc], writes=[Rr[l]])

    def mm4(s):
        hp, j, k, N = steps[s]
        kb = 4 * j + 3 - k
        for l in range(2):
            h = 2 * hp + l
            Vt, Vr, _ = Vsb[h % NB]
            P.group(P.pe, [lambda: nc.tensor.matmul(Op[:, l, :], Vt[:, kb, :], Ab.t[:, s % 2, l, :], start=(k == 0), stop=(k == N - 1))],
                    reads=[Vr, Ab.r[s % 2]], writes=[Or[l]])
        if k == N - 1:
            ci = hp * nq + j
            self.CP(Ob.t[:, ci % 2, :, :], Op[:, :, :], reads=Or, writes=[Ob.r[ci % 2]])
            sem = osem[ci % 2]
            pc0 = PRE + j * 512
            for l in range(2):
                h = 2 * hp + l
                ob = Ob.t[:, ci % 2, l, :]
                if pc0 + 512 <= NP2:
                    P.dma(P.sp, io["oT"][h, 0, :, pc0:pc0 + 512], ob, sem, reads=[Ob.r[ci % 2]])
                    if pc0 + 512 > HALF:
                        P.dma(P.sp, io["oT"][h, 1, :, 0:pc0 + 512 - HALF], Ob.t[:, ci % 2, l, HALF - pc0:512], sem, reads=[Ob.r[ci % 2]])
                else:
                    P.dma(P.sp, io["oT"][h, 1, :, pc0 - HALF:pc0 - HALF + 512], ob, sem, reads=[Ob.r[ci % 2]])

    mm1(0)
    if NS > 1:
        mm1(1)
    el(0)
    mm2(0)
    for s in range(NS):
        if s + 2 < NS:
            mm1(s + 2)
        if s + 1 < NS:
            el(s + 1)
        self.A(Gb.t[:, s % 2, :, :], Rp[:, :, :], AF.Exp, reads=Rr, writes=[Gb.r[s % 2]], scale=-1.0)
        self.TT(Ab.t[:, s % 2, :, :], Eb.t[:, s % 3, :, :], Gb.t[:, s % 2, :, :], ALU.mult,
                reads=[Eb.r[s % 3], Gb.r[s % 2]], writes=[Ab.r[s % 2]])
        mm3(s)
        if s + 1 < NS:
            mm2(s + 1)
        if s >= 1:
            mm4(s - 1)
    mm4(NS - 1)
    self.end_phase()


Builder.phase_attn2 = phase_attn2


def phase_b2(self, io, tiles, dyn_o):
    nc, P = self.nc, self.P
    self.begin_phase()
    self.common_init()
    self.init_wstream(3)
    rv = self.r_vec
    mv = self.load_mods(io)
    sv = self.load_rows_T("svB", io["svecs2"], 2, D)
    fdw = self.load_rows_T("fdwB", io["ffn_dw"], 4, 2 * FF)
    pm = self.sb("pm_sb", [128, 1], F32)
    P.dma(P.sp, pm[:, :], io["pm"][:, :], self.dsem("pm"), writes=[rv])
    a3, sh3, g3 = self.mod_vecs(mv, 3, sv[:, :, 0], "ffn1")
    g_mix1 = mv[:, :, 8]
    afin = self.sb("afin", [128, KC], F32)
    self.TS(afin[:, :], sv[:, :, 1], float(np.sqrt(D)), None, ALU.mult, None, reads=[rv], writes=[rv])
    xT = Buf(self.sb("xT", [128, KC, 512], F32), KC, "xT")
    hT = Buf(self.sb("hT", [128, KC, 512], BF16), KC, "hT")
    gTt = self.sb("gTt", [128, FC, 512], BF16)
    gT = Buf(gTt, FC, "gT")
    yv = gTt[:, 0:32, :].bitcast(F32).rearrange("p a b -> p (a b)").rearrange("p (c n) -> p c n", c=KC)
    Tg = Buf(self.sb("Tg", [128, 2, 512], F32), 2, "Tg")
    Tv = Buf(self.sb("Tv", [128, 2, 512], F32), 2, "Tv")
    Hff = self.sb("Hff", [128, 2 * FC, 2], F32)
    rH = [Res("Hff%d" % i) for i in range(2 * FC)]
    P.op(P.pool, lambda: nc.gpsimd.memset(Hff[:, :, :], 0.0), writes=rH)
    for _ in tiles:
        for i in range(4):
            self.wplan_add(io["o_w"], D, [(512 * i, 512)])
        for i in range(22):
            self.wplan_add(io["up_w"], D, [(256 * i, 256), (FF + 256 * i, 256)])
        for i in range(16):
            self.wplan_add(io["down_w"], FF, [(128 * i, 128)])
    sx, so = self.dsem("ldx"), self.dsem("ldo")
    if dyn_o:
        par = self.par
    for ti, (start, n) in enumerate(tiles):
        P.dma(P.sp, xT.t[:, :, :n], io["x1T"][:, :, start:start + n].rearrange("c p n -> p c n"), sx, writes=xT.r)
        for rk in range(2):
            if n == 512:
                src = io["oT"][:, bass.ds(par, 1), rk, :, start:start + n].rearrange("h o p n -> p (h o) n")
                P.dma(P.sp, hT.t[:, 8 * rk:8 * rk + 8, :n], src, so, writes=hT.r[8 * rk:8 * rk + 8])
            else:
                parg = nc.gpsimd.partition_id() % 2
                src = io["oT"][:, bass.ds(parg, 1), rk, :, start:start + n].rearrange("h o p n -> p (h o) n")
                P.dma(P.pool, hT.t[:, 8 * rk:8 * rk + 8, :n], src, so, writes=hT.r[8 * rk:8 * rk + 8])

        def ep_o(bk, cid, n=n):
            self.STT(xT.t[:, cid, :n], bk[0][:, :n], g_mix1[:, cid:cid + 1], xT.t[:, cid, :n], ALU.mult, ALU.add,
                     reads=[bk[1], rv, xT.r[cid]], writes=[xT.r[cid]])
        self.linear(hT, KC, n, [[4 * i + q for q in range(4)] for i in range(4)], ep_o)
        self.rms_mod(xT, hT, n, a3, sh3)
        if ti == 0:
            self.TS(hT.t[:, :, 0:PRE], hT.t[:, :, 0:PRE], pm[:, 0:1], None, ALU.mult, None, reads=hT.r + [rv], writes=hT.r)
        self.ffn(hT, xT, gT, Tg, Tv, Hff, rH, fdw, g3, n)
        self.rms_mod_multi(xT, yv, [[gT.r[2 * c], gT.r[2 * c + 1]] for c in range(KC)], n, afin)
        nb = (n + 127) // 128
        for tb in range(nb):
            nt = min(128, n - tb * 128)
            orow, orr, osm = self.xrow[self.xrow_i % 2]
            self.xrow_i += 1
            for gq in range(4):
                bk = self.bank()
                fns = []
                for q in range(4):
                    c = 4 * gq + q
                    fns.append(lambda q=q, c=c, bk=bk: nc.tensor.transpose(bk[0][:nt, q * 128:(q + 1) * 128],
                                                                           yv[:, c, tb * 128:tb * 128 + nt], self.ident[:, :]))
                rr = []
                for q in range(4):
                    rr += [gT.r[2 * (4 * gq + q)], gT.r[2 * (4 * gq + q) + 1]]
                P.group(P.pe, fns, reads=rr + [self.r_const], writes=[bk[1]])
                if gq % 2 == 0:
                    P.op(P.act, lambda gq=gq, bk=bk: nc.scalar.copy(orow[:nt, gq * 512:(gq + 1) * 512], bk[0][:nt, :]),
                         reads=[bk[1]], writes=[orr])
                else:
                    self.CP(orow[:nt, gq * 512:(gq + 1) * 512], bk[0][:nt, :], reads=[bk[1]], writes=[orr])
            P.dma(P.sp, io["y"][start + tb * 128:start + tb * 128 + nt, :], orow[:nt, :], osm, reads=[orr])
    self.end_phase()


def rms_mod_multi(self, xT, yv, yres, n, a_vec):
    nc, P = self.nc, self.P
    sq = self.tmpbf
    bk = self.bank()
    for c in range(KC):
        self.A(sq.t[:, c % 2, :n], xT.t[:, c, :n], AF.Square, reads=[xT.r[c]], writes=[sq.r[c % 2]])
        P.group(P.pe, [lambda c=c: nc.tensor.matmul(bk[0][:, :n], self.ones_bf[:], sq.t[:, c % 2, :n],
                                                    start=(c == 0), stop=(c == KC - 1))],
                reads=[sq.r[c % 2], self.r_const], writes=[bk[1]])
    rstd = self.rstd
    self.A(rstd.t[:, 0, :n], bk[0][:, :n], AF.Ln, reads=[bk[1]], writes=[rstd.r[0]], bias=self.eps_rms[:, 0:1], scale=1.0)
    self.A(rstd.t[:, 0, :n], rstd.t[:, 0, :n], AF.Exp, reads=[rstd.r[0]], writes=[rstd.r[0]], scale=-0.5)
    for c in range(KC):
        tm = self.tmpf
        i = c % 2
        self.TT(tm.t[:, i, :n], xT.t[:, c, :n], rstd.t[:, 0, :n], ALU.mult, reads=[xT.r[c], rstd.r[0]], writes=[tm.r[i]])
        self.A(yv[:, c, :n], tm.t[:, i, :n], AF.Identity, reads=[tm.r[i], self.r_vec], writes=yres[c],
               bias=0.0, scale=a_vec[:, c:c + 1])


Builder.phase_mods = phase_mods
Builder.phase_w = phase_w
Builder.phase_qkv = phase_qkv
Builder.phase_attn = phase_attn
Builder.phase_b2 = phase_b2
Builder.rms_mod_multi = rms_mod_multi


def tiles_for(ntok):
    t = []
    st = 0
    while st < ntok:
        n = min(512, ntok - st)
        t.append((st, n))
        st += n
    return t


WSPEC = [
    ("mod0", D, 3 * D), ("mod1", D, 3 * D), ("mod2", D, 3 * D), ("mod3", D, 3 * D),
    ("pw1", D, 2 * D), ("pw2", D, D), ("up0", D, 2 * FF), ("down0", FF, D),
    ("qkv", D, 3 * D), ("ow", D, D), ("up1", D, 2 * FF), ("down1", FF, D),
]


def build_fused():
    nc = bass.Bass("TRN2", target_bir_lowering=False)
    ext = lambda name, shape, d=F32, kind="ExternalInput": nc.dram_tensor(name, shape, d, kind=kind).ap()
    itn = lambda name, shape, d: nc.dram_tensor(name, shape, d).ap()
    B = Builder(nc)
    W = {}
    n_first = 0
    for wi, (name, K_, N_) in enumerate(WSPEC):
        Ks = K_ // 4
        if name.startswith("mod"):
            W[name] = ext("w_" + name, [Ks, N_])
            continue
        wb = 1024 if K_ == D else 256
        nb = N_ // wb
        w = {"shard": ext("w_" + name, [Ks, N_]), "Ks": Ks, "wb": wb, "nb": nb, "res": Res("wf_" + name),
             "wsh": itn("wsh_" + name, [nb, Ks, wb], BF16), "full": itn("wf_" + name, [nb, K_, wb], BF16)}
        W[name] = w
        B.bg_add_weight(wi, w)
        if name == "pw2":
            n_first = len(B.bg)
        if name == "down0":
            B.n_bg_mid = len(B.bg) - n_first
    B.bg_pump(n_first)
    B.n_bg_rest = len(B.bg) - n_first - B.n_bg_mid
    part = itn("modpart", [4, 2, 3 * D], F32)
    msum = itn("modsum", [4, 2, 3 * D], F32)
    mb = ext("mb", [4, 3 * D])
    B.phase_mods({"cq": ext("cq", [2, 512]), "mw": [W["mod%d" % m] for m in range(4)], "part": part, "msum": msum})
    x1T = itn("x1T", [KC, 128, NLOC], F32)
    h1loc = itn("h1loc", [KC, 128, NLOC], BF16)
    pm = ext("pm", [128, 1])
    B.phase_a({"xin": ext("xin", [NLOC, D]), "msum": msum, "mb": mb, "svecs": ext("svecs", [7, D]), "pw1b": ext("pw1b", [2, D]),
               "dww": ext("dww", [CW, D]), "ffn_dw": ext("ffn_dw0", [4, 2 * FF]), "pm": pm,
               "diagw": itn("diagw", [KC, 128, CW * 128], BF16),
               "pw1_w": W["pw1"], "pw2_w": W["pw2"], "up_w": W["up0"], "down_w": W["down0"],
               "x1T": x1T, "h1T": h1loc}, tiles_for(NLOC))
    B.bg_pump()
    h1all = itn("h1all", [KC, 2, 128, NLOC], BF16)
    sem = B.dsem("ag_h1")
    for cc in range(KC):
        B.P.collective("AllGather", [h1loc[cc]], [h1all[cc].rearrange("r p n -> (r p) n")], PAIRS, sem)
    B.barrier(full=True)
    nh = NH // 2
    QT = itn("QT", [nh, 128, SEQ], BF16)
    KT = itn("KT", [nh, 128, SEQ], BF16)
    V = itn("V", [SEQ, nh * 128], BF16)
    oTloc = itn("oTloc", [nh, 2, 128, NP2], BF16)
    io = {"h1all": h1all, "qkv": W["qkv"], "QT": QT, "KT": KT, "V": V, "oT": oTloc}
    B.phase_qkv(io, HALF // 512, nh)
    B.phase_attn2(io, SEQ // 512, nh)
    oall = itn("oall", [nh, 2, 2, 128, NP2], BF16)
    sem = B.dsem("ag_o")
    for h in range(nh):
        for pt in range(2):
            B.P.collective("AllGather", [oTloc[h, pt]], [oall[h, pt].rearrange("r p n -> (r p) n")], PAIRS, sem)
    B.barrier(full=True)
    B.phase_b2({"x1T": x1T, "oT": oall, "msum": msum, "mb": mb, "svecs2": ext("svecs2", [2, D]), "ffn_dw": ext("ffn_dw1", [4, 2 * FF]),
                "pm": pm, "o_w": W["ow"], "up_w": W["up1"], "down_w": W["down1"],
                "y": ext("y", [NLOC, D], F32, "ExternalOutput")}, tiles_for(NLOC), dyn_o=True)
    B.barrier(full=True)
    return nc


def _f32(a):
    return np.ascontiguousarray(np.asarray(a, dtype=np.float32))


def kernel(x, c, mix_norm_g, mix_mod_w, mix_mod_b, cv_pw1_w, cv_pw1_b, cv_dw_w, cv_dw_b, cv_ln_g, cv_ln_b,
           cv_pw2_w, cv_pw2_b, sb_qkv_w, sb_o_w, ffn_norm_g, ffn_mod_w, ffn_mod_b, ffn_up_w, ffn_dw_w,
           ffn_dw_b, ffn_down_w, final_norm_g):
    x = np.asarray(x)
    c = np.asarray(c)
    cores = list(range(8))
    full = {"mod0": mix_mod_w[0], "mod1": ffn_mod_w[0], "mod2": mix_mod_w[1], "mod3": ffn_mod_w[1],
            "pw1": cv_pw1_w[0], "pw2": cv_pw2_w[0], "up0": ffn_up_w[0], "down0": ffn_down_w[0],
            "qkv": sb_qkv_w[0], "ow": sb_o_w[0], "up1": ffn_up_w[1], "down1": ffn_down_w[1]}
    shared = {
        "mb": _f32(np.stack([mix_mod_b[0], ffn_mod_b[0], mix_mod_b[1], ffn_mod_b[1]])),
        "svecs": _f32(np.stack([mix_norm_g[0], ffn_norm_g[0], mix_norm_g[1], cv_dw_b[0], cv_ln_g[0], cv_ln_b[0], cv_pw2_b[0]])),
        "pw1b": _f32(np.asarray(cv_pw1_b[0]).reshape(2, D)),
        "dww": _f32(cv_dw_w[0]),
        "ffn_dw0": _f32(np.concatenate([np.asarray(ffn_dw_w[0]), np.asarray(ffn_dw_b[0])[None]], 0)),
        "ffn_dw1": _f32(np.concatenate([np.asarray(ffn_dw_w[1]), np.asarray(ffn_dw_b[1])[None]], 0)),
        "svecs2": _f32(np.stack([ffn_norm_g[1], final_norm_g])),
    }
    ims = []
    for i in cores:
        b, r = i // 2, i % 2
        im = dict(shared)
        for name, K_, N_ in WSPEC:
            ks = K_ // 4
            q = i % 4
            im["w_" + name] = _f32(np.asarray(full[name])[q * ks:(q + 1) * ks])
        qd = i // 4
        im["cq"] = _f32(c[2 * qd:2 * qd + 2, 512 * q:512 * q + 512])
        im["pm"] = np.full((128, 1), float(r), np.float32)
        if r == 0:
            im["xin"] = _f32(np.concatenate([np.zeros((PRE, D), np.float32), x[b, 0:HALF]], 0))
        else:
            im["xin"] = _f32(x[b, HALF - PRE:SEQ])
        ims.append(im)
    res = run_bass_kernel_spmd(build_fused(), ims, core_ids=cores)
    out = np.empty((4, SEQ, D), np.float32)
    for i in cores:
        b, r = i // 2, i % 2
        out[b, HALF * r:HALF * (r + 1)] = np.asarray(res.results[i]["y"])[PRE:]
    return out


def build_attn_test(nq, nh):
    T = nq * 512
    nc = bass.Bass("TRN2", target_bir_lowering=False)
    dt = lambda name, shape, d=F32, kind="ExternalInput": nc.dram_tensor(name, shape, d, kind=kind).ap()
    io = {"QT": dt("QT", [nh, 128, T], BF16), "KT": dt("KT", [nh, 128, T], BF16), "V": dt("V", [T, nh * 128], BF16),
          "oT": dt("oT", [nh, 2, 128, NP2], BF16, "ExternalOutput")}
    B = Builder(nc)
    B.phase_attn2(io, nq, nh)
    B.barrier(full=True)
    return nc
```

```python
import numpy as np
import ml_dtypes
import concourse.bass as bass
import concourse.mybir as mybir
from concourse.bass_utils import run_bass_kernel_spmd
from contextlib import ExitStack

F32 = mybir.dt.float32
BF16 = mybir.dt.bfloat16
AF = mybir.ActivationFunctionType
ALU = mybir.AluOpType

D = 2048
KC = 16
FF = 5632
FC = 44
CW = 31
NH = 16
SEQ = 8192
HALF = 4096
PRE = 64
NLOC = HALF + PRE
RMS_EPS = 1e-6
LN_EPS = 1e-5
PAIRS = [[0, 1], [2, 3], [4, 5], [6, 7]]
QUADS = [[0, 1, 2, 3], [4, 5, 6, 7]]
NP2 = NLOC


class Res:
    __slots__ = ("name", "w", "r")

    def __init__(self, name=""):
        self.name = name
        self.w = None
        self.r = {}


class Eng:
    def __init__(self, P, name, eng, is_pe=False):
        self.name = name
        self.eng = eng
        self.is_pe = is_pe
        self.sem = P.new_sem("e_" + name)
        self.count = 0
        self.waited = {}


class Prog:
    def __init__(self, nc):
        self.nc = nc
        self.sems = {}
        self.nsem = 0
        self.pe = Eng(self, "pe", nc.tensor, is_pe=True)
        self.act = Eng(self, "act", nc.scalar)
        self.dve = Eng(self, "dve", nc.vector)
        self.pool = Eng(self, "pool", nc.gpsimd)
        self.sp = Eng(self, "sp", nc.sync)
        self.dma_vals = {}

    def new_sem(self, name):
        h = self.nc.semaphore(name).__enter__()
        key = "s%d_%s" % (self.nsem, name)
        self.nsem += 1
        self.sems[key] = h
        return key

    def _wait(self, E, needs):
        best = {}
        for (k, v) in needs:
            if best.get(k, 0) < v:
                best[k] = v
        for k, v in best.items():
            if k == E.sem:
                if E.is_pe:
                    continue
                if E.count - v >= 2:
                    continue
            if E.waited.get(k, 0) >= v:
                continue
            E.eng.wait_ge(self.sems[k], v)
            E.waited[k] = v

    @staticmethod
    def _deps(reads, writes):
        needs = []
        for r in reads:
            if r.w is not None:
                needs.append(r.w)
        for w in writes:
            if w.w is not None:
                needs.append(w.w)
            needs.extend(w.r.items())
        return needs

    @staticmethod
    def _mark(tok, reads, writes):
        k, v = tok
        for w in writes:
            w.w = tok
            w.r = {}
        for r in reads:
            if r.w is tok:
                continue
            if r.r.get(k, 0) < v:
                r.r[k] = v

    def op(self, E, fn, reads=(), writes=()):
        self._wait(E, self._deps(reads, writes))
        ins = fn()
        ins.then_inc(self.sems[E.sem], 1)
        E.count += 1
        self._mark((E.sem, E.count), reads, writes)
        return ins

    def group(self, E, fns, reads=(), writes=()):
        self._wait(E, self._deps(reads, writes))
        ins = None
        for fn in fns:
            ins = fn()
        ins.then_inc(self.sems[E.sem], 1)
        E.count += 1
        self._mark((E.sem, E.count), reads, writes)
        return ins

    def dma(self, Q, out, in_, sem, reads=(), writes=()):
        self._wait(Q, self._deps(reads, writes))
        ins = Q.eng.dma_start(out=out, in_=in_)
        ins.then_inc(self.sems[sem], 16)
        v = self.dma_vals.get(sem, 0) + 16
        self.dma_vals[sem] = v
        self._mark((sem, v), reads, writes)
        return ins

    def collective(self, kind, ins, outs, groups, sem, reads=(), writes=(), op=None):
        Q = self.pool
        self._wait(Q, self._deps(reads, writes))
        ins_ = self.nc.gpsimd.collective_compute(kind, op if op is not None else ALU.bypass, replica_groups=groups,
                                                 ins=ins, outs=outs)
        ins_.then_inc(self.sems[sem], 1)
        v = self.dma_vals.get(sem, 0) + 1
        self.dma_vals[sem] = v
        self._mark((sem, v), reads, writes)

    def wait_all(self, E, ress):
        needs = []
        for r in ress:
            if r.w is not None:
                needs.append(r.w)
            needs.extend(r.r.items())
        self._wait(E, needs)


class Buf:
    def __init__(self, t, n, name):
        self.t = t
        self.r = [Res("%s%d" % (name, i)) for i in range(n)]


class Builder:
    def __init__(self, nc):
        self.nc = nc
        self.P = Prog(nc)
        P = self.P
        self.stack = None
        self.uid = 0
        self.bg_sems = set()
        self.banks = []
        self.bankpairs = []
        for i in range(4):
            t2 = nc.alloc_psum_tensor("bankp%d" % i, [128, 2, 512], F32)
            self.bankpairs.append(t2)
            for j in range(2):
                self.banks.append((t2[:, j, :], Res("bank%d" % (2 * i + j))))
        self.bank_i = 0
        self.dsems = {}
        self.ident = self.sb("ident", [128, 128], F32)
        self.ones_bf = self.sb("ones_bf", [128, 128], BF16)
        self.r_const = Res("const")
        ones_f = self.sb("ones_f", [128, 128], F32)
        P.op(P.pool, lambda: nc.gpsimd.memset(ones_f[:], 1.0), writes=[self.r_const])
        P.op(P.pool, lambda: nc.gpsimd.affine_select(self.ident[:], ones_f[:], [[1, 128]], ALU.is_equal, 0.0,
                                                     base=0, channel_multiplier=-1),
             reads=[self.r_const], writes=[self.r_const])
        P.op(P.pool, lambda: nc.gpsimd.memset(self.ones_bf[:], 1.0), writes=[self.r_const])
        self.ones_f = ones_f
        self.wstage = [(self.sb("wstage%d" % i, [128, 2048], BF16), Res("wstage%d" % i), self.dsem("wstage%d" % i)) for i in range(2)]
        self.wstage_i = 0
        pid = nc.sync.partition_id()
        self.par = pid % 2
        self.bsel = (pid % 4) // 2
        self.bg = []
        self.bg_i = 0

    def bg_add_weight(self, wi, w):
        P = self.P
        sem = self.dsem("agw%d" % wi)
        self.bg_sems.add(sem)
        Ks, wb, nb = w["Ks"], w["wb"], w["nb"]
        N = wb * nb
        sres = [Res("wsh%d_%d" % (wi, k)) for k in range(2)]

        def unit(r0, c0, wd):
            t, r, sm = self.wstage[self.wstage_i % 2]
            k = self.wstage_i % 2
            self.wstage_i += 1
            P.dma(P.pool, t[:, :wd], w["shard"][r0:r0 + 128, c0:c0 + wd], sm, writes=[r])
            P.dma(P.sp, w["wsh"][c0 // wb:(c0 + wd) // wb, r0:r0 + 128, :].rearrange("b p n -> p b n"),
                  t[:, :wd].rearrange("p (b n) -> p b n", n=wb), sm, reads=[r], writes=[sres[k]])
        for r0 in range(0, Ks, 128):
            for c0 in range(0, N, 2048):
                wd = min(2048, N - c0)
                self.bg.append(lambda r0=r0, c0=c0, wd=wd: unit(r0, c0, wd))

        def gather(bi):
            P.collective("AllGather", [w["wsh"][bi]], [w["full"][bi]], QUADS, sem, reads=sres, writes=[w["res"]])
        for bi in range(nb):
            self.bg.append(lambda bi=bi: gather(bi))
        w["bg_end"] = len(self.bg)

    def bg_pump(self, n=None):
        end = len(self.bg) if n is None else min(len(self.bg), self.bg_i + n)
        while self.bg_i < end:
            self.bg[self.bg_i]()
            self.bg_i += 1

    def sb(self, name, shape, dt):
        if self.stack is None:
            return self.nc.alloc_sbuf_tensor(name, shape, dt)
        self.uid += 1
        return self.stack.enter_context(self.nc.sbuf_tensor("%s_%d" % (name, self.uid), shape, dt))

    def begin_phase(self):
        self.stack = ExitStack()

    def barrier(self, full=False):
        P = self.P
        engs = [P.pe, P.act, P.dve, P.pool, P.sp]
        needs = [(e.sem, e.count) for e in engs if e.count > 0] + \
            [kv for kv in P.dma_vals.items() if full or kv[0] not in self.bg_sems]
        for e in engs:
            P._wait(e, [x for x in needs if x[0] != e.sem])

    def end_phase(self):
        self.barrier()
        self.stack.close()
        self.stack = None

    def dsem(self, name):
        if name not in self.dsems:
            self.dsems[name] = self.P.new_sem("d_" + name)
        return self.dsems[name]

    def bank(self):
        b = self.banks[self.bank_i]
        self.bank_i = (self.bank_i + 1) % 8
        return b

    def mm(self, bank, pairs, n, reads, m=128):
        nc = self.nc
        t, r = bank
        fns = []
        last = len(pairs) - 1
        for i, (l, rh) in enumerate(pairs):
            fns.append(lambda l=l, rh=rh, i=i: nc.tensor.matmul(t[:m, :n], l, rh, start=(i == 0), stop=(i == last)))
        self.P.group(self.P.pe, fns, reads=reads, writes=[r])

    def A(self, out, in_, func, reads, writes, bias=0.0, scale=1.0):
        nc = self.nc
        return self.P.op(self.P.act, lambda: nc.scalar.activation(out, in_, func, bias=bias, scale=scale),
                         reads=reads, writes=writes)

    def TT(self, out, a, b, op, reads, writes, eng=None):
        E = eng or self.P.dve
        return self.P.op(E, lambda: E.eng.tensor_tensor(out, a, b, op), reads=reads, writes=writes)

    def TS(self, out, a, s1, s2, op0, op1, reads, writes, eng=None):
        E = eng or self.P.dve
        if op1 is None:
            return self.P.op(E, lambda: E.eng.tensor_scalar(out, a, s1, None, op0), reads=reads, writes=writes)
        return self.P.op(E, lambda: E.eng.tensor_scalar(out, a, s1, s2, op0, op1), reads=reads, writes=writes)

    def STT(self, out, a, s, b, op0, op1, reads, writes):
        nc = self.nc
        return self.P.op(self.P.dve, lambda: nc.vector.scalar_tensor_tensor(out, a, s, b, op0, op1),
                         reads=reads, writes=writes)

    def CP(self, out, in_, reads, writes, eng=None):
        E = eng or self.P.dve
        return self.P.op(E, lambda: E.eng.tensor_copy(out, in_), reads=reads, writes=writes)

    def init_wstream(self, nslots=3):
        self.wslots = []
        for i in range(nslots):
            t = self.sb("wslot%d" % i, [128, 8192], BF16)
            self.wslots.append((t, Res("wslot%d" % i), self.dsem("wslot%d" % i)))
        self.wplan = []
        self.wnext_issue = 0
        self.wnext_use = 0

    def wplan_add(self, w, k_rows, pieces):
        kc = k_rows // 128
        tot = sum(p[-1] for p in pieces)
        assert kc * tot <= 8192
        self.wplan.append((w, kc, tot, pieces))

    def _wissue(self, upto):
        P = self.P
        while self.wnext_issue < min(upto, len(self.wplan)):
            i = self.wnext_issue
            w, kc, tot, pieces = self.wplan[i]
            t, r, s = self.wslots[i % len(self.wslots)]
            if pieces is None:
                P.dma(P.pool, t[:, 0:kc * tot], w[0], s, reads=[w[1]], writes=[r])
                self.wnext_issue += 1
                continue
            if self.bg_i < w.get("bg_end", 0):
                self.bg_pump(w["bg_end"] - self.bg_i)
            dst = t[:, 0:kc * tot].rearrange("p (kc n) -> p kc n", kc=kc)
            wb = w["wb"]
            o = 0
            for pc in pieces:
                if len(pc) == 2:
                    c0, ncols = pc
                    bi, off = c0 // wb, c0 % wb
                    assert off + ncols <= wb
                    src = w["full"][bi].rearrange("(kc p) n -> p kc n", p=128)[:, :, off:off + ncols]
                else:
                    bi, off, ncols = pc
                    src = w["full"][bass.ds(bi, 1)].rearrange("o (kc p) n -> p (o kc) n", p=128)[:, :, off:off + ncols]
                P.dma(P.pool, dst[:, :, o:o + ncols], src, s, reads=[w["res"]], writes=[r])
                o += ncols
            self.wnext_issue += 1

    def wget(self):
        i = self.wnext_use
        self._wissue(i + len(self.wslots))
        w, kc, tot, pieces = self.wplan[i]
        t, r, s = self.wslots[i % len(self.wslots)]
        self.wnext_use += 1
        return t[:, 0:kc * tot].rearrange("p (kc n) -> p kc n", kc=kc), r

    def rms_mod(self, xT, hT, n, a_vec, sh_vec):
        nc, P = self.nc, self.P
        sq = self.tmpbf
        bk = self.bank()
        for c in range(KC):
            self.A(sq.t[:, c % 2, :n], xT.t[:, c, :n], AF.Square, reads=[xT.r[c]], writes=[sq.r[c % 2]])
            nc_ = nc
            P.group(P.pe, [lambda c=c: nc_.tensor.matmul(bk[0][:, :n], self.ones_bf[:], sq.t[:, c % 2, :n],
                                                          start=(c == 0), stop=(c == KC - 1))],
                    reads=[sq.r[c % 2], self.r_const], writes=[bk[1]])
        rstd = self.rstd
        self.A(rstd.t[:, 0, :n], bk[0][:, :n], AF.Ln, reads=[bk[1]], writes=[rstd.r[0]], bias=self.eps_rms[:, 0:1], scale=1.0)
        self.A(rstd.t[:, 0, :n], rstd.t[:, 0, :n], AF.Exp, reads=[rstd.r[0]], writes=[rstd.r[0]], scale=-0.5)
        for c in range(KC):
            tm = self.tmpf
            i = c % 2
            self.TT(tm.t[:, i, :n], xT.t[:, c, :n], rstd.t[:, 0, :n], ALU.mult, reads=[xT.r[c], rstd.r[0]], writes=[tm.r[i]])
            self.A(hT.t[:, c, :n], tm.t[:, i, :n], AF.Identity, reads=[tm.r[i], self.r_vec], writes=[hT.r[c]],
                   bias=(sh_vec[:, c:c + 1] if sh_vec is not None else 0.0), scale=a_vec[:, c:c + 1])

    def linear(self, act, kc_n, n, slots, epilogue):
        for ids in slots:
            wv, wr = self.wget()
            for jj, cid in enumerate(ids):
                bk = self.bank()
                pairs = [(wv[:, k, jj * 128:(jj + 1) * 128], act.t[:, k, :n]) for k in range(kc_n)]
                self.mm(bk, pairs, n, reads=[wr] + [act.r[k] for k in range(kc_n)])
                epilogue(bk, cid)

    def load_rows_T(self, name, ap2d, rows, ncols):
        nc, P = self.nc, self.P
        nch = ncols // 128
        out = self.sb("m_" + name, [128, nch, rows], F32)
        r = self.r_vec
        stg, rs = self.vec_stage, self.vec_stage_r
        for c0 in range(0, ncols, 2048):
            w = min(2048, ncols - c0)
            if isinstance(ap2d, list):
                ro = 0
                for (apx, nr) in ap2d:
                    P.dma(P.sp, stg[ro:ro + nr, :w], apx[0:nr, c0:c0 + w], self.xrow[0][2], writes=[rs])
                    ro += nr
            else:
                P.dma(P.sp, stg[:rows, :w], ap2d[0:rows, c0:c0 + w], self.xrow[0][2], writes=[rs])
            for cc in range(w // 128):
                bk = self.bank()
                P.group(P.pe, [lambda cc=cc, bk=bk: nc.tensor.transpose(bk[0][:, :rows], stg[:rows, cc * 128:(cc + 1) * 128],
                                                                        self.ident[:rows, :rows])],
                        reads=[rs, self.r_const], writes=[bk[1]])
                self.CP(out[:, c0 // 128 + cc, :], bk[0][:, :rows], reads=[bk[1]], writes=[r])
        return out

    def common_init(self):
        nc, P = self.nc, self.P
        self.xrow = [(self.sb("xrow0", [128, D], F32), Res("xrow0"), self.dsem("xrow0"))] * 2
        self.xrow_i = 0
        self.vec_stage = self.xrow[0][0]
        self.vec_stage_r = self.xrow[0][1]
        self.r_vec = Res("vecs")
        self.sqb = Buf(self.sb("sqb", [128, 2, 512], BF16), 2, "sqb")
        self.tmpf = Buf(self.sb("tmpf", [128, 2, 512], F32), 2, "tmpf")
        self.tmpbf = Buf(self.sb("tmpbf", [128, 2, 512], BF16), 2, "tmpbf")
        self.rstd = Buf(self.sb("rstd", [128, 1, 512], F32), 1, "rstd")
        self.eps_rms = self.sb("eps_rms", [128, 2], F32)
        P.op(P.pool, lambda: nc.gpsimd.memset(self.eps_rms[:, 0:1], float(D * RMS_EPS)), writes=[self.r_const])
        P.op(P.pool, lambda: nc.gpsimd.memset(self.eps_rms[:, 1:2], float(LN_EPS)), writes=[self.r_const])

    def mod_vecs(self, mv, m, gvec, name):
        a = self.sb("a_" + name, [128, KC], F32)
        r = self.r_vec
        self.TS(a[:, :], mv[:, :, 3 * m + 1], 1.0, float(np.sqrt(D)), ALU.add, ALU.mult, reads=[r], writes=[r])
        self.TT(a[:, :], a[:, :], gvec, ALU.mult, reads=[r], writes=[r])
        return a, mv[:, :, 3 * m], mv[:, :, 3 * m + 2]

    def load_mods(self, io):
        nc = self.nc
        bsel = self.bsel
        self.P._wait(self.P.sp, [self.ar_token])
        src = [(io["msum"][m, bass.ds(bsel, 1), :].rearrange("o (t k) -> (o t) k", t=3), 3) for m in range(4)]
        mv = self.load_rows_T("mods", src, 12, D)
        bv = self.load_rows_T("modb", io["mb"].rearrange("m (t k) -> (m t) k", t=3), 12, D)
        self.TT(mv[:, :, :], mv[:, :, :], bv[:, :, :], ALU.add, reads=[self.r_vec], writes=[self.r_vec])
        return mv

    def load_mods_old(self, modall):
        nc, P = self.nc, self.P
        out = self.sb("m_mods", [128, KC, 12], F32)
        stg, rs = self.vec_stage, self.vec_stage_r
        P.dma(P.sp, stg[:12, :].rearrange("q (r k) -> q r k", r=2), modall.rearrange("r m t k -> (m t) r k"),
              self.xrow[0][2], writes=[rs])
        for cc in range(KC):
            bk = self.bank()
            P.group(P.pe, [lambda cc=cc, bk=bk: nc.tensor.transpose(bk[0][:, :12], stg[:12, cc * 128:(cc + 1) * 128],
                                                                    self.ident[:12, :12])],
                    reads=[rs, self.r_const], writes=[bk[1]])
            self.CP(out[:, cc, :], bk[0][:, :12], reads=[bk[1]], writes=[self.r_vec])
        return out

    def load_x_tile(self, xin, start, n, xT):
        nc, P = self.nc, self.P
        nb = (n + 127) // 128
        for b in range(nb):
            nt = min(128, n - b * 128)
            xr, rr, sm = self.xrow[self.xrow_i % 2]
            self.xrow_i += 1
            P.dma(P.sp, xr[:nt, :], xin[start + b * 128:start + b * 128 + nt, :], sm, writes=[rr])
            for g in range(4):
                bk = self.bank()
                fns = []
                for q in range(4):
                    c = 4 * g + q
                    fns.append(lambda q=q, c=c, bk=bk: nc.tensor.transpose(bk[0][:, q * 128:q * 128 + nt],
                                                                           xr[:nt, c * 128:(c + 1) * 128], self.ident[:nt, :nt]))
                P.group(P.pe, fns, reads=[rr, self.r_const], writes=[bk[1]])
                src = bk[0][:, :].rearrange("p (q t) -> p q t", q=4)[:, :, :nt]
                dst = xT.t[:, 4 * g:4 * g + 4, b * 128:b * 128 + nt]
                eng = self.P.act if (g % 2 == 0) else self.P.dve
                if eng is self.P.act:
                    P.op(P.act, lambda dst=dst, src=src: nc.scalar.copy(dst, src), reads=[bk[1]],
                         writes=[xT.r[4 * g + q] for q in range(4)])
                else:
                    self.CP(dst, src, reads=[bk[1]], writes=[xT.r[4 * g + q] for q in range(4)])

    def phase_a(self, io, tiles):
        nc, P = self.nc, self.P
        self.begin_phase()
        self.common_init()
        self.init_wstream(3)
        mv = self.load_mods(io)
        sv = self.load_rows_T("svA", io["svecs"], 7, D)
        pw1b = self.load_rows_T("pw1b", io["pw1b"], 2, D)
        dww = self.load_rows_T("dww", io["dww"], CW, D)
        fdw = self.load_rows_T("fdw", io["ffn_dw"], 4, 2 * FF)
        pm = self.sb("pm_sb", [128, 1], F32)
        P.dma(P.sp, pm[:, :], io["pm"][:, :], self.dsem("pm"), writes=[self.r_vec])
        rv = self.r_vec
        a0, sh0, g0 = self.mod_vecs(mv, 0, sv[:, :, 0], "mix0")
        a1, sh1, g1 = self.mod_vecs(mv, 1, sv[:, :, 1], "ffn0")
        a2, sh2, g2 = self.mod_vecs(mv, 2, sv[:, :, 2], "mix1")
        gb2 = self.sb("gb2", [128, KC], F32)
        self.TT(gb2[:, :], g0, sv[:, :, 6], ALU.mult, reads=[rv], writes=[rv])

        xT = Buf(self.sb("xT", [128, KC, 512], F32), KC, "xT")
        hT = Buf(self.sb("hT", [128, KC, 512], BF16), KC, "hT")
        uT = Buf(self.sb("uT", [128, KC, CW - 1 + 512], BF16), KC, "uT")
        gTt = self.sb("gTt", [128, FC, 512], BF16)
        gT = Buf(gTt, FC, "gT")
        cvt = gTt[:, 0:32, :].bitcast(F32).rearrange("p a b -> p (a b)").rearrange("p (c n) -> p c n", c=KC)
        Tg = Buf(self.sb("Tg", [128, 2, 512], F32), 2, "Tg")
        Tv = Buf(self.sb("Tv", [128, 2, 512], F32), 2, "Tv")
        Hff = self.sb("Hff", [128, 2 * FC, 2], F32)
        rH = [Res("Hff%d" % i) for i in range(2 * FC)]
        stat = Buf(self.sb("stat", [128, 3, 512], F32), 3, "stat")
        P.op(P.pool, lambda: nc.gpsimd.memset(Hff[:, :, :], 0.0), writes=rH)
        P.op(P.pool, lambda: nc.gpsimd.memset(uT.t[:, :, 0:CW - 1], 0.0), writes=uT.r)

        r_diag = Res("diagw")
        for _ in tiles:
            for i in range(8):
                self.wplan_add(io["pw1_w"], D, [(256 * i, 256), (D + 256 * i, 256)])
            for c in range(KC):
                self.wplan.append(((io["diagw"][c], r_diag), CW, 128, None))
            for i in range(4):
                self.wplan_add(io["pw2_w"], D, [(512 * i, 512)])
            for i in range(22):
                self.wplan_add(io["up_w"], D, [(256 * i, 256), (FF + 256 * i, 256)])
            for i in range(16):
                self.wplan_add(io["down_w"], FF, [(128 * i, 128)])

        ident_bf = self.sb("ident_bf", [128, 128], BF16)
        self.CP(ident_bf[:, :], self.ident[:, :], reads=[self.r_const], writes=[self.r_const])
        for c in range(KC):
            t, r, sm = self.wslots[c % 3]
            dv = t[:, 0:CW * 128].rearrange("p (k m) -> p k m", k=CW)
            for k in range(CW):
                self.TS(dv[:, k, :], ident_bf[:, :], dww[:, c, k:k + 1], None, ALU.mult, None, reads=[self.r_const, rv],
                        writes=[r])
            P.dma(P.sp, io["diagw"][c], t[:, 0:CW * 128], sm, reads=[r], writes=[r_diag])

        x1T_d, h1T_d = io["x1T"], io["h1T"]
        osem_x = [self.dsem("ox0"), self.dsem("ox1")]
        osem_h = [self.dsem("oh0"), self.dsem("oh1")]

        per_tile = (getattr(self, "n_bg_rest", 0) + len(tiles) - 2) // max(1, len(tiles) - 1)
        for ti, (start, n) in enumerate(tiles):
            first = (ti == 0)
            if not first:
                self.bg_pump(per_tile)
            self.load_x_tile(io["xin"], start, n, xT)
            self.rms_mod(xT, hT, n, a0, sh0)
            if first:
                pass
            sg = self.tmpf

            def ep_pw1(bk, cid, n=n, first=first):
                if cid < KC:
                    self._valbank[cid] = bk
                else:
                    j = cid - KC
                    vb = self._valbank.pop(j)
                    i = j % 2
                    self.A(sg.t[:, i, :n], bk[0][:, :n], AF.Sigmoid, reads=[bk[1], rv], writes=[sg.r[i]],
                           bias=pw1b[:, j, 1:2], scale=1.0)
                    self.STT(uT.t[:, j, CW - 1:CW - 1 + n], vb[0][:, :n], pw1b[:, j, 0:1], sg.t[:, i, :n], ALU.add, ALU.mult,
                             reads=[vb[1], sg.r[i], rv], writes=[uT.r[j]])
            self._valbank = {}
            slots = [[2 * i, 2 * i + 1, KC + 2 * i, KC + 2 * i + 1] for i in range(8)]
            self.linear(hT, KC, n, slots, ep_pw1)
            if first:
                self.TS(uT.t[:, :, CW - 1:CW - 1 + PRE], uT.t[:, :, CW - 1:CW - 1 + PRE], pm[:, 0:1], None, ALU.mult, None,
                        reads=uT.r + [rv], writes=uT.r)
            cv_r = [[gT.r[2 * c], gT.r[2 * c + 1]] for c in range(KC)]
            for c in range(KC):
                wv, wr = self.wget()
                dvw = wv[:, :, :]
                bkc = self.bank()
                pairs = [(dvw[:, k, :], uT.t[:, c, k:k + n]) for k in range(CW)]
                self.mm(bkc, pairs, n, reads=[wr, uT.r[c]])
                self.A(cvt[:, c, :n], bkc[0][:, :n], AF.Identity, reads=[bkc[1], rv], writes=cv_r[c],
                       bias=sv[:, c, 3:4], scale=1.0)
            bk_m = self.bank()
            bk_s = self.bank()
            for c in range(KC):
                i = c % 2
                self.A(self.tmpbf.t[:, i, :n], cvt[:, c, :n], AF.Identity, reads=cv_r[c], writes=[self.tmpbf.r[i]])
                P.group(P.pe, [lambda c=c, i=i: nc.tensor.matmul(bk_m[0][:, :n], self.ones_bf[:], self.tmpbf.t[:, i, :n],
                                                                 start=(c == 0), stop=(c == KC - 1))],
                        reads=[self.tmpbf.r[i], self.r_const], writes=[bk_m[1]])
                self.A(self.sqb.t[:, i, :n], cvt[:, c, :n], AF.Square, reads=cv_r[c], writes=[self.sqb.r[i]])
                P.group(P.pe, [lambda c=c, i=i: nc.tensor.matmul(bk_s[0][:, :n], self.ones_bf[:], self.sqb.t[:, i, :n],
                                                                 start=(c == 0), stop=(c == KC - 1))],
                        reads=[self.sqb.r[i], self.r_const], writes=[bk_s[1]])
            mean, var, rs = stat.t[:, 0, :n], stat.t[:, 1, :n], stat.t[:, 2, :n]
            self.A(mean, bk_m[0][:, :n], AF.Identity, reads=[bk_m[1]], writes=[stat.r[0]], scale=1.0 / D)
            self.TT(var, mean, mean, ALU.mult, reads=[stat.r[0]], writes=[stat.r[1]])
            self.STT(var, bk_s[0][:, :n], 1.0 / D, var, ALU.mult, ALU.subtract, reads=[bk_s[1], stat.r[1]], writes=[stat.r[1]])
            self.A(rs, var, AF.Ln, reads=[stat.r[1], self.r_const], writes=[stat.r[2]], bias=self.eps_rms[:, 1:2], scale=1.0)
            self.A(rs, rs, AF.Exp, reads=[stat.r[2]], writes=[stat.r[2]], scale=-0.5)
            for c in range(KC):
                i = c % 2
                self.TT(sg.t[:, i, :n], cvt[:, c, :n], mean, ALU.subtract, reads=cv_r[c] + [stat.r[0]], writes=[sg.r[i]])
                self.TT(sg.t[:, i, :n], sg.t[:, i, :n], rs, ALU.mult, reads=[sg.r[i], stat.r[2]], writes=[sg.r[i]])
                self.A(hT.t[:, c, :n], sg.t[:, i, :n], AF.Silu, reads=[sg.r[i], rv], writes=[hT.r[c]],
                       bias=sv[:, c, 5:6], scale=sv[:, c, 4:5])
            self.CP(uT.t[:, :, 0:CW - 1], uT.t[:, :, n:n + CW - 1], reads=uT.r, writes=uT.r)

            def ep_pw2(bk, cid, n=n):
                i = cid % 2
                self.A(sg.t[:, i, :n], bk[0][:, :n], AF.Identity, reads=[bk[1], rv], writes=[sg.r[i]],
                       bias=gb2[:, cid:cid + 1], scale=g0[:, cid:cid + 1])
                self.TT(xT.t[:, cid, :n], xT.t[:, cid, :n], sg.t[:, i, :n], ALU.add, reads=[xT.r[cid], sg.r[i]], writes=[xT.r[cid]])
            self.linear(hT, KC, n, [[4 * i + q for q in range(4)] for i in range(4)], ep_pw2)

            self.rms_mod(xT, hT, n, a1, sh1)
            if first:
                self.TS(hT.t[:, :, 0:PRE], hT.t[:, :, 0:PRE], pm[:, 0:1], None, ALU.mult, None, reads=hT.r + [rv], writes=hT.r)
            self.ffn(hT, xT, gT, Tg, Tv, Hff, rH, fdw, g1, n)

            k = ti % 2
            P.dma(P.sp, x1T_d[:, :, start:start + n].rearrange("c p n -> p c n"), xT.t[:, :, :n], osem_x[k], reads=xT.r)
            self.rms_mod(xT, hT, n, a2, sh2)
            P.dma(P.sp, h1T_d[:, :, start:start + n].rearrange("c p n -> p c n"), hT.t[:, :, :n], osem_h[k], reads=hT.r)
        self.end_phase()

    def ffn(self, hT, xT, gT, Tg, Tv, Hff, rH, fdw, gate, n):
        nc, P = self.nc, self.P
        rv = self.r_vec

        def conv3(bk, ch, T, i):
            p = bk[0]
            w0, w1, w2, b = fdw[:, ch, 0:1], fdw[:, ch, 1:2], fdw[:, ch, 2:3], fdw[:, ch, 3:4]
            t = T.t[:, i, :]
            self.A(t[:, :n], p[:, :n], AF.Identity, reads=[bk[1], rv], writes=[T.r[i]], bias=b, scale=w2)
            self.STT(t[:, 1:n], p[:, 0:n - 1], w1, t[:, 1:n], ALU.mult, ALU.add, reads=[bk[1], rv, T.r[i]], writes=[T.r[i]])
            self.STT(t[:, 2:n], p[:, 0:n - 2], w0, t[:, 2:n], ALU.mult, ALU.add, reads=[bk[1], rv, T.r[i]], writes=[T.r[i]])
            h = Hff[:, ch, :]
            self.STT(t[:, 0:1], h[:, 1:2], w1, t[:, 0:1], ALU.mult, ALU.add, reads=[rH[ch], rv, T.r[i]], writes=[T.r[i]])
            self.STT(t[:, 0:2], h[:, 0:2], w0, t[:, 0:2], ALU.mult, ALU.add, reads=[rH[ch], rv, T.r[i]], writes=[T.r[i]])
            P.op(P.act, lambda: nc.scalar.copy(h[:, 0:2], p[:, n - 2:n]), reads=[bk[1], T.r[i]], writes=[rH[ch]])

        def ep_up(bk, cid, n=n):
            if cid < FC:
                conv3(bk, cid, Tg, cid % 2)
            else:
                j = cid - FC
                i = j % 2
                conv3(bk, cid, Tv, i)
                self.A(Tg.t[:, i, :n], Tg.t[:, i, :n], AF.Silu, reads=[Tg.r[i]], writes=[Tg.r[i]])
                self.TT(gT.t[:, j, :n], Tg.t[:, i, :n], Tv.t[:, i, :n], ALU.mult, reads=[Tg.r[i], Tv.r[i]], writes=[gT.r[j]])
        slots = [[2 * i, 2 * i + 1, FC + 2 * i, FC + 2 * i + 1] for i in range(22)]
        self.linear(hT, KC, n, slots, ep_up)

        def ep_down(bk, cid, n=n):
            self.STT(xT.t[:, cid, :n], bk[0][:, :n], gate[:, cid:cid + 1], xT.t[:, cid, :n], ALU.mult, ALU.add,
                     reads=[bk[1], rv, xT.r[cid]], writes=[xT.r[cid]])
        self.linear(gT, FC, n, [[i] for i in range(16)], ep_down)


def phase_mods(self, io):
    nc, P = self.nc, self.P
    self.begin_phase()
    self.common_init()
    self.init_wstream(3)
    rv = self.r_vec
    cT = self.load_rows_T("cq", io["cq"], 2, 512)
    sc = self.sb("sc", [128, 4, 2], BF16)
    self.A(sc[:, :, :], cT[:, :, :], AF.Silu, reads=[rv], writes=[rv])
    NM = 3 * D
    orow = self.sb("orow", [2, NM], F32)
    r_o = Res("orow")
    sem_o = self.dsem("orow")
    wi = 0
    for m in range(4):
        for g3 in range(3):
            t, r, sm = self.wslots[wi % 3]
            wi += 1
            wv = t[:, :].rearrange("p (kc n) -> p kc n", kc=4)
            P.dma(P.pool, wv, io["mw"][m].rearrange("(kc p) n -> p kc n", p=128)[:, :, g3 * 2048:(g3 + 1) * 2048], sm, writes=[r])
            for g in range(4):
                bk = self.bank()
                pairs = [(sc[:, k, :], wv[:, k, g * 512:(g + 1) * 512]) for k in range(4)]
                self.mm(bk, pairs, 512, reads=[r, rv], m=2)
                o = g3 * 2048 + g * 512
                self.CP(orow[0:2, o:o + 512], bk[0][0:2, :512], reads=[bk[1]], writes=[r_o])
        P.dma(P.sp, io["part"][m], orow[0:2, :], sem_o, reads=[r_o])
    ar_sem = self.dsem("ar_mods")
    P.collective("AllReduce", [io["part"].rearrange("m b k -> (m b) k")], [io["msum"].rearrange("m b k -> (m b) k")], QUADS,
                 ar_sem, reads=[r_o], op=ALU.add)
    self.ar_token = (ar_sem, P.dma_vals[ar_sem])
    self.bg_sems.add(ar_sem)
    self.end_phase()


def phase_w(self, wl):
    nc, P = self.nc, self.P
    self.begin_phase()
    self.init_wstream(3)
    i = 0
    for wi, w in enumerate(wl):
        sem = self.dsem("agw%d" % wi)
        self.bg_sems.add(sem)
        Ks, wb, nb = w["Ks"], w["wb"], w["nb"]
        N = wb * nb
        sres = [Res("wsh%d_%d" % (wi, k)) for k in range(3)]
        for r0 in range(0, Ks, 128):
            for c0 in range(0, N, 8192):
                wd = min(8192, N - c0)
                t, r, sm = self.wslots[i % 3]
                P.dma(P.pool, t[:, :wd], w["shard"][r0:r0 + 128, c0:c0 + wd], sm, writes=[r])
                P.dma(P.sp, w["wsh"][c0 // wb:(c0 + wd) // wb, r0:r0 + 128, :].rearrange("b p n -> p b n"),
                      t[:, :wd].rearrange("p (b n) -> p b n", n=wb), sm, reads=[r], writes=[sres[i % 3]])
                i += 1
        for bi in range(nb):
            P.collective("AllGather", [w["wsh"][bi]], [w["full"][bi]], QUADS, sem, reads=sres, writes=[w["res"]])
    self.end_phase()


def phase_qkv(self, io, ntile, nh):
    nc, P = self.nc, self.P
    self.begin_phase()
    self.init_wstream(3)
    ns = nh // 4
    hTs = [Buf(self.sb("hq%d" % i, [128, KC, 512], BF16), KC, "hq%d" % i) for i in range(2)]
    hsem = [self.dsem("hq0"), self.dsem("hq1")]
    qst = Buf(self.sb("qst", [128, nh, 512], BF16), nh, "qst")
    kst = Buf(self.sb("kst", [128, nh, 512], BF16), nh, "kst")
    vst = Buf(self.sb("vst", [128, 4, nh * 128], BF16), 4, "vst")
    sq, sk, sv_ = self.dsem("oq"), self.dsem("ok"), self.dsem("ov")
    par = nc.gpsimd.partition_id() % 2
    for _ in range(2 * ntile):
        for which in range(3):
            for i in range(ns):
                self.wplan_add(io["qkv"], D, [(par + 2 * which, 512 * i, 512)])
    it = 0
    for s in range(2):
        for t in range(ntile):
            p0 = (s * ntile + t) * 512
            lc = PRE + 512 * t
            hT = hTs[it % 2]
            P.dma(P.sp, hT.t[:, :, :], io["h1all"][:, s, :, lc:lc + 512].rearrange("c p n -> p c n"), hsem[it % 2], writes=hT.r)
            it += 1
            for (st, dst, sem) in ((qst, io["QT"], sq), (kst, io["KT"], sk)):
                def ep(bk, cid, st=st):
                    if cid % 2 == 0:
                        P.op(P.act, lambda: nc.scalar.copy(st.t[:, cid, :], bk[0][:, :512]), reads=[bk[1]], writes=[st.r[cid]])
                    else:
                        self.CP(st.t[:, cid, :], bk[0][:, :512], reads=[bk[1]], writes=[st.r[cid]])
                self.linear(hT, KC, 512, [[4 * i + q for q in range(4)] for i in range(ns)], ep)
                P.dma(P.sp, dst[:, :, p0:p0 + 512].rearrange("h p n -> p h n"), st.t[:, :, :], sem, reads=st.r)
            for cg in range(ns):
                wv, wr = self.wget()
                for tb in range(4):
                    bk = self.bank()
                    pairs = [(hT.t[:, k, tb * 128:(tb + 1) * 128], wv[:, k, :]) for k in range(KC)]
                    self.mm(bk, pairs, 512, reads=[wr] + hT.r)
                    if tb % 2 == 0:
                        P.op(P.act, lambda tb=tb, bk=bk: nc.scalar.copy(vst.t[:, tb, cg * 512:(cg + 1) * 512], bk[0][:, :512]),
                             reads=[bk[1]], writes=[vst.r[tb]])
                    else:
                        self.CP(vst.t[:, tb, cg * 512:(cg + 1) * 512], bk[0][:, :512], reads=[bk[1]], writes=[vst.r[tb]])
            P.dma(P.sp, io["V"][p0:p0 + 512, :].rearrange("(tb p) f -> p tb f", p=128), vst.t[:, :, :], sv_, reads=vst.r)
    self.end_phase()


def phase_attn(self, io, nq, nh):
    nc, P = self.nc, self.P
    self.begin_phase()
    T = nq * 512
    NKB = nq * 4
    scale = 1.0 / float(np.sqrt(128.0))
    rc = self.r_const
    masks = self.sb("masks", [128, 4, 512], F32)
    tri = self.sb("tri", [128, 128], BF16)
    comp = self.sb("comp", [128, 128], BF16)
    onesw = self.sb("onesw", [128, 512], F32)
    P.op(P.pool, lambda: nc.gpsimd.memset(onesw[:, :], 1.0), writes=[rc])
    for i in range(4):
        P.op(P.pool, lambda i=i: nc.gpsimd.affine_select(masks[:, i, :], onesw[:, :], [[1, 512]], ALU.is_gt, 0.0,
                                                         base=-128 * i, channel_multiplier=-1), reads=[rc], writes=[rc])
    P.op(P.pool, lambda: nc.gpsimd.affine_select(tri[:, :], onesw[:, 0:128], [[-1, 128]], ALU.is_gt, 0.0,
                                                 base=1, channel_multiplier=1), reads=[rc], writes=[rc])
    P.op(P.pool, lambda: nc.gpsimd.affine_select(comp[:, :], onesw[:, 0:128], [[1, 128]], ALU.is_gt, 0.0,
                                                 base=0, channel_multiplier=-1), reads=[rc], writes=[rc])
    Ksb = [(self.sb("Ksb%d" % i, [128, T], BF16), Res("Ksb%d" % i), self.dsem("Ksb%d" % i)) for i in range(2)]
    Vsb = [(self.sb("Vsb%d" % i, [128, NKB, 128], BF16), Res("Vsb%d" % i), self.dsem("Vsb%d" % i)) for i in range(2)]
    NL = 2
    Eb = [Buf(self.sb("Eb%d" % l, [128, 3, 512], F32), 3, "Eb%d" % l) for l in range(NL)]
    Lb = [Buf(self.sb("Lb%d" % l, [128, 3, 512], BF16), 3, "Lb%d" % l) for l in range(NL)]
    Gb = [Buf(self.sb("Gb%d" % l, [128, 2, 512], F32), 2, "Gb%d" % l) for l in range(NL)]
    Ab = [Buf(self.sb("Ab%d" % l, [128, 2, 512], BF16), 2, "Ab%d" % l) for l in range(NL)]
    Ob = [Buf(self.sb("Ob%d" % l, [128, 2, 512], BF16), 2, "Ob%d" % l) for l in range(NL)]
    Qsb = [[(self.sb("Qsb%d_%d" % (l, i), [128, 512], BF16), Res("Qsb%d_%d" % (l, i)), self.dsem("Qsb%d_%d" % (l, i)))
            for i in range(2)] for l in range(NL)]
    osem = [[self.dsem("oo%d_%d" % (l, i)) for i in range(2)] for l in range(NL)]
    zt = self.sb("zpad", [128, nh, PRE], BF16)
    r_z = Res("zpad")
    P.op(P.pool, lambda: nc.gpsimd.memset(zt[:, :, :], 0.0), writes=[r_z])
    P.dma(P.sp, io["oT"][:, 0, :, 0:PRE].rearrange("h p n -> p h n"), zt[:, :, :], self.dsem("zpad"), reads=[r_z])
    Sbk = [self.banks[0:2], self.banks[2:4]]
    Rbk = self.banks[4:6]
    Obk = self.banks[6:8]
    loaded = {}

    def load_head(h):
        if h in loaded or h >= nh:
            return
        Kt, Kr, Ks = Ksb[h % 2]
        Vt, Vr, Vs = Vsb[h % 2]
        P.dma(P.sp, Kt[:, :], io["KT"][h, :, :], Ks, writes=[Kr])
        P.dma(P.sp, Vt[:, :, :], io["V"][:, h * 128:(h + 1) * 128].rearrange("(kb p) d -> p kb d", p=128), Vs, writes=[Vr])
        loaded[h] = True

    cnt = [0] * NL

    class Chain:
        pass

    def start_chain(l, h, j):
        load_head(h)
        if j == nq // 2:
            load_head(h + 1)
        c = Chain()
        c.l, c.h, c.j = l, h, j
        c.Kt, c.Kr, _ = Ksb[h % 2]
        c.Vt, c.Vr, _ = Vsb[h % 2]
        c.ci = cnt[l]
        cnt[l] += 1
        c.Qt, c.Qr, Qs = Qsb[l][c.ci % 2]
        P.dma(P.sp, c.Qt[:, :], io["QT"][h, :, j * 512:(j + 1) * 512], Qs, writes=[c.Qr])
        c.steps = list(range(4 * j + 3, -1, -1))
        c.N = len(c.steps)
        c.k = 0
        mm1(c, 0)
        if c.N > 1:
            mm1(c, 1)
        e_(c, 0)
        l_(c, 0)
        mm2(c, 0)
        return c

    def mm1(c, k):
        kb = c.steps[k]
        sbk = Sbk[c.l][k % 2]
        self.mm(sbk, [(c.Kt[:, kb * 128:(kb + 1) * 128], c.Qt[:, :])], 512, reads=[c.Kr, c.Qr])

    def e_(c, k):
        kb = c.steps[k]
        sbk = Sbk[c.l][k % 2]
        E = Eb[c.l]
        e = E.t[:, k % 3, :]
        self.A(e, sbk[0][:, :], AF.Exp, reads=[sbk[1]], writes=[E.r[k % 3]], scale=scale)
        i = kb - 4 * c.j
        if i >= 0:
            self.TT(e, e, masks[:, i, :], ALU.mult, reads=[E.r[k % 3], rc], writes=[E.r[k % 3]], eng=P.pool)

    def l_(c, k):
        E, L = Eb[c.l], Lb[c.l]
        self.A(L.t[:, k % 3, :], E.t[:, k % 3, :], AF.Ln, reads=[E.r[k % 3]], writes=[L.r[k % 3]], bias=1.0, scale=1.0)

    def mm2(c, k):
        L, Rb = Lb[c.l], Rbk[c.l]
        P.group(P.pe, [lambda: nc.tensor.matmul(Rb[0][:, :], tri[:, :], L.t[:, k % 3, :], start=(k == 0), stop=False)],
                reads=[L.r[k % 3], rc], writes=[Rb[1]])

    def g_(c, k):
        G, Rb = Gb[c.l], Rbk[c.l]
        self.A(G.t[:, k % 2, :], Rb[0][:, :], AF.Exp, reads=[Rb[1]], writes=[G.r[k % 2]], scale=-1.0)

    def a_(c, k):
        E, G, A_ = Eb[c.l], Gb[c.l], Ab[c.l]
        self.TT(A_.t[:, k % 2, :], E.t[:, k % 3, :], G.t[:, k % 2, :], ALU.mult,
                reads=[E.r[k % 3], G.r[k % 2]], writes=[A_.r[k % 2]])

    def mm3(c, k):
        L, Rb = Lb[c.l], Rbk[c.l]
        P.group(P.pe, [lambda: nc.tensor.matmul(Rb[0][:, :], comp[:, :], L.t[:, k % 3, :], start=False, stop=(k == c.N - 1))],
                reads=[L.r[k % 3], rc], writes=[Rb[1]])

    def mm4(c, k):
        A_, OB = Ab[c.l], Obk[c.l]
        kb = c.steps[k]
        P.group(P.pe, [lambda: nc.tensor.matmul(OB[0][:, :], c.Vt[:, kb, :], A_.t[:, k % 2, :], start=(k == 0), stop=(k == c.N - 1))],
                reads=[c.Vr, A_.r[k % 2]], writes=[OB[1]])

    def finish(c):
        l, h, j, ci = c.l, c.h, c.j, c.ci
        ob, OB = Ob[l], Obk[l]
        self.CP(ob.t[:, ci % 2, :], OB[0][:, :], reads=[OB[1]], writes=[ob.r[ci % 2]])
        sem = osem[l][ci % 2]
        pc0 = PRE + j * 512
        if pc0 + 512 <= NP2:
            P.dma(P.sp, io["oT"][h, 0, :, pc0:pc0 + 512], ob.t[:, ci % 2, :], sem, reads=[ob.r[ci % 2]])
            if pc0 + 512 > HALF:
                P.dma(P.sp, io["oT"][h, 1, :, 0:pc0 + 512 - HALF], ob.t[:, ci % 2, HALF - pc0:512], sem, reads=[ob.r[ci % 2]])
        else:
            P.dma(P.sp, io["oT"][h, 1, :, pc0 - HALF:pc0 - HALF + 512], ob.t[:, ci % 2, :], sem, reads=[ob.r[ci % 2]])

    work = [(h, j) for h in range(nh) for j in range(nq)]
    lanes = [None] * NL
    wi = 0
    while True:
        for l in range(NL):
            if lanes[l] is None and wi < len(work):
                lanes[l] = start_chain(l, *work[wi])
                wi += 1
        act = [c for c in lanes if c is not None]
        if not act:
            break
        for c in act:
            if c.k + 2 < c.N:
                mm1(c, c.k + 2)
        for c in act:
            if c.k + 1 < c.N:
                e_(c, c.k + 1)
        for c in act:
            if c.k + 1 < c.N:
                l_(c, c.k + 1)
        for c in act:
            g_(c, c.k)
        for c in act:
            a_(c, c.k)
        for c in act:
            mm3(c, c.k)
            if c.k + 1 < c.N:
                mm2(c, c.k + 1)
            if c.k >= 1:
                mm4(c, c.k - 1)
            if c.k == c.N - 1:
                mm4(c, c.k)
                finish(c)
                lanes[c.l] = None
            c.k += 1
    self.end_phase()


def phase_attn2(self, io, nq, nh):
    nc, P = self.nc, self.P
    self.begin_phase()
    T = nq * 512
    NKB = nq * 4
    scale = 1.0 / float(np.sqrt(128.0))
    rc = self.r_const
    masks = self.sb("masks", [128, 4, 2, 512], F32)
    tri = self.sb("tri", [128, 128], BF16)
    comp = self.sb("comp", [128, 128], BF16)
    onesw = self.sb("onesw", [128, 512], F32)
    P.op(P.pool, lambda: nc.gpsimd.memset(onesw[:, :], 1.0), writes=[rc])
    for i in range(4):
        for l in range(2):
            P.op(P.pool, lambda i=i, l=l: nc.gpsimd.affine_select(masks[:, i, l, :], onesw[:, :], [[1, 512]], ALU.is_gt, 0.0,
                                                                  base=-128 * i, channel_multiplier=-1), reads=[rc], writes=[rc])
    P.op(P.pool, lambda: nc.gpsimd.affine_select(tri[:, :], onesw[:, 0:128], [[-1, 128]], ALU.is_gt, 0.0,
                                                 base=1, channel_multiplier=1), reads=[rc], writes=[rc])
    P.op(P.pool, lambda: nc.gpsimd.affine_select(comp[:, :], onesw[:, 0:128], [[1, 128]], ALU.is_gt, 0.0,
                                                 base=0, channel_multiplier=-1), reads=[rc], writes=[rc])
    NB = 4
    Ksb = [(self.sb("Ksb%d" % i, [128, T], BF16), Res("Ksb%d" % i), self.dsem("Ksb%d" % i)) for i in range(NB)]
    Vsb = [(self.sb("Vsb%d" % i, [128, NKB, 128], BF16), Res("Vsb%d" % i), self.dsem("Vsb%d" % i)) for i in range(NB)]
    Eb = Buf(self.sb("Eb", [128, 3, 2, 512], F32), 3, "Eb")
    Lb = Buf(self.sb("Lb", [128, 3, 2, 512], BF16), 3, "Lb")
    Gb = Buf(self.sb("Gb", [128, 2, 2, 512], F32), 2, "Gb")
    Ab = Buf(self.sb("Ab", [128, 2, 2, 512], BF16), 2, "Ab")
    Ob = Buf(self.sb("Ob", [128, 2, 2, 512], BF16), 2, "Ob")
    NQB = 3
    Qsb = [(self.sb("Qsb%d" % i, [128, 2, 512], BF16), Res("Qsb%d" % i), self.dsem("Qsb%d" % i)) for i in range(NQB)]
    osem = [self.dsem("oo%d" % i) for i in range(2)]
    zt = self.sb("zpad", [128, nh, PRE], BF16)
    r_z = Res("zpad")
    P.op(P.pool, lambda: nc.gpsimd.memset(zt[:, :, :], 0.0), writes=[r_z])
    P.dma(P.sp, io["oT"][:, 0, :, 0:PRE].rearrange("h p n -> p h n"), zt[:, :, :], self.dsem("zpad"), reads=[r_z])
    Sp = [self.bankpairs[0], self.bankpairs[1]]
    Sr = [[self.banks[0][1], self.banks[1][1]], [self.banks[2][1], self.banks[3][1]]]
    Rp = self.bankpairs[2]
    Rr = [self.banks[4][1], self.banks[5][1]]
    Op = self.bankpairs[3]
    Or = [self.banks[6][1], self.banks[7][1]]
    loaded = {}

    def load_head(h):
        if h in loaded or h >= nh:
            return
        Kt, Kr, Ks = Ksb[h % NB]
        Vt, Vr, Vs = Vsb[h % NB]
        P.dma(P.sp, Kt[:, :], io["KT"][h, :, :], Ks, writes=[Kr])
        P.dma(P.sp, Vt[:, :, :], io["V"][:, h * 128:(h + 1) * 128].rearrange("(kb p) d -> p kb d", p=128), Vs, writes=[Vr])
        loaded[h] = True

    steps = []
    for hp in range(nh // 2):
        for j in range(nq):
            N = 4 * j + 4
            for k in range(N):
                steps.append((hp, j, k, N))
    NS = len(steps)
    qbuf = {}
    qcount = [0]

    def get_q(hp, j):
        key = (hp, j)
        if key not in qbuf:
            load_head(2 * hp)
            load_head(2 * hp + 1)
            if j == nq // 2:
                load_head(2 * hp + 2)
                load_head(2 * hp + 3)
            Qt, Qr, Qs = Qsb[qcount[0] % NQB]
            qcount[0] += 1
            for l in range(2):
                P.dma(P.sp, Qt[:, l, :], io["QT"][2 * hp + l, :, j * 512:(j + 1) * 512], Qs, writes=[Qr])
            qbuf[key] = (Qt, Qr)
        return qbuf[key]

    def mm1(s):
        hp, j, k, N = steps[s]
        kb = 4 * j + 3 - k
        Qt, Qr = get_q(hp, j)
        for l in range(2):
            h = 2 * hp + l
            Kt, Kr, _ = Ksb[h % NB]
            t = Sp[s % 2]
            P.group(P.pe, [lambda: nc.tensor.matmul(t[:, l, :], Kt[:, kb * 128:(kb + 1) * 128], Qt[:, l, :], start=True, stop=True)],
                    reads=[Kr, Qr], writes=[Sr[s % 2][l]])

    def el(s):
        hp, j, k, N = steps[s]
        kb = 4 * j + 3 - k
        self.A(Eb.t[:, s % 3, :, :], Sp[s % 2][:, :, :], AF.Exp, reads=Sr[s % 2], writes=[Eb.r[s % 3]], scale=scale)
        i = kb - 4 * j
        if i >= 0:
            self.TT(Eb.t[:, s % 3, :, :], Eb.t[:, s % 3, :, :], masks[:, i, :, :], ALU.mult, reads=[Eb.r[s % 3], rc],
                    writes=[Eb.r[s % 3]])
        self.A(Lb.t[:, s % 3, :, :], Eb.t[:, s % 3, :, :], AF.Ln, reads=[Eb.r[s % 3]], writes=[Lb.r[s % 3]], bias=1.0, scale=1.0)

    def mm2(s):
        hp, j, k, N = steps[s]
        for l in range(2):
            P.group(P.pe, [lambda: nc.tensor.matmul(Rp[:, l, :], tri[:, :], Lb.t[:, s % 3, l, :], start=(k == 0), stop=False)],
                    reads=[Lb.r[s % 3], rc], writes=[Rr[l]])

    def mm3(s):
        hp, j, k, N = steps[s]
        for l in range(2):
            P.group(P.pe, [lambda: nc.tensor.matmul(Rp[:, l, :], comp[:, :], Lb.t[:, s % 3, l, :], start=False, stop=(k == N - 1))],
                    reads=[Lb.r[s % 3], rc], writes=[Rr[l]])

    def mm4(s):
        hp, j, k, N = steps[s]
        kb = 4 * j + 3 - k
        for l in range(2):
            h = 2 * hp + l
            Vt, Vr, _ = Vsb[h % NB]
            P.group(P.pe, [lambda: nc.tensor.matmul(Op[:, l, :], Vt[:, kb, :], Ab.t[:, s % 2, l, :], start=(k == 0), stop=(k == N - 1))],
                    reads=[Vr, Ab.r[s % 2]], writes=[Or[l]])
        if k == N - 1:
            ci = hp * nq + j
            self.CP(Ob.t[:, ci % 2, :, :], Op[:, :, :], reads=Or, writes=[Ob.r[ci % 2]])
            sem = osem[ci % 2]
            pc0 = PRE + j * 512
            for l in range(2):
                h = 2 * hp + l
                ob = Ob.t[:, ci % 2, l, :]
                if pc0 + 512 <= NP2:
                    P.dma(P.sp, io["oT"][h, 0, :, pc0:pc0 + 512], ob, sem, reads=[Ob.r[ci % 2]])
                    if pc0 + 512 > HALF:
                        P.dma(P.sp, io["oT"][h, 1, :, 0:pc0 + 512 - HALF], Ob.t[:, ci % 2, l, HALF - pc0:512], sem, reads=[Ob.r[ci % 2]])
                else:
                    P.dma(P.sp, io["oT"][h, 1, :, pc0 - HALF:pc0 - HALF + 512], ob, sem, reads=[Ob.r[ci % 2]])

    self.bg_pump()
    mm1(0)
    if NS > 1:
        mm1(1)
    el(0)
    mm2(0)
    for s in range(NS):
        if s + 2 < NS:
            mm1(s + 2)
        if s + 1 < NS:
            el(s + 1)
        self.A(Gb.t[:, s % 2, :, :], Rp[:, :, :], AF.Exp, reads=Rr, writes=[Gb.r[s % 2]], scale=-1.0)
        self.TT(Ab.t[:, s % 2, :, :], Eb.t[:, s % 3, :, :], Gb.t[:, s % 2, :, :], ALU.mult,
                reads=[Eb.r[s % 3], Gb.r[s % 2]], writes=[Ab.r[s % 2]])
        mm3(s)
        if s + 1 < NS:
            mm2(s + 1)
        if s >= 1:
            mm4(s - 1)
    mm4(NS - 1)
    self.end_phase()


Builder.phase_attn2 = phase_attn2


def phase_b2(self, io, tiles, dyn_o):
    nc, P = self.nc, self.P
    self.begin_phase()
    self.common_init()
    self.init_wstream(3)
    rv = self.r_vec
    mv = self.load_mods(io)
    sv = self.load_rows_T("svB", io["svecs2"], 2, D)
    fdw = self.load_rows_T("fdwB", io["ffn_dw"], 4, 2 * FF)
    pm = self.sb("pm_sb", [128, 1], F32)
    P.dma(P.sp, pm[:, :], io["pm"][:, :], self.dsem("pm"), writes=[rv])
    a3, sh3, g3 = self.mod_vecs(mv, 3, sv[:, :, 0], "ffn1")
    g_mix1 = mv[:, :, 8]
    afin = self.sb("afin", [128, KC], F32)
    self.TS(afin[:, :], sv[:, :, 1], float(np.sqrt(D)), None, ALU.mult, None, reads=[rv], writes=[rv])
    xT = Buf(self.sb("xT", [128, KC, 512], F32), KC, "xT")
    hT = Buf(self.sb("hT", [128, KC, 512], BF16), KC, "hT")
    gTt = self.sb("gTt", [128, FC, 512], BF16)
    gT = Buf(gTt, FC, "gT")
    yv = gTt[:, 0:32, :].bitcast(F32).rearrange("p a b -> p (a b)").rearrange("p (c n) -> p c n", c=KC)
    Tg = Buf(self.sb("Tg", [128, 2, 512], F32), 2, "Tg")
    Tv = Buf(self.sb("Tv", [128, 2, 512], F32), 2, "Tv")
    Hff = self.sb("Hff", [128, 2 * FC, 2], F32)
    rH = [Res("Hff%d" % i) for i in range(2 * FC)]
    P.op(P.pool, lambda: nc.gpsimd.memset(Hff[:, :, :], 0.0), writes=rH)
    for _ in tiles:
        for i in range(4):
            self.wplan_add(io["o_w"], D, [(512 * i, 512)])
        for i in range(22):
            self.wplan_add(io["up_w"], D, [(256 * i, 256), (FF + 256 * i, 256)])
        for i in range(16):
            self.wplan_add(io["down_w"], FF, [(128 * i, 128)])
    sx, so = self.dsem("ldx"), self.dsem("ldo")
    if dyn_o:
        par = self.par
    for ti, (start, n) in enumerate(tiles):
        P.dma(P.sp, xT.t[:, :, :n], io["x1T"][:, :, start:start + n].rearrange("c p n -> p c n"), sx, writes=xT.r)
        for rk in range(2):
            if n == 512:
                src = io["oT"][:, bass.ds(par, 1), rk, :, start:start + n].rearrange("h o p n -> p (h o) n")
                P.dma(P.sp, hT.t[:, 8 * rk:8 * rk + 8, :n], src, so, writes=hT.r[8 * rk:8 * rk + 8])
            else:
                parg = nc.gpsimd.partition_id() % 2
                src = io["oT"][:, bass.ds(parg, 1), rk, :, start:start + n].rearrange("h o p n -> p (h o) n")
                P.dma(P.pool, hT.t[:, 8 * rk:8 * rk + 8, :n], src, so, writes=hT.r[8 * rk:8 * rk + 8])

        def ep_o(bk, cid, n=n):
            self.STT(xT.t[:, cid, :n], bk[0][:, :n], g_mix1[:, cid:cid + 1], xT.t[:, cid, :n], ALU.mult, ALU.add,
                     reads=[bk[1], rv, xT.r[cid]], writes=[xT.r[cid]])
        self.linear(hT, KC, n, [[4 * i + q for q in range(4)] for i in range(4)], ep_o)
        self.rms_mod(xT, hT, n, a3, sh3)
        if ti == 0:
            self.TS(hT.t[:, :, 0:PRE], hT.t[:, :, 0:PRE], pm[:, 0:1], None, ALU.mult, None, reads=hT.r + [rv], writes=hT.r)
        self.ffn(hT, xT, gT, Tg, Tv, Hff, rH, fdw, g3, n)
        self.rms_mod_multi(xT, yv, [[gT.r[2 * c], gT.r[2 * c + 1]] for c in range(KC)], n, afin)
        nb = (n + 127) // 128
        for tb in range(nb):
            nt = min(128, n - tb * 128)
            orow, orr, osm = self.xrow[self.xrow_i % 2]
            self.xrow_i += 1
            for gq in range(4):
                bk = self.bank()
                fns = []
                for q in range(4):
                    c = 4 * gq + q
                    fns.append(lambda q=q, c=c, bk=bk: nc.tensor.transpose(bk[0][:nt, q * 128:(q + 1) * 128],
                                                                           yv[:, c, tb * 128:tb * 128 + nt], self.ident[:, :]))
                rr = []
                for q in range(4):
                    rr += [gT.r[2 * (4 * gq + q)], gT.r[2 * (4 * gq + q) + 1]]
                P.group(P.pe, fns, reads=rr + [self.r_const], writes=[bk[1]])
                if gq % 2 == 0:
                    P.op(P.act, lambda gq=gq, bk=bk: nc.scalar.copy(orow[:nt, gq * 512:(gq + 1) * 512], bk[0][:nt, :]),
                         reads=[bk[1]], writes=[orr])
                else:
                    self.CP(orow[:nt, gq * 512:(gq + 1) * 512], bk[0][:nt, :], reads=[bk[1]], writes=[orr])
            P.dma(P.sp, io["y"][start + tb * 128:start + tb * 128 + nt, :], orow[:nt, :], osm, reads=[orr])
    self.end_phase()


def rms_mod_multi(self, xT, yv, yres, n, a_vec):
    nc, P = self.nc, self.P
    sq = self.tmpbf
    bk = self.bank()
    for c in range(KC):
        self.A(sq.t[:, c % 2, :n], xT.t[:, c, :n], AF.Square, reads=[xT.r[c]], writes=[sq.r[c % 2]])
        P.group(P.pe, [lambda c=c: nc.tensor.matmul(bk[0][:, :n], self.ones_bf[:], sq.t[:, c % 2, :n],
                                                    start=(c == 0), stop=(c == KC - 1))],
                reads=[sq.r[c % 2], self.r_const], writes=[bk[1]])
    rstd = self.rstd
    self.A(rstd.t[:, 0, :n], bk[0][:, :n], AF.Ln, reads=[bk[1]], writes=[rstd.r[0]], bias=self.eps_rms[:, 0:1], scale=1.0)
    self.A(rstd.t[:, 0, :n], rstd.t[:, 0, :n], AF.Exp, reads=[rstd.r[0]], writes=[rstd.r[0]], scale=-0.5)
    for c in range(KC):
        tm = self.tmpf
        i = c % 2
        self.TT(tm.t[:, i, :n], xT.t[:, c, :n], rstd.t[:, 0, :n], ALU.mult, reads=[xT.r[c], rstd.r[0]], writes=[tm.r[i]])
        self.A(yv[:, c, :n], tm.t[:, i, :n], AF.Identity, reads=[tm.r[i], self.r_vec], writes=yres[c],
               bias=0.0, scale=a_vec[:, c:c + 1])


Builder.phase_mods = phase_mods
Builder.phase_w = phase_w
Builder.phase_qkv = phase_qkv
Builder.phase_attn = phase_attn
Builder.phase_b2 = phase_b2
Builder.rms_mod_multi = rms_mod_multi


def tiles_for(ntok):
    t = []
    st = 0
    while st < ntok:
        n = min(512, ntok - st)
        t.append((st, n))
        st += n
    return t


WSPEC = [
    ("mod0", D, 3 * D), ("mod1", D, 3 * D), ("mod2", D, 3 * D), ("mod3", D, 3 * D),
    ("pw1", D, 2 * D), ("pw2", D, D), ("up0", D, 2 * FF), ("down0", FF, D),
    ("qkv", D, 3 * D), ("ow", D, D), ("up1", D, 2 * FF), ("down1", FF, D),
]


def build_fused():
    nc = bass.Bass("TRN2", target_bir_lowering=False)
    ext = lambda name, shape, d=F32, kind="ExternalInput": nc.dram_tensor(name, shape, d, kind=kind).ap()
    itn = lambda name, shape, d: nc.dram_tensor(name, shape, d).ap()
    B = Builder(nc)
    W = {}
    n_first = 0
    for wi, (name, K_, N_) in enumerate(WSPEC):
        Ks = K_ // 4
        if name.startswith("mod"):
            W[name] = ext("w_" + name, [Ks, N_])
            continue
        wb = 1024 if K_ == D else 256
        nb = N_ // wb
        w = {"shard": ext("w_" + name, [Ks, N_]), "Ks": Ks, "wb": wb, "nb": nb, "res": Res("wf_" + name),
             "wsh": itn("wsh_" + name, [nb, Ks, wb], BF16), "full": itn("wf_" + name, [nb, K_, wb], BF16)}
        W[name] = w
        B.bg_add_weight(wi, w)
        if name == "pw2":
            n_first = len(B.bg)
        if name == "down0":
            B.n_bg_mid = len(B.bg) - n_first
        if name == "qkv":
            n_qkv_end = len(B.bg)
    B.bg_pump(n_first)
    B.n_bg_rest = n_qkv_end - n_first - B.n_bg_mid
    part = itn("modpart", [4, 2, 3 * D], F32)
    msum = itn("modsum", [4, 2, 3 * D], F32)
    mb = ext("mb", [4, 3 * D])
    B.phase_mods({"cq": ext("cq", [2, 512]), "mw": [W["mod%d" % m] for m in range(4)], "part": part, "msum": msum})
    x1T = itn("x1T", [KC, 128, NLOC], F32)
    h1loc = itn("h1loc", [KC, 128, NLOC], BF16)
    pm = ext("pm", [128, 1])
    B.phase_a({"xin": ext("xin", [NLOC, D]), "msum": msum, "mb": mb, "svecs": ext("svecs", [7, D]), "pw1b": ext("pw1b", [2, D]),
               "dww": ext("dww", [CW, D]), "ffn_dw": ext("ffn_dw0", [4, 2 * FF]), "pm": pm,
               "diagw": itn("diagw", [KC, 128, CW * 128], BF16),
               "pw1_w": W["pw1"], "pw2_w": W["pw2"], "up_w": W["up0"], "down_w": W["down0"],
               "x1T": x1T, "h1T": h1loc}, tiles_for(NLOC))
    B.bg_pump(max(0, n_qkv_end - B.bg_i))
    h1all = itn("h1all", [KC, 2, 128, NLOC], BF16)
    sem = B.dsem("ag_h1")
    for cc in range(KC):
        B.P.collective("AllGather", [h1loc[cc]], [h1all[cc].rearrange("r p n -> (r p) n")], PAIRS, sem)
    B.barrier(full=True)
    nh = NH // 2
    QT = itn("QT", [nh, 128, SEQ], BF16)
    KT = itn("KT", [nh, 128, SEQ], BF16)
    V = itn("V", [SEQ, nh * 128], BF16)
    oTloc = itn("oTloc", [nh, 2, 128, NP2], BF16)
    io = {"h1all": h1all, "qkv": W["qkv"], "QT": QT, "KT": KT, "V": V, "oT": oTloc}
    B.phase_qkv(io, HALF // 512, nh)
    B.phase_attn2(io, SEQ // 512, nh)
    oall = itn("oall", [nh, 2, 2, 128, NP2], BF16)
    sem = B.dsem("ag_o")
    for h in range(nh):
        for pt in range(2):
            B.P.collective("AllGather", [oTloc[h, pt]], [oall[h, pt].rearrange("r p n -> (r p) n")], PAIRS, sem)
    B.barrier(full=True)
    B.phase_b2({"x1T": x1T, "oT": oall, "msum": msum, "mb": mb, "svecs2": ext("svecs2", [2, D]), "ffn_dw": ext("ffn_dw1", [4, 2 * FF]),
                "pm": pm, "o_w": W["ow"], "up_w": W["up1"], "down_w": W["down1"],
                "y": ext("y", [NLOC, D], F32, "ExternalOutput")}, tiles_for(NLOC), dyn_o=True)
    B.barrier(full=True)
    return nc


def _f32(a):
    return np.ascontiguousarray(np.asarray(a, dtype=np.float32))


def kernel(x, c, mix_norm_g, mix_mod_w, mix_mod_b, cv_pw1_w, cv_pw1_b, cv_dw_w, cv_dw_b, cv_ln_g, cv_ln_b,
           cv_pw2_w, cv_pw2_b, sb_qkv_w, sb_o_w, ffn_norm_g, ffn_mod_w, ffn_mod_b, ffn_up_w, ffn_dw_w,
           ffn_dw_b, ffn_down_w, final_norm_g):
    x = np.asarray(x)
    c = np.asarray(c)
    cores = list(range(8))
    full = {"mod0": mix_mod_w[0], "mod1": ffn_mod_w[0], "mod2": mix_mod_w[1], "mod3": ffn_mod_w[1],
            "pw1": cv_pw1_w[0], "pw2": cv_pw2_w[0], "up0": ffn_up_w[0], "down0": ffn_down_w[0],
            "qkv": sb_qkv_w[0], "ow": sb_o_w[0], "up1": ffn_up_w[1], "down1": ffn_down_w[1]}
    shared = {
        "mb": _f32(np.stack([mix_mod_b[0], ffn_mod_b[0], mix_mod_b[1], ffn_mod_b[1]])),
        "svecs": _f32(np.stack([mix_norm_g[0], ffn_norm_g[0], mix_norm_g[1], cv_dw_b[0], cv_ln_g[0], cv_ln_b[0], cv_pw2_b[0]])),
        "pw1b": _f32(np.asarray(cv_pw1_b[0]).reshape(2, D)),
        "dww": _f32(cv_dw_w[0]),
        "ffn_dw0": _f32(np.concatenate([np.asarray(ffn_dw_w[0]), np.asarray(ffn_dw_b[0])[None]], 0)),
        "ffn_dw1": _f32(np.concatenate([np.asarray(ffn_dw_w[1]), np.asarray(ffn_dw_b[1])[None]], 0)),
        "svecs2": _f32(np.stack([ffn_norm_g[1], final_norm_g])),
    }
    ims = []
    for i in cores:
        b, r = i // 2, i % 2
        im = dict(shared)
        for name, K_, N_ in WSPEC:
            ks = K_ // 4
            q = i % 4
            im["w_" + name] = _f32(np.asarray(full[name])[q * ks:(q + 1) * ks])
        qd = i // 4
        im["cq"] = _f32(c[2 * qd:2 * qd + 2, 512 * q:512 * q + 512])
        im["pm"] = np.full((128, 1), float(r), np.float32)
        if r == 0:
            im["xin"] = _f32(np.concatenate([np.zeros((PRE, D), np.float32), x[b, 0:HALF]], 0))
        else:
            im["xin"] = _f32(x[b, HALF - PRE:SEQ])
        ims.append(im)
    res = run_bass_kernel_spmd(build_fused(), ims, core_ids=cores)
    out = np.empty((4, SEQ, D), np.float32)
    for i in cores:
        b, r = i // 2, i % 2
        out[b, HALF * r:HALF * (r + 1)] = np.asarray(res.results[i]["y"])[PRE:]
    return out


def build_attn_test(nq, nh):
    T = nq * 512
    nc = bass.Bass("TRN2", target_bir_lowering=False)
    dt = lambda name, shape, d=F32, kind="ExternalInput": nc.dram_tensor(name, shape, d, kind=kind).ap()
    io = {"QT": dt("QT", [nh, 128, T], BF16), "KT": dt("KT", [nh, 128, T], BF16), "V": dt("V", [T, nh * 128], BF16),
          "oT": dt("oT", [nh, 2, 128, NP2], BF16, "ExternalOutput")}
    B = Builder(nc)
    B.phase_attn2(io, nq, nh)
    B.barrier(full=True)
    return nc
```

```python
import numpy as np
import ml_dtypes
import concourse.bass as bass
import concourse.mybir as mybir
from concourse.bass_utils import run_bass_kernel_spmd
from contextlib import ExitStack

F32 = mybir.dt.float32
BF16 = mybir.dt.bfloat16
AF = mybir.ActivationFunctionType
ALU = mybir.AluOpType

D = 2048
KC = 16
FF = 5632
FC = 44
CW = 31
NH = 16
SEQ = 8192
HALF = 4096
PRE = 64
NLOC = HALF + PRE
RMS_EPS = 1e-6
LN_EPS = 1e-5
PAIRS = [[0, 1], [2, 3], [4, 5], [6, 7]]
QUADS = [[0, 1, 2, 3], [4, 5, 6, 7]]
NP2 = NLOC


class Res:
    __slots__ = ("name", "w", "r")

    def __init__(self, name=""):
        self.name = name
        self.w = None
        self.r = {}


class Eng:
    def __init__(self, P, name, eng, is_pe=False):
        self.name = name
        self.eng = eng
        self.is_pe = is_pe
        self.sem = P.new_sem("e_" + name)
        self.count = 0
        self.waited = {}


class Prog:
    def __init__(self, nc):
        self.nc = nc
        self.sems = {}
        self.nsem = 0
        self.pe = Eng(self, "pe", nc.tensor, is_pe=True)
        self.act = Eng(self, "act", nc.scalar)
        self.dve = Eng(self, "dve", nc.vector)
        self.pool = Eng(self, "pool", nc.gpsimd)
        self.sp = Eng(self, "sp", nc.sync)
        self.dma_vals = {}

    def new_sem(self, name):
        h = self.nc.semaphore(name).__enter__()
        key = "s%d_%s" % (self.nsem, name)
        self.nsem += 1
        self.sems[key] = h
        return key

    def _wait(self, E, needs):
        best = {}
        for (k, v) in needs:
            if best.get(k, 0) < v:
                best[k] = v
        for k, v in best.items():
            if k == E.sem:
                if E.is_pe:
                    continue
                if E.count - v >= 2:
                    continue
            if E.waited.get(k, 0) >= v:
                continue
            E.eng.wait_ge(self.sems[k], v)
            E.waited[k] = v

    @staticmethod
    def _deps(reads, writes):
        needs = []
        for r in reads:
            if r.w is not None:
                needs.append(r.w)
        for w in writes:
            if w.w is not None:
                needs.append(w.w)
            needs.extend(w.r.items())
        return needs

    @staticmethod
    def _mark(tok, reads, writes):
        k, v = tok
        for w in writes:
            w.w = tok
            w.r = {}
        for r in reads:
            if r.w is tok:
                continue
            if r.r.get(k, 0) < v:
                r.r[k] = v

    def op(self, E, fn, reads=(), writes=()):
        self._wait(E, self._deps(reads, writes))
        ins = fn()
        ins.then_inc(self.sems[E.sem], 1)
        E.count += 1
        self._mark((E.sem, E.count), reads, writes)
        return ins

    def group(self, E, fns, reads=(), writes=()):
        self._wait(E, self._deps(reads, writes))
        ins = None
        for fn in fns:
            ins = fn()
        ins.then_inc(self.sems[E.sem], 1)
        E.count += 1
        self._mark((E.sem, E.count), reads, writes)
        return ins

    def dma(self, Q, out, in_, sem, reads=(), writes=()):
        self._wait(Q, self._deps(reads, writes))
        ins = Q.eng.dma_start(out=out, in_=in_)
        ins.then_inc(self.sems[sem], 16)
        v = self.dma_vals.get(sem, 0) + 16
        self.dma_vals[sem] = v
        self._mark((sem, v), reads, writes)
        return ins

    def collective(self, kind, ins, outs, groups, sem, reads=(), writes=(), op=None):
        Q = self.pool
        self._wait(Q, self._deps(reads, writes))
        ins_ = self.nc.gpsimd.collective_compute(kind, op if op is not None else ALU.bypass, replica_groups=groups,
                                                 ins=ins, outs=outs)
        ins_.then_inc(self.sems[sem], 1)
        v = self.dma_vals.get(sem, 0) + 1
        self.dma_vals[sem] = v
        self._mark((sem, v), reads, writes)

    def wait_all(self, E, ress):
        needs = []
        for r in ress:
            if r.w is not None:
                needs.append(r.w)
            needs.extend(r.r.items())
        self._wait(E, needs)


class Buf:
    def __init__(self, t, n, name):
        self.t = t
        self.r = [Res("%s%d" % (name, i)) for i in range(n)]


class Builder:
    def __init__(self, nc):
        self.nc = nc
        self.P = Prog(nc)
        P = self.P
        self.stack = None
        self.uid = 0
        self.bg_sems = set()
        self.banks = []
        self.bankpairs = []
        for i in range(4):
            t2 = nc.alloc_psum_tensor("bankp%d" % i, [128, 2, 512], F32)
            self.bankpairs.append(t2)
            for j in range(2):
                self.banks.append((t2[:, j, :], Res("bank%d" % (2 * i + j))))
        self.bank_i = 0
        self.dsems = {}
        self.ident = self.sb("ident", [128, 128], F32)
        self.ones_bf = self.sb("ones_bf", [128, 128], BF16)
        self.r_const = Res("const")
        ones_f = self.sb("ones_f", [128, 128], F32)
        P.op(P.pool, lambda: nc.gpsimd.memset(ones_f[:], 1.0), writes=[self.r_const])
        P.op(P.pool, lambda: nc.gpsimd.affine_select(self.ident[:], ones_f[:], [[1, 128]], ALU.is_equal, 0.0,
                                                     base=0, channel_multiplier=-1),
             reads=[self.r_const], writes=[self.r_const])
        P.op(P.pool, lambda: nc.gpsimd.memset(self.ones_bf[:], 1.0), writes=[self.r_const])
        self.ones_f = ones_f
        self.wstage = [(self.sb("wstage%d" % i, [128, 2048], BF16), Res("wstage%d" % i), self.dsem("wstage%d" % i)) for i in range(2)]
        self.wstage_i = 0
        pid = nc.sync.partition_id()
        self.par = pid % 2
        self.bsel = (pid % 4) // 2
        self.bg = []
        self.bg_i = 0
        self.bg_wo_queue = None

    def bg_add_weight(self, wi, w):
        P = self.P
        sem = self.dsem("agw%d" % wi)
        self.bg_sems.add(sem)
        Ks, wb, nb = w["Ks"], w["wb"], w["nb"]
        N = wb * nb
        sres = [Res("wsh%d_%d" % (wi, k)) for k in range(2)]

        def unit(r0, c0, wd):
            t, r, sm = self.wstage[self.wstage_i % 2]
            k = self.wstage_i % 2
            self.wstage_i += 1
            P.dma(P.pool, t[:, :wd], w["shard"][r0:r0 + 128, c0:c0 + wd], sm, writes=[r])
            P.dma(self.bg_wo_queue or P.sp, w["wsh"][c0 // wb:(c0 + wd) // wb, r0:r0 + 128, :].rearrange("b p n -> p b n"),
                  t[:, :wd].rearrange("p (b n) -> p b n", n=wb), sm, reads=[r], writes=[sres[k]])
        for r0 in range(0, Ks, 128):
            for c0 in range(0, N, 2048):
                wd = min(2048, N - c0)
                self.bg.append(lambda r0=r0, c0=c0, wd=wd: unit(r0, c0, wd))

        def gather(bi):
            P.collective("AllGather", [w["wsh"][bi]], [w["full"][bi]], QUADS, sem, reads=sres, writes=[w["res"]])
        for bi in range(nb):
            self.bg.append(lambda bi=bi: gather(bi))
        w["bg_end"] = len(self.bg)

    def bg_pump(self, n=None):
        end = len(self.bg) if n is None else min(len(self.bg), self.bg_i + n)
        while self.bg_i < end:
            self.bg[self.bg_i]()
            self.bg_i += 1

    def sb(self, name, shape, dt):
        if self.stack is None:
            return self.nc.alloc_sbuf_tensor(name, shape, dt)
        self.uid += 1
        return self.stack.enter_context(self.nc.sbuf_tensor("%s_%d" % (name, self.uid), shape, dt))

    def begin_phase(self):
        self.stack = ExitStack()

    def barrier(self, full=False):
        P = self.P
        engs = [P.pe, P.act, P.dve, P.pool, P.sp]
        needs = [(e.sem, e.count) for e in engs if e.count > 0] + \
            [kv for kv in P.dma_vals.items() if full or kv[0] not in self.bg_sems]
        for e in engs:
            P._wait(e, [x for x in needs if x[0] != e.sem])

    def end_phase(self):
        self.barrier()
        self.stack.close()
        self.stack = None

    def dsem(self, name):
        if name not in self.dsems:
            self.dsems[name] = self.P.new_sem("d_" + name)
        return self.dsems[name]

    def bank(self):
        b = self.banks[self.bank_i]
        self.bank_i = (self.bank_i + 1) % 8
        return b

    def mm(self, bank, pairs, n, reads, m=128):
        nc = self.nc
        t, r = bank
        fns = []
        last = len(pairs) - 1
        for i, (l, rh) in enumerate(pairs):
            fns.append(lambda l=l, rh=rh, i=i: nc.tensor.matmul(t[:m, :n], l, rh, start=(i == 0), stop=(i == last)))
        self.P.group(self.P.pe, fns, reads=reads, writes=[r])

    def A(self, out, in_, func, reads, writes, bias=0.0, scale=1.0):
        nc = self.nc
        return self.P.op(self.P.act, lambda: nc.scalar.activation(out, in_, func, bias=bias, scale=scale),
                         reads=reads, writes=writes)

    def TT(self, out, a, b, op, reads, writes, eng=None):
        E = eng or self.P.dve
        return self.P.op(E, lambda: E.eng.tensor_tensor(out, a, b, op), reads=reads, writes=writes)

    def TS(self, out, a, s1, s2, op0, op1, reads, writes, eng=None):
        E = eng or self.P.dve
        if op1 is None:
            return self.P.op(E, lambda: E.eng.tensor_scalar(out, a, s1, None, op0), reads=reads, writes=writes)
        return self.P.op(E, lambda: E.eng.tensor_scalar(out, a, s1, s2, op0, op1), reads=reads, writes=writes)

    def STT(self, out, a, s, b, op0, op1, reads, writes):
        nc = self.nc
        return self.P.op(self.P.dve, lambda: nc.vector.scalar_tensor_tensor(out, a, s, b, op0, op1),
                         reads=reads, writes=writes)

    def CP(self, out, in_, reads, writes, eng=None):
        E = eng or self.P.dve
        return self.P.op(E, lambda: E.eng.tensor_copy(out, in_), reads=reads, writes=writes)

    def init_wstream(self, nslots=3):
        self.wslots = []
        for i in range(nslots):
            t = self.sb("wslot%d" % i, [128, 8192], BF16)
            self.wslots.append((t, Res("wslot%d" % i), self.dsem("wslot%d" % i)))
        self.wplan = []
        self.wnext_issue = 0
        self.wnext_use = 0

    def wplan_add(self, w, k_rows, pieces):
        kc = k_rows // 128
        tot = sum(p[-1] for p in pieces)
        assert kc * tot <= 8192
        self.wplan.append((w, kc, tot, pieces))

    def _wissue(self, upto):
        P = self.P
        while self.wnext_issue < min(upto, len(self.wplan)):
            i = self.wnext_issue
            w, kc, tot, pieces = self.wplan[i]
            t, r, s = self.wslots[i % len(self.wslots)]
            if pieces is None:
                P.dma(P.pool, t[:, 0:kc * tot], w[0], s, reads=[w[1]], writes=[r])
                self.wnext_issue += 1
                continue
            if self.bg_i < w.get("bg_end", 0):
                self.bg_pump(w["bg_end"] - self.bg_i)
            dst = t[:, 0:kc * tot].rearrange("p (kc n) -> p kc n", kc=kc)
            wb = w["wb"]
            o = 0
            for pc in pieces:
                if len(pc) == 2:
                    c0, ncols = pc
                    bi, off = c0 // wb, c0 % wb
                    assert off + ncols <= wb
                    src = w["full"][bi].rearrange("(kc p) n -> p kc n", p=128)[:, :, off:off + ncols]
                else:
                    bi, off, ncols = pc
                    src = w["full"][bass.ds(bi, 1)].rearrange("o (kc p) n -> p (o kc) n", p=128)[:, :, off:off + ncols]
                P.dma(P.pool, dst[:, :, o:o + ncols], src, s, reads=[w["res"]], writes=[r])
                o += ncols
            self.wnext_issue += 1

    def wget(self):
        i = self.wnext_use
        self._wissue(i + len(self.wslots))
        w, kc, tot, pieces = self.wplan[i]
        t, r, s = self.wslots[i % len(self.wslots)]
        self.wnext_use += 1
        return t[:, 0:kc * tot].rearrange("p (kc n) -> p kc n", kc=kc), r

    def rms_mod(self, xT, hT, n, a_vec, sh_vec):
        nc, P = self.nc, self.P
        sq = self.tmpbf
        bk = self.bank()
        for c in range(KC):
            self.A(sq.t[:, c % 2, :n], xT.t[:, c, :n], AF.Square, reads=[xT.r[c]], writes=[sq.r[c % 2]])
            nc_ = nc
            P.group(P.pe, [lambda c=c: nc_.tensor.matmul(bk[0][:, :n], self.ones_bf[:], sq.t[:, c % 2, :n],
                                                          start=(c == 0), stop=(c == KC - 1))],
                    reads=[sq.r[c % 2], self.r_const], writes=[bk[1]])
        rstd = self.rstd
        self.A(rstd.t[:, 0, :n], bk[0][:, :n], AF.Ln, reads=[bk[1]], writes=[rstd.r[0]], bias=self.eps_rms[:, 0:1], scale=1.0)
        self.A(rstd.t[:, 0, :n], rstd.t[:, 0, :n], AF.Exp, reads=[rstd.r[0]], writes=[rstd.r[0]], scale=-0.5)
        for c in range(KC):
            tm = self.tmpf
            i = c % 2
            self.TT(tm.t[:, i, :n], xT.t[:, c, :n], rstd.t[:, 0, :n], ALU.mult, reads=[xT.r[c], rstd.r[0]], writes=[tm.r[i]])
            self.A(hT.t[:, c, :n], tm.t[:, i, :n], AF.Identity, reads=[tm.r[i], self.r_vec], writes=[hT.r[c]],
                   bias=(sh_vec[:, c:c + 1] if sh_vec is not None else 0.0), scale=a_vec[:, c:c + 1])

    def linear(self, act, kc_n, n, slots, epilogue):
        for ids in slots:
            wv, wr = self.wget()
            for jj, cid in enumerate(ids):
                bk = self.bank()
                pairs = [(wv[:, k, jj * 128:(jj + 1) * 128], act.t[:, k, :n]) for k in range(kc_n)]
                self.mm(bk, pairs, n, reads=[wr] + [act.r[k] for k in range(kc_n)])
                epilogue(bk, cid)

    def load_rows_T(self, name, ap2d, rows, ncols):
        nc, P = self.nc, self.P
        nch = ncols // 128
        out = self.sb("m_" + name, [128, nch, rows], F32)
        r = self.r_vec
        stg, rs = self.vec_stage, self.vec_stage_r
        for c0 in range(0, ncols, 2048):
            w = min(2048, ncols - c0)
            if isinstance(ap2d, list):
                ro = 0
                for (apx, nr) in ap2d:
                    P.dma(P.sp, stg[ro:ro + nr, :w], apx[0:nr, c0:c0 + w], self.xrow[0][2], writes=[rs])
                    ro += nr
            else:
                P.dma(P.sp, stg[:rows, :w], ap2d[0:rows, c0:c0 + w], self.xrow[0][2], writes=[rs])
            for cc in range(w // 128):
                bk = self.bank()
                P.group(P.pe, [lambda cc=cc, bk=bk: nc.tensor.transpose(bk[0][:, :rows], stg[:rows, cc * 128:(cc + 1) * 128],
                                                                        self.ident[:rows, :rows])],
                        reads=[rs, self.r_const], writes=[bk[1]])
                self.CP(out[:, c0 // 128 + cc, :], bk[0][:, :rows], reads=[bk[1]], writes=[r])
        return out

    def common_init(self):
        nc, P = self.nc, self.P
        self.xrow = [(self.sb("xrow0", [128, D], F32), Res("xrow0"), self.dsem("xrow0"))] * 2
        self.xrow_i = 0
        self.vec_stage = self.xrow[0][0]
        self.vec_stage_r = self.xrow[0][1]
        self.r_vec = Res("vecs")
        self.sqb = Buf(self.sb("sqb", [128, 2, 512], BF16), 2, "sqb")
        self.tmpf = Buf(self.sb("tmpf", [128, 2, 512], F32), 2, "tmpf")
        self.tmpbf = Buf(self.sb("tmpbf", [128, 2, 512], BF16), 2, "tmpbf")
        self.rstd = Buf(self.sb("rstd", [128, 1, 512], F32), 1, "rstd")
        self.eps_rms = self.sb("eps_rms", [128, 2], F32)
        P.op(P.pool, lambda: nc.gpsimd.memset(self.eps_rms[:, 0:1], float(D * RMS_EPS)), writes=[self.r_const])
        P.op(P.pool, lambda: nc.gpsimd.memset(self.eps_rms[:, 1:2], float(LN_EPS)), writes=[self.r_const])

    def mod_vecs(self, mv, m, gvec, name):
        a = self.sb("a_" + name, [128, KC], F32)
        r = self.r_vec
        self.TS(a[:, :], mv[:, :, 3 * m + 1], 1.0, float(np.sqrt(D)), ALU.add, ALU.mult, reads=[r], writes=[r])
        self.TT(a[:, :], a[:, :], gvec, ALU.mult, reads=[r], writes=[r])
        return a, mv[:, :, 3 * m], mv[:, :, 3 * m + 2]

    def load_mods(self, io):
        nc = self.nc
        bsel = self.bsel
        self.P._wait(self.P.sp, [self.ar_token])
        src = [(io["msum"][m, bass.ds(bsel, 1), :].rearrange("o (t k) -> (o t) k", t=3), 3) for m in range(4)]
        mv = self.load_rows_T("mods", src, 12, D)
        bv = self.load_rows_T("modb", io["mb"].rearrange("m (t k) -> (m t) k", t=3), 12, D)
        self.TT(mv[:, :, :], mv[:, :, :], bv[:, :, :], ALU.add, reads=[self.r_vec], writes=[self.r_vec])
        return mv

    def load_mods_old(self, modall):
        nc, P = self.nc, self.P
        out = self.sb("m_mods", [128, KC, 12], F32)
        stg, rs = self.vec_stage, self.vec_stage_r
        P.dma(P.sp, stg[:12, :].rearrange("q (r k) -> q r k", r=2), modall.rearrange("r m t k -> (m t) r k"),
              self.xrow[0][2], writes=[rs])
        for cc in range(KC):
            bk = self.bank()
            P.group(P.pe, [lambda cc=cc, bk=bk: nc.tensor.transpose(bk[0][:, :12], stg[:12, cc * 128:(cc + 1) * 128],
                                                                    self.ident[:12, :12])],
                    reads=[rs, self.r_const], writes=[bk[1]])
            self.CP(out[:, cc, :], bk[0][:, :12], reads=[bk[1]], writes=[self.r_vec])
        return out

    def load_x_tile(self, xin, start, n, xT):
        nc, P = self.nc, self.P
        nb = (n + 127) // 128
        for b in range(nb):
            nt = min(128, n - b * 128)
            xr, rr, sm = self.xrow[self.xrow_i % 2]
            self.xrow_i += 1
            P.dma(P.sp, xr[:nt, :], xin[start + b * 128:start + b * 128 + nt, :], sm, writes=[rr])
            for g in range(4):
                bk = self.bank()
                fns = []
                for q in range(4):
                    c = 4 * g + q
                    fns.append(lambda q=q, c=c, bk=bk: nc.tensor.transpose(bk[0][:, q * 128:q * 128 + nt],
                                                                           xr[:nt, c * 128:(c + 1) * 128], self.ident[:nt, :nt]))
                P.group(P.pe, fns, reads=[rr, self.r_const], writes=[bk[1]])
                src = bk[0][:, :].rearrange("p (q t) -> p q t", q=4)[:, :, :nt]
                dst = xT.t[:, 4 * g:4 * g + 4, b * 128:b * 128 + nt]
                eng = self.P.act if (g % 2 == 0) else self.P.dve
                if eng is self.P.act:
                    P.op(P.act, lambda dst=dst, src=src: nc.scalar.copy(dst, src), reads=[bk[1]],
                         writes=[xT.r[4 * g + q] for q in range(4)])
                else:
                    self.CP(dst, src, reads=[bk[1]], writes=[xT.r[4 * g + q] for q in range(4)])

    def phase_a(self, io, tiles):
        nc, P = self.nc, self.P
        self.begin_phase()
        self.common_init()
        self.init_wstream(3)
        mv = self.load_mods(io)
        sv = self.load_rows_T("svA", io["svecs"], 7, D)
        pw1b = self.load_rows_T("pw1b", io["pw1b"], 2, D)
        dww = self.load_rows_T("dww", io["dww"], CW, D)
        fdw = self.load_rows_T("fdw", io["ffn_dw"], 4, 2 * FF)
        pm = self.sb("pm_sb", [128, 1], F32)
        P.dma(P.sp, pm[:, :], io["pm"][:, :], self.dsem("pm"), writes=[self.r_vec])
        rv = self.r_vec
        a0, sh0, g0 = self.mod_vecs(mv, 0, sv[:, :, 0], "mix0")
        a1, sh1, g1 = self.mod_vecs(mv, 1, sv[:, :, 1], "ffn0")
        a2, sh2, g2 = self.mod_vecs(mv, 2, sv[:, :, 2], "mix1")
        gb2 = self.sb("gb2", [128, KC], F32)
        self.TT(gb2[:, :], g0, sv[:, :, 6], ALU.mult, reads=[rv], writes=[rv])

        xT = Buf(self.sb("xT", [128, KC, 512], F32), KC, "xT")
        hT = Buf(self.sb("hT", [128, KC, 512], BF16), KC, "hT")
        uT = Buf(self.sb("uT", [128, KC, CW - 1 + 512], BF16), KC, "uT")
        gTt = self.sb("gTt", [128, FC, 512], BF16)
        gT = Buf(gTt, FC, "gT")
        cvt = gTt[:, 0:32, :].bitcast(F32).rearrange("p a b -> p (a b)").rearrange("p (c n) -> p c n", c=KC)
        Tg = Buf(self.sb("Tg", [128, 2, 512], F32), 2, "Tg")
        Tv = Buf(self.sb("Tv", [128, 2, 512], F32), 2, "Tv")
        Hff = self.sb("Hff", [128, 2 * FC, 2], F32)
        rH = [Res("Hff%d" % i) for i in range(2 * FC)]
        stat = Buf(self.sb("stat", [128, 3, 512], F32), 3, "stat")
        P.op(P.pool, lambda: nc.gpsimd.memset(Hff[:, :, :], 0.0), writes=rH)
        P.op(P.pool, lambda: nc.gpsimd.memset(uT.t[:, :, 0:CW - 1], 0.0), writes=uT.r)

        r_diag = Res("diagw")
        for _ in tiles:
            for i in range(8):
                self.wplan_add(io["pw1_w"], D, [(256 * i, 256), (D + 256 * i, 256)])
            for c in range(KC):
                self.wplan.append(((io["diagw"][c], r_diag), CW, 128, None))
            for i in range(4):
                self.wplan_add(io["pw2_w"], D, [(512 * i, 512)])
            for i in range(22):
                self.wplan_add(io["up_w"], D, [(256 * i, 256), (FF + 256 * i, 256)])
            for i in range(16):
                self.wplan_add(io["down_w"], FF, [(128 * i, 128)])

        ident_bf = self.sb("ident_bf", [128, 128], BF16)
        self.CP(ident_bf[:, :], self.ident[:, :], reads=[self.r_const], writes=[self.r_const])
        for c in range(KC):
            t, r, sm = self.wslots[c % 3]
            dv = t[:, 0:CW * 128].rearrange("p (k m) -> p k m", k=CW)
            for k in range(CW):
                self.TS(dv[:, k, :], ident_bf[:, :], dww[:, c, k:k + 1], None, ALU.mult, None, reads=[self.r_const, rv],
                        writes=[r])
            P.dma(P.sp, io["diagw"][c], t[:, 0:CW * 128], sm, reads=[r], writes=[r_diag])

        x1T_d, h1T_d = io["x1T"], io["h1T"]
        osem_x = [self.dsem("ox0"), self.dsem("ox1")]
        osem_h = [self.dsem("oh0"), self.dsem("oh1")]

        per_tile = (getattr(self, "n_bg_rest", 0) + len(tiles) - 2) // max(1, len(tiles) - 1)
        for ti, (start, n) in enumerate(tiles):
            first = (ti == 0)
            if not first:
                self.bg_pump(per_tile)
            self.load_x_tile(io["xin"], start, n, xT)
            self.rms_mod(xT, hT, n, a0, sh0)
            if first:
                pass
            sg = self.tmpf

            def ep_pw1(bk, cid, n=n, first=first):
                if cid < KC:
                    self._valbank[cid] = bk
                else:
                    j = cid - KC
                    vb = self._valbank.pop(j)
                    i = j % 2
                    self.A(sg.t[:, i, :n], bk[0][:, :n], AF.Sigmoid, reads=[bk[1], rv], writes=[sg.r[i]],
                           bias=pw1b[:, j, 1:2], scale=1.0)
                    self.STT(uT.t[:, j, CW - 1:CW - 1 + n], vb[0][:, :n], pw1b[:, j, 0:1], sg.t[:, i, :n], ALU.add, ALU.mult,
                             reads=[vb[1], sg.r[i], rv], writes=[uT.r[j]])
            self._valbank = {}
            slots = [[2 * i, 2 * i + 1, KC + 2 * i, KC + 2 * i + 1] for i in range(8)]
            self.linear(hT, KC, n, slots, ep_pw1)
            if first:
                self.TS(uT.t[:, :, CW - 1:CW - 1 + PRE], uT.t[:, :, CW - 1:CW - 1 + PRE], pm[:, 0:1], None, ALU.mult, None,
                        reads=uT.r + [rv], writes=uT.r)
            cv_r = [[gT.r[2 * c], gT.r[2 * c + 1]] for c in range(KC)]
            for c in range(KC):
                wv, wr = self.wget()
                dvw = wv[:, :, :]
                bkc = self.bank()
                pairs = [(dvw[:, k, :], uT.t[:, c, k:k + n]) for k in range(CW)]
                self.mm(bkc, pairs, n, reads=[wr, uT.r[c]])
                self.A(cvt[:, c, :n], bkc[0][:, :n], AF.Identity, reads=[bkc[1], rv], writes=cv_r[c],
                       bias=sv[:, c, 3:4], scale=1.0)
            bk_m = self.bank()
            bk_s = self.bank()
            for c in range(KC):
                i = c % 2
                self.A(self.tmpbf.t[:, i, :n], cvt[:, c, :n], AF.Identity, reads=cv_r[c], writes=[self.tmpbf.r[i]])
                P.group(P.pe, [lambda c=c, i=i: nc.tensor.matmul(bk_m[0][:, :n], self.ones_bf[:], self.tmpbf.t[:, i, :n],
                                                                 start=(c == 0), stop=(c == KC - 1))],
                        reads=[self.tmpbf.r[i], self.r_const], writes=[bk_m[1]])
                self.A(self.sqb.t[:, i, :n], cvt[:, c, :n], AF.Square, reads=cv_r[c], writes=[self.sqb.r[i]])
                P.group(P.pe, [lambda c=c, i=i: nc.tensor.matmul(bk_s[0][:, :n], self.ones_bf[:], self.sqb.t[:, i, :n],
                                                                 start=(c == 0), stop=(c == KC - 1))],
                        reads=[self.sqb.r[i], self.r_const], writes=[bk_s[1]])
            mean, var, rs = stat.t[:, 0, :n], stat.t[:, 1, :n], stat.t[:, 2, :n]
            self.A(mean, bk_m[0][:, :n], AF.Identity, reads=[bk_m[1]], writes=[stat.r[0]], scale=1.0 / D)
            self.TT(var, mean, mean, ALU.mult, reads=[stat.r[0]], writes=[stat.r[1]])
            self.STT(var, bk_s[0][:, :n], 1.0 / D, var, ALU.mult, ALU.subtract, reads=[bk_s[1], stat.r[1]], writes=[stat.r[1]])
            self.A(rs, var, AF.Ln, reads=[stat.r[1], self.r_const], writes=[stat.r[2]], bias=self.eps_rms[:, 1:2], scale=1.0)
            self.A(rs, rs, AF.Exp, reads=[stat.r[2]], writes=[stat.r[2]], scale=-0.5)
            for c in range(KC):
                i = c % 2
                self.TT(sg.t[:, i, :n], cvt[:, c, :n], mean, ALU.subtract, reads=cv_r[c] + [stat.r[0]], writes=[sg.r[i]])
                self.TT(sg.t[:, i, :n], sg.t[:, i, :n], rs, ALU.mult, reads=[sg.r[i], stat.r[2]], writes=[sg.r[i]])
                self.A(hT.t[:, c, :n], sg.t[:, i, :n], AF.Silu, reads=[sg.r[i], rv], writes=[hT.r[c]],
                       bias=sv[:, c, 5:6], scale=sv[:, c, 4:5])
            self.CP(uT.t[:, :, 0:CW - 1], uT.t[:, :, n:n + CW - 1], reads=uT.r, writes=uT.r)

            def ep_pw2(bk, cid, n=n):
                i = cid % 2
                self.A(sg.t[:, i, :n], bk[0][:, :n], AF.Identity, reads=[bk[1], rv], writes=[sg.r[i]],
                       bias=gb2[:, cid:cid + 1], scale=g0[:, cid:cid + 1])
                self.TT(xT.t[:, cid, :n], xT.t[:, cid, :n], sg.t[:, i, :n], ALU.add, reads=[xT.r[cid], sg.r[i]], writes=[xT.r[cid]])
            self.linear(hT, KC, n, [[4 * i + q for q in range(4)] for i in range(4)], ep_pw2)

            self.rms_mod(xT, hT, n, a1, sh1)
            if first:
                self.TS(hT.t[:, :, 0:PRE], hT.t[:, :, 0:PRE], pm[:, 0:1], None, ALU.mult, None, reads=hT.r + [rv], writes=hT.r)
            self.ffn(hT, xT, gT, Tg, Tv, Hff, rH, fdw, g1, n)

            k = ti % 2
            P.dma(P.sp, x1T_d[:, :, start:start + n].rearrange("c p n -> p c n"), xT.t[:, :, :n], osem_x[k], reads=xT.r)
            self.rms_mod(xT, hT, n, a2, sh2)
            P.dma(P.sp, h1T_d[:, :, start:start + n].rearrange("c p n -> p c n"), hT.t[:, :, :n], osem_h[k], reads=hT.r)
        self.end_phase()

    def ffn(self, hT, xT, gT, Tg, Tv, Hff, rH, fdw, gate, n):
        nc, P = self.nc, self.P
        rv = self.r_vec

        def conv3(bk, ch, T, i):
            p = bk[0]
            w0, w1, w2, b = fdw[:, ch, 0:1], fdw[:, ch, 1:2], fdw[:, ch, 2:3], fdw[:, ch, 3:4]
            t = T.t[:, i, :]
            self.A(t[:, :n], p[:, :n], AF.Identity, reads=[bk[1], rv], writes=[T.r[i]], bias=b, scale=w2)
            self.STT(t[:, 1:n], p[:, 0:n - 1], w1, t[:, 1:n], ALU.mult, ALU.add, reads=[bk[1], rv, T.r[i]], writes=[T.r[i]])
            self.STT(t[:, 2:n], p[:, 0:n - 2], w0, t[:, 2:n], ALU.mult, ALU.add, reads=[bk[1], rv, T.r[i]], writes=[T.r[i]])
            h = Hff[:, ch, :]
            self.STT(t[:, 0:1], h[:, 1:2], w1, t[:, 0:1], ALU.mult, ALU.add, reads=[rH[ch], rv, T.r[i]], writes=[T.r[i]])
            self.STT(t[:, 0:2], h[:, 0:2], w0, t[:, 0:2], ALU.mult, ALU.add, reads=[rH[ch], rv, T.r[i]], writes=[T.r[i]])
            P.op(P.act, lambda: nc.scalar.copy(h[:, 0:2], p[:, n - 2:n]), reads=[bk[1], T.r[i]], writes=[rH[ch]])

        def ep_up(bk, cid, n=n):
            if cid < FC:
                conv3(bk, cid, Tg, cid % 2)
            else:
                j = cid - FC
                i = j % 2
                conv3(bk, cid, Tv, i)
                self.A(Tg.t[:, i, :n], Tg.t[:, i, :n], AF.Silu, reads=[Tg.r[i]], writes=[Tg.r[i]])
                self.TT(gT.t[:, j, :n], Tg.t[:, i, :n], Tv.t[:, i, :n], ALU.mult, reads=[Tg.r[i], Tv.r[i]], writes=[gT.r[j]])
        slots = [[2 * i, 2 * i + 1, FC + 2 * i, FC + 2 * i + 1] for i in range(22)]
        self.linear(hT, KC, n, slots, ep_up)

        def ep_down(bk, cid, n=n):
            self.STT(xT.t[:, cid, :n], bk[0][:, :n], gate[:, cid:cid + 1], xT.t[:, cid, :n], ALU.mult, ALU.add,
                     reads=[bk[1], rv, xT.r[cid]], writes=[xT.r[cid]])
        self.linear(gT, FC, n, [[i] for i in range(16)], ep_down)


def phase_mods(self, io):
    nc, P = self.nc, self.P
    self.begin_phase()
    self.common_init()
    self.init_wstream(3)
    rv = self.r_vec
    cT = self.load_rows_T("cq", io["cq"], 2, 512)
    sc = self.sb("sc", [128, 4, 2], BF16)
    self.A(sc[:, :, :], cT[:, :, :], AF.Silu, reads=[rv], writes=[rv])
    NM = 3 * D
    orow = self.sb("orow", [2, NM], F32)
    r_o = Res("orow")
    sem_o = self.dsem("orow")
    wi = 0
    for m in range(4):
        for g3 in range(3):
            t, r, sm = self.wslots[wi % 3]
            wi += 1
            wv = t[:, :].rearrange("p (kc n) -> p kc n", kc=4)
            P.dma(P.pool, wv, io["mw"][m].rearrange("(kc p) n -> p kc n", p=128)[:, :, g3 * 2048:(g3 + 1) * 2048], sm, writes=[r])
            for g in range(4):
                bk = self.bank()
                pairs = [(sc[:, k, :], wv[:, k, g * 512:(g + 1) * 512]) for k in range(4)]
                self.mm(bk, pairs, 512, reads=[r, rv], m=2)
                o = g3 * 2048 + g * 512
                self.CP(orow[0:2, o:o + 512], bk[0][0:2, :512], reads=[bk[1]], writes=[r_o])
        P.dma(P.sp, io["part"][m], orow[0:2, :], sem_o, reads=[r_o])
    ar_sem = self.dsem("ar_mods")
    P.collective("AllReduce", [io["part"].rearrange("m b k -> (m b) k")], [io["msum"].rearrange("m b k -> (m b) k")], QUADS,
                 ar_sem, reads=[r_o], op=ALU.add)
    self.ar_token = (ar_sem, P.dma_vals[ar_sem])
    self.bg_sems.add(ar_sem)
    self.end_phase()


def phase_w(self, wl):
    nc, P = self.nc, self.P
    self.begin_phase()
    self.init_wstream(3)
    i = 0
    for wi, w in enumerate(wl):
        sem = self.dsem("agw%d" % wi)
        self.bg_sems.add(sem)
        Ks, wb, nb = w["Ks"], w["wb"], w["nb"]
        N = wb * nb
        sres = [Res("wsh%d_%d" % (wi, k)) for k in range(3)]
        for r0 in range(0, Ks, 128):
            for c0 in range(0, N, 8192):
                wd = min(8192, N - c0)
                t, r, sm = self.wslots[i % 3]
                P.dma(P.pool, t[:, :wd], w["shard"][r0:r0 + 128, c0:c0 + wd], sm, writes=[r])
                P.dma(P.sp, w["wsh"][c0 // wb:(c0 + wd) // wb, r0:r0 + 128, :].rearrange("b p n -> p b n"),
                      t[:, :wd].rearrange("p (b n) -> p b n", n=wb), sm, reads=[r], writes=[sres[i % 3]])
                i += 1
        for bi in range(nb):
            P.collective("AllGather", [w["wsh"][bi]], [w["full"][bi]], QUADS, sem, reads=sres, writes=[w["res"]])
    self.end_phase()


def phase_qkv(self, io, ntile, nh):
    nc, P = self.nc, self.P
    self.begin_phase()
    self.init_wstream(3)
    ns = nh // 4
    hTs = [Buf(self.sb("hq%d" % i, [128, KC, 512], BF16), KC, "hq%d" % i) for i in range(2)]
    hsem = [self.dsem("hq0"), self.dsem("hq1")]
    qst = Buf(self.sb("qst", [128, nh, 512], BF16), nh, "qst")
    kst = Buf(self.sb("kst", [128, nh, 512], BF16), nh, "kst")
    vst = Buf(self.sb("vst", [128, 4, nh * 128], BF16), 4, "vst")
    sq, sk, sv_ = self.dsem("oq"), self.dsem("ok"), self.dsem("ov")
    par = nc.gpsimd.partition_id() % 2
    for _ in range(2 * ntile):
        for which in range(3):
            for i in range(ns):
                self.wplan_add(io["qkv"], D, [(par + 2 * which, 512 * i, 512)])
    it = 0
    for s in range(2):
        for t in range(ntile):
            p0 = (s * ntile + t) * 512
            lc = PRE + 512 * t
            hT = hTs[it % 2]
            P.dma(P.sp, hT.t[:, :, :], io["h1all"][:, s, :, lc:lc + 512].rearrange("c p n -> p c n"), hsem[it % 2], writes=hT.r)
            it += 1
            for (st, dst, sem) in ((qst, io["QT"], sq), (kst, io["KT"], sk)):
                def ep(bk, cid, st=st):
                    if cid % 2 == 0:
                        P.op(P.act, lambda: nc.scalar.copy(st.t[:, cid, :], bk[0][:, :512]), reads=[bk[1]], writes=[st.r[cid]])
                    else:
                        self.CP(st.t[:, cid, :], bk[0][:, :512], reads=[bk[1]], writes=[st.r[cid]])
                self.linear(hT, KC, 512, [[4 * i + q for q in range(4)] for i in range(ns)], ep)
                P.dma(P.sp, dst[:, :, p0:p0 + 512].rearrange("h p n -> p h n"), st.t[:, :, :], sem, reads=st.r)
            for cg in range(ns):
                wv, wr = self.wget()
                for tb in range(4):
                    bk = self.bank()
                    pairs = [(hT.t[:, k, tb * 128:(tb + 1) * 128], wv[:, k, :]) for k in range(KC)]
                    self.mm(bk, pairs, 512, reads=[wr] + hT.r)
                    if tb % 2 == 0:
                        P.op(P.act, lambda tb=tb, bk=bk: nc.scalar.copy(vst.t[:, tb, cg * 512:(cg + 1) * 512], bk[0][:, :512]),
                             reads=[bk[1]], writes=[vst.r[tb]])
                    else:
                        self.CP(vst.t[:, tb, cg * 512:(cg + 1) * 512], bk[0][:, :512], reads=[bk[1]], writes=[vst.r[tb]])
            P.dma(P.sp, io["V"][p0:p0 + 512, :].rearrange("(tb p) f -> p tb f", p=128), vst.t[:, :, :], sv_, reads=vst.r)
    self.end_phase()


def phase_attn(self, io, nq, nh):
    nc, P = self.nc, self.P
    self.begin_phase()
    T = nq * 512
    NKB = nq * 4
    scale = 1.0 / float(np.sqrt(128.0))
    rc = self.r_const
    masks = self.sb("masks", [128, 4, 512], F32)
    tri = self.sb("tri", [128, 128], BF16)
    comp = self.sb("comp", [128, 128], BF16)
    onesw = self.sb("onesw", [128, 512], F32)
    P.op(P.pool, lambda: nc.gpsimd.memset(onesw[:, :], 1.0), writes=[rc])
    for i in range(4):
        P.op(P.pool, lambda i=i: nc.gpsimd.affine_select(masks[:, i, :], onesw[:, :], [[1, 512]], ALU.is_gt, 0.0,
                                                         base=-128 * i, channel_multiplier=-1), reads=[rc], writes=[rc])
    P.op(P.pool, lambda: nc.gpsimd.affine_select(tri[:, :], onesw[:, 0:128], [[-1, 128]], ALU.is_gt, 0.0,
                                                 base=1, channel_multiplier=1), reads=[rc], writes=[rc])
    P.op(P.pool, lambda: nc.gpsimd.affine_select(comp[:, :], onesw[:, 0:128], [[1, 128]], ALU.is_gt, 0.0,
                                                 base=0, channel_multiplier=-1), reads=[rc], writes=[rc])
    Ksb = [(self.sb("Ksb%d" % i, [128, T], BF16), Res("Ksb%d" % i), self.dsem("Ksb%d" % i)) for i in range(2)]
    Vsb = [(self.sb("Vsb%d" % i, [128, NKB, 128], BF16), Res("Vsb%d" % i), self.dsem("Vsb%d" % i)) for i in range(2)]
    NL = 2
    Eb = [Buf(self.sb("Eb%d" % l, [128, 3, 512], F32), 3, "Eb%d" % l) for l in range(NL)]
    Lb = [Buf(self.sb("Lb%d" % l, [128, 3, 512], BF16), 3, "Lb%d" % l) for l in range(NL)]
    Gb = [Buf(self.sb("Gb%d" % l, [128, 2, 512], F32), 2, "Gb%d" % l) for l in range(NL)]
    Ab = [Buf(self.sb("Ab%d" % l, [128, 2, 512], BF16), 2, "Ab%d" % l) for l in range(NL)]
    Ob = [Buf(self.sb("Ob%d" % l, [128, 2, 512], BF16), 2, "Ob%d" % l) for l in range(NL)]
    Qsb = [[(self.sb("Qsb%d_%d" % (l, i), [128, 512], BF16), Res("Qsb%d_%d" % (l, i)), self.dsem("Qsb%d_%d" % (l, i)))
            for i in range(2)] for l in range(NL)]
    osem = [[self.dsem("oo%d_%d" % (l, i)) for i in range(2)] for l in range(NL)]
    zt = self.sb("zpad", [128, nh, PRE], BF16)
    r_z = Res("zpad")
    P.op(P.pool, lambda: nc.gpsimd.memset(zt[:, :, :], 0.0), writes=[r_z])
    P.dma(P.sp, io["oT"][:, 0, :, 0:PRE].rearrange("h p n -> p h n"), zt[:, :, :], self.dsem("zpad"), reads=[r_z])
    Sbk = [self.banks[0:2], self.banks[2:4]]
    Rbk = self.banks[4:6]
    Obk = self.banks[6:8]
    loaded = {}

    def load_head(h):
        if h in loaded or h >= nh:
            return
        Kt, Kr, Ks = Ksb[h % 2]
        Vt, Vr, Vs = Vsb[h % 2]
        P.dma(P.sp, Kt[:, :], io["KT"][h, :, :], Ks, writes=[Kr])
        P.dma(P.sp, Vt[:, :, :], io["V"][:, h * 128:(h + 1) * 128].rearrange("(kb p) d -> p kb d", p=128), Vs, writes=[Vr])
        loaded[h] = True

    cnt = [0] * NL

    class Chain:
        pass

    def start_chain(l, h, j):
        load_head(h)
        if j == nq // 2:
            load_head(h + 1)
        c = Chain()
        c.l, c.h, c.j = l, h, j
        c.Kt, c.Kr, _ = Ksb[h % 2]
        c.Vt, c.Vr, _ = Vsb[h % 2]
        c.ci = cnt[l]
        cnt[l] += 1
        c.Qt, c.Qr, Qs = Qsb[l][c.ci % 2]
        P.dma(P.sp, c.Qt[:, :], io["QT"][h, :, j * 512:(j + 1) * 512], Qs, writes=[c.Qr])
        c.steps = list(range(4 * j + 3, -1, -1))
        c.N = len(c.steps)
        c.k = 0
        mm1(c, 0)
        if c.N > 1:
            mm1(c, 1)
        e_(c, 0)
        l_(c, 0)
        mm2(c, 0)
        return c

    def mm1(c, k):
        kb = c.steps[k]
        sbk = Sbk[c.l][k % 2]
        self.mm(sbk, [(c.Kt[:, kb * 128:(kb + 1) * 128], c.Qt[:, :])], 512, reads=[c.Kr, c.Qr])

    def e_(c, k):
        kb = c.steps[k]
        sbk = Sbk[c.l][k % 2]
        E = Eb[c.l]
        e = E.t[:, k % 3, :]
        self.A(e, sbk[0][:, :], AF.Exp, reads=[sbk[1]], writes=[E.r[k % 3]], scale=scale)
        i = kb - 4 * c.j
        if i >= 0:
            self.TT(e, e, masks[:, i, :], ALU.mult, reads=[E.r[k % 3], rc], writes=[E.r[k % 3]], eng=P.pool)

    def l_(c, k):
        E, L = Eb[c.l], Lb[c.l]
        self.A(L.t[:, k % 3, :], E.t[:, k % 3, :], AF.Ln, reads=[E.r[k % 3]], writes=[L.r[k % 3]], bias=1.0, scale=1.0)

    def mm2(c, k):
        L, Rb = Lb[c.l], Rbk[c.l]
        P.group(P.pe, [lambda: nc.tensor.matmul(Rb[0][:, :], tri[:, :], L.t[:, k % 3, :], start=(k == 0), stop=False)],
                reads=[L.r[k % 3], rc], writes=[Rb[1]])

    def g_(c, k):
        G, Rb = Gb[c.l], Rbk[c.l]
        self.A(G.t[:, k % 2, :], Rb[0][:, :], AF.Exp, reads=[Rb[1]], writes=[G.r[k % 2]], scale=-1.0)

    def a_(c, k):
        E, G, A_ = Eb[c.l], Gb[c.l], Ab[c.l]
        self.TT(A_.t[:, k % 2, :], E.t[:, k % 3, :], G.t[:, k % 2, :], ALU.mult,
                reads=[E.r[k % 3], G.r[k % 2]], writes=[A_.r[k % 2]])

    def mm3(c, k):
        L, Rb = Lb[c.l], Rbk[c.l]
        P.group(P.pe, [lambda: nc.tensor.matmul(Rb[0][:, :], comp[:, :], L.t[:, k % 3, :], start=False, stop=(k == c.N - 1))],
                reads=[L.r[k % 3], rc], writes=[Rb[1]])

    def mm4(c, k):
        A_, OB = Ab[c.l], Obk[c.l]
        kb = c.steps[k]
        P.group(P.pe, [lambda: nc.tensor.matmul(OB[0][:, :], c.Vt[:, kb, :], A_.t[:, k % 2, :], start=(k == 0), stop=(k == c.N - 1))],
                reads=[c.Vr, A_.r[k % 2]], writes=[OB[1]])

    def finish(c):
        l, h, j, ci = c.l, c.h, c.j, c.ci
        ob, OB = Ob[l], Obk[l]
        self.CP(ob.t[:, ci % 2, :], OB[0][:, :], reads=[OB[1]], writes=[ob.r[ci % 2]])
        sem = osem[l][ci % 2]
        pc0 = PRE + j * 512
        if pc0 + 512 <= NP2:
            P.dma(P.sp, io["oT"][h, 0, :, pc0:pc0 + 512], ob.t[:, ci % 2, :], sem, reads=[ob.r[ci % 2]])
            if pc0 + 512 > HALF:
                P.dma(P.sp, io["oT"][h, 1, :, 0:pc0 + 512 - HALF], ob.t[:, ci % 2, HALF - pc0:512], sem, reads=[ob.r[ci % 2]])
        else:
            P.dma(P.sp, io["oT"][h, 1, :, pc0 - HALF:pc0 - HALF + 512], ob.t[:, ci % 2, :], sem, reads=[ob.r[ci % 2]])

    work = [(h, j) for h in range(nh) for j in range(nq)]
    lanes = [None] * NL
    wi = 0
    while True:
        for l in range(NL):
            if lanes[l] is None and wi < len(work):
                lanes[l] = start_chain(l, *work[wi])
                wi += 1
        act = [c for c in lanes if c is not None]
        if not act:
            break
        for c in act:
            if c.k + 2 < c.N:
                mm1(c, c.k + 2)
        for c in act:
            if c.k + 1 < c.N:
                e_(c, c.k + 1)
        for c in act:
            if c.k + 1 < c.N:
                l_(c, c.k + 1)
        for c in act:
            g_(c, c.k)
        for c in act:
            a_(c, c.k)
        for c in act:
            mm3(c, c.k)
            if c.k + 1 < c.N:
                mm2(c, c.k + 1)
            if c.k >= 1:
                mm4(c, c.k - 1)
            if c.k == c.N - 1:
                mm4(c, c.k)
                finish(c)
                lanes[c.l] = None
            c.k += 1
    self.end_phase()


def phase_attn2(self, io, nq, nh):
    nc, P = self.nc, self.P
    self.begin_phase()
    T = nq * 512
    NKB = nq * 4
    scale = 1.0 / float(np.sqrt(128.0))
    rc = self.r_const
    masks = self.sb("masks", [128, 4, 2, 512], F32)
    tri = self.sb("tri", [128, 128], BF16)
    comp = self.sb("comp", [128, 128], BF16)
    onesw = self.sb("onesw", [128, 512], F32)
    P.op(P.pool, lambda: nc.gpsimd.memset(onesw[:, :], 1.0), writes=[rc])
    for i in range(4):
        for l in range(2):
            P.op(P.pool, lambda i=i, l=l: nc.gpsimd.affine_select(masks[:, i, l, :], onesw[:, :], [[1, 512]], ALU.is_gt, 0.0,
                                                                  base=-128 * i, channel_multiplier=-1), reads=[rc], writes=[rc])
    P.op(P.pool, lambda: nc.gpsimd.affine_select(tri[:, :], onesw[:, 0:128], [[-1, 128]], ALU.is_gt, 0.0,
                                                 base=1, channel_multiplier=1), reads=[rc], writes=[rc])
    P.op(P.pool, lambda: nc.gpsimd.affine_select(comp[:, :], onesw[:, 0:128], [[1, 128]], ALU.is_gt, 0.0,
                                                 base=0, channel_multiplier=-1), reads=[rc], writes=[rc])
    NB = 4
    Ksb = [(self.sb("Ksb%d" % i, [128, T], BF16), Res("Ksb%d" % i), self.dsem("Ksb%d" % i)) for i in range(NB)]
    Vsb = [(self.sb("Vsb%d" % i, [128, NKB, 128], BF16), Res("Vsb%d" % i), self.dsem("Vsb%d" % i)) for i in range(NB)]
    Eb = Buf(self.sb("Eb", [128, 3, 2, 512], F32), 3, "Eb")
    Lb = Buf(self.sb("Lb", [128, 3, 2, 512], BF16), 3, "Lb")
    Gb = Buf(self.sb("Gb", [128, 2, 2, 512], F32), 2, "Gb")
    Ab = Buf(self.sb("Ab", [128, 2, 2, 512], BF16), 2, "Ab")
    Ob = Buf(self.sb("Ob", [128, 2, 2, 512], BF16), 2, "Ob")
    NQB = 3
    Qsb = [(self.sb("Qsb%d" % i, [128, 2, 512], BF16), Res("Qsb%d" % i), self.dsem("Qsb%d" % i)) for i in range(NQB)]
    osem = [self.dsem("oo%d" % i) for i in range(2)]
    zt = self.sb("zpad", [128, nh, PRE], BF16)
    r_z = Res("zpad")
    P.op(P.pool, lambda: nc.gpsimd.memset(zt[:, :, :], 0.0), writes=[r_z])
    P.dma(P.sp, io["oT"][:, 0, :, 0:PRE].rearrange("h p n -> p h n"), zt[:, :, :], self.dsem("zpad"), reads=[r_z])
    Sp = [self.bankpairs[0], self.bankpairs[1]]
    Sr = [[self.banks[0][1], self.banks[1][1]], [self.banks[2][1], self.banks[3][1]]]
    Rp = self.bankpairs[2]
    Rr = [self.banks[4][1], self.banks[5][1]]
    Op = self.bankpairs[3]
    Or = [self.banks[6][1], self.banks[7][1]]
    loaded = {}

    def load_head(h):
        if h in loaded or h >= nh:
            return
        Kt, Kr, Ks = Ksb[h % NB]
        Vt, Vr, Vs = Vsb[h % NB]
        P.dma(P.sp, Kt[:, :], io["KT"][h, :, :], Ks, writes=[Kr])
        P.dma(P.sp, Vt[:, :, :], io["V"][:, h * 128:(h + 1) * 128].rearrange("(kb p) d -> p kb d", p=128), Vs, writes=[Vr])
        loaded[h] = True

    steps = []
    for hp in range(nh // 2):
        for j in range(nq):
            N = 4 * j + 4
            for k in range(N):
                steps.append((hp, j, k, N))
    NS = len(steps)
    qbuf = {}
    qcount = [0]

    def get_q(hp, j):
        key = (hp, j)
        if key not in qbuf:
            load_head(2 * hp)
            load_head(2 * hp + 1)
            if j == nq // 2:
                load_head(2 * hp + 2)
                load_head(2 * hp + 3)
            Qt, Qr, Qs = Qsb[qcount[0] % NQB]
            qcount[0] += 1
            for l in range(2):
                P.dma(P.sp, Qt[:, l, :], io["QT"][2 * hp + l, :, j * 512:(j + 1) * 512], Qs, writes=[Qr])
            qbuf[key] = (Qt, Qr)
        return qbuf[key]

    def mm1(s):
        hp, j, k, N = steps[s]
        kb = 4 * j + 3 - k
        Qt, Qr = get_q(hp, j)
        for l in range(2):
            h = 2 * hp + l
            Kt, Kr, _ = Ksb[h % NB]
            t = Sp[s % 2]
            P.group(P.pe, [lambda: nc.tensor.matmul(t[:, l, :], Kt[:, kb * 128:(kb + 1) * 128], Qt[:, l, :], start=True, stop=True)],
                    reads=[Kr, Qr], writes=[Sr[s % 2][l]])

    def el(s):
        hp, j, k, N = steps[s]
        kb = 4 * j + 3 - k
        self.A(Eb.t[:, s % 3, :, :], Sp[s % 2][:, :, :], AF.Exp, reads=Sr[s % 2], writes=[Eb.r[s % 3]], scale=scale)
        i = kb - 4 * j
        if i >= 0:
            self.TT(Eb.t[:, s % 3, :, :], Eb.t[:, s % 3, :, :], masks[:, i, :, :], ALU.mult, reads=[Eb.r[s % 3], rc],
                    writes=[Eb.r[s % 3]])
        self.A(Lb.t[:, s % 3, :, :], Eb.t[:, s % 3, :, :], AF.Ln, reads=[Eb.r[s % 3]], writes=[Lb.r[s % 3]], bias=1.0, scale=1.0)

    def mm2(s):
        hp, j, k, N = steps[s]
        for l in range(2):
            P.group(P.pe, [lambda: nc.tensor.matmul(Rp[:, l, :], tri[:, :], Lb.t[:, s % 3, l, :], start=(k == 0), stop=False)],
                    reads=[Lb.r[s % 3], rc], writes=[Rr[l]])

    def mm3(s):
        hp, j, k, N = steps[s]
        for l in range(2):
            P.group(P.pe, [lambda: nc.tensor.matmul(Rp[:, l, :], comp[:, :], Lb.t[:, s % 3, l, :], start=False, stop=(k == N - 1))],
                    reads=[Lb.r[s % 3], rc], writes=[Rr[l]])

    def mm4(s):
        hp, j, k, N = steps[s]
        kb = 4 * j + 3 - k
        for l in range(2):
            h = 2 * hp + l
            Vt, Vr, _ = Vsb[h % NB]
            P.group(P.pe, [lambda: nc.tensor.matmul(Op[:, l, :], Vt[:, kb, :], Ab.t[:, s % 2, l, :], start=(k == 0), stop=(k == N - 1))],
                    reads=[Vr, Ab.r[s % 2]], writes=[Or[l]])
        if k == N - 1:
            ci = hp * nq + j
            self.CP(Ob.t[:, ci % 2, :, :], Op[:, :, :], reads=Or, writes=[Ob.r[ci % 2]])
            sem = osem[ci % 2]
            pc0 = PRE + j * 512
            for l in range(2):
                h = 2 * hp + l
                ob = Ob.t[:, ci % 2, l, :]
                if pc0 + 512 <= NP2:
                    P.dma(P.sp, io["oT"][h, 0, :, pc0:pc0 + 512], ob, sem, reads=[Ob.r[ci % 2]])
                    if pc0 + 512 > HALF:
                        P.dma(P.sp, io["oT"][h, 1, :, 0:pc0 + 512 - HALF], Ob.t[:, ci % 2, l, HALF - pc0:512], sem, reads=[Ob.r[ci % 2]])
                else:
                    P.dma(P.sp, io["oT"][h, 1, :, pc0 - HALF:pc0 - HALF + 512], ob, sem, reads=[Ob.r[ci % 2]])

    self.bg_wo_queue = P.pool
    self.bg_pump()
    self.bg_wo_queue = None
    mm1(0)
    if NS > 1:
        mm1(1)
    el(0)
    mm2(0)
    for s in range(NS):
        if s + 2 < NS:
            mm1(s + 2)
        if s + 1 < NS:
            el(s + 1)
        self.A(Gb.t[:, s % 2, :, :], Rp[:, :, :], AF.Exp, reads=Rr, writes=[Gb.r[s % 2]], scale=-1.0)
        self.TT(Ab.t[:, s % 2, :, :], Eb.t[:, s % 3, :, :], Gb.t[:, s % 2, :, :], ALU.mult,
                reads=[Eb.r[s % 3], Gb.r[s % 2]], writes=[Ab.r[s % 2]])
        mm3(s)
        if s + 1 < NS:
            mm2(s + 1)
        if s >= 1:
            mm4(s - 1)
    mm4(NS - 1)
    self.end_phase()


Builder.phase_attn2 = phase_attn2


def phase_b2(self, io, tiles, dyn_o):
    nc, P = self.nc, self.P
    self.begin_phase()
    self.common_init()
    self.init_wstream(3)
    rv = self.r_vec
    mv = self.load_mods(io)
    sv = self.load_rows_T("svB", io["svecs2"], 2, D)
    fdw = self.load_rows_T("fdwB", io["ffn_dw"], 4, 2 * FF)
    pm = self.sb("pm_sb", [128, 1], F32)
    P.dma(P.sp, pm[:, :], io["pm"][:, :], self.dsem("pm"), writes=[rv])
    a3, sh3, g3 = self.mod_vecs(mv, 3, sv[:, :, 0], "ffn1")
    g_mix1 = mv[:, :, 8]
    afin = self.sb("afin", [128, KC], F32)
    self.TS(afin[:, :], sv[:, :, 1], float(np.sqrt(D)), None, ALU.mult, None, reads=[rv], writes=[rv])
    xT = Buf(self.sb("xT", [128, KC, 512], F32), KC, "xT")
    hT = Buf(self.sb("hT", [128, KC, 512], BF16), KC, "hT")
    gTt = self.sb("gTt", [128, FC, 512], BF16)
    gT = Buf(gTt, FC, "gT")
    yv = gTt[:, 0:32, :].bitcast(F32).rearrange("p a b -> p (a b)").rearrange("p (c n) -> p c n", c=KC)
    Tg = Buf(self.sb("Tg", [128, 2, 512], F32), 2, "Tg")
    Tv = Buf(self.sb("Tv", [128, 2, 512], F32), 2, "Tv")
    Hff = self.sb("Hff", [128, 2 * FC, 2], F32)
    rH = [Res("Hff%d" % i) for i in range(2 * FC)]
    P.op(P.pool, lambda: nc.gpsimd.memset(Hff[:, :, :], 0.0), writes=rH)
    for _ in tiles:
        for i in range(4):
            self.wplan_add(io["o_w"], D, [(512 * i, 512)])
        for i in range(22):
            self.wplan_add(io["up_w"], D, [(256 * i, 256), (FF + 256 * i, 256)])
        for i in range(16):
            self.wplan_add(io["down_w"], FF, [(128 * i, 128)])
    sx, so = self.dsem("ldx"), self.dsem("ldo")
    if dyn_o:
        par = self.par
    for ti, (start, n) in enumerate(tiles):
        P.dma(P.sp, xT.t[:, :, :n], io["x1T"][:, :, start:start + n].rearrange("c p n -> p c n"), sx, writes=xT.r)
        for rk in range(2):
            if n == 512:
                src = io["oT"][:, bass.ds(par, 1), rk, :, start:start + n].rearrange("h o p n -> p (h o) n")
                P.dma(P.sp, hT.t[:, 8 * rk:8 * rk + 8, :n], src, so, writes=hT.r[8 * rk:8 * rk + 8])
            else:
                parg = nc.gpsimd.partition_id() % 2
                src = io["oT"][:, bass.ds(parg, 1), rk, :, start:start + n].rearrange("h o p n -> p (h o) n")
                P.dma(P.pool, hT.t[:, 8 * rk:8 * rk + 8, :n], src, so, writes=hT.r[8 * rk:8 * rk + 8])

        def ep_o(bk, cid, n=n):
            self.STT(xT.t[:, cid, :n], bk[0][:, :n], g_mix1[:, cid:cid + 1], xT.t[:, cid, :n], ALU.mult, ALU.add,
                     reads=[bk[1], rv, xT.r[cid]], writes=[xT.r[cid]])
        self.linear(hT, KC, n, [[4 * i + q for q in range(4)] for i in range(4)], ep_o)
        self.rms_mod(xT, hT, n, a3, sh3)
        if ti == 0:
            self.TS(hT.t[:, :, 0:PRE], hT.t[:, :, 0:PRE], pm[:, 0:1], None, ALU.mult, None, reads=hT.r + [rv], writes=hT.r)
        self.ffn(hT, xT, gT, Tg, Tv, Hff, rH, fdw, g3, n)
        self.rms_mod_multi(xT, yv, [[gT.r[2 * c], gT.r[2 * c + 1]] for c in range(KC)], n, afin)
        nb = (n + 127) // 128
        for tb in range(nb):
            nt = min(128, n - tb * 128)
            orow, orr, osm = self.xrow[self.xrow_i % 2]
            self.xrow_i += 1
            for gq in range(4):
                bk = self.bank()
                fns = []
                for q in range(4):
                    c = 4 * gq + q
                    fns.append(lambda q=q, c=c, bk=bk: nc.tensor.transpose(bk[0][:nt, q * 128:(q + 1) * 128],
                                                                           yv[:, c, tb * 128:tb * 128 + nt], self.ident[:, :]))
                rr = []
                for q in range(4):
                    rr += [gT.r[2 * (4 * gq + q)], gT.r[2 * (4 * gq + q) + 1]]
                P.group(P.pe, fns, reads=rr + [self.r_const], writes=[bk[1]])
                if gq % 2 == 0:
                    P.op(P.act, lambda gq=gq, bk=bk: nc.scalar.copy(orow[:nt, gq * 512:(gq + 1) * 512], bk[0][:nt, :]),
                         reads=[bk[1]], writes=[orr])
                else:
                    self.CP(orow[:nt, gq * 512:(gq + 1) * 512], bk[0][:nt, :], reads=[bk[1]], writes=[orr])
            P.dma(P.sp, io["y"][start + tb * 128:start + tb * 128 + nt, :], orow[:nt, :], osm, reads=[orr])
    self.end_phase()


def rms_mod_multi(self, xT, yv, yres, n, a_vec):
    nc, P = self.nc, self.P
    sq = self.tmpbf
    bk = self.bank()
    for c in range(KC):
        self.A(sq.t[:, c % 2, :n], xT.t[:, c, :n], AF.Square, reads=[xT.r[c]], writes=[sq.r[c % 2]])
        P.group(P.pe, [lambda c=c: nc.tensor.matmul(bk[0][:, :n], self.ones_bf[:], sq.t[:, c % 2, :n],
                                                    start=(c == 0), stop=(c == KC - 1))],
                reads=[sq.r[c % 2], self.r_const], writes=[bk[1]])
    rstd = self.rstd
    self.A(rstd.t[:, 0, :n], bk[0][:, :n], AF.Ln, reads=[bk[1]], writes=[rstd.r[0]], bias=self.eps_rms[:, 0:1], scale=1.0)
    self.A(rstd.t[:, 0, :n], rstd.t[:, 0, :n], AF.Exp, reads=[rstd.r[0]], writes=[rstd.r[0]], scale=-0.5)
    for c in range(KC):
        tm = self.tmpf
        i = c % 2
        self.TT(tm.t[:, i, :n], xT.t[:, c, :n], rstd.t[:, 0, :n], ALU.mult, reads=[xT.r[c], rstd.r[0]], writes=[tm.r[i]])
        self.A(yv[:, c, :n], tm.t[:, i, :n], AF.Identity, reads=[tm.r[i], self.r_vec], writes=yres[c],
               bias=0.0, scale=a_vec[:, c:c + 1])


Builder.phase_mods = phase_mods
Builder.phase_w = phase_w
Builder.phase_qkv = phase_qkv
Builder.phase_attn = phase_attn
Builder.phase_b2 = phase_b2
Builder.rms_mod_multi = rms_mod_multi


def tiles_for(ntok):
    t = []
    st = 0
    while st < ntok:
        n = min(512, ntok - st)
        t.append((st, n))
        st += n
    return t


WSPEC = [
    ("mod0", D, 3 * D), ("mod1", D, 3 * D), ("mod2", D, 3 * D), ("mod3", D, 3 * D),
    ("pw1", D, 2 * D), ("pw2", D, D), ("up0", D, 2 * FF), ("down0", FF, D),
    ("qkv", D, 3 * D), ("ow", D, D), ("up1", D, 2 * FF), ("down1", FF, D),
]


def build_fused():
    nc = bass.Bass("TRN2", target_bir_lowering=False)
    ext = lambda name, shape, d=F32, kind="ExternalInput": nc.dram_tensor(name, shape, d, kind=kind).ap()
    itn = lambda name, shape, d: nc.dram_tensor(name, shape, d).ap()
    B = Builder(nc)
    W = {}
    n_first = 0
    for wi, (name, K_, N_) in enumerate(WSPEC):
        Ks = K_ // 4
        if name.startswith("mod"):
            W[name] = ext("w_" + name, [Ks, N_])
            continue
        wb = 1024 if K_ == D else 256
        nb = N_ // wb
        w = {"shard": ext("w_" + name, [Ks, N_]), "Ks": Ks, "wb": wb, "nb": nb, "res": Res("wf_" + name),
             "wsh": itn("wsh_" + name, [nb, Ks, wb], BF16), "full": itn("wf_" + name, [nb, K_, wb], BF16)}
        W[name] = w
        B.bg_add_weight(wi, w)
        if name == "pw2":
            n_first = len(B.bg)
        if name == "down0":
            B.n_bg_mid = len(B.bg) - n_first
        if name == "qkv":
            n_qkv_end = len(B.bg)
    B.bg_pump(n_first)
    B.n_bg_rest = n_qkv_end - n_first - B.n_bg_mid
    part = itn("modpart", [4, 2, 3 * D], F32)
    msum = itn("modsum", [4, 2, 3 * D], F32)
    mb = ext("mb", [4, 3 * D])
    B.phase_mods({"cq": ext("cq", [2, 512]), "mw": [W["mod%d" % m] for m in range(4)], "part": part, "msum": msum})
    x1T = itn("x1T", [KC, 128, NLOC], F32)
    h1loc = itn("h1loc", [KC, 128, NLOC], BF16)
    pm = ext("pm", [128, 1])
    B.phase_a({"xin": ext("xin", [NLOC, D]), "msum": msum, "mb": mb, "svecs": ext("svecs", [7, D]), "pw1b": ext("pw1b", [2, D]),
               "dww": ext("dww", [CW, D]), "ffn_dw": ext("ffn_dw0", [4, 2 * FF]), "pm": pm,
               "diagw": itn("diagw", [KC, 128, CW * 128], BF16),
               "pw1_w": W["pw1"], "pw2_w": W["pw2"], "up_w": W["up0"], "down_w": W["down0"],
               "x1T": x1T, "h1T": h1loc}, tiles_for(NLOC))
    B.bg_pump(max(0, n_qkv_end - B.bg_i))
    h1all = itn("h1all", [KC, 2, 128, NLOC], BF16)
    sem = B.dsem("ag_h1")
    for cc in range(KC):
        B.P.collective("AllGather", [h1loc[cc]], [h1all[cc].rearrange("r p n -> (r p) n")], PAIRS, sem)
    B.barrier(full=True)
    nh = NH // 2
    QT = itn("QT", [nh, 128, SEQ], BF16)
    KT = itn("KT", [nh, 128, SEQ], BF16)
    V = itn("V", [SEQ, nh * 128], BF16)
    oTloc = itn("oTloc", [nh, 2, 128, NP2], BF16)
    io = {"h1all": h1all, "qkv": W["qkv"], "QT": QT, "KT": KT, "V": V, "oT": oTloc}
    B.phase_qkv(io, HALF // 512, nh)
    B.phase_attn2(io, SEQ // 512, nh)
    oall = itn("oall", [nh, 2, 2, 128, NP2], BF16)
    sem = B.dsem("ag_o")
    for h in range(nh):
        for pt in range(2):
            B.P.collective("AllGather", [oTloc[h, pt]], [oall[h, pt].rearrange("r p n -> (r p) n")], PAIRS, sem)
    B.barrier(full=True)
    B.phase_b2({"x1T": x1T, "oT": oall, "msum": msum, "mb": mb, "svecs2": ext("svecs2", [2, D]), "ffn_dw": ext("ffn_dw1", [4, 2 * FF]),
                "pm": pm, "o_w": W["ow"], "up_w": W["up1"], "down_w": W["down1"],
                "y": ext("y", [NLOC, D], F32, "ExternalOutput")}, tiles_for(NLOC), dyn_o=True)
    B.barrier(full=True)
    return nc


def _f32(a):
    return np.ascontiguousarray(np.asarray(a, dtype=np.float32))


def kernel(x, c, mix_norm_g, mix_mod_w, mix_mod_b, cv_pw1_w, cv_pw1_b, cv_dw_w, cv_dw_b, cv_ln_g, cv_ln_b,
           cv_pw2_w, cv_pw2_b, sb_qkv_w, sb_o_w, ffn_norm_g, ffn_mod_w, ffn_mod_b, ffn_up_w, ffn_dw_w,
           ffn_dw_b, ffn_down_w, final_norm_g):
    x = np.asarray(x)
    c = np.asarray(c)
    cores = list(range(8))
    full = {"mod0": mix_mod_w[0], "mod1": ffn_mod_w[0], "mod2": mix_mod_w[1], "mod3": ffn_mod_w[1],
            "pw1": cv_pw1_w[0], "pw2": cv_pw2_w[0], "up0": ffn_up_w[0], "down0": ffn_down_w[0],
            "qkv": sb_qkv_w[0], "ow": sb_o_w[0], "up1": ffn_up_w[1], "down1": ffn_down_w[1]}
    shared = {
        "mb": _f32(np.stack([mix_mod_b[0], ffn_mod_b[0], mix_mod_b[1], ffn_mod_b[1]])),
        "svecs": _f32(np.stack([mix_norm_g[0], ffn_norm_g[0], mix_norm_g[1], cv_dw_b[0], cv_ln_g[0], cv_ln_b[0], cv_pw2_b[0]])),
        "pw1b": _f32(np.asarray(cv_pw1_b[0]).reshape(2, D)),
        "dww": _f32(cv_dw_w[0]),
        "ffn_dw0": _f32(np.concatenate([np.asarray(ffn_dw_w[0]), np.asarray(ffn_dw_b[0])[None]], 0)),
        "ffn_dw1": _f32(np.concatenate([np.asarray(ffn_dw_w[1]), np.asarray(ffn_dw_b[1])[None]], 0)),
        "svecs2": _f32(np.stack([ffn_norm_g[1], final_norm_g])),
    }
    ims = []
    for i in cores:
        b, r = i // 2, i % 2
        im = dict(shared)
        for name, K_, N_ in WSPEC:
            ks = K_ // 4
            q = i % 4
            im["w_" + name] = _f32(np.asarray(full[name])[q * ks:(q + 1) * ks])
        qd = i // 4
        im["cq"] = _f32(c[2 * qd:2 * qd + 2, 512 * q:512 * q + 512])
        im["pm"] = np.full((128, 1), float(r), np.float32)
        if r == 0:
            im["xin"] = _f32(np.concatenate([np.zeros((PRE, D), np.float32), x[b, 0:HALF]], 0))
        else:
            im["xin"] = _f32(x[b, HALF - PRE:SEQ])
        ims.append(im)
    res = run_bass_kernel_spmd(build_fused(), ims, core_ids=cores)
    out = np.empty((4, SEQ, D), np.float32)
    for i in cores:
        b, r = i // 2, i % 2
        out[b, HALF * r:HALF * (r + 1)] = np.asarray(res.results[i]["y"])[PRE:]
    return out


def build_attn_test(nq, nh):
    T = nq * 512
    nc = bass.Bass("TRN2", target_bir_lowering=False)
    dt = lambda name, shape, d=F32, kind="ExternalInput": nc.dram_tensor(name, shape, d, kind=kind).ap()
    io = {"QT": dt("QT", [nh, 128, T], BF16), "KT": dt("KT", [nh, 128, T], BF16), "V": dt("V", [T, nh * 128], BF16),
          "oT": dt("oT", [nh, 2, 128, NP2], BF16, "ExternalOutput")}
    B = Builder(nc)
    B.phase_attn2(io, nq, nh)
    B.barrier(full=True)
    return nc
```

```python
import numpy as np
import ml_dtypes
import concourse.bass as bass
import concourse.mybir as mybir
from concourse.bass_utils import run_bass_kernel_spmd
from contextlib import ExitStack

F32 = mybir.dt.float32
BF16 = mybir.dt.bfloat16
AF = mybir.ActivationFunctionType
ALU = mybir.AluOpType

D = 2048
KC = 16
FF = 5632
FC = 44
CW = 31
NH = 16
SEQ = 8192
HALF = 4096
PRE = 64
NLOC = HALF + PRE
RMS_EPS = 1e-6
LN_EPS = 1e-5
PAIRS = [[0, 1], [2, 3], [4, 5], [6, 7]]
QUADS = [[0, 1, 2, 3], [4, 5, 6, 7]]
NP2 = NLOC


class Res:
    __slots__ = ("name", "w", "r")

    def __init__(self, name=""):
        self.name = name
        self.w = None
        self.r = {}


class Eng:
    def __init__(self, P, name, eng, is_pe=False):
        self.name = name
        self.eng = eng
        self.is_pe = is_pe
        self.sem = P.new_sem("e_" + name)
        self.count = 0
        self.waited = {}


class Prog:
    def __init__(self, nc):
        self.nc = nc
        self.sems = {}
        self.nsem = 0
        self.pe = Eng(self, "pe", nc.tensor, is_pe=True)
        self.act = Eng(self, "act", nc.scalar)
        self.dve = Eng(self, "dve", nc.vector)
        self.pool = Eng(self, "pool", nc.gpsimd)
        self.sp = Eng(self, "sp", nc.sync)
        self.dma_vals = {}

    def new_sem(self, name):
        h = self.nc.semaphore(name).__enter__()
        key = "s%d_%s" % (self.nsem, name)
        self.nsem += 1
        self.sems[key] = h
        return key

    def _wait(self, E, needs):
        best = {}
        for (k, v) in needs:
            if best.get(k, 0) < v:
                best[k] = v
        for k, v in best.items():
            if k == E.sem:
                if E.is_pe:
                    continue
                if E.count - v >= 2:
                    continue
            if E.waited.get(k, 0) >= v:
                continue
            E.eng.wait_ge(self.sems[k], v)
            E.waited[k] = v

    @staticmethod
    def _deps(reads, writes):
        needs = []
        for r in reads:
            if r.w is not None:
                needs.append(r.w)
        for w in writes:
            if w.w is not None:
                needs.append(w.w)
            needs.extend(w.r.items())
        return needs

    @staticmethod
    def _mark(tok, reads, writes):
        k, v = tok
        for w in writes:
            w.w = tok
            w.r = {}
        for r in reads:
            if r.w is tok:
                continue
            if r.r.get(k, 0) < v:
                r.r[k] = v

    def op(self, E, fn, reads=(), writes=()):
        self._wait(E, self._deps(reads, writes))
        ins = fn()
        ins.then_inc(self.sems[E.sem], 1)
        E.count += 1
        self._mark((E.sem, E.count), reads, writes)
        return ins

    def group(self, E, fns, reads=(), writes=()):
        self._wait(E, self._deps(reads, writes))
        ins = None
        for fn in fns:
            ins = fn()
        ins.then_inc(self.sems[E.sem], 1)
        E.count += 1
        self._mark((E.sem, E.count), reads, writes)
        return ins

    def dma(self, Q, out, in_, sem, reads=(), writes=()):
        self._wait(Q, self._deps(reads, writes))
        ins = Q.eng.dma_start(out=out, in_=in_)
        ins.then_inc(self.sems[sem], 16)
        v = self.dma_vals.get(sem, 0) + 16
        self.dma_vals[sem] = v
        self._mark((sem, v), reads, writes)
        return ins

    def collective(self, kind, ins, outs, groups, sem, reads=(), writes=(), op=None):
        Q = self.pool
        self._wait(Q, self._deps(reads, writes))
        ins_ = self.nc.gpsimd.collective_compute(kind, op if op is not None else ALU.bypass, replica_groups=groups,
                                                 ins=ins, outs=outs)
        ins_.then_inc(self.sems[sem], 1)
        v = self.dma_vals.get(sem, 0) + 1
        self.dma_vals[sem] = v
        self._mark((sem, v), reads, writes)

    def wait_all(self, E, ress):
        needs = []
        for r in ress:
            if r.w is not None:
                needs.append(r.w)
            needs.extend(r.r.items())
        self._wait(E, needs)


class Buf:
    def __init__(self, t, n, name):
        self.t = t
        self.r = [Res("%s%d" % (name, i)) for i in range(n)]


class Builder:
    def __init__(self, nc):
        self.nc = nc
        self.P = Prog(nc)
        P = self.P
        self.stack = None
        self.uid = 0
        self.bg_sems = set()
        self.banks = []
        self.bankpairs = []
        for i in range(4):
            t2 = nc.alloc_psum_tensor("bankp%d" % i, [128, 2, 512], F32)
            self.bankpairs.append(t2)
            for j in range(2):
                self.banks.append((t2[:, j, :], Res("bank%d" % (2 * i + j))))
        self.bank_i = 0
        self.dsems = {}
        self.ident = self.sb("ident", [128, 128], F32)
        self.ones_bf = self.sb("ones_bf", [128, 128], BF16)
        self.r_const = Res("const")
        ones_f = self.sb("ones_f", [128, 128], F32)
        P.op(P.pool, lambda: nc.gpsimd.memset(ones_f[:], 1.0), writes=[self.r_const])
        P.op(P.pool, lambda: nc.gpsimd.affine_select(self.ident[:], ones_f[:], [[1, 128]], ALU.is_equal, 0.0,
                                                     base=0, channel_multiplier=-1),
             reads=[self.r_const], writes=[self.r_const])
        P.op(P.pool, lambda: nc.gpsimd.memset(self.ones_bf[:], 1.0), writes=[self.r_const])
        self.ones_f = ones_f
        self.wstage = [(self.sb("wstage%d" % i, [128, 2048], BF16), Res("wstage%d" % i), self.dsem("wstage%d" % i)) for i in range(2)]
        self.wstage_i = 0
        pid = nc.sync.partition_id()
        self.par = pid % 2
        self.bsel = (pid % 4) // 2
        self.bg = []
        self.bg_i = 0
        self.bg_wo_queue = None

    def bg_add_weight(self, wi, w):
        P = self.P
        sem = self.dsem("agw%d" % wi)
        self.bg_sems.add(sem)
        Ks, wb, nb = w["Ks"], w["wb"], w["nb"]
        N = wb * nb
        sres = [Res("wsh%d_%d" % (wi, k)) for k in range(2)]

        def unit(r0, c0, wd):
            t, r, sm = self.wstage[self.wstage_i % 2]
            k = self.wstage_i % 2
            self.wstage_i += 1
            P.dma(P.pool, t[:, :wd], w["shard"][r0:r0 + 128, c0:c0 + wd], sm, writes=[r])
            P.dma(self.bg_wo_queue or P.sp, w["wsh"][c0 // wb:(c0 + wd) // wb, r0:r0 + 128, :].rearrange("b p n -> p b n"),
                  t[:, :wd].rearrange("p (b n) -> p b n", n=wb), sm, reads=[r], writes=[sres[k]])
        for r0 in range(0, Ks, 128):
            for c0 in range(0, N, 2048):
                wd = min(2048, N - c0)
                self.bg.append(lambda r0=r0, c0=c0, wd=wd: unit(r0, c0, wd))

        def gather(bi):
            P.collective("AllGather", [w["wsh"][bi]], [w["full"][bi]], QUADS, sem, reads=sres, writes=[w["res"]])
        for bi in range(nb):
            self.bg.append(lambda bi=bi: gather(bi))
        w["bg_end"] = len(self.bg)

    def bg_pump(self, n=None):
        end = len(self.bg) if n is None else min(len(self.bg), self.bg_i + n)
        while self.bg_i < end:
            self.bg[self.bg_i]()
            self.bg_i += 1

    def sb(self, name, shape, dt):
        if self.stack is None:
            return self.nc.alloc_sbuf_tensor(name, shape, dt)
        self.uid += 1
        return self.stack.enter_context(self.nc.sbuf_tensor("%s_%d" % (name, self.uid), shape, dt))

    def begin_phase(self):
        self.stack = ExitStack()

    def barrier(self, full=False):
        P = self.P
        engs = [P.pe, P.act, P.dve, P.pool, P.sp]
        needs = [(e.sem, e.count) for e in engs if e.count > 0] + \
            [kv for kv in P.dma_vals.items() if full or kv[0] not in self.bg_sems]
        for e in engs:
            P._wait(e, [x for x in needs if x[0] != e.sem])

    def end_phase(self):
        self.barrier()
        self.stack.close()
        self.stack = None

    def dsem(self, name):
        if name not in self.dsems:
            self.dsems[name] = self.P.new_sem("d_" + name)
        return self.dsems[name]

    def bank(self):
        b = self.banks[self.bank_i]
        self.bank_i = (self.bank_i + 1) % 8
        return b

    def mm(self, bank, pairs, n, reads, m=128):
        nc = self.nc
        t, r = bank
        fns = []
        last = len(pairs) - 1
        for i, (l, rh) in enumerate(pairs):
            fns.append(lambda l=l, rh=rh, i=i: nc.tensor.matmul(t[:m, :n], l, rh, start=(i == 0), stop=(i == last)))
        self.P.group(self.P.pe, fns, reads=reads, writes=[r])

    def A(self, out, in_, func, reads, writes, bias=0.0, scale=1.0):
        nc = self.nc
        return self.P.op(self.P.act, lambda: nc.scalar.activation(out, in_, func, bias=bias, scale=scale),
                         reads=reads, writes=writes)

    def TT(self, out, a, b, op, reads, writes, eng=None):
        E = eng or self.P.dve
        return self.P.op(E, lambda: E.eng.tensor_tensor(out, a, b, op), reads=reads, writes=writes)

    def TS(self, out, a, s1, s2, op0, op1, reads, writes, eng=None):
        E = eng or self.P.dve
        if op1 is None:
            return self.P.op(E, lambda: E.eng.tensor_scalar(out, a, s1, None, op0), reads=reads, writes=writes)
        return self.P.op(E, lambda: E.eng.tensor_scalar(out, a, s1, s2, op0, op1), reads=reads, writes=writes)

    def STT(self, out, a, s, b, op0, op1, reads, writes):
        nc = self.nc
        return self.P.op(self.P.dve, lambda: nc.vector.scalar_tensor_tensor(out, a, s, b, op0, op1),
                         reads=reads, writes=writes)

    def CP(self, out, in_, reads, writes, eng=None):
        E = eng or self.P.dve
        return self.P.op(E, lambda: E.eng.tensor_copy(out, in_), reads=reads, writes=writes)

    def init_wstream(self, nslots=3):
        self.wslots = []
        for i in range(nslots):
            t = self.sb("wslot%d" % i, [128, 8192], BF16)
            self.wslots.append((t, Res("wslot%d" % i), self.dsem("wslot%d" % i)))
        self.wplan = []
        self.wnext_issue = 0
        self.wnext_use = 0

    def wplan_add(self, w, k_rows, pieces):
        kc = k_rows // 128
        tot = sum(p[-1] for p in pieces)
        assert kc * tot <= 8192
        self.wplan.append((w, kc, tot, pieces))

    def _wissue(self, upto):
        P = self.P
        while self.wnext_issue < min(upto, len(self.wplan)):
            i = self.wnext_issue
            w, kc, tot, pieces = self.wplan[i]
            t, r, s = self.wslots[i % len(self.wslots)]
            if pieces is None:
                P.dma(P.pool, t[:, 0:kc * tot], w[0], s, reads=[w[1]], writes=[r])
                self.wnext_issue += 1
                continue
            if self.bg_i < w.get("bg_end", 0):
                self.bg_pump(w["bg_end"] - self.bg_i)
            dst = t[:, 0:kc * tot].rearrange("p (kc n) -> p kc n", kc=kc)
            wb = w["wb"]
            o = 0
            for pc in pieces:
                if len(pc) == 2:
                    c0, ncols = pc
                    bi, off = c0 // wb, c0 % wb
                    assert off + ncols <= wb
                    src = w["full"][bi].rearrange("(kc p) n -> p kc n", p=128)[:, :, off:off + ncols]
                else:
                    bi, off, ncols = pc
                    src = w["full"][bass.ds(bi, 1)].rearrange("o (kc p) n -> p (o kc) n", p=128)[:, :, off:off + ncols]
                P.dma(P.pool, dst[:, :, o:o + ncols], src, s, reads=[w["res"]], writes=[r])
                o += ncols
            self.wnext_issue += 1

    def wget(self):
        i = self.wnext_use
        self._wissue(i + len(self.wslots))
        if getattr(self, "bg_budget", 0) > 0 and i % 4 == 0:
            self.bg_pump(1)
            self.bg_budget -= 1
        w, kc, tot, pieces = self.wplan[i]
        t, r, s = self.wslots[i % len(self.wslots)]
        self.wnext_use += 1
        return t[:, 0:kc * tot].rearrange("p (kc n) -> p kc n", kc=kc), r

    def rms_mod(self, xT, hT, n, a_vec, sh_vec):
        nc, P = self.nc, self.P
        sq = self.tmpbf
        bk = self.bank()
        for c in range(KC):
            self.A(sq.t[:, c % 2, :n], xT.t[:, c, :n], AF.Square, reads=[xT.r[c]], writes=[sq.r[c % 2]])
            nc_ = nc
            P.group(P.pe, [lambda c=c: nc_.tensor.matmul(bk[0][:, :n], self.ones_bf[:], sq.t[:, c % 2, :n],
                                                          start=(c == 0), stop=(c == KC - 1))],
                    reads=[sq.r[c % 2], self.r_const], writes=[bk[1]])
        rstd = self.rstd
        self.A(rstd.t[:, 0, :n], bk[0][:, :n], AF.Ln, reads=[bk[1]], writes=[rstd.r[0]], bias=self.eps_rms[:, 0:1], scale=1.0)
        self.A(rstd.t[:, 0, :n], rstd.t[:, 0, :n], AF.Exp, reads=[rstd.r[0]], writes=[rstd.r[0]], scale=-0.5)
        for c in range(KC):
            tm = self.tmpf
            i = c % 2
            self.TT(tm.t[:, i, :n], xT.t[:, c, :n], rstd.t[:, 0, :n], ALU.mult, reads=[xT.r[c], rstd.r[0]], writes=[tm.r[i]])
            self.A(hT.t[:, c, :n], tm.t[:, i, :n], AF.Identity, reads=[tm.r[i], self.r_vec], writes=[hT.r[c]],
                   bias=(sh_vec[:, c:c + 1] if sh_vec is not None else 0.0), scale=a_vec[:, c:c + 1])

    def linear(self, act, kc_n, n, slots, epilogue):
        for ids in slots:
            wv, wr = self.wget()
            for jj, cid in enumerate(ids):
                bk = self.bank()
                pairs = [(wv[:, k, jj * 128:(jj + 1) * 128], act.t[:, k, :n]) for k in range(kc_n)]
                self.mm(bk, pairs, n, reads=[wr] + [act.r[k] for k in range(kc_n)])
                epilogue(bk, cid)

    def load_rows_T(self, name, ap2d, rows, ncols):
        nc, P = self.nc, self.P
        nch = ncols // 128
        out = self.sb("m_" + name, [128, nch, rows], F32)
        r = self.r_vec
        stg, rs = self.vec_stage, self.vec_stage_r
        for c0 in range(0, ncols, 2048):
            w = min(2048, ncols - c0)
            if isinstance(ap2d, list):
                ro = 0
                for (apx, nr) in ap2d:
                    P.dma(P.sp, stg[ro:ro + nr, :w], apx[0:nr, c0:c0 + w], self.xrow[0][2], writes=[rs])
                    ro += nr
            else:
                P.dma(P.sp, stg[:rows, :w], ap2d[0:rows, c0:c0 + w], self.xrow[0][2], writes=[rs])
            for cc in range(w // 128):
                bk = self.bank()
                P.group(P.pe, [lambda cc=cc, bk=bk: nc.tensor.transpose(bk[0][:, :rows], stg[:rows, cc * 128:(cc + 1) * 128],
                                                                        self.ident[:rows, :rows])],
                        reads=[rs, self.r_const], writes=[bk[1]])
                self.CP(out[:, c0 // 128 + cc, :], bk[0][:, :rows], reads=[bk[1]], writes=[r])
        return out

    def common_init(self):
        nc, P = self.nc, self.P
        self.xrow = [(self.sb("xrow0", [128, D], F32), Res("xrow0"), self.dsem("xrow0"))] * 2
        self.xrow_i = 0
        self.vec_stage = self.xrow[0][0]
        self.vec_stage_r = self.xrow[0][1]
        self.r_vec = Res("vecs")
        self.sqb = Buf(self.sb("sqb", [128, 2, 512], BF16), 2, "sqb")
        self.tmpf = Buf(self.sb("tmpf", [128, 2, 512], F32), 2, "tmpf")
        self.tmpbf = Buf(self.sb("tmpbf", [128, 2, 512], BF16), 2, "tmpbf")
        self.rstd = Buf(self.sb("rstd", [128, 1, 512], F32), 1, "rstd")
        self.eps_rms = self.sb("eps_rms", [128, 2], F32)
        P.op(P.pool, lambda: nc.gpsimd.memset(self.eps_rms[:, 0:1], float(D * RMS_EPS)), writes=[self.r_const])
        P.op(P.pool, lambda: nc.gpsimd.memset(self.eps_rms[:, 1:2], float(LN_EPS)), writes=[self.r_const])

    def mod_vecs(self, mv, m, gvec, name):
        a = self.sb("a_" + name, [128, KC], F32)
        r = self.r_vec
        self.TS(a[:, :], mv[:, :, 3 * m + 1], 1.0, float(np.sqrt(D)), ALU.add, ALU.mult, reads=[r], writes=[r])
        self.TT(a[:, :], a[:, :], gvec, ALU.mult, reads=[r], writes=[r])
        return a, mv[:, :, 3 * m], mv[:, :, 3 * m + 2]

    def load_mods(self, io):
        nc = self.nc
        bsel = self.bsel
        self.P._wait(self.P.sp, [self.ar_token])
        src = [(io["msum"][m, bass.ds(bsel, 1), :].rearrange("o (t k) -> (o t) k", t=3), 3) for m in range(4)]
        mv = self.load_rows_T("mods", src, 12, D)
        bv = self.load_rows_T("modb", io["mb"].rearrange("m (t k) -> (m t) k", t=3), 12, D)
        self.TT(mv[:, :, :], mv[:, :, :], bv[:, :, :], ALU.add, reads=[self.r_vec], writes=[self.r_vec])
        return mv

    def load_mods_old(self, modall):
        nc, P = self.nc, self.P
        out = self.sb("m_mods", [128, KC, 12], F32)
        stg, rs = self.vec_stage, self.vec_stage_r
        P.dma(P.sp, stg[:12, :].rearrange("q (r k) -> q r k", r=2), modall.rearrange("r m t k -> (m t) r k"),
              self.xrow[0][2], writes=[rs])
        for cc in range(KC):
            bk = self.bank()
            P.group(P.pe, [lambda cc=cc, bk=bk: nc.tensor.transpose(bk[0][:, :12], stg[:12, cc * 128:(cc + 1) * 128],
                                                                    self.ident[:12, :12])],
                    reads=[rs, self.r_const], writes=[bk[1]])
            self.CP(out[:, cc, :], bk[0][:, :12], reads=[bk[1]], writes=[self.r_vec])
        return out

    def load_x_tile(self, xin, start, n, xT):
        nc, P = self.nc, self.P
        nb = (n + 127) // 128
        for b in range(nb):
            nt = min(128, n - b * 128)
            xr, rr, sm = self.xrow[self.xrow_i % 2]
            self.xrow_i += 1
            P.dma(P.sp, xr[:nt, :], xin[start + b * 128:start + b * 128 + nt, :], sm, writes=[rr])
            for g in range(4):
                bk = self.bank()
                fns = []
                for q in range(4):
                    c = 4 * g + q
                    fns.append(lambda q=q, c=c, bk=bk: nc.tensor.transpose(bk[0][:, q * 128:q * 128 + nt],
                                                                           xr[:nt, c * 128:(c + 1) * 128], self.ident[:nt, :nt]))
                P.group(P.pe, fns, reads=[rr, self.r_const], writes=[bk[1]])
                src = bk[0][:, :].rearrange("p (q t) -> p q t", q=4)[:, :, :nt]
                dst = xT.t[:, 4 * g:4 * g + 4, b * 128:b * 128 + nt]
                eng = self.P.act if (g % 2 == 0) else self.P.dve
                if eng is self.P.act:
                    P.op(P.act, lambda dst=dst, src=src: nc.scalar.copy(dst, src), reads=[bk[1]],
                         writes=[xT.r[4 * g + q] for q in range(4)])
                else:
                    self.CP(dst, src, reads=[bk[1]], writes=[xT.r[4 * g + q] for q in range(4)])

    def phase_a(self, io, tiles):
        nc, P = self.nc, self.P
        self.begin_phase()
        self.common_init()
        self.init_wstream(3)
        mv = self.load_mods(io)
        sv = self.load_rows_T("svA", io["svecs"], 7, D)
        pw1b = self.load_rows_T("pw1b", io["pw1b"], 2, D)
        dww = self.load_rows_T("dww", io["dww"], CW, D)
        fdw = self.load_rows_T("fdw", io["ffn_dw"], 4, 2 * FF)
        pm = self.sb("pm_sb", [128, 1], F32)
        P.dma(P.sp, pm[:, :], io["pm"][:, :], self.dsem("pm"), writes=[self.r_vec])
        rv = self.r_vec
        a0, sh0, g0 = self.mod_vecs(mv, 0, sv[:, :, 0], "mix0")
        a1, sh1, g1 = self.mod_vecs(mv, 1, sv[:, :, 1], "ffn0")
        a2, sh2, g2 = self.mod_vecs(mv, 2, sv[:, :, 2], "mix1")
        gb2 = self.sb("gb2", [128, KC], F32)
        self.TT(gb2[:, :], g0, sv[:, :, 6], ALU.mult, reads=[rv], writes=[rv])

        xT = Buf(self.sb("xT", [128, KC, 512], F32), KC, "xT")
        hT = Buf(self.sb("hT", [128, KC, 512], BF16), KC, "hT")
        uT = Buf(self.sb("uT", [128, KC, CW - 1 + 512], BF16), KC, "uT")
        gTt = self.sb("gTt", [128, FC, 512], BF16)
        gT = Buf(gTt, FC, "gT")
        cvt = gTt[:, 0:32, :].bitcast(F32).rearrange("p a b -> p (a b)").rearrange("p (c n) -> p c n", c=KC)
        Tg = Buf(self.sb("Tg", [128, 2, 512], F32), 2, "Tg")
        Tv = Buf(self.sb("Tv", [128, 2, 512], F32), 2, "Tv")
        Hff = self.sb("Hff", [128, 2 * FC, 2], F32)
        rH = [Res("Hff%d" % i) for i in range(2 * FC)]
        stat = Buf(self.sb("stat", [128, 3, 512], F32), 3, "stat")
        P.op(P.pool, lambda: nc.gpsimd.memset(Hff[:, :, :], 0.0), writes=rH)
        P.op(P.pool, lambda: nc.gpsimd.memset(uT.t[:, :, 0:CW - 1], 0.0), writes=uT.r)

        r_diag = Res("diagw")
        for _ in tiles:
            for i in range(8):
                self.wplan_add(io["pw1_w"], D, [(256 * i, 256), (D + 256 * i, 256)])
            for c in range(KC):
                self.wplan.append(((io["diagw"][c], r_diag), CW, 128, None))
            for i in range(4):
                self.wplan_add(io["pw2_w"], D, [(512 * i, 512)])
            for i in range(22):
                self.wplan_add(io["up_w"], D, [(256 * i, 256), (FF + 256 * i, 256)])
            for i in range(16):
                self.wplan_add(io["down_w"], FF, [(128 * i, 128)])

        ident_bf = self.sb("ident_bf", [128, 128], BF16)
        self.CP(ident_bf[:, :], self.ident[:, :], reads=[self.r_const], writes=[self.r_const])
        for c in range(KC):
            t, r, sm = self.wslots[c % 3]
            dv = t[:, 0:CW * 128].rearrange("p (k m) -> p k m", k=CW)
            for k in range(CW):
                self.TS(dv[:, k, :], ident_bf[:, :], dww[:, c, k:k + 1], None, ALU.mult, None, reads=[self.r_const, rv],
                        writes=[r])
            P.dma(P.sp, io["diagw"][c], t[:, 0:CW * 128], sm, reads=[r], writes=[r_diag])

        x1T_d, h1T_d = io["x1T"], io["h1T"]
        osem_x = [self.dsem("ox0"), self.dsem("ox1")]
        osem_h = [self.dsem("oh0"), self.dsem("oh1")]

        per_tile = (getattr(self, "n_bg_rest", 0) + len(tiles) - 2) // max(1, len(tiles) - 1)
        for ti, (start, n) in enumerate(tiles):
            first = (ti == 0)
            if not first:
                self.bg_pump(getattr(self, "bg_budget", 0))
                self.bg_budget = per_tile
            self.load_x_tile(io["xin"], start, n, xT)
            self.rms_mod(xT, hT, n, a0, sh0)
            if first:
                pass
            sg = self.tmpf

            def ep_pw1(bk, cid, n=n, first=first):
                if cid < KC:
                    self._valbank[cid] = bk
                else:
                    j = cid - KC
                    vb = self._valbank.pop(j)
                    i = j % 2
                    self.A(sg.t[:, i, :n], bk[0][:, :n], AF.Sigmoid, reads=[bk[1], rv], writes=[sg.r[i]],
                           bias=pw1b[:, j, 1:2], scale=1.0)
                    self.STT(uT.t[:, j, CW - 1:CW - 1 + n], vb[0][:, :n], pw1b[:, j, 0:1], sg.t[:, i, :n], ALU.add, ALU.mult,
                             reads=[vb[1], sg.r[i], rv], writes=[uT.r[j]])
            self._valbank = {}
            slots = [[2 * i, 2 * i + 1, KC + 2 * i, KC + 2 * i + 1] for i in range(8)]
            self.linear(hT, KC, n, slots, ep_pw1)
            if first:
                self.TS(uT.t[:, :, CW - 1:CW - 1 + PRE], uT.t[:, :, CW - 1:CW - 1 + PRE], pm[:, 0:1], None, ALU.mult, None,
                        reads=uT.r + [rv], writes=uT.r)
            cv_r = [[gT.r[2 * c], gT.r[2 * c + 1]] for c in range(KC)]
            for c in range(KC):
                wv, wr = self.wget()
                dvw = wv[:, :, :]
                bkc = self.bank()
                pairs = [(dvw[:, k, :], uT.t[:, c, k:k + n]) for k in range(CW)]
                self.mm(bkc, pairs, n, reads=[wr, uT.r[c]])
                self.A(cvt[:, c, :n], bkc[0][:, :n], AF.Identity, reads=[bkc[1], rv], writes=cv_r[c],
                       bias=sv[:, c, 3:4], scale=1.0)
            bk_m = self.bank()
            bk_s = self.bank()
            for c in range(KC):
                i = c % 2
                self.A(self.tmpbf.t[:, i, :n], cvt[:, c, :n], AF.Identity, reads=cv_r[c], writes=[self.tmpbf.r[i]])
                P.group(P.pe, [lambda c=c, i=i: nc.tensor.matmul(bk_m[0][:, :n], self.ones_bf[:], self.tmpbf.t[:, i, :n],
                                                                 start=(c == 0), stop=(c == KC - 1))],
                        reads=[self.tmpbf.r[i], self.r_const], writes=[bk_m[1]])
                self.A(self.sqb.t[:, i, :n], cvt[:, c, :n], AF.Square, reads=cv_r[c], writes=[self.sqb.r[i]])
                P.group(P.pe, [lambda c=c, i=i: nc.tensor.matmul(bk_s[0][:, :n], self.ones_bf[:], self.sqb.t[:, i, :n],
                                                                 start=(c == 0), stop=(c == KC - 1))],
                        reads=[self.sqb.r[i], self.r_const], writes=[bk_s[1]])
            mean, var, rs = stat.t[:, 0, :n], stat.t[:, 1, :n], stat.t[:, 2, :n]
            self.A(mean, bk_m[0][:, :n], AF.Identity, reads=[bk_m[1]], writes=[stat.r[0]], scale=1.0 / D)
            self.TT(var, mean, mean, ALU.mult, reads=[stat.r[0]], writes=[stat.r[1]])
            self.STT(var, bk_s[0][:, :n], 1.0 / D, var, ALU.mult, ALU.subtract, reads=[bk_s[1], stat.r[1]], writes=[stat.r[1]])
            self.A(rs, var, AF.Ln, reads=[stat.r[1], self.r_const], writes=[stat.r[2]], bias=self.eps_rms[:, 1:2], scale=1.0)
            self.A(rs, rs, AF.Exp, reads=[stat.r[2]], writes=[stat.r[2]], scale=-0.5)
            for c in range(KC):
                i = c % 2
                self.TT(sg.t[:, i, :n], cvt[:, c, :n], mean, ALU.subtract, reads=cv_r[c] + [stat.r[0]], writes=[sg.r[i]])
                self.TT(sg.t[:, i, :n], sg.t[:, i, :n], rs, ALU.mult, reads=[sg.r[i], stat.r[2]], writes=[sg.r[i]])
                self.A(hT.t[:, c, :n], sg.t[:, i, :n], AF.Silu, reads=[sg.r[i], rv], writes=[hT.r[c]],
                       bias=sv[:, c, 5:6], scale=sv[:, c, 4:5])
            self.CP(uT.t[:, :, 0:CW - 1], uT.t[:, :, n:n + CW - 1], reads=uT.r, writes=uT.r)

            def ep_pw2(bk, cid, n=n):
                i = cid % 2
                self.A(sg.t[:, i, :n], bk[0][:, :n], AF.Identity, reads=[bk[1], rv], writes=[sg.r[i]],
                       bias=gb2[:, cid:cid + 1], scale=g0[:, cid:cid + 1])
                self.TT(xT.t[:, cid, :n], xT.t[:, cid, :n], sg.t[:, i, :n], ALU.add, reads=[xT.r[cid], sg.r[i]], writes=[xT.r[cid]])
            self.linear(hT, KC, n, [[4 * i + q for q in range(4)] for i in range(4)], ep_pw2)

            self.rms_mod(xT, hT, n, a1, sh1)
            if first:
                self.TS(hT.t[:, :, 0:PRE], hT.t[:, :, 0:PRE], pm[:, 0:1], None, ALU.mult, None, reads=hT.r + [rv], writes=hT.r)
            self.ffn(hT, xT, gT, Tg, Tv, Hff, rH, fdw, g1, n)

            k = ti % 2
            P.dma(P.sp, x1T_d[:, :, start:start + n].rearrange("c p n -> p c n"), xT.t[:, :, :n], osem_x[k], reads=xT.r)
            self.rms_mod(xT, hT, n, a2, sh2)
            P.dma(P.sp, h1T_d[:, :, start:start + n].rearrange("c p n -> p c n"), hT.t[:, :, :n], osem_h[k], reads=hT.r)
        self.end_phase()

    def ffn(self, hT, xT, gT, Tg, Tv, Hff, rH, fdw, gate, n):
        nc, P = self.nc, self.P
        rv = self.r_vec

        def conv3(bk, ch, T, i):
            p = bk[0]
            w0, w1, w2, b = fdw[:, ch, 0:1], fdw[:, ch, 1:2], fdw[:, ch, 2:3], fdw[:, ch, 3:4]
            t = T.t[:, i, :]
            self.A(t[:, :n], p[:, :n], AF.Identity, reads=[bk[1], rv], writes=[T.r[i]], bias=b, scale=w2)
            self.STT(t[:, 1:n], p[:, 0:n - 1], w1, t[:, 1:n], ALU.mult, ALU.add, reads=[bk[1], rv, T.r[i]], writes=[T.r[i]])
            self.STT(t[:, 2:n], p[:, 0:n - 2], w0, t[:, 2:n], ALU.mult, ALU.add, reads=[bk[1], rv, T.r[i]], writes=[T.r[i]])
            h = Hff[:, ch, :]
            self.STT(t[:, 0:1], h[:, 1:2], w1, t[:, 0:1], ALU.mult, ALU.add, reads=[rH[ch], rv, T.r[i]], writes=[T.r[i]])
            self.STT(t[:, 0:2], h[:, 0:2], w0, t[:, 0:2], ALU.mult, ALU.add, reads=[rH[ch], rv, T.r[i]], writes=[T.r[i]])
            P.op(P.act, lambda: nc.scalar.copy(h[:, 0:2], p[:, n - 2:n]), reads=[bk[1], T.r[i]], writes=[rH[ch]])

        def ep_up(bk, cid, n=n):
            if cid < FC:
                conv3(bk, cid, Tg, cid % 2)
            else:
                j = cid - FC
                i = j % 2
                conv3(bk, cid, Tv, i)
                self.A(Tg.t[:, i, :n], Tg.t[:, i, :n], AF.Silu, reads=[Tg.r[i]], writes=[Tg.r[i]])
                self.TT(gT.t[:, j, :n], Tg.t[:, i, :n], Tv.t[:, i, :n], ALU.mult, reads=[Tg.r[i], Tv.r[i]], writes=[gT.r[j]])
        slots = [[2 * i, 2 * i + 1, FC + 2 * i, FC + 2 * i + 1] for i in range(22)]
        self.linear(hT, KC, n, slots, ep_up)

        def ep_down(bk, cid, n=n):
            self.STT(xT.t[:, cid, :n], bk[0][:, :n], gate[:, cid:cid + 1], xT.t[:, cid, :n], ALU.mult, ALU.add,
                     reads=[bk[1], rv, xT.r[cid]], writes=[xT.r[cid]])
        self.linear(gT, FC, n, [[i] for i in range(16)], ep_down)


def phase_mods(self, io):
    nc, P = self.nc, self.P
    self.begin_phase()
    self.common_init()
    self.init_wstream(3)
    rv = self.r_vec
    cT = self.load_rows_T("cq", io["cq"], 2, 512)
    sc = self.sb("sc", [128, 4, 2], BF16)
    self.A(sc[:, :, :], cT[:, :, :], AF.Silu, reads=[rv], writes=[rv])
    NM = 3 * D
    orow = self.sb("orow", [2, NM], F32)
    r_o = Res("orow")
    sem_o = self.dsem("orow")
    wi = 0
    for m in range(4):
        for g3 in range(3):
            t, r, sm = self.wslots[wi % 3]
            wi += 1
            wv = t[:, :].rearrange("p (kc n) -> p kc n", kc=4)
            P.dma(P.pool, wv, io["mw"][m].rearrange("(kc p) n -> p kc n", p=128)[:, :, g3 * 2048:(g3 + 1) * 2048], sm, writes=[r])
            for g in range(4):
                bk = self.bank()
                pairs = [(sc[:, k, :], wv[:, k, g * 512:(g + 1) * 512]) for k in range(4)]
                self.mm(bk, pairs, 512, reads=[r, rv], m=2)
                o = g3 * 2048 + g * 512
                self.CP(orow[0:2, o:o + 512], bk[0][0:2, :512], reads=[bk[1]], writes=[r_o])
        P.dma(P.sp, io["part"][m], orow[0:2, :], sem_o, reads=[r_o])
    ar_sem = self.dsem("ar_mods")
    P.collective("AllReduce", [io["part"].rearrange("m b k -> (m b) k")], [io["msum"].rearrange("m b k -> (m b) k")], QUADS,
                 ar_sem, reads=[r_o], op=ALU.add)
    self.ar_token = (ar_sem, P.dma_vals[ar_sem])
    self.bg_sems.add(ar_sem)
    self.end_phase()


def phase_w(self, wl):
    nc, P = self.nc, self.P
    self.begin_phase()
    self.init_wstream(3)
    i = 0
    for wi, w in enumerate(wl):
        sem = self.dsem("agw%d" % wi)
        self.bg_sems.add(sem)
        Ks, wb, nb = w["Ks"], w["wb"], w["nb"]
        N = wb * nb
        sres = [Res("wsh%d_%d" % (wi, k)) for k in range(3)]
        for r0 in range(0, Ks, 128):
            for c0 in range(0, N, 8192):
                wd = min(8192, N - c0)
                t, r, sm = self.wslots[i % 3]
                P.dma(P.pool, t[:, :wd], w["shard"][r0:r0 + 128, c0:c0 + wd], sm, writes=[r])
                P.dma(P.sp, w["wsh"][c0 // wb:(c0 + wd) // wb, r0:r0 + 128, :].rearrange("b p n -> p b n"),
                      t[:, :wd].rearrange("p (b n) -> p b n", n=wb), sm, reads=[r], writes=[sres[i % 3]])
                i += 1
        for bi in range(nb):
            P.collective("AllGather", [w["wsh"][bi]], [w["full"][bi]], QUADS, sem, reads=sres, writes=[w["res"]])
    self.end_phase()


def phase_qkv(self, io, ntile, nh):
    nc, P = self.nc, self.P
    self.begin_phase()
    self.init_wstream(3)
    ns = nh // 4
    hTs = [Buf(self.sb("hq%d" % i, [128, KC, 512], BF16), KC, "hq%d" % i) for i in range(2)]
    hsem = [self.dsem("hq0"), self.dsem("hq1")]
    qst = Buf(self.sb("qst", [128, nh, 512], BF16), nh, "qst")
    kst = Buf(self.sb("kst", [128, nh, 512], BF16), nh, "kst")
    vst = Buf(self.sb("vst", [128, 4, nh * 128], BF16), 4, "vst")
    sq, sk, sv_ = self.dsem("oq"), self.dsem("ok"), self.dsem("ov")
    par = nc.gpsimd.partition_id() % 2
    for _ in range(2 * ntile):
        for which in range(3):
            for i in range(ns):
                self.wplan_add(io["qkv"], D, [(par + 2 * which, 512 * i, 512)])
    it = 0
    for s in range(2):
        for t in range(ntile):
            p0 = (s * ntile + t) * 512
            lc = PRE + 512 * t
            hT = hTs[it % 2]
            P.dma(P.sp, hT.t[:, :, :], io["h1all"][:, s, :, lc:lc + 512].rearrange("c p n -> p c n"), hsem[it % 2], writes=hT.r)
            it += 1
            for (st, dst, sem) in ((qst, io["QT"], sq), (kst, io["KT"], sk)):
                def ep(bk, cid, st=st):
                    if cid % 2 == 0:
                        P.op(P.act, lambda: nc.scalar.copy(st.t[:, cid, :], bk[0][:, :512]), reads=[bk[1]], writes=[st.r[cid]])
                    else:
                        self.CP(st.t[:, cid, :], bk[0][:, :512], reads=[bk[1]], writes=[st.r[cid]])
                self.linear(hT, KC, 512, [[4 * i + q for q in range(4)] for i in range(ns)], ep)
                P.dma(P.sp, dst[:, :, p0:p0 + 512].rearrange("h p n -> p h n"), st.t[:, :, :], sem, reads=st.r)
            for cg in range(ns):
                wv, wr = self.wget()
                for tb in range(4):
                    bk = self.bank()
                    pairs = [(hT.t[:, k, tb * 128:(tb + 1) * 128], wv[:, k, :]) for k in range(KC)]
                    self.mm(bk, pairs, 512, reads=[wr] + hT.r)
                    if tb % 2 == 0:
                        P.op(P.act, lambda tb=tb, bk=bk: nc.scalar.copy(vst.t[:, tb, cg * 512:(cg + 1) * 512], bk[0][:, :512]),
                             reads=[bk[1]], writes=[vst.r[tb]])
                    else:
                        self.CP(vst.t[:, tb, cg * 512:(cg + 1) * 512], bk[0][:, :512], reads=[bk[1]], writes=[vst.r[tb]])
            P.dma(P.sp, io["V"][p0:p0 + 512, :].rearrange("(tb p) f -> p tb f", p=128), vst.t[:, :, :], sv_, reads=vst.r)
    self.end_phase()


def phase_attn(self, io, nq, nh):
    nc, P = self.nc, self.P
    self.begin_phase()
    T = nq * 512
    NKB = nq * 4
    scale = 1.0 / float(np.sqrt(128.0))
    rc = self.r_const
    masks = self.sb("masks", [128, 4, 512], F32)
    tri = self.sb("tri", [128, 128], BF16)
    comp = self.sb("comp", [128, 128], BF16)
    onesw = self.sb("onesw", [128, 512], F32)
    P.op(P.pool, lambda: nc.gpsimd.memset(onesw[:, :], 1.0), writes=[rc])
    for i in range(4):
        P.op(P.pool, lambda i=i: nc.gpsimd.affine_select(masks[:, i, :], onesw[:, :], [[1, 512]], ALU.is_gt, 0.0,
                                                         base=-128 * i, channel_multiplier=-1), reads=[rc], writes=[rc])
    P.op(P.pool, lambda: nc.gpsimd.affine_select(tri[:, :], onesw[:, 0:128], [[-1, 128]], ALU.is_gt, 0.0,
                                                 base=1, channel_multiplier=1), reads=[rc], writes=[rc])
    P.op(P.pool, lambda: nc.gpsimd.affine_select(comp[:, :], onesw[:, 0:128], [[1, 128]], ALU.is_gt, 0.0,
                                                 base=0, channel_multiplier=-1), reads=[rc], writes=[rc])
    Ksb = [(self.sb("Ksb%d" % i, [128, T], BF16), Res("Ksb%d" % i), self.dsem("Ksb%d" % i)) for i in range(2)]
    Vsb = [(self.sb("Vsb%d" % i, [128, NKB, 128], BF16), Res("Vsb%d" % i), self.dsem("Vsb%d" % i)) for i in range(2)]
    NL = 2
    Eb = [Buf(self.sb("Eb%d" % l, [128, 3, 512], F32), 3, "Eb%d" % l) for l in range(NL)]
    Lb = [Buf(self.sb("Lb%d" % l, [128, 3, 512], BF16), 3, "Lb%d" % l) for l in range(NL)]
    Gb = [Buf(self.sb("Gb%d" % l, [128, 2, 512], F32), 2, "Gb%d" % l) for l in range(NL)]
    Ab = [Buf(self.sb("Ab%d" % l, [128, 2, 512], BF16), 2, "Ab%d" % l) for l in range(NL)]
    Ob = [Buf(self.sb("Ob%d" % l, [128, 2, 512], BF16), 2, "Ob%d" % l) for l in range(NL)]
    Qsb = [[(self.sb("Qsb%d_%d" % (l, i), [128, 512], BF16), Res("Qsb%d_%d" % (l, i)), self.dsem("Qsb%d_%d" % (l, i)))
            for i in range(2)] for l in range(NL)]
    osem = [[self.dsem("oo%d_%d" % (l, i)) for i in range(2)] for l in range(NL)]
    zt = self.sb("zpad", [128, nh, PRE], BF16)
    r_z = Res("zpad")
    P.op(P.pool, lambda: nc.gpsimd.memset(zt[:, :, :], 0.0), writes=[r_z])
    P.dma(P.sp, io["oT"][:, 0, :, 0:PRE].rearrange("h p n -> p h n"), zt[:, :, :], self.dsem("zpad"), reads=[r_z])
    Sbk = [self.banks[0:2], self.banks[2:4]]
    Rbk = self.banks[4:6]
    Obk = self.banks[6:8]
    loaded = {}

    def load_head(h):
        if h in loaded or h >= nh:
            return
        Kt, Kr, Ks = Ksb[h % 2]
        Vt, Vr, Vs = Vsb[h % 2]
        P.dma(P.sp, Kt[:, :], io["KT"][h, :, :], Ks, writes=[Kr])
        P.dma(P.sp, Vt[:, :, :], io["V"][:, h * 128:(h + 1) * 128].rearrange("(kb p) d -> p kb d", p=128), Vs, writes=[Vr])
        loaded[h] = True

    cnt = [0] * NL

    class Chain:
        pass

    def start_chain(l, h, j):
        load_head(h)
        if j == nq // 2:
            load_head(h + 1)
        c = Chain()
        c.l, c.h, c.j = l, h, j
        c.Kt, c.Kr, _ = Ksb[h % 2]
        c.Vt, c.Vr, _ = Vsb[h % 2]
        c.ci = cnt[l]
        cnt[l] += 1
        c.Qt, c.Qr, Qs = Qsb[l][c.ci % 2]
        P.dma(P.sp, c.Qt[:, :], io["QT"][h, :, j * 512:(j + 1) * 512], Qs, writes=[c.Qr])
        c.steps = list(range(4 * j + 3, -1, -1))
        c.N = len(c.steps)
        c.k = 0
        mm1(c, 0)
        if c.N > 1:
            mm1(c, 1)
        e_(c, 0)
        l_(c, 0)
        mm2(c, 0)
        return c

    def mm1(c, k):
        kb = c.steps[k]
        sbk = Sbk[c.l][k % 2]
        self.mm(sbk, [(c.Kt[:, kb * 128:(kb + 1) * 128], c.Qt[:, :])], 512, reads=[c.Kr, c.Qr])

    def e_(c, k):
        kb = c.steps[k]
        sbk = Sbk[c.l][k % 2]
        E = Eb[c.l]
        e = E.t[:, k % 3, :]
        self.A(e, sbk[0][:, :], AF.Exp, reads=[sbk[1]], writes=[E.r[k % 3]], scale=scale)
        i = kb - 4 * c.j
        if i >= 0:
            self.TT(e, e, masks[:, i, :], ALU.mult, reads=[E.r[k % 3], rc], writes=[E.r[k % 3]], eng=P.pool)

    def l_(c, k):
        E, L = Eb[c.l], Lb[c.l]
        self.A(L.t[:, k % 3, :], E.t[:, k % 3, :], AF.Ln, reads=[E.r[k % 3]], writes=[L.r[k % 3]], bias=1.0, scale=1.0)

    def mm2(c, k):
        L, Rb = Lb[c.l], Rbk[c.l]
        P.group(P.pe, [lambda: nc.tensor.matmul(Rb[0][:, :], tri[:, :], L.t[:, k % 3, :], start=(k == 0), stop=False)],
                reads=[L.r[k % 3], rc], writes=[Rb[1]])

    def g_(c, k):
        G, Rb = Gb[c.l], Rbk[c.l]
        self.A(G.t[:, k % 2, :], Rb[0][:, :], AF.Exp, reads=[Rb[1]], writes=[G.r[k % 2]], scale=-1.0)

    def a_(c, k):
        E, G, A_ = Eb[c.l], Gb[c.l], Ab[c.l]
        self.TT(A_.t[:, k % 2, :], E.t[:, k % 3, :], G.t[:, k % 2, :], ALU.mult,
                reads=[E.r[k % 3], G.r[k % 2]], writes=[A_.r[k % 2]])

    def mm3(c, k):
        L, Rb = Lb[c.l], Rbk[c.l]
        P.group(P.pe, [lambda: nc.tensor.matmul(Rb[0][:, :], comp[:, :], L.t[:, k % 3, :], start=False, stop=(k == c.N - 1))],
                reads=[L.r[k % 3], rc], writes=[Rb[1]])

    def mm4(c, k):
        A_, OB = Ab[c.l], Obk[c.l]
        kb = c.steps[k]
        P.group(P.pe, [lambda: nc.tensor.matmul(OB[0][:, :], c.Vt[:, kb, :], A_.t[:, k % 2, :], start=(k == 0), stop=(k == c.N - 1))],
                reads=[c.Vr, A_.r[k % 2]], writes=[OB[1]])

    def finish(c):
        l, h, j, ci = c.l, c.h, c.j, c.ci
        ob, OB = Ob[l], Obk[l]
        self.CP(ob.t[:, ci % 2, :], OB[0][:, :], reads=[OB[1]], writes=[ob.r[ci % 2]])
        sem = osem[l][ci % 2]
        pc0 = PRE + j * 512
        if pc0 + 512 <= NP2:
            P.dma(P.sp, io["oT"][h, 0, :, pc0:pc0 + 512], ob.t[:, ci % 2, :], sem, reads=[ob.r[ci % 2]])
            if pc0 + 512 > HALF:
                P.dma(P.sp, io["oT"][h, 1, :, 0:pc0 + 512 - HALF], ob.t[:, ci % 2, HALF - pc0:512], sem, reads=[ob.r[ci % 2]])
        else:
            P.dma(P.sp, io["oT"][h, 1, :, pc0 - HALF:pc0 - HALF + 512], ob.t[:, ci % 2, :], sem, reads=[ob.r[ci % 2]])

    work = [(h, j) for h in range(nh) for j in range(nq)]
    lanes = [None] * NL
    wi = 0
    while True:
        for l in range(NL):
            if lanes[l] is None and wi < len(work):
                lanes[l] = start_chain(l, *work[wi])
                wi += 1
        act = [c for c in lanes if c is not None]
        if not act:
            break
        for c in act:
            if c.k + 2 < c.N:
                mm1(c, c.k + 2)
        for c in act:
            if c.k + 1 < c.N:
                e_(c, c.k + 1)
        for c in act:
            if c.k + 1 < c.N:
                l_(c, c.k + 1)
        for c in act:
            g_(c, c.k)
        for c in act:
            a_(c, c.k)
        for c in act:
            mm3(c, c.k)
            if c.k + 1 < c.N:
                mm2(c, c.k + 1)
            if c.k >= 1:
                mm4(c, c.k - 1)
            if c.k == c.N - 1:
                mm4(c, c.k)
                finish(c)
                lanes[c.l] = None
            c.k += 1
    self.end_phase()


def phase_attn2(self, io, nq, nh):
    nc, P = self.nc, self.P
    self.begin_phase()
    T = nq * 512
    NKB = nq * 4
    scale = 1.0 / float(np.sqrt(128.0))
    rc = self.r_const
    masks = self.sb("masks", [128, 4, 2, 512], F32)
    tri = self.sb("tri", [128, 128], BF16)
    comp = self.sb("comp", [128, 128], BF16)
    onesw = self.sb("onesw", [128, 512], F32)
    P.op(P.pool, lambda: nc.gpsimd.memset(onesw[:, :], 1.0), writes=[rc])
    for i in range(4):
        for l in range(2):
            P.op(P.pool, lambda i=i, l=l: nc.gpsimd.affine_select(masks[:, i, l, :], onesw[:, :], [[1, 512]], ALU.is_gt, 0.0,
                                                                  base=-128 * i, channel_multiplier=-1), reads=[rc], writes=[rc])
    P.op(P.pool, lambda: nc.gpsimd.affine_select(tri[:, :], onesw[:, 0:128], [[-1, 128]], ALU.is_gt, 0.0,
                                                 base=1, channel_multiplier=1), reads=[rc], writes=[rc])
    P.op(P.pool, lambda: nc.gpsimd.affine_select(comp[:, :], onesw[:, 0:128], [[1, 128]], ALU.is_gt, 0.0,
                                                 base=0, channel_multiplier=-1), reads=[rc], writes=[rc])
    NB = 4
    Ksb = [(self.sb("Ksb%d" % i, [128, T], BF16), Res("Ksb%d" % i), self.dsem("Ksb%d" % i)) for i in range(NB)]
    Vsb = [(self.sb("Vsb%d" % i, [128, NKB, 128], BF16), Res("Vsb%d" % i), self.dsem("Vsb%d" % i)) for i in range(NB)]
    Eb = Buf(self.sb("Eb", [128, 3, 2, 512], F32), 3, "Eb")
    Lb = Buf(self.sb("Lb", [128, 3, 2, 512], BF16), 3, "Lb")
    Gb = Buf(self.sb("Gb", [128, 2, 2, 512], F32), 2, "Gb")
    Ab = Buf(self.sb("Ab", [128, 2, 2, 512], BF16), 2, "Ab")
    Ob = Buf(self.sb("Ob", [128, 2, 2, 512], BF16), 2, "Ob")
    NQB = 3
    Qsb = [(self.sb("Qsb%d" % i, [128, 2, 512], BF16), Res("Qsb%d" % i), self.dsem("Qsb%d" % i)) for i in range(NQB)]
    osem = [self.dsem("oo%d" % i) for i in range(2)]
    zt = self.sb("zpad", [128, nh, PRE], BF16)
    r_z = Res("zpad")
    P.op(P.pool, lambda: nc.gpsimd.memset(zt[:, :, :], 0.0), writes=[r_z])
    P.dma(P.sp, io["oT"][:, 0, :, 0:PRE].rearrange("h p n -> p h n"), zt[:, :, :], self.dsem("zpad"), reads=[r_z])
    Sp = [self.bankpairs[0], self.bankpairs[1]]
    Sr = [[self.banks[0][1], self.banks[1][1]], [self.banks[2][1], self.banks[3][1]]]
    Rp = self.bankpairs[2]
    Rr = [self.banks[4][1], self.banks[5][1]]
    Op = self.bankpairs[3]
    Or = [self.banks[6][1], self.banks[7][1]]
    loaded = {}

    def load_head(h):
        if h in loaded or h >= nh:
            return
        Kt, Kr, Ks = Ksb[h % NB]
        Vt, Vr, Vs = Vsb[h % NB]
        P.dma(P.sp, Kt[:, :], io["KT"][h, :, :], Ks, writes=[Kr])
        P.dma(P.sp, Vt[:, :, :], io["V"][:, h * 128:(h + 1) * 128].rearrange("(kb p) d -> p kb d", p=128), Vs, writes=[Vr])
        loaded[h] = True

    steps = []
    for hp in range(nh // 2):
        for j in range(nq):
            N = 4 * j + 4
            for k in range(N):
                steps.append((hp, j, k, N))
    NS = len(steps)
    qbuf = {}
    qcount = [0]

    def get_q(hp, j):
        key = (hp, j)
        if key not in qbuf:
            load_head(2 * hp)
            load_head(2 * hp + 1)
            if j == nq // 2:
                load_head(2 * hp + 2)
                load_head(2 * hp + 3)
            Qt, Qr, Qs = Qsb[qcount[0] % NQB]
            qcount[0] += 1
            for l in range(2):
                P.dma(P.sp, Qt[:, l, :], io["QT"][2 * hp + l, :, j * 512:(j + 1) * 512], Qs, writes=[Qr])
            qbuf[key] = (Qt, Qr)
        return qbuf[key]

    def mm1(s):
        hp, j, k, N = steps[s]
        kb = 4 * j + 3 - k
        Qt, Qr = get_q(hp, j)
        for l in range(2):
            h = 2 * hp + l
            Kt, Kr, _ = Ksb[h % NB]
            t = Sp[s % 2]
            P.group(P.pe, [lambda: nc.tensor.matmul(t[:, l, :], Kt[:, kb * 128:(kb + 1) * 128], Qt[:, l, :], start=True, stop=True)],
                    reads=[Kr, Qr], writes=[Sr[s % 2][l]])

    def el(s):
        hp, j, k, N = steps[s]
        kb = 4 * j + 3 - k
        self.A(Eb.t[:, s % 3, :, :], Sp[s % 2][:, :, :], AF.Exp, reads=Sr[s % 2], writes=[Eb.r[s % 3]], scale=scale)
        i = kb - 4 * j
        if i >= 0:
            self.TT(Eb.t[:, s % 3, :, :], Eb.t[:, s % 3, :, :], masks[:, i, :, :], ALU.mult, reads=[Eb.r[s % 3], rc],
                    writes=[Eb.r[s % 3]])
        self.A(Lb.t[:, s % 3, :, :], Eb.t[:, s % 3, :, :], AF.Ln, reads=[Eb.r[s % 3]], writes=[Lb.r[s % 3]], bias=1.0, scale=1.0)

    def mm2(s):
        hp, j, k, N = steps[s]
        for l in range(2):
            P.group(P.pe, [lambda: nc.tensor.matmul(Rp[:, l, :], tri[:, :], Lb.t[:, s % 3, l, :], start=(k == 0), stop=False)],
                    reads=[Lb.r[s % 3], rc], writes=[Rr[l]])

    def mm3(s):
        hp, j, k, N = steps[s]
        for l in range(2):
            P.group(P.pe, [lambda: nc.tensor.matmul(Rp[:, l, :], comp[:, :], Lb.t[:, s % 3, l, :], start=False, stop=(k == N - 1))],
                    reads=[Lb.r[s % 3], rc], writes=[Rr[l]])

    def mm4(s):
        hp, j, k, N = steps[s]
        kb = 4 * j + 3 - k
        for l in range(2):
            h = 2 * hp + l
            Vt, Vr, _ = Vsb[h % NB]
            P.group(P.pe, [lambda: nc.tensor.matmul(Op[:, l, :], Vt[:, kb, :], Ab.t[:, s % 2, l, :], start=(k == 0), stop=(k == N - 1))],
                    reads=[Vr, Ab.r[s % 2]], writes=[Or[l]])
        if k == N - 1:
            ci = hp * nq + j
            self.CP(Ob.t[:, ci % 2, :, :], Op[:, :, :], reads=Or, writes=[Ob.r[ci % 2]])
            sem = osem[ci % 2]
            pc0 = PRE + j * 512
            for l in range(2):
                h = 2 * hp + l
                ob = Ob.t[:, ci % 2, l, :]
                if pc0 + 512 <= NP2:
                    P.dma(P.sp, io["oT"][h, 0, :, pc0:pc0 + 512], ob, sem, reads=[Ob.r[ci % 2]])
                    if pc0 + 512 > HALF:
                        P.dma(P.sp, io["oT"][h, 1, :, 0:pc0 + 512 - HALF], Ob.t[:, ci % 2, l, HALF - pc0:512], sem, reads=[Ob.r[ci % 2]])
                else:
                    P.dma(P.sp, io["oT"][h, 1, :, pc0 - HALF:pc0 - HALF + 512], ob, sem, reads=[Ob.r[ci % 2]])

    self.bg_wo_queue = P.pool
    self.bg_pump()
    self.bg_wo_queue = None
    mm1(0)
    if NS > 1:
        mm1(1)
    el(0)
    mm2(0)
    for s in range(NS):
        if s + 2 < NS:
            mm1(s + 2)
        if s + 1 < NS:
            el(s + 1)
        self.A(Gb.t[:, s % 2, :, :], Rp[:, :, :], AF.Exp, reads=Rr, writes=[Gb.r[s % 2]], scale=-1.0)
        self.TT(Ab.t[:, s % 2, :, :], Eb.t[:, s % 3, :, :], Gb.t[:, s % 2, :, :], ALU.mult,
                reads=[Eb.r[s % 3], Gb.r[s % 2]], writes=[Ab.r[s % 2]])
        mm3(s)
        if s + 1 < NS:
            mm2(s + 1)
        if s >= 1:
            mm4(s - 1)
    mm4(NS - 1)
    self.end_phase()


Builder.phase_attn2 = phase_attn2


def phase_b2(self, io, tiles, dyn_o):
    nc, P = self.nc, self.P
    self.begin_phase()
    self.common_init()
    self.init_wstream(3)
    rv = self.r_vec
    mv = self.load_mods(io)
    sv = self.load_rows_T("svB", io["svecs2"], 2, D)
    fdw = self.load_rows_T("fdwB", io["ffn_dw"], 4, 2 * FF)
    pm = self.sb("pm_sb", [128, 1], F32)
    P.dma(P.sp, pm[:, :], io["pm"][:, :], self.dsem("pm"), writes=[rv])
    a3, sh3, g3 = self.mod_vecs(mv, 3, sv[:, :, 0], "ffn1")
    g_mix1 = mv[:, :, 8]
    afin = self.sb("afin", [128, KC], F32)
    self.TS(afin[:, :], sv[:, :, 1], float(np.sqrt(D)), None, ALU.mult, None, reads=[rv], writes=[rv])
    xT = Buf(self.sb("xT", [128, KC, 512], F32), KC, "xT")
    hT = Buf(self.sb("hT", [128, KC, 512], BF16), KC, "hT")
    gTt = self.sb("gTt", [128, FC, 512], BF16)
    gT = Buf(gTt, FC, "gT")
    yv = gTt[:, 0:32, :].bitcast(F32).rearrange("p a b -> p (a b)").rearrange("p (c n) -> p c n", c=KC)
    Tg = Buf(self.sb("Tg", [128, 2, 512], F32), 2, "Tg")
    Tv = Buf(self.sb("Tv", [128, 2, 512], F32), 2, "Tv")
    Hff = self.sb("Hff", [128, 2 * FC, 2], F32)
    rH = [Res("Hff%d" % i) for i in range(2 * FC)]
    P.op(P.pool, lambda: nc.gpsimd.memset(Hff[:, :, :], 0.0), writes=rH)
    for _ in tiles:
        for i in range(4):
            self.wplan_add(io["o_w"], D, [(512 * i, 512)])
        for i in range(22):
            self.wplan_add(io["up_w"], D, [(256 * i, 256), (FF + 256 * i, 256)])
        for i in range(16):
            self.wplan_add(io["down_w"], FF, [(128 * i, 128)])
    sx, so = self.dsem("ldx"), self.dsem("ldo")
    if dyn_o:
        par = self.par
    for ti, (start, n) in enumerate(tiles):
        P.dma(P.sp, xT.t[:, :, :n], io["x1T"][:, :, start:start + n].rearrange("c p n -> p c n"), sx, writes=xT.r)
        for rk in range(2):
            if n == 512:
                src = io["oT"][:, bass.ds(par, 1), rk, :, start:start + n].rearrange("h o p n -> p (h o) n")
                P.dma(P.sp, hT.t[:, 8 * rk:8 * rk + 8, :n], src, so, writes=hT.r[8 * rk:8 * rk + 8])
            else:
                parg = nc.gpsimd.partition_id() % 2
                src = io["oT"][:, bass.ds(parg, 1), rk, :, start:start + n].rearrange("h o p n -> p (h o) n")
                P.dma(P.pool, hT.t[:, 8 * rk:8 * rk + 8, :n], src, so, writes=hT.r[8 * rk:8 * rk + 8])

        def ep_o(bk, cid, n=n):
            self.STT(xT.t[:, cid, :n], bk[0][:, :n], g_mix1[:, cid:cid + 1], xT.t[:, cid, :n], ALU.mult, ALU.add,
                     reads=[bk[1], rv, xT.r[cid]], writes=[xT.r[cid]])
        self.linear(hT, KC, n, [[4 * i + q for q in range(4)] for i in range(4)], ep_o)
        self.rms_mod(xT, hT, n, a3, sh3)
        if ti == 0:
            self.TS(hT.t[:, :, 0:PRE], hT.t[:, :, 0:PRE], pm[:, 0:1], None, ALU.mult, None, reads=hT.r + [rv], writes=hT.r)
        self.ffn(hT, xT, gT, Tg, Tv, Hff, rH, fdw, g3, n)
        self.rms_mod_multi(xT, yv, [[gT.r[2 * c], gT.r[2 * c + 1]] for c in range(KC)], n, afin)
        nb = (n + 127) // 128
        for tb in range(nb):
            nt = min(128, n - tb * 128)
            orow, orr, osm = self.xrow[self.xrow_i % 2]
            self.xrow_i += 1
            for gq in range(4):
                bk = self.bank()
                fns = []
                for q in range(4):
                    c = 4 * gq + q
                    fns.append(lambda q=q, c=c, bk=bk: nc.tensor.transpose(bk[0][:nt, q * 128:(q + 1) * 128],
                                                                           yv[:, c, tb * 128:tb * 128 + nt], self.ident[:, :]))
                rr = []
                for q in range(4):
                    rr += [gT.r[2 * (4 * gq + q)], gT.r[2 * (4 * gq + q) + 1]]
                P.group(P.pe, fns, reads=rr + [self.r_const], writes=[bk[1]])
                if gq % 2 == 0:
                    P.op(P.act, lambda gq=gq, bk=bk: nc.scalar.copy(orow[:nt, gq * 512:(gq + 1) * 512], bk[0][:nt, :]),
                         reads=[bk[1]], writes=[orr])
                else:
                    self.CP(orow[:nt, gq * 512:(gq + 1) * 512], bk[0][:nt, :], reads=[bk[1]], writes=[orr])
            P.dma(P.sp, io["y"][start + tb * 128:start + tb * 128 + nt, :], orow[:nt, :], osm, reads=[orr])
    self.end_phase()


def rms_mod_multi(self, xT, yv, yres, n, a_vec):
    nc, P = self.nc, self.P
    sq = self.tmpbf
    bk = self.bank()
    for c in range(KC):
        self.A(sq.t[:, c % 2, :n], xT.t[:, c, :n], AF.Square, reads=[xT.r[c]], writes=[sq.r[c % 2]])
        P.group(P.pe, [lambda c=c: nc.tensor.matmul(bk[0][:, :n], self.ones_bf[:], sq.t[:, c % 2, :n],
                                                    start=(c == 0), stop=(c == KC - 1))],
                reads=[sq.r[c % 2], self.r_const], writes=[bk[1]])
    rstd = self.rstd
    self.A(rstd.t[:, 0, :n], bk[0][:, :n], AF.Ln, reads=[bk[1]], writes=[rstd.r[0]], bias=self.eps_rms[:, 0:1], scale=1.0)
    self.A(rstd.t[:, 0, :n], rstd.t[:, 0, :n], AF.Exp, reads=[rstd.r[0]], writes=[rstd.r[0]], scale=-0.5)
    for c in range(KC):
        tm = self.tmpf
        i = c % 2
        self.TT(tm.t[:, i, :n], xT.t[:, c, :n], rstd.t[:, 0, :n], ALU.mult, reads=[xT.r[c], rstd.r[0]], writes=[tm.r[i]])
        self.A(yv[:, c, :n], tm.t[:, i, :n], AF.Identity, reads=[tm.r[i], self.r_vec], writes=yres[c],
               bias=0.0, scale=a_vec[:, c:c + 1])


Builder.phase_mods = phase_mods
Builder.phase_w = phase_w
Builder.phase_qkv = phase_qkv
Builder.phase_attn = phase_attn
Builder.phase_b2 = phase_b2
Builder.rms_mod_multi = rms_mod_multi


def tiles_for(ntok):
    t = []
    st = 0
    while st < ntok:
        n = min(512, ntok - st)
        t.append((st, n))
        st += n
    return t


WSPEC = [
    ("mod0", D, 3 * D), ("mod1", D, 3 * D), ("mod2", D, 3 * D), ("mod3", D, 3 * D),
    ("pw1", D, 2 * D), ("pw2", D, D), ("up0", D, 2 * FF), ("down0", FF, D),
    ("qkv", D, 3 * D), ("ow", D, D), ("up1", D, 2 * FF), ("down1", FF, D),
]


def build_fused():
    nc = bass.Bass("TRN2", target_bir_lowering=False)
    ext = lambda name, shape, d=F32, kind="ExternalInput": nc.dram_tensor(name, shape, d, kind=kind).ap()
    itn = lambda name, shape, d: nc.dram_tensor(name, shape, d).ap()
    B = Builder(nc)
    W = {}
    n_first = 0
    for wi, (name, K_, N_) in enumerate(WSPEC):
        Ks = K_ // 4
        if name.startswith("mod"):
            W[name] = ext("w_" + name, [Ks, N_])
            continue
        wb = 1024 if K_ == D else 256
        nb = N_ // wb
        w = {"shard": ext("w_" + name, [Ks, N_]), "Ks": Ks, "wb": wb, "nb": nb, "res": Res("wf_" + name),
             "wsh": itn("wsh_" + name, [nb, Ks, wb], BF16), "full": itn("wf_" + name, [nb, K_, wb], BF16)}
        W[name] = w
        B.bg_add_weight(wi, w)
        if name == "pw2":
            n_first = len(B.bg)
        if name == "down0":
            B.n_bg_mid = len(B.bg) - n_first
        if name == "qkv":
            n_qkv_end = len(B.bg)
    B.bg_pump(n_first)
    B.n_bg_rest = n_qkv_end - n_first - B.n_bg_mid
    part = itn("modpart", [4, 2, 3 * D], F32)
    msum = itn("modsum", [4, 2, 3 * D], F32)
    mb = ext("mb", [4, 3 * D])
    B.phase_mods({"cq": ext("cq", [2, 512]), "mw": [W["mod%d" % m] for m in range(4)], "part": part, "msum": msum})
    x1T = itn("x1T", [KC, 128, NLOC], F32)
    h1loc = itn("h1loc", [KC, 128, NLOC], BF16)
    pm = ext("pm", [128, 1])
    B.phase_a({"xin": ext("xin", [NLOC, D]), "msum": msum, "mb": mb, "svecs": ext("svecs", [7, D]), "pw1b": ext("pw1b", [2, D]),
               "dww": ext("dww", [CW, D]), "ffn_dw": ext("ffn_dw0", [4, 2 * FF]), "pm": pm,
               "diagw": itn("diagw", [KC, 128, CW * 128], BF16),
               "pw1_w": W["pw1"], "pw2_w": W["pw2"], "up_w": W["up0"], "down_w": W["down0"],
               "x1T": x1T, "h1T": h1loc}, tiles_for(NLOC))
    B.bg_budget = 0
    B.bg_pump(max(0, n_qkv_end - B.bg_i))
    h1all = itn("h1all", [KC, 2, 128, NLOC], BF16)
    sem = B.dsem("ag_h1")
    for cc in range(KC):
        B.P.collective("AllGather", [h1loc[cc]], [h1all[cc].rearrange("r p n -> (r p) n")], PAIRS, sem)
    B.barrier(full=True)
    nh = NH // 2
    QT = itn("QT", [nh, 128, SEQ], BF16)
    KT = itn("KT", [nh, 128, SEQ], BF16)
    V = itn("V", [SEQ, nh * 128], BF16)
    oTloc = itn("oTloc", [nh, 2, 128, NP2], BF16)
    io = {"h1all": h1all, "qkv": W["qkv"], "QT": QT, "KT": KT, "V": V, "oT": oTloc}
    B.phase_qkv(io, HALF // 512, nh)
    B.phase_attn2(io, SEQ // 512, nh)
    oall = itn("oall", [nh, 2, 2, 128, NP2], BF16)
    sem = B.dsem("ag_o")
    for h in range(nh):
        for pt in range(2):
            B.P.collective("AllGather", [oTloc[h, pt]], [oall[h, pt].rearrange("r p n -> (r p) n")], PAIRS, sem)
    B.barrier(full=True)
    B.phase_b2({"x1T": x1T, "oT": oall, "msum": msum, "mb": mb, "svecs2": ext("svecs2", [2, D]), "ffn_dw": ext("ffn_dw1", [4, 2 * FF]),
                "pm": pm, "o_w": W["ow"], "up_w": W["up1"], "down_w": W["down1"],
                "y": ext("y", [NLOC, D], F32, "ExternalOutput")}, tiles_for(NLOC), dyn_o=True)
    B.barrier(full=True)
    return nc


def _f32(a):
    return np.ascontiguousarray(np.asarray(a, dtype=np.float32))


def kernel(x, c, mix_norm_g, mix_mod_w, mix_mod_b, cv_pw1_w, cv_pw1_b, cv_dw_w, cv_dw_b, cv_ln_g, cv_ln_b,
           cv_pw2_w, cv_pw2_b, sb_qkv_w, sb_o_w, ffn_norm_g, ffn_mod_w, ffn_mod_b, ffn_up_w, ffn_dw_w,
           ffn_dw_b, ffn_down_w, final_norm_g):
    x = np.asarray(x)
    c = np.asarray(c)
    cores = list(range(8))
    full = {"mod0": mix_mod_w[0], "mod1": ffn_mod_w[0], "mod2": mix_mod_w[1], "mod3": ffn_mod_w[1],
            "pw1": cv_pw1_w[0], "pw2": cv_pw2_w[0], "up0": ffn_up_w[0], "down0": ffn_down_w[0],
            "qkv": sb_qkv_w[0], "ow": sb_o_w[0], "up1": ffn_up_w[1], "down1": ffn_down_w[1]}
    shared = {
        "mb": _f32(np.stack([mix_mod_b[0], ffn_mod_b[0], mix_mod_b[1], ffn_mod_b[1]])),
        "svecs": _f32(np.stack([mix_norm_g[0], ffn_norm_g[0], mix_norm_g[1], cv_dw_b[0], cv_ln_g[0], cv_ln_b[0], cv_pw2_b[0]])),
        "pw1b": _f32(np.asarray(cv_pw1_b[0]).reshape(2, D)),
        "dww": _f32(cv_dw_w[0]),
        "ffn_dw0": _f32(np.concatenate([np.asarray(ffn_dw_w[0]), np.asarray(ffn_dw_b[0])[None]], 0)),
        "ffn_dw1": _f32(np.concatenate([np.asarray(ffn_dw_w[1]), np.asarray(ffn_dw_b[1])[None]], 0)),
        "svecs2": _f32(np.stack([ffn_norm_g[1], final_norm_g])),
    }
    ims = []
    for i in cores:
        b, r = i // 2, i % 2
        im = dict(shared)
        for name, K_, N_ in WSPEC:
            ks = K_ // 4
            q = i % 4
            im["w_" + name] = _f32(np.asarray(full[name])[q * ks:(q + 1) * ks])
        qd = i // 4
        im["cq"] = _f32(c[2 * qd:2 * qd + 2, 512 * q:512 * q + 512])
        im["pm"] = np.full((128, 1), float(r), np.float32)
        if r == 0:
            im["xin"] = _f32(np.concatenate([np.zeros((PRE, D), np.float32), x[b, 0:HALF]], 0))
        else:
            im["xin"] = _f32(x[b, HALF - PRE:SEQ])
        ims.append(im)
    res = run_bass_kernel_spmd(build_fused(), ims, core_ids=cores)
    out = np.empty((4, SEQ, D), np.float32)
    for i in cores:
        b, r = i // 2, i % 2
        out[b, HALF * r:HALF * (r + 1)] = np.asarray(res.results[i]["y"])[PRE:]
    return out


def build_attn_test(nq, nh):
    T = nq * 512
    nc = bass.Bass("TRN2", target_bir_lowering=False)
    dt = lambda name, shape, d=F32, kind="ExternalInput": nc.dram_tensor(name, shape, d, kind=kind).ap()
    io = {"QT": dt("QT", [nh, 128, T], BF16), "KT": dt("KT", [nh, 128, T], BF16), "V": dt("V", [T, nh * 128], BF16),
          "oT": dt("oT", [nh, 2, 128, NP2], BF16, "ExternalOutput")}
    B = Builder(nc)
    B.phase_attn2(io, nq, nh)
    B.barrier(full=True)
    return nc
```
